# Optimizing a Trainium2 kernel written in Bass

```python
import math
import jax, jax.numpy as jnp
from jax import lax
import numpy as np

D_MODEL = 2048
BATCH = 16
SEQ = 2048
DEPTH = 2
DEC_BATCH = 16
DEC_SEQ = 64
PAST_LEN = 4096

CHUNK = 64
GDN_HEADS = 16
GDN_DK = 128
GDN_DV = 128
GDN_WIDTH = GDN_HEADS * GDN_DV
CONV_W = 4
CONV_CH = 2 * GDN_HEADS * GDN_DK + GDN_WIDTH
ATT_HEADS = 16
KV_HEADS = 2
HEAD_DIM = 128
ATT_WIDTH = ATT_HEADS * HEAD_DIM
IDX_HEADS = 16
IDX_DIM = 64
TOPK_MAX = 256
Q_BLOCK = 128
REL_BUCKETS = 32
REL_MAX_DIST = 128
EPS = 1e-6

IN_SIZES = (CONV_CH, GDN_WIDTH, GDN_HEADS, GDN_HEADS,
            ATT_WIDTH, KV_HEADS * HEAD_DIM, KV_HEADS * HEAD_DIM, ATT_WIDTH,
            IDX_HEADS * IDX_DIM, IDX_DIM, IDX_HEADS,
            D_MODEL, D_MODEL)
IN_DIM = sum(IN_SIZES)

kernel_name = 'hybrid_gdn_dsa_stream_step'


def rmsnorm(x, w):
    xf = x.astype(jnp.float32)
    y = xf * lax.rsqrt(jnp.mean(xf * xf, axis=-1, keepdims=True) + EPS)
    return (y * w.astype(jnp.float32)).astype(x.dtype)


def l2norm(x):
    return x * lax.rsqrt(jnp.sum(x * x, axis=-1, keepdims=True) + EPS)


def causal_conv(x, prev, w):
    xp = jnp.concatenate([prev.astype(x.dtype), x], axis=1)
    T = x.shape[1]
    y = xp[:, 0:T] * w[0]
    for j in range(1, CONV_W):
        y = y + xp[:, j:j + T] * w[j]
    return y, xp[:, -(CONV_W - 1):]


def rel_bucket(rel):
    nb = REL_BUCKETS // 2
    max_exact = nb // 2
    n = jnp.abs(rel)
    nf = jnp.maximum(n, 1).astype(jnp.float32)
    large = max_exact + (jnp.log(nf / max_exact) / math.log(REL_MAX_DIST / max_exact)
                         * (nb - max_exact)).astype(jnp.int32)
    large = jnp.minimum(large, nb - 1)
    return jnp.where(rel > 0, nb, 0) + jnp.where(n < max_exact, n, large)


def gated_delta_rule(q, k, v, g, beta, S0):
    B, T, H, DK = q.shape
    C = min(CHUNK, T)
    N = T // C

    def chunks(a):
        a = a.reshape((B, N, C, H) + a.shape[3:])
        return jnp.moveaxis(a, (1, 3), (0, 2))

    q, k, v, g, beta = chunks(q), chunks(k), chunks(v), chunks(g), chunks(beta)
    gc = jnp.cumsum(g, axis=-1)
    idx = jnp.arange(C)
    tril = idx[:, None] >= idx[None, :]
    strict = idx[:, None] > idx[None, :]
    decay = jnp.exp(jnp.where(tril, gc[..., :, None] - gc[..., None, :], -jnp.inf))
    kb = k * beta[..., None]
    A = jnp.where(strict, jnp.einsum('nbhid,nbhjd->nbhij', kb, k) * decay, 0.0)
    eye = jnp.eye(C, dtype=jnp.float32)
    Tm = lax.linalg.triangular_solve(eye + A, jnp.broadcast_to(eye, A.shape),
                                     left_side=True, lower=True, unit_diagonal=True)
    u = jnp.einsum('nbhij,nbhjd->nbhid', Tm, v * beta[..., None])
    w = jnp.einsum('nbhij,nbhjd->nbhid', Tm, kb * jnp.exp(gc)[..., None])
    qk = jnp.einsum('nbhid,nbhjd->nbhij', q, k) * decay
    q_dec = q * jnp.exp(gc)[..., None]
    k_dec = k * jnp.exp(gc[..., -1:] - gc)[..., None]
    g_last = jnp.exp(gc[..., -1])

    def step(S, xs):
        u_i, w_i, qk_i, qd_i, kd_i, gl_i = xs
        v_new = u_i - jnp.einsum('bhcd,bhde->bhce', w_i, S)
        o_i = jnp.einsum('bhcd,bhde->bhce', qd_i, S) + jnp.einsum('bhij,bhje->bhie', qk_i, v_new)
        S = S * gl_i[..., None, None] + jnp.einsum('bhcd,bhce->bhde', kd_i, v_new)
        return S, o_i

    S, o = lax.scan(step, S0, (u, w, qk, q_dec, k_dec, g_last))
    o = jnp.moveaxis(o, (0, 2), (1, 3)).reshape(B, T, H, -1)
    return o, S


def sparse_attention(q, qi, wi, q_pos, K, V, KI, topk, rel_bias):
    B, Tq = q.shape[:2]
    L = K.shape[1]
    k_pos = jnp.arange(L)
    dots = jnp.einsum('bthd,bsd->bths', qi.astype(jnp.float32), KI.astype(jnp.float32))
    score = jnp.einsum('bth,bths->bts', wi.astype(jnp.float32), jax.nn.relu(dots))
    visible = (k_pos[None, :] // CHUNK) <= (q_pos[:, None] // CHUNK)
    score = jnp.where(visible[None], score, -jnp.inf)
    top_val, top_idx = lax.top_k(score, topk)
    valid = jnp.isfinite(top_val)
    Ks = jax.vmap(lambda kb, ib: kb[ib])(K, top_idx)
    Vs = jax.vmap(lambda vb, ib: vb[ib])(V, top_idx)
    G = ATT_HEADS // KV_HEADS
    qg = q.reshape(B, Tq, KV_HEADS, G, HEAD_DIM)
    logits = jnp.einsum('btngd,btsnd->btngs', qg, Ks).astype(jnp.float32) * (HEAD_DIM ** -0.5)
    rel = top_idx - q_pos[None, :, None]
    bias = rel_bias[rel_bucket(rel)].astype(jnp.float32)
    bias = jnp.moveaxis(bias.reshape(B, Tq, topk, KV_HEADS, G), 2, 4)
    logits = jnp.where(valid[:, :, None, None, :], logits + bias, -jnp.inf)
    p = jax.nn.softmax(logits, axis=-1).astype(V.dtype)
    o = jnp.einsum('btngs,btsnd->btngd', p, Vs)
    return o.reshape(B, Tq, ATT_WIDTH)


def trunk_layer(x, c, conv_prev, S0, past, norm_w, w_ada, b_ada, w_in, w_conv, a_log, dt_bias,
                gdn_norm_w, w_branch_a, w_branch_b, w_out, rel_bias):
    B, T, _ = x.shape
    shift, scale, gate = jnp.split(jax.nn.silu(c) @ w_ada + b_ada, 3, axis=-1)
    h = rmsnorm(x, norm_w) * (1.0 + scale[:, None]) + shift[:, None]
    proj = h @ w_in
    (qkv_a, z_a, b_a, a_a, q_b, k_b, v_b, z_b, q_i, k_i, w_i, gl_a, gl_b) = jnp.split(
        proj, np.cumsum(IN_SIZES)[:-1].tolist(), axis=-1)

    conv_out, conv_new = causal_conv(qkv_a, conv_prev, w_conv)
    conv_out = jax.nn.silu(conv_out).astype(jnp.float32)
    qa, ka, va = jnp.split(conv_out, [GDN_HEADS * GDN_DK, 2 * GDN_HEADS * GDN_DK], axis=-1)
    qa = l2norm(qa.reshape(B, T, GDN_HEADS, GDN_DK)) * (GDN_DK ** -0.5)
    ka = l2norm(ka.reshape(B, T, GDN_HEADS, GDN_DK))
    va = va.reshape(B, T, GDN_HEADS, GDN_DV)
    beta = jax.nn.sigmoid(b_a.astype(jnp.float32))
    g = -jnp.exp(a_log.astype(jnp.float32)) * jax.nn.softplus(a_a.astype(jnp.float32) + dt_bias.astype(jnp.float32))
    o_a, S_new = gated_delta_rule(qa, ka, va, g, beta, S0.astype(jnp.float32))
    o_a = rmsnorm(o_a, gdn_norm_w) * jax.nn.silu(z_a.astype(jnp.float32)).reshape(B, T, GDN_HEADS, GDN_DV)
    y_a = o_a.reshape(B, T, GDN_WIDTH).astype(x.dtype) @ w_branch_a

    q_b = q_b.reshape(B, T, ATT_HEADS, HEAD_DIM)
    k_b = k_b.reshape(B, T, KV_HEADS, HEAD_DIM)
    v_b = v_b.reshape(B, T, KV_HEADS, HEAD_DIM)
    q_i = q_i.reshape(B, T, IDX_HEADS, IDX_DIM)
    w_i = w_i * ((IDX_HEADS ** -0.5) * (IDX_DIM ** -0.5))
    if past is None:
        K, V, KI = k_b, v_b, k_i
        topk = min(TOPK_MAX, K.shape[1] // 4)
        qb = min(Q_BLOCK, T)
        nblk = T // qb

        def blk(a):
            return jnp.moveaxis(a.reshape((B, nblk, qb) + a.shape[2:]), 1, 0)

        pos = jnp.arange(T).reshape(nblk, qb)
        o_b = lax.map(lambda xs: sparse_attention(xs[0], xs[1], xs[2], xs[3], K, V, KI, topk, rel_bias),
                      (blk(q_b), blk(q_i), blk(w_i), pos))
        o_b = jnp.moveaxis(o_b, 0, 1).reshape(B, T, ATT_WIDTH)
    else:
        k_past, v_past, ki_past = past
        P = k_past.shape[1]
        K = jnp.concatenate([k_past.astype(k_b.dtype), k_b], axis=1)
        V = jnp.concatenate([v_past.astype(v_b.dtype), v_b], axis=1)
        KI = jnp.concatenate([ki_past.astype(k_i.dtype), k_i], axis=1)
        topk = min(TOPK_MAX, K.shape[1] // 4)
        o_b = sparse_attention(q_b, q_i, w_i, P + jnp.arange(T), K, V, KI, topk, rel_bias)
    y_b = (o_b * jax.nn.silu(z_b)) @ w_branch_b

    merged = jax.nn.sigmoid(gl_a) * y_a + jax.nn.sigmoid(gl_b) * y_b
    x = x + gate[:, None] * (merged @ w_out)
    return x, (k_b, v_b, k_i, S_new, conv_new)


def setup_inputs(seed: int = 0) -> dict:
    key = jax.random.key(seed)
    ks = jax.random.split(key, 24)
    f32 = jnp.float32

    def nrm(k, shape, s):
        return jax.random.normal(k, shape, f32) * s

    dt = jnp.exp(jax.random.uniform(ks[15], (DEPTH, GDN_HEADS), f32, math.log(1e-3), math.log(1e-1)))
    return {
        'x_prompt': nrm(ks[0], (BATCH, SEQ, D_MODEL), 1.0),
        'x_sample': nrm(ks[1], (DEC_BATCH, DEC_SEQ, D_MODEL), 1.0),
        'c_prompt': nrm(ks[2], (BATCH, D_MODEL), 1.0),
        'c_sample': nrm(ks[3], (DEC_BATCH, D_MODEL), 1.0),
        'cache_k': nrm(ks[4], (DEPTH, DEC_BATCH, PAST_LEN, KV_HEADS, HEAD_DIM), 1.0),
        'cache_v': nrm(ks[5], (DEPTH, DEC_BATCH, PAST_LEN, KV_HEADS, HEAD_DIM), 1.0),
        'cache_idx_k': nrm(ks[6], (DEPTH, DEC_BATCH, PAST_LEN, IDX_DIM), 1.0),
        'state_gdn': nrm(ks[7], (DEPTH, DEC_BATCH, GDN_HEADS, GDN_DK, GDN_DV), 0.1),
        'state_conv': nrm(ks[8], (DEPTH, DEC_BATCH, CONV_W - 1, CONV_CH), 1.0),
        'norm_w': 1.0 + nrm(ks[9], (DEPTH, D_MODEL), 0.02),
        'w_ada': nrm(ks[10], (DEPTH, D_MODEL, 3 * D_MODEL), 0.5 * D_MODEL ** -0.5),
        'b_ada': nrm(ks[11], (DEPTH, 3 * D_MODEL), 0.02),
        'w_in': nrm(ks[12], (DEPTH, D_MODEL, IN_DIM), D_MODEL ** -0.5),
        'w_conv': nrm(ks[13], (DEPTH, CONV_W, CONV_CH), CONV_W ** -0.5),
        'a_log': jnp.log(jax.random.uniform(ks[14], (DEPTH, GDN_HEADS), f32, 1.0, 16.0)),
        'dt_bias': dt + jnp.log(-jnp.expm1(-dt)),
        'gdn_norm_w': 1.0 + nrm(ks[16], (DEPTH, GDN_DV), 0.02),
        'w_branch_a': nrm(ks[17], (DEPTH, GDN_WIDTH, D_MODEL), GDN_WIDTH ** -0.5),
        'w_branch_b': nrm(ks[18], (DEPTH, ATT_WIDTH, D_MODEL), ATT_WIDTH ** -0.5),
        'w_out': nrm(ks[19], (DEPTH, D_MODEL, D_MODEL), D_MODEL ** -0.5),
        'rel_bias': nrm(ks[20], (REL_BUCKETS, ATT_HEADS), 0.5),
        'final_norm_w': 1.0 + nrm(ks[21], (D_MODEL,), 0.02),
    }


def reference(x_prompt, x_sample, c_prompt, c_sample, cache_k, cache_v, cache_idx_k, state_gdn, state_conv,
              norm_w, w_ada, b_ada, w_in, w_conv, a_log, dt_bias, gdn_norm_w, w_branch_a, w_branch_b,
              w_out, rel_bias, final_norm_w):
    xp, xs = x_prompt, x_sample
    new_p, new_s = [], []
    for l in range(DEPTH):
        lw = (norm_w[l], w_ada[l], b_ada[l], w_in[l], w_conv[l], a_log[l], dt_bias[l], gdn_norm_w[l],
              w_branch_a[l], w_branch_b[l], w_out[l], rel_bias)
        Bp = xp.shape[0]
        conv0 = jnp.zeros((Bp, CONV_W - 1, CONV_CH), xp.dtype)
        S0 = jnp.zeros((Bp, GDN_HEADS, GDN_DK, GDN_DV), jnp.float32)
        xp, sp = trunk_layer(xp, c_prompt, conv0, S0, None, *lw)
        xs, ss = trunk_layer(xs, c_sample, state_conv[l], state_gdn[l],
                             (cache_k[l], cache_v[l], cache_idx_k[l]), *lw)
        new_p.append(sp)
        new_s.append(ss)
    y_prompt = rmsnorm(xp, final_norm_w)
    y_sample = rmsnorm(xs, final_norm_w)
    new_k_prompt = jnp.stack([s[0] for s in new_p])
    new_v_prompt = jnp.stack([s[1] for s in new_p])
    new_idx_k_prompt = jnp.stack([s[2] for s in new_p])
    new_gdn_prompt = jnp.stack([s[3] for s in new_p])
    new_conv_prompt = jnp.stack([s[4] for s in new_p])
    new_k_sample = jnp.stack([s[0] for s in new_s])
    new_v_sample = jnp.stack([s[1] for s in new_s])
    new_idx_k_sample = jnp.stack([s[2] for s in new_s])
    new_gdn_sample = jnp.stack([s[3] for s in new_s])
    new_conv_sample = jnp.stack([s[4] for s in new_s])
    return (y_prompt, y_sample, new_k_prompt, new_v_prompt, new_idx_k_prompt, new_gdn_prompt, new_conv_prompt,
            new_k_sample, new_v_sample, new_idx_k_sample, new_gdn_sample, new_conv_sample)
```

```python
import contextlib
import numpy as np
import concourse.bass as bass
import concourse.mybir as mybir
from concourse.bass_utils import run_bass_kernel_spmd

F32 = mybir.dt.float32
BF16 = mybir.dt.bfloat16
AF = mybir.ActivationFunctionType
ALU = mybir.AluOpType
AX = mybir.AxisListType

NCORES = 8
GH = 16
HD = 128
CONV_CH = 6144
EPS = 1e-6
NIT = 14


def in_offsets(D):
    sizes = (6144, 2048, 16, 16, 2048, 256, 256, 2048, 1024, 64, 16, D, D)
    names = ("qkv", "za", "ba", "aa", "qb", "kb", "vb", "zb", "qi", "ki", "wi", "gla", "glb")
    off = {}
    o = 0
    for n, s in zip(names, sizes):
        off[n] = o
        o += s
    return off, o


class _Stop(Exception):
    pass


class Buf:
    def __init__(self, t, name):
        self.t = t
        self.name = name
        self.lw = None
        self.rd = []
        self.dsem = None

    def __getitem__(self, k):
        return self.t[k]


class Ctx:
    def __init__(self, nc, ndsem=40):
        self.nc = nc
        self.eng = {'pe': nc.tensor, 'act': nc.scalar, 'dve': nc.vector, 'pool': nc.gpsimd, 'sp': nc.sync}
        self.es = contextlib.ExitStack()
        self.sem = {}
        self.cnt = {}
        for e in self.eng:
            self.sem[e] = self.es.enter_context(nc.semaphore("s_" + e))
            self.cnt[e] = 0
        self.dsems = [self.es.enter_context(nc.semaphore("d%d" % i)) for i in range(ndsem)]
        self.dcnt = [0] * ndsem
        self.dnext = 0
        self.waited = {}

    def wait(self, e, tok):
        if tok is None:
            return
        sem, val = tok
        key = (e, id(sem))
        if self.waited.get(key, 0) >= val:
            return
        self.eng[e].wait_ge(sem, val)
        self.waited[key] = val

    def _deps(self, e, r, w):
        for b in r:
            self.wait(e, b.lw)
            if getattr(b, 'psum', False):
                for t in b.rd:
                    self.wait(e, t)
        for b in w:
            self.wait(e, b.lw)
            for t in b.rd:
                self.wait(e, t)

    def _commit(self, tok, r, w):
        for b in r:
            b.rd.append(tok)
            if len(b.rd) > 24:
                b.rd = b.rd[-24:]
        for b in w:
            b.lw = tok
            b.rd = []

    def op(self, e, fn, r=(), w=()):
        self._deps(e, r, w)
        ins = fn(self.eng[e])
        self.cnt[e] += 1
        ins.then_inc(self.sem[e], 1)
        tok = (self.sem[e], self.cnt[e])
        self._commit(tok, r, w)
        return tok

    def dma(self, q, out, in_, r=(), w=(), slow=False):
        self._deps(q, r, w)
        b = (list(w) + list(r))[0]
        if b.dsem is None:
            b.dsem = self.dnext % len(self.dsems)
            self.dnext += 1
        i = b.dsem
        kw = {}
        if slow:
            kw['allow_slow_non_contiguous'] = True
        self.eng[q].dma_start(out=out, in_=in_, **kw).then_inc(self.dsems[i], 16)
        self.dcnt[i] += 16
        tok = (self.dsems[i], self.dcnt[i])
        self._commit(tok, r, w)
        return tok

    def barrier(self):
        toks = [(self.sem[e], self.cnt[e]) for e in self.eng if self.cnt[e] > 0]
        toks += [(self.dsems[i], self.dcnt[i]) for i in range(len(self.dsems)) if self.dcnt[i] > 0]
        for e in self.eng:
            for t in toks:
                self.wait(e, t)


def build(cfg, debug_outs=()):
    D, T, P, DEPTH = cfg['D'], cfg['T'], cfg['P'], cfg['DEPTH']
    KC = D // 128
    NTP = T // 128
    NTOK = 2 * T + 128
    OFF, IN_DIM = in_offsets(D)
    LS = P + 64
    LMAX = max(T, LS)
    NKB = (LMAX + 127) // 128
    topk_p = min(256, T // 4)
    topk_s = min(256, LS // 4)

    nc = bass.Bass("TRN2", target_bir_lowering=False)

    def din(name, shape):
        return nc.dram_tensor(name, list(shape), F32, kind="ExternalInput").ap()

    def dout(name, shape):
        return nc.dram_tensor(name, list(shape), F32, kind="ExternalOutput").ap()

    def dscr(name, shape, dt):
        kind = "ExternalOutput" if name in debug_outs else "Internal"
        return nc.dram_tensor(name, list(shape), dt, kind=kind).ap()

    xp = din("xp", [2, T, D]); xs = din("xs", [2, 64, D]); c4 = din("c4", [4, D])
    ck = din("ck", [DEPTH, 2, P, 256]); cv = din("cv", [DEPTH, 2, P, 256]); cik = din("cik", [DEPTH, 2, P, 64])
    sg = din("sg", [DEPTH, 2, GH, 128, 128]); scv = din("scv", [DEPTH, 2, 3, CONV_CH])
    norm_w = din("norm_w", [DEPTH, D]); w_ada = din("w_ada", [DEPTH, D, 3 * D]); b_ada = din("b_ada", [DEPTH, 3 * D])
    w_in = din("w_in", [DEPTH, D, IN_DIM]); w_conv = din("w_conv", [DEPTH, 4, CONV_CH])
    a_log = din("a_log", [DEPTH, GH]); dt_bias = din("dt_bias", [DEPTH, GH]); gdn_norm_w = din("gdn_norm_w", [DEPTH, 128])
    w_a = din("w_branch_a", [DEPTH, 2048, D]); w_b = din("w_branch_b", [DEPTH, 2048, D]); w_out = din("w_out", [DEPTH, D, D])
    rel_bias = din("rel_bias", [32, 16]); fnw = din("final_norm_w", [D])
    ohr = din("ohr", [32, 384])

    yp = dout("yp", [2, T, D]); ys = dout("ys", [2, 64, D])
    nkp = dout("nkp", [DEPTH, 2, T, 256]); nvp = dout("nvp", [DEPTH, 2, T, 256]); nikp = dout("nikp", [DEPTH, 2, T, 64])
    ngp = dout("ngp", [DEPTH, 2, GH, 128, 128]); ncp = dout("ncp", [DEPTH, 2, 3, CONV_CH])
    nks = dout("nks", [DEPTH, 2, 64, 256]); nvs = dout("nvs", [DEPTH, 2, 64, 256]); niks = dout("niks", [DEPTH, 2, 64, 64])
    ngs = dout("ngs", [DEPTH, 2, GH, 128, 128]); ncs = dout("ncs", [DEPTH, 2, 3, CONV_CH])

    xs_d = dscr("xs_d", [NTOK, D], F32)
    mod_d = dscr("mod_d", [4, 3 * D], F32)
    tm_d = dscr("tm_d", [NTOK, 624], F32)
    tb_d = dscr("tb_d", [384, 16], F32)
    qT_d = dscr("qT_d", [2048, NTOK], BF16); kT_d = dscr("kT_d", [2048, NTOK], BF16); vT_d = dscr("vT_d", [2048, NTOK], BF16)
    zaT_d = dscr("zaT_d", [2048, NTOK], BF16); qbT_d = dscr("qbT_d", [2048, NTOK], BF16); zbT_d = dscr("zbT_d", [2048, NTOK], BF16)
    qiT_d = dscr("qiT_d", [1024, NTOK], BF16)
    sgaT_d = dscr("sgaT_d", [D, NTOK], BF16); sgbT_d = dscr("sgbT_d", [D, NTOK], BF16)
    ozT_d = dscr("ozT_d", [2048, NTOK], BF16); obT_d = dscr("obT_d", [2048, NTOK], BF16)

    c = Ctx(nc)
    uid = [0]

    def SB(es, shape, dt, name=None):
        uid[0] += 1
        nm = "%s_%d" % (name or "sb", uid[0])
        return Buf(es.enter_context(nc.sbuf_tensor(nm, list(shape), dt)), nm)

    def PS(es, shape, dt, name=None):
        uid[0] += 1
        nm = "%s_%d" % (name or "ps", uid[0])
        b = Buf(es.enter_context(nc.psum_tensor(nm, list(shape), dt)), nm)
        b.psum = True
        return b

    seqs = [('p', 0, 0, T), ('p', 1, T, T), ('s', 0, 2 * T, 64), ('s', 1, 2 * T + 64, 64)]
    halves = [(0, T), (T, T + 128)]

    with contextlib.ExitStack() as ges:
        identF = SB(ges, [128, 128], F32, "identF"); identB = SB(ges, [128, 128], BF16, "identB")
        onesB = SB(ges, [128, 128], BF16, "onesB"); onesF = SB(ges, [128, 128], F32, "onesF")
        Lm = SB(ges, [128, 128], F32, "Lm"); MLs = SB(ges, [128, 128], F32, "MLs")
        UP = SB(ges, [128, 128], F32, "UP"); LO = SB(ges, [128, 128], F32, "LO")
        NB = SB(ges, [128, 256, 16], F32, "NB"); NBh = SB(ges, [128, 16, 256], BF16, "NBh")
        CH = SB(ges, [128, 16], F32, "CH")
        epsc = SB(ges, [128, 1], F32, "epsc"); onec = SB(ges, [128, 1], F32, "onec"); zero3 = SB(ges, [128, 3], F32, "zero3")
        pow2 = SB(ges, [128, NIT], F32, "pow2")
        FNW = SB(ges, [128, D], F32, "FNW")

        def cmem(b, ap, v):
            c.op('pool', lambda e: e.memset(ap, v), w=[b])

        def csel(b, ap, pattern, op, fill, cm, base=0):
            c.op('pool', lambda e: e.affine_select(out=ap, in_=ap, pattern=pattern, compare_op=op, fill=fill,
                                                   base=base, channel_multiplier=cm), r=[b], w=[b])
        cmem(identF, identF[:], 1.0); csel(identF, identF[:], [[-1, 128]], ALU.is_equal, 0.0, 1)
        c.op('pool', lambda e: e.tensor_copy(out=identB[:], in_=identF[:]), r=[identF], w=[identB])
        cmem(onesB, onesB[:], 1.0); cmem(onesF, onesF[:], 1.0)
        cmem(Lm, Lm[:], 1.0); csel(Lm, Lm[:], [[1, 128]], ALU.is_ge, 0.0, -1)
        cmem(MLs, MLs[:], 1.0); csel(MLs, MLs[:], [[-1, 128]], ALU.is_gt, 0.0, 1)
        cmem(UP, UP[:], 1.0e4); csel(UP, UP[:], [[1, 128]], ALU.is_gt, 0.0, -1)
        cmem(LO, LO[:], 1.0e4); csel(LO, LO[:], [[-1, 128]], ALU.is_gt, 0.0, 1)
        cmem(epsc, epsc[:], EPS); cmem(onec, onec[:], 1.0); cmem(zero3, zero3[:], 0.0)
        for k in range(NIT):
            cmem(pow2, pow2[:, k:k + 1], 0.5 ** (k + 1))
        c.dma('sp', FNW[:], fnw.rearrange("(o d) -> o d", o=1).partition_broadcast(128), w=[FNW])
        c.dma('sp', CH[:], rel_bias[15:16, :].partition_broadcast(128), w=[CH])
        with contextlib.ExitStack() as es:
            oh = SB(es, [32, 384], F32); rb = SB(es, [32, 16], F32); tbs = SB(es, [128, 3, 16], F32)
            pt = PS(es, [128, 16], F32)
            c.dma('sp', oh[:], ohr[:, :], w=[oh]); c.dma('sp', rb[:], rel_bias[:, :], w=[rb])
            for j in range(3):
                c.op('pe', lambda e: e.matmul(pt[:], oh[:, j * 128:(j + 1) * 128], rb[:], start=True, stop=True), r=[oh, rb], w=[pt])
                c.op('dve', lambda e: e.tensor_tensor(out=tbs[:, j, :], in0=pt[:], in1=CH[:], op=ALU.subtract), r=[pt, CH], w=[tbs])
            c.dma('sp', tb_d.rearrange("(j p) h -> p j h", p=128), tbs[:], r=[tbs])
            c.barrier()
            for i in range(128):
                c.dma('sp', NB[i:i + 1, :, :], tb_d[127 - i:127 - i + 256, :].rearrange("(o c) h -> o c h", o=1), w=[NB])
            c.op('dve', lambda e: e.tensor_copy(out=NBh[:], in_=NB[:].rearrange("p c h -> p h c")), r=[NB], w=[NBh])
            c.barrier()

        def x_src(l, kind, idx, r0, n):
            if l == 0:
                return (xp if kind == 'p' else xs)[idx, r0:r0 + n, :]
            base = {('p', 0): 0, ('p', 1): T, ('s', 0): 2 * T, ('s', 1): 2 * T + 64}[(kind, idx)]
            return xs_d[base + r0:base + r0 + n, :]

        def range_tiles(tok0, ntok):
            tiles = []
            for a in range(tok0, tok0 + ntok, 128):
                if a < 2 * T:
                    parts = [('p', a // T, a % T, 128, 0)]
                else:
                    parts = [('s', 0, 0, 64, 0), ('s', 1, 0, 64, 64)]
                tiles.append((a - tok0, parts))
            return tiles

        def half_tiles(hf):
            return range_tiles(*halves[hf])
        oparts = [(0, T // 2), (T // 2, T // 2), (T, T // 2), (3 * T // 2, T // 2 + 128)]

        stop = cfg.get('stop')

        def chk(ph):
            if stop == ph:
                raise _Stop()
        try:
          chk('const')
          for l in range(DEPTH):
            with contextlib.ExitStack() as les:
                AT = SB(les, [128, KC, 4], F32, "AT"); BT = SB(les, [128, KC, 4], F32, "BT")
                wcT = SB(les, [128, 48, 4], F32, "wcT"); scT0 = SB(les, [128, 48, 3], F32, "scT0"); scT1 = SB(les, [128, 48, 3], F32, "scT1")
                gnw = SB(les, [128, 1], F32, "gnw"); dtb = SB(les, [128, 16], F32, "dtb"); nexpA = SB(les, [128, 16], F32, "nexpA")
                with contextlib.ExitStack() as es:
                    csb = SB(es, [4, D], F32); sc = SB(es, [4, D], F32); scT = SB(es, [128, KC, 4], F32)
                    mod = SB(es, [4, 3 * D], F32); bada = SB(es, [4, 3 * D], F32)
                    wst = [SB(es, [128, KC, 256], F32) for _ in range(2)]
                    modT = SB(es, [128, 2 * KC, 4], F32); nwT = SB(es, [128, KC], F32)
                    wc4 = SB(es, [4, CONV_CH], F32); sc3 = [SB(es, [3, CONV_CH], F32)] * 2
                    pt = PS(es, [128, 2 * KC, 4], F32); pm = [PS(es, [4, 512], F32) for _ in range(2)]
                    pw = PS(es, [128, 48, 4], F32)
                    c.dma('sp', csb[:], c4[:, :], w=[csb])
                    c.dma('sp', bada[:], b_ada[l:l + 1, :].partition_broadcast(4), w=[bada])
                    c.dma('sp', nwT[:], norm_w[l].rearrange("(k p) -> p k", p=128), w=[nwT], slow=True)
                    c.dma('sp', gnw[:], gdn_norm_w[l].rearrange("(p o) -> p o", o=1), w=[gnw], slow=True)
                    c.dma('sp', dtb[:], dt_bias[l:l + 1, :].partition_broadcast(128), w=[dtb])
                    c.dma('sp', nexpA[:], a_log[l:l + 1, :].partition_broadcast(128), w=[nexpA])
                    c.dma('sp', wc4[:], w_conv[l], w=[wc4])
                    c.op('act', lambda e: e.activation(out=nexpA[:], in_=nexpA[:], func=AF.Exp), r=[nexpA], w=[nexpA])
                    c.op('dve', lambda e: e.tensor_scalar(out=nexpA[:], in0=nexpA[:], scalar1=-1.0, scalar2=None, op0=ALU.mult), r=[nexpA], w=[nexpA])
                    c.op('act', lambda e: e.activation(out=sc[:], in_=csb[:], func=AF.Silu), r=[csb], w=[sc])
                    for kc in range(KC):
                        c.op('pe', lambda e: e.transpose(pt[:, kc, :], sc[:, kc * 128:(kc + 1) * 128], identF[0:4, 0:4]), r=[sc, identF], w=[pt])
                    c.op('dve', lambda e: e.tensor_copy(out=scT[:], in_=pt[:, 0:KC, :]), r=[pt], w=[scT])
                    for cb in range(48):
                        c.op('pe', lambda e: e.transpose(pw[:, cb, :], wc4[:, cb * 128:(cb + 1) * 128], identF[0:4, 0:4]), r=[wc4, identF], w=[pw])
                    c.op('dve', lambda e: e.tensor_copy(out=wcT[:], in_=pw[:]), r=[pw], w=[wcT])
                    for s, dst in ((0, scT0), (1, scT1)):
                        c.dma('sp', sc3[s][:], scv[l, s], w=[sc3[s]])
                        for cb in range(48):
                            c.op('pe', lambda e: e.transpose(pw[:, cb, 0:3], sc3[s][:, cb * 128:(cb + 1) * 128], identF[0:3, 0:3]), r=[sc3[s], identF], w=[pw])
                        c.op('dve', lambda e: e.tensor_copy(out=dst[:], in_=pw[:, :, 0:3]), r=[pw], w=[dst])
                    ncg = 3 * D // 256
                    for cg in range(ncg):
                        ws = wst[cg % 2]; pmm = pm[cg % 2]
                        c.dma('sp', ws[:], w_ada[l, :, cg * 256:(cg + 1) * 256].rearrange("(k p) n -> p k n", p=128), w=[ws])

                        def mm(e):
                            for kc in range(KC):
                                ins = e.matmul(pmm[:, 0:256], scT[:, kc, :], ws[:, kc, :], start=(kc == 0), stop=(kc == KC - 1))
                            return ins
                        c.op('pe', mm, r=[scT, ws], w=[pmm])
                        c.op('dve', lambda e: e.tensor_tensor(out=mod[:, cg * 256:(cg + 1) * 256], in0=pmm[:, 0:256], in1=bada[:, cg * 256:(cg + 1) * 256], op=ALU.add), r=[pmm, bada], w=[mod])
                    c.dma('pool', mod_d[:, :], mod[:], r=[mod])
                    for j in range(2 * KC):
                        c.op('pe', lambda e: e.transpose(pt[:, j, :], mod[:, j * 128:(j + 1) * 128], identF[0:4, 0:4]), r=[mod, identF], w=[pt])
                    c.op('dve', lambda e: e.tensor_copy(out=modT[:], in_=pt[:]), r=[pt], w=[modT])
                    c.op('dve', lambda e: e.tensor_copy(out=BT[:], in_=modT[:, 0:KC, :]), r=[modT], w=[BT])
                    c.op('dve', lambda e: e.scalar_tensor_tensor(out=AT[:], in0=modT[:, KC:2 * KC, :], scalar=1.0,
                                                                  in1=nwT[:].unsqueeze(2).to_broadcast([128, KC, 4]),
                                                                  op0=ALU.add, op1=ALU.mult), r=[modT, nwT], w=[AT])
                    c.barrier()

                chk('A')
                for hf, (htok0, hn) in enumerate(halves):
                    with contextlib.ExitStack() as es:
                        hT = SB(es, [128, KC, hn], BF16, "hT")
                        with contextlib.ExitStack() as es2:
                            xt = [SB(es2, [128, D], F32) for _ in range(2)]
                            junk = SB(es2, [128, D], F32)
                            xn = [SB(es2, [128, D], BF16) for _ in range(2)]
                            ssq = [SB(es2, [128, 1], F32) for _ in range(2)]
                            rstd = [SB(es2, [128, 1], F32) for _ in range(2)]
                            ptr = [PS(es2, [128, 4, 128], BF16) for _ in range(2)]
                            for ti, (t0, parts) in enumerate(half_tiles(hf)):
                                x_ = xt[ti % 2]; xn_ = xn[ti % 2]; ss_ = ssq[ti % 2]; rs_ = rstd[ti % 2]
                                for (kind, idx, r0, n, p0) in parts:
                                    c.dma('sp', x_[p0:p0 + n, :], x_src(l, kind, idx, r0, n), w=[x_])
                                c.op('act', lambda e: e.activation(out=junk[:], in_=x_[:], func=AF.Square, accum_out=ss_[:]), r=[x_], w=[junk, ss_])
                                c.op('act', lambda e: e.activation(out=ss_[:], in_=ss_[:], func=AF.Sqrt, scale=1.0 / D, bias=epsc[:]), r=[ss_, epsc], w=[ss_])
                                c.op('dve', lambda e: e.reciprocal(out=rs_[:], in_=ss_[:]), r=[ss_], w=[rs_])
                                c.op('dve', lambda e: e.tensor_scalar(out=xn_[:], in0=x_[:], scalar1=rs_[:], scalar2=None, op0=ALU.mult), r=[x_, rs_], w=[xn_])
                                for kg in range(KC // 4 if KC >= 4 else 1):
                                    kcs = list(range(kg * 4, min(KC, kg * 4 + 4)))
                                    p_ = ptr[kg % 2]

                                    def tr(e):
                                        for j, kc in enumerate(kcs):
                                            ins = e.transpose(p_[:, j, :], xn_[:, kc * 128:(kc + 1) * 128], identB[:])
                                        return ins
                                    c.op('pe', tr, r=[xn_, identB], w=[p_])
                                    for j, kc in enumerate(kcs):
                                        for (kind, idx, r0, n, p0) in parts:
                                            row = idx if kind == 'p' else 2 + idx
                                            c.op('act', lambda e: e.activation(out=hT[:, kc, t0 + p0:t0 + p0 + n], in_=p_[:, j, p0:p0 + n], func=AF.Identity,
                                                                               scale=AT[:, kc, row:row + 1], bias=BT[:, kc, row:row + 1]),
                                                 r=[p_, AT, BT], w=[hT])
                            c.barrier()
                        chk('N')
                        with contextlib.ExitStack() as es2:
                            wstg = [SB(es2, [128, KC, 256], F32) for _ in range(2)]
                            wbf = [SB(es2, [128, KC, 256], BF16) for _ in range(2)]
                            Wtm = SB(es2, [128, KC, 624], BF16)
                            wctr = [0]

                            def load_w(src, c0, ncol, dst=None, dcol=0):
                                i = wctr[0] % 2; wctr[0] += 1
                                st = wstg[i]
                                c.dma('sp', st[:, :, 0:ncol], src[:, c0:c0 + ncol].rearrange("(k p) n -> p k n", p=128), w=[st])
                                if dst is None:
                                    wb = wbf[i]
                                    c.op('dve', lambda e: e.tensor_copy(out=wb[:, :, 0:ncol], in_=st[:, :, 0:ncol]), r=[st], w=[wb])
                                    return wb
                                c.op('dve', lambda e: e.tensor_copy(out=dst[:, :, dcol:dcol + ncol], in_=st[:, :, 0:ncol]), r=[st], w=[dst])
                                return dst
                            W = w_in[l]
                            load_w(W, OFF['kb'], 256, Wtm, 0); load_w(W, OFF['vb'], 256, Wtm, 256)
                            load_w(W, OFF['ba'], 32, Wtm, 512); load_w(W, OFF['ki'], 80, Wtm, 544)
                            with contextlib.ExitStack() as es3:
                                ptm_a = [PS(es3, [128, 512], F32) for _ in range(2)]; ptm_b = [PS(es3, [128, 112], F32) for _ in range(2)]
                                ptm = [[ptm_a[0], ptm_b[0]], [ptm_a[1], ptm_b[1]]]
                                tms = [SB(es3, [128, 624], F32) for _ in range(2)]
                                for ti, (t0, parts) in enumerate(half_tiles(hf)):
                                    pa, pb = ptm[ti % 2]; tm_ = tms[ti % 2]

                                    def mm(e):
                                        for kc in range(KC):
                                            e.matmul(pa[:], hT[:, kc, t0:t0 + 128], Wtm[:, kc, 0:512], start=(kc == 0), stop=(kc == KC - 1))
                                        for kc in range(KC):
                                            ins = e.matmul(pb[:], hT[:, kc, t0:t0 + 128], Wtm[:, kc, 512:624], start=(kc == 0), stop=(kc == KC - 1))
                                        return ins
                                    c.op('pe', mm, r=[hT, Wtm], w=[pa, pb])
                                    c.op('act', lambda e: e.copy(out=tm_[:, 0:512], in_=pa[:]), r=[pa], w=[tm_])
                                    c.op('dve', lambda e: e.tensor_copy(out=tm_[:, 512:624], in_=pb[:]), r=[pb], w=[tm_])
                                    c.dma('pool', tm_d[htok0 + t0:htok0 + t0 + 128, :], tm_[:], r=[tm_])
                                    for (kind, idx, r0, n, p0) in parts:
                                        ok, ov, oi = (nkp, nvp, nikp) if kind == 'p' else (nks, nvs, niks)
                                        c.dma('pool', ok[l, idx, r0:r0 + n, :], tm_[p0:p0 + n, 0:256], r=[tm_])
                                        c.dma('pool', ov[l, idx, r0:r0 + n, :], tm_[p0:p0 + n, 256:512], r=[tm_])
                                        c.dma('pool', oi[l, idx, r0:r0 + n, :], tm_[p0:p0 + n, 544:608], r=[tm_])
                                c.barrier()
                            chunks = []
                            for t0 in range(0, T, 512):
                                wdt = min(512, T - t0)
                                chunks.append((t0, wdt, 'p', hf, t0 == 0, t0 + wdt == T))
                            if hf == 1:
                                chunks.append((T, 64, 's', 0, True, True)); chunks.append((T + 64, 64, 's', 1, True, True))
                            groups = [('conv', OFF['qkv'], 48, None), ('silu', OFF['za'], 16, zaT_d), ('qb', OFF['qb'], 16, qbT_d),
                                      ('silu', OFF['zb'], 16, zbT_d), ('copy', OFF['qi'], 8, qiT_d),
                                      ('sig', OFF['gla'], KC, sgaT_d), ('sig', OFF['glb'], KC, sgbT_d)]
                            with contextlib.ExitStack() as es3:
                                pp = [PS(es3, [128, 512], F32) for _ in range(3)]
                                pq = [PS(es3, [128, 512], F32) for _ in range(3)]
                                xpad = [SB(es3, [128, 3 + 512], F32) for _ in range(3)]
                                cvas = [SB(es3, [128, 512], F32) for _ in range(3)]
                                sqbs = [SB(es3, [128, 512], BF16) for _ in range(3)]; rns = [SB(es3, [128, 512], F32) for _ in range(3)]
                                ob = [SB(es3, [128, 512], BF16) for _ in range(3)]
                                pctr = [0]; octr = [0]
                                goff = OFF['qkv']
                                items = [(b2, bi, ch) for b2 in range(0, 48, 2) for bi in range(2) for ch in chunks]
                                NI = len(items)
                                wbd = {}

                                def st0_pe(i):
                                    b2, bi, (c0, wd, kind, idx, first, last) = items[i]
                                    if bi == 0 and (c0, wd) == chunks[0][0:2]:
                                        wbd[b2] = load_w(W, goff + b2 * 128, 256)
                                    wb = wbd[b2]; p_ = pp[i % 3]

                                    def mm(e):
                                        for kc in range(KC):
                                            ins = e.matmul(p_[:, 0:wd], wb[:, kc, bi * 128:(bi + 1) * 128], hT[:, kc, c0:c0 + wd], start=(kc == 0), stop=(kc == KC - 1))
                                        return ins
                                    c.op('pe', mm, r=[wb, hT], w=[p_])

                                def st0_rest(i):
                                    b2, bi, (c0, wd, kind, idx, first, last) = items[i]
                                    blk = b2 + bi
                                    p_ = pp[i % 3]; xq = xpad[i % 3]; cva = cvas[i % 3]
                                    if first:
                                        if kind == 'p':
                                            c.op('act', lambda e: e.activation(out=xq[:, 0:3], in_=zero3[:], func=AF.Copy), r=[zero3], w=[xq])
                                        else:
                                            sct = scT0 if idx == 0 else scT1
                                            c.op('act', lambda e: e.copy(out=xq[:, 0:3], in_=sct[:, blk, :]), r=[sct], w=[xq])
                                    else:
                                        xprev = xpad[(i - 1) % 3]; pw_ = items[i - 1][2][1]
                                        c.op('act', lambda e: e.copy(out=xq[:, 0:3], in_=xprev[:, pw_:pw_ + 3]), r=[xprev], w=[xq])
                                    c.op('act', lambda e: e.copy(out=xq[:, 3:3 + wd], in_=p_[:, 0:wd]), r=[p_], w=[xq])
                                    if last:
                                        oc = ncp if kind == 'p' else ncs
                                        c.dma('pool', oc[l, idx, :, blk * 128:(blk + 1) * 128].rearrange("j p -> p j"), xq[:, wd:wd + 3], r=[xq], slow=True)
                                    c.op('dve', lambda e: e.tensor_scalar(out=cva[:, 0:wd], in0=xq[:, 0:wd], scalar1=wcT[:, blk, 0:1], scalar2=None, op0=ALU.mult), r=[xq, wcT], w=[cva])
                                    for j in range(1, 4):
                                        c.op('dve', lambda e: e.scalar_tensor_tensor(out=cva[:, 0:wd], in0=xq[:, j:j + wd], scalar=wcT[:, blk, j:j + 1], in1=cva[:, 0:wd],
                                                                                      op0=ALU.mult, op1=ALU.add), r=[xq, wcT, cva], w=[cva])

                                def st1(i):
                                    b2, bi, (c0, wd, kind, idx, first, last) = items[i]
                                    blk = b2 + bi
                                    cva = cvas[i % 3]; sqb = sqbs[i % 3]; q_ = pq[i % 3]; o_ = ob[i % 3]
                                    tsl = slice(htok0 + c0, htok0 + c0 + wd)
                                    rn = rns[i % 3]
                                    c.op('act', lambda e: e.activation(out=rn[:, 0:wd], in_=cva[:, 0:wd], func=AF.Exp, scale=-1.0), r=[cva], w=[rn])
                                    c.op('act', lambda e: e.activation(out=rn[:, 0:wd], in_=rn[:, 0:wd], func=AF.Ln, bias=onec[:]), r=[rn, onec], w=[rn])
                                    c.op('act', lambda e: e.activation(out=rn[:, 0:wd], in_=rn[:, 0:wd], func=AF.Exp, scale=-1.0), r=[rn], w=[rn])
                                    if blk >= 32:
                                        c.op('dve', lambda e: e.tensor_tensor(out=o_[:, 0:wd], in0=cva[:, 0:wd], in1=rn[:, 0:wd], op=ALU.mult), r=[cva, rn], w=[o_])
                                        c.dma('pool', vT_d[(blk - 32) * 128:(blk - 31) * 128, tsl], o_[:, 0:wd], r=[o_])
                                    else:
                                        c.op('dve', lambda e: e.tensor_tensor(out=cva[:, 0:wd], in0=cva[:, 0:wd], in1=rn[:, 0:wd], op=ALU.mult), r=[cva, rn], w=[cva])
                                        c.op('act', lambda e: e.activation(out=sqb[:, 0:wd], in_=cva[:, 0:wd], func=AF.Square), r=[cva], w=[sqb])
                                        c.op('pe', lambda e: e.matmul(q_[:, 0:wd], onesB[:], sqb[:, 0:wd], start=True, stop=True), r=[onesB, sqb], w=[q_])

                                def st2(i):
                                    b2, bi, (c0, wd, kind, idx, first, last) = items[i]
                                    blk = b2 + bi
                                    if blk >= 32:
                                        return
                                    cva = cvas[i % 3]; rn = rns[i % 3]; q_ = pq[i % 3]; o_ = ob[i % 3]
                                    tsl = slice(htok0 + c0, htok0 + c0 + wd)
                                    c.op('act', lambda e: e.activation(out=rn[:, 0:wd], in_=q_[:, 0:wd], func=AF.Ln, bias=epsc[:]), r=[q_, epsc], w=[rn])
                                    c.op('act', lambda e: e.activation(out=rn[:, 0:wd], in_=rn[:, 0:wd], func=AF.Exp, scale=-0.5), r=[rn], w=[rn])
                                    qs = float(HD) ** -0.5 if blk < 16 else 1.0
                                    c.op('dve', lambda e: e.scalar_tensor_tensor(out=o_[:, 0:wd], in0=cva[:, 0:wd], scalar=qs, in1=rn[:, 0:wd], op0=ALU.mult, op1=ALU.mult),
                                         r=[cva, rn], w=[o_])
                                    dst = qT_d if blk < 16 else kT_d; drow = (blk % 16) * 128
                                    c.dma('pool', dst[drow:drow + 128, tsl], o_[:, 0:wd], r=[o_])
                                for t in range(NI + 2):
                                    if t < NI:
                                        st0_pe(t)
                                    if 0 <= t - 2 < NI:
                                        st2(t - 2)
                                    if 0 <= t - 1 < NI:
                                        st1(t - 1)
                                    if t < NI:
                                        st0_rest(t)
                                for (gk, goff, nblk, dst) in groups[1:]:
                                    for b2 in range(0, nblk, 2):
                                        wb = load_w(W, goff + b2 * 128, 256)
                                        for bi in range(2):
                                            blk = b2 + bi
                                            for (c0, wd, kind, idx, first, last) in chunks:
                                                p_ = pp[pctr[0] % 3]; pctr[0] += 1
                                                o_ = ob[octr[0] % 3]; octr[0] += 1

                                                def mm(e):
                                                    for kc in range(KC):
                                                        ins = e.matmul(p_[:, 0:wd], wb[:, kc, bi * 128:(bi + 1) * 128], hT[:, kc, c0:c0 + wd], start=(kc == 0), stop=(kc == KC - 1))
                                                    return ins
                                                c.op('pe', mm, r=[wb, hT], w=[p_])
                                                tsl = slice(htok0 + c0, htok0 + c0 + wd)
                                                if gk == 'silu':
                                                    c.op('act', lambda e: e.activation(out=o_[:, 0:wd], in_=p_[:, 0:wd], func=AF.Silu), r=[p_], w=[o_])
                                                elif gk == 'sig':
                                                    c.op('act', lambda e: e.activation(out=o_[:, 0:wd], in_=p_[:, 0:wd], func=AF.Sigmoid), r=[p_], w=[o_])
                                                elif gk == 'qb':
                                                    c.op('act', lambda e: e.activation(out=o_[:, 0:wd], in_=p_[:, 0:wd], func=AF.Copy, scale=float(HD) ** -0.5), r=[p_], w=[o_])
                                                else:
                                                    c.op('act', lambda e: e.copy(out=o_[:, 0:wd], in_=p_[:, 0:wd]), r=[p_], w=[o_])
                                                c.dma('pool', dst[blk * 128:(blk + 1) * 128, tsl], o_[:, 0:wd], r=[o_])
                        c.barrier()

                chk('P')
                with contextlib.ExitStack() as es:
                    S = SB(es, [128, GH, 128], F32, "S"); Sb = SB(es, [128, GH, 128], BF16, "Sb")
                    Ibc = SB(es, [128, 4, 128], F32, "Ibc")
                    for hh in range(4):
                        c.op('pool', lambda e: e.tensor_copy(out=Ibc[:, hh, :], in_=identF[:]), r=[identF], w=[Ibc])
                    qT = SB(es, [128, GH, 128], BF16); kT = SB(es, [128, GH, 128], BF16); vT = SB(es, [128, GH, 128], BF16); zaT = SB(es, [128, GH, 128], BF16)
                    bgr = SB(es, [128, 32], F32); beta = SB(es, [128, 16], F32); g_ = SB(es, [128, 16], F32)
                    gc = SB(es, [128, 16], F32); gcT = SB(es, [16, 128], F32); egc = SB(es, [128, 16], F32)
                    kdsc = SB(es, [128, 16], F32); gle = SB(es, [128, 16], F32); bege = SB(es, [128, 16], F32)
                    MB = SB(es, [128, 16, 128], F32); GB = SB(es, [128, 16, 128], F32)
                    Xs = [SB(es, [128, 4, 128], F32) for _ in range(2)] * 2
                    Dm = [SB(es, [128, 4, 128], F32) for _ in range(2)] * 2
                    DTm = [SB(es, [128, 4, 128], F32) for _ in range(2)] * 2
                    egr = [SB(es, [128, 4, 128], F32) for _ in range(2)] * 2
                    Pa = [[SB(es, [128, 4, 128], F32) for _ in range(2)] for _ in range(4)]
                    PTa = [[SB(es, [128, 4, 128], F32) for _ in range(2)] for _ in range(4)]
                    Y = [SB(es, [128, 4, 128], F32) for _ in range(4)]
                    YB = [SB(es, [128, 4, 128], BF16) for _ in range(4)]
                    vb = [SB(es, [128, 4, 128], BF16) for _ in range(4)]
                    kbg = [SB(es, [128, 4, 128], BF16) for _ in range(4)]
                    kdec = [SB(es, [128, 4, 128], BF16) for _ in range(4)]
                    wTb = [SB(es, [128, 4, 128], BF16) for _ in range(4)]
                    us = [SB(es, [128, 4, 128], F32) for _ in range(4)]
                    QKm = [SB(es, [128, 4, 128], BF16) for _ in range(4)]
                    qdT = [SB(es, [128, 4, 128], BF16) for _ in range(4)]
                    vn = [SB(es, [128, 4, 128], BF16) for _ in range(4)]
                    oTs = [SB(es, [128, 4, 128], F32) for _ in range(2)]
                    osq = [SB(es, [128, 4, 128], BF16) for _ in range(2)]
                    rst = [SB(es, [128, 4, 128], F32) for _ in range(2)]
                    ozs = [SB(es, [128, 4, 128], BF16) for _ in range(2)]
                    psA = [PS(es, [128, 4, 128], F32) for _ in range(5)]
                    psB = [PS(es, [128, 4, 128], BF16) for _ in range(2)]
                    psS = PS(es, [128, 128], F32)
                    pctr = [0]

                    def nps():
                        p = psA[pctr[0] % 5]; pctr[0] += 1
                        return p
                    for (kind, idx, tok0, slen) in seqs:
                        n = 128 if kind == 'p' else 64
                        nst = 6 if n == 128 else 5
                        if kind == 'p':
                            c.op('pool', lambda e: e.memset(S[:], 0.0), w=[S])
                        else:
                            c.dma('sp', S[:], sg[l, idx].rearrange("h k v -> k h v"), w=[S])
                        c.op('act', lambda e: e.copy(out=Sb[:], in_=S[:]), r=[S], w=[Sb])
                        for t in range(slen // n):
                            ts_ = slice(tok0 + t * n, tok0 + (t + 1) * n)
                            for (dstb, srcd) in ((qT, qT_d), (kT, kT_d), (vT, vT_d), (zaT, zaT_d)):
                                c.dma('sp', dstb[:, :, 0:n], srcd[:, ts_].rearrange("(h p) t -> p h t", p=128), w=[dstb])
                            c.dma('sp', bgr[0:n, :], tm_d[ts_, 512:544], w=[bgr])
                            c.op('act', lambda e: e.activation(out=beta[0:n], in_=bgr[0:n, 0:16], func=AF.Sigmoid), r=[bgr], w=[beta])
                            c.op('dve', lambda e: e.tensor_tensor(out=g_[0:n], in0=bgr[0:n, 16:32], in1=dtb[0:n], op=ALU.add), r=[bgr, dtb], w=[g_])
                            c.op('act', lambda e: e.activation(out=gc[0:n], in_=g_[0:n], func=AF.Abs), r=[g_], w=[gc])
                            c.op('act', lambda e: e.activation(out=gc[0:n], in_=gc[0:n], func=AF.Exp, scale=-1.0), r=[gc], w=[gc])
                            c.op('act', lambda e: e.activation(out=gc[0:n], in_=gc[0:n], func=AF.Ln, bias=onec[0:n]), r=[gc, onec], w=[gc])
                            c.op('dve', lambda e: e.scalar_tensor_tensor(out=g_[0:n], in0=g_[0:n], scalar=0.0, in1=gc[0:n], op0=ALU.max, op1=ALU.add), r=[g_, gc], w=[g_])
                            c.op('dve', lambda e: e.tensor_tensor(out=g_[0:n], in0=g_[0:n], in1=nexpA[0:n], op=ALU.mult), r=[g_, nexpA], w=[g_])
                            c.op('pe', lambda e: e.matmul(psS[0:n, 0:16], Lm[0:n, 0:n], g_[0:n, :], start=True, stop=True), r=[Lm, g_], w=[psS])
                            c.op('dve', lambda e: e.tensor_copy(out=gc[0:n], in_=psS[0:n, 0:16]), r=[psS], w=[gc])
                            c.op('pe', lambda e: e.matmul(psS[0:16, 0:n], g_[0:n, :], Lm[0:n, 0:n], start=True, stop=True), r=[Lm, g_], w=[psS])
                            c.op('dve', lambda e: e.tensor_copy(out=gcT[:, 0:n], in_=psS[0:16, 0:n]), r=[psS], w=[gcT])
                            c.op('pe', lambda e: e.matmul(psS[:, 0:16], onesF[0:n, :], g_[0:n, :], start=True, stop=True), r=[onesF, g_], w=[psS])
                            c.op('act', lambda e: e.activation(out=gle[:], in_=psS[:, 0:16], func=AF.Exp), r=[psS], w=[gle])
                            c.op('dve', lambda e: e.tensor_tensor(out=kdsc[0:n], in0=psS[0:n, 0:16], in1=gc[0:n], op=ALU.subtract), r=[psS, gc], w=[kdsc])
                            c.op('act', lambda e: e.activation(out=kdsc[0:n], in_=kdsc[0:n], func=AF.Exp), r=[kdsc], w=[kdsc])
                            c.op('act', lambda e: e.activation(out=egc[0:n], in_=gc[0:n], func=AF.Exp), r=[gc], w=[egc])
                            c.op('dve', lambda e: e.tensor_tensor(out=bege[0:n], in0=egc[0:n], in1=beta[0:n], op=ALU.mult), r=[egc, beta], w=[bege])
                            c.op('dve', lambda e: e.tensor_copy(out=GB[0:n], in_=g_[0:n, :].unsqueeze(2).to_broadcast([n, 16, 128])), r=[g_], w=[GB])
                            c.op('dve', lambda e: e.tensor_tensor(out=MB[0:n, :, 0:n], in0=MLs[0:n, 0:n].unsqueeze(1).to_broadcast([n, 16, n]),
                                                                   in1=beta[0:n, :].unsqueeze(2).to_broadcast([n, 16, n]), op=ALU.mult), r=[MLs, beta], w=[MB])
                            chk('G1')
                            for hg in range(4):
                                hs = slice(hg * 4, hg * 4 + 4)
                                pg = nps()

                                def mm(e):
                                    for hh in range(4):
                                        ins = e.matmul(pg[:, hh, 0:n], GB[0:n, hg * 4 + hh, :], Lm[0:n, 0:n], start=True, stop=True)
                                    return ins
                                c.op('pe', mm, r=[GB, Lm], w=[pg])
                                chk('G1a')
                                c.op('act', lambda e: e.activation(out=egr[hg][:, :, 0:n], in_=pg[:, :, 0:n], func=AF.Exp), r=[pg], w=[egr[hg]])
                                chk('G1b')
                                c.op('dve', lambda e: e.tensor_tensor(out=Xs[hg][0:n, :, 0:n], in0=pg[0:n, :, 0:n], in1=gc[0:n, hs].unsqueeze(2).to_broadcast([n, 4, n]), op=ALU.subtract),
                                     r=[pg, gc], w=[Xs[hg]])
                                chk('Ga')
                                c.op('dve', lambda e: e.tensor_tensor(out=Dm[hg][0:n, :, 0:n], in0=Xs[hg][0:n, :, 0:n], in1=UP[0:n, 0:n].unsqueeze(1).to_broadcast([n, 4, n]), op=ALU.add),
                                     r=[Xs[hg], UP], w=[Dm[hg]])
                                chk('H1')
                                c.op('act', lambda e: e.activation(out=Dm[hg][0:n, :, 0:n], in_=Dm[hg][0:n, :, 0:n], func=AF.Exp, scale=-1.0), r=[Dm[hg]], w=[Dm[hg]])
                                chk('H2')
                                c.op('pool', lambda e: e.tensor_tensor(out=DTm[hg][0:n, :, 0:n], in0=Xs[hg][0:n, :, 0:n], in1=LO[0:n, 0:n].unsqueeze(1).to_broadcast([n, 4, n]), op=ALU.subtract),
                                     r=[Xs[hg], LO], w=[DTm[hg]])
                                c.op('act', lambda e: e.activation(out=DTm[hg][0:n, :, 0:n], in_=DTm[hg][0:n, :, 0:n], func=AF.Exp), r=[DTm[hg]], w=[DTm[hg]])
                                chk('H3')
                                c.op('dve', lambda e: e.tensor_tensor(out=Dm[hg][0:n, :, 0:n], in0=Dm[hg][0:n, :, 0:n], in1=MB[0:n, hs, 0:n], op=ALU.mult), r=[Dm[hg], MB], w=[Dm[hg]])
                                chk('Gb')
                                pk = nps()

                                def mm(e):
                                    for hh in range(4):
                                        h = hg * 4 + hh
                                        ins = e.matmul(pk[0:n, hh, 0:n], kT[:, h, 0:n], kT[:, h, 0:n], start=True, stop=True)
                                    return ins
                                c.op('pe', mm, r=[kT], w=[pk])
                                A_ = PTa[hg][0]; B_ = Pa[hg][0]
                                c.op('dve', lambda e: e.tensor_tensor(out=A_[0:n, :, 0:n], in0=pk[0:n, :, 0:n], in1=Dm[hg][0:n, :, 0:n], op=ALU.mult), r=[pk, Dm[hg]], w=[A_])
                                pb_ = nps()

                                def mm(e):
                                    for hh in range(4):
                                        ins = e.transpose(pb_[0:n, hh, 0:n], A_[0:n, hh, 0:n], identF[0:n, 0:n])
                                    return ins
                                c.op('pe', mm, r=[A_, identF], w=[pb_])
                                c.op('act', lambda e: e.copy(out=B_[0:n, :, 0:n], in_=pb_[0:n, :, 0:n]), r=[pb_], w=[B_])
                                c.op('dve', lambda e: e.tensor_tensor(out=Y[hg][0:n, :, 0:n], in0=Ibc[0:n, :, 0:n], in1=pb_[0:n, :, 0:n], op=ALU.subtract), r=[pb_, Ibc], w=[Y[hg]])
                                chk('Gc')
                                ptk = psB[0]

                                def mm(e):
                                    for hh in range(4):
                                        ins = e.transpose(ptk[0:n, hh, :], kT[:, hg * 4 + hh, 0:n], identB[:])
                                    return ins
                                c.op('pe', mm, r=[kT, identB], w=[ptk])
                                c.op('dve', lambda e: e.tensor_tensor(out=kbg[hg][0:n], in0=ptk[0:n], in1=bege[0:n, hs].unsqueeze(2).to_broadcast([n, 4, 128]), op=ALU.mult), r=[ptk, bege], w=[kbg[hg]])
                                c.op('dve', lambda e: e.tensor_tensor(out=kdec[hg][0:n], in0=ptk[0:n], in1=kdsc[0:n, hs].unsqueeze(2).to_broadcast([n, 4, 128]), op=ALU.mult), r=[ptk, kdsc], w=[kdec[hg]])
                                ptv = psB[1]

                                def mm(e):
                                    for hh in range(4):
                                        ins = e.transpose(ptv[0:n, hh, :], vT[:, hg * 4 + hh, 0:n], identB[:])
                                    return ins
                                c.op('pe', mm, r=[vT, identB], w=[ptv])
                                c.op('dve', lambda e: e.tensor_tensor(out=vb[hg][0:n], in0=ptv[0:n], in1=beta[0:n, hs].unsqueeze(2).to_broadcast([n, 4, 128]), op=ALU.mult), r=[ptv, beta], w=[vb[hg]])
                                chk('Gd')
                                pqk = nps()

                                def mm(e):
                                    for hh in range(4):
                                        h = hg * 4 + hh
                                        ins = e.matmul(pqk[0:n, hh, 0:n], kT[:, h, 0:n], qT[:, h, 0:n], start=True, stop=True)
                                    return ins
                                c.op('pe', mm, r=[kT, qT], w=[pqk])
                                c.op('dve', lambda e: e.tensor_tensor(out=QKm[hg][0:n, :, 0:n], in0=pqk[0:n, :, 0:n], in1=DTm[hg][0:n, :, 0:n], op=ALU.mult), r=[pqk, DTm[hg]], w=[QKm[hg]])
                                c.op('pool', lambda e: e.tensor_tensor(out=qdT[hg][:, :, 0:n], in0=qT[:, hs, 0:n], in1=egr[hg][:, :, 0:n], op=ALU.mult), r=[qT, egr[hg]], w=[qdT[hg]])
                            chk('G2')
                            cur = 0
                            for m in range(nst):
                                lastm = (m == nst - 1)
                                for hg in range(4):
                                    P_ = Pa[hg][cur]; PT_ = PTa[hg][cur]; Pn = Pa[hg][1 - cur]; PTn = PTa[hg][1 - cur]
                                    if not lastm:
                                        p1 = nps()

                                        def mm(e):
                                            for hh in range(4):
                                                ins = e.matmul(p1[0:n, hh, 0:n], PT_[0:n, hh, 0:n], P_[0:n, hh, 0:n], start=True, stop=True)
                                            return ins
                                        c.op('pe', mm, r=[PT_, P_], w=[p1])
                                        c.op('act', lambda e: e.copy(out=Pn[0:n, :, 0:n], in_=p1[0:n, :, 0:n]), r=[p1], w=[Pn])
                                    p2 = nps()

                                    def mm(e):
                                        for hh in range(4):
                                            ins = e.matmul(p2[0:n, hh, 0:n], P_[0:n, hh, 0:n], PT_[0:n, hh, 0:n], start=True, stop=True)
                                        return ins
                                    c.op('pe', mm, r=[PT_, P_], w=[p2])
                                    c.op('dve', lambda e: e.tensor_copy(out=PTn[0:n, :, 0:n], in_=p2[0:n, :, 0:n]), r=[p2], w=[PTn])
                                    p3 = nps()

                                    def mm(e):
                                        for hh in range(4):
                                            ins = e.matmul(p3[0:n, hh, 0:n], PTn[0:n, hh, 0:n], Y[hg][0:n, hh, 0:n], start=True, stop=True)
                                        return ins
                                    c.op('pe', mm, r=[PTn, Y[hg]], w=[p3])
                                    c.op('dve', lambda e: e.tensor_tensor(out=Y[hg][0:n, :, 0:n], in0=Y[hg][0:n, :, 0:n], in1=p3[0:n, :, 0:n], op=ALU.add), r=[p3, Y[hg]], w=[Y[hg]])
                                cur = 1 - cur
                            chk('G3')
                            for hg in range(4):
                                hs = slice(hg * 4, hg * 4 + 4)
                                c.op('act', lambda e: e.copy(out=YB[hg][0:n, :, 0:n], in_=Y[hg][0:n, :, 0:n]), r=[Y[hg]], w=[YB[hg]])
                                pu = nps()

                                def mm(e):
                                    for hh in range(4):
                                        ins = e.matmul(pu[0:n, hh, :], YB[hg][0:n, hh, 0:n], vb[hg][0:n, hh, :], start=True, stop=True)
                                    return ins
                                c.op('pe', mm, r=[YB[hg], vb[hg]], w=[pu])
                                c.op('act', lambda e: e.copy(out=us[hg][0:n], in_=pu[0:n]), r=[pu], w=[us[hg]])
                                pw_ = nps()

                                def mm(e):
                                    for hh in range(4):
                                        ins = e.matmul(pw_[:, hh, 0:n], kbg[hg][0:n, hh, :], YB[hg][0:n, hh, 0:n], start=True, stop=True)
                                    return ins
                                c.op('pe', mm, r=[YB[hg], kbg[hg]], w=[pw_])
                                c.op('dve', lambda e: e.tensor_copy(out=wTb[hg][:, :, 0:n], in_=pw_[:, :, 0:n]), r=[pw_], w=[wTb[hg]])
                            chk('G4')
                            for hg in range(4):
                                hs = slice(hg * 4, hg * 4 + 4)
                                pv_ = nps()

                                def mm(e):
                                    for hh in range(4):
                                        ins = e.matmul(pv_[0:n, hh, :], wTb[hg][:, hh, 0:n], Sb[:, hg * 4 + hh, :], start=True, stop=True)
                                    return ins
                                c.op('pe', mm, r=[wTb[hg], Sb], w=[pv_])
                                c.op('dve', lambda e: e.tensor_tensor(out=vn[hg][0:n], in0=us[hg][0:n], in1=pv_[0:n], op=ALU.subtract), r=[us[hg], pv_], w=[vn[hg]])
                                po = nps()

                                def mm(e):
                                    for hh in range(4):
                                        e.matmul(po[:, hh, 0:n], Sb[:, hg * 4 + hh, :], qdT[hg][:, hh, 0:n], start=True, stop=False)
                                        ins = e.matmul(po[:, hh, 0:n], vn[hg][0:n, hh, :], QKm[hg][0:n, hh, 0:n], start=False, stop=True)
                                    return ins
                                c.op('pe', mm, r=[Sb, qdT[hg], vn[hg], QKm[hg]], w=[po])
                                o_ = oTs[hg % 2]; q_ = osq[hg % 2]; r_ = rst[hg % 2]; z_ = ozs[hg % 2]
                                c.op('act', lambda e: e.copy(out=o_[:, :, 0:n], in_=po[:, :, 0:n]), r=[po], w=[o_])
                                c.op('act', lambda e: e.activation(out=q_[:, :, 0:n], in_=po[:, :, 0:n], func=AF.Square), r=[po], w=[q_])
                                pss = nps()
                                if n == 128:
                                    c.op('pe', lambda e: e.matmul(pss[:].rearrange("p a b -> p (a b)"), onesB[:], q_[:].rearrange("p a b -> p (a b)"), start=True, stop=True), r=[onesB, q_], w=[pss])
                                else:
                                    def mm(e):
                                        for hh in range(4):
                                            ins = e.matmul(pss[:, hh, 0:n], onesB[:], q_[:, hh, 0:n], start=True, stop=True)
                                        return ins
                                    c.op('pe', mm, r=[onesB, q_], w=[pss])
                                c.op('act', lambda e: e.activation(out=r_[:, :, 0:n], in_=pss[:, :, 0:n], func=AF.Sqrt, scale=1.0 / 128, bias=epsc[:]), r=[pss, epsc], w=[r_])
                                c.op('dve', lambda e: e.reciprocal(out=r_[:, :, 0:n], in_=r_[:, :, 0:n]), r=[r_], w=[r_])
                                c.op('dve', lambda e: e.tensor_tensor(out=o_[:, :, 0:n], in0=o_[:, :, 0:n], in1=r_[:, :, 0:n], op=ALU.mult), r=[o_, r_], w=[o_])
                                c.op('dve', lambda e: e.scalar_tensor_tensor(out=z_[:, :, 0:n], in0=o_[:, :, 0:n], scalar=gnw[:, 0:1], in1=zaT[:, hs, 0:n], op0=ALU.mult, op1=ALU.mult),
                                     r=[o_, gnw, zaT], w=[z_])
                                c.dma('pool', ozT_d[hg * 512:(hg + 1) * 512, ts_].rearrange("(h p) t -> p h t", p=128), z_[:, :, 0:n], r=[z_])
                                pS = nps()

                                def mm(e):
                                    for hh in range(4):
                                        ins = e.matmul(pS[:, hh, :], kdec[hg][0:n, hh, :], vn[hg][0:n, hh, :], start=True, stop=True)
                                    return ins
                                c.op('pe', mm, r=[kdec[hg], vn[hg]], w=[pS])
                                c.op('pool', lambda e: e.tensor_tensor(out=S[:, hs, :], in0=S[:, hs, :], in1=gle[:, hs].unsqueeze(2).to_broadcast([128, 4, 128]), op=ALU.mult), r=[S, gle, Sb], w=[S])
                                c.op('dve', lambda e: e.tensor_tensor(out=S[:, hs, :], in0=S[:, hs, :], in1=pS[:], op=ALU.add), r=[S, pS], w=[S])
                            c.op('act', lambda e: e.copy(out=Sb[:], in_=S[:]), r=[S], w=[Sb])
                            chk('G5')
                        og = ngp if kind == 'p' else ngs
                        c.dma('pool', og[l, idx].rearrange("h k v -> k h v"), S[:], r=[S])
                    c.barrier()

                chk('G')
                with contextlib.ExitStack() as es:
                    KT = SB(es, [128, 2, NKB * 128], BF16, "KT"); Vt = SB(es, [128, NKB, 256], BF16, "Vt"); KIT = SB(es, [128, NKB * 128], BF16, "KIT")
                    kst = [SB(es, [128, 576], F32) for _ in range(2)]
                    kbf = [SB(es, [128, 384], BF16) for _ in range(2)]
                    qiT = SB(es, [128, 8, 128], BF16); qbT = SB(es, [128, 16, 128], BF16); zbT = SB(es, [128, 16, 128], BF16)
                    wi = SB(es, [128, 16], F32); absw = SB(es, [128, 16], F32); sgn = SB(es, [128, 16], F32)
                    scb = SB(es, [128, LMAX], F32, "scb"); mb = SB(es, [128, LMAX], BF16, "mb"); junk = SB(es, [128, LMAX], F32, "junkb")
                    rr = [SB(es, [128, 512], F32) for _ in range(2)]
                    hi = SB(es, [128, 1], F32); lo = SB(es, [128, 1], F32); lo2 = SB(es, [128, 1], F32); mid = SB(es, [128, 1], F32)
                    cnt = SB(es, [128, 1], F32); tmp1 = SB(es, [128, 1], F32); wv = SB(es, [128, NIT], F32)
                    pbuf = [SB(es, [128, LMAX], BF16) for _ in range(2)]
                    rs = [SB(es, [128, 16], F32) for _ in range(2)]
                    rinv = [SB(es, [128, 1], F32) for _ in range(2)]
                    pT = [SB(es, [128, NKB, 128], BF16) for _ in range(2)]
                    obs = [SB(es, [128, 4, 128], BF16) for _ in range(2)]
                    psd = [PS(es, [128, 512], F32) for _ in range(3)]
                    pst = [PS(es, [128, 4, 128], BF16) for _ in range(2)]
                    pso = [PS(es, [128, 4, 128], F32) for _ in range(2)]
                    dctr = [0]; tctr = [0]; kctr = [0]

                    def add_keys(src_k, src_v, src_ki, nrows, kbi):
                        st = kst[kctr[0] % 2]; bf = kbf[kctr[0] % 2]; kctr[0] += 1
                        c.dma('sp', st[0:nrows, 0:256], src_k, w=[st])
                        c.dma('sp', st[0:nrows, 256:512], src_v, w=[st])
                        c.dma('sp', st[0:nrows, 512:576], src_ki, w=[st])
                        c.op('pool', lambda e: e.tensor_copy(out=bf[0:nrows, 0:256], in_=st[0:nrows, 0:256]), r=[st], w=[bf])
                        c.op('pool', lambda e: e.tensor_copy(out=bf[0:nrows, 256:320], in_=st[0:nrows, 512:576]), r=[st], w=[bf])
                        c.op('pool', lambda e: e.tensor_copy(out=bf[0:nrows, 320:384], in_=st[0:nrows, 512:576]), r=[st], w=[bf])
                        c.op('pool', lambda e: e.tensor_copy(out=Vt[0:nrows, kbi, :], in_=st[0:nrows, 256:512]), r=[st], w=[Vt])
                        p_ = pst[tctr[0] % 2]; tctr[0] += 1

                        def mm(e):
                            for j in range(3):
                                ins = e.transpose(p_[:, j, 0:nrows], bf[0:nrows, j * 128:(j + 1) * 128], identB[0:nrows, 0:nrows])
                            return ins
                        c.op('pe', mm, r=[bf, identB], w=[p_])
                        c.op('act', lambda e: e.copy(out=KT[:, :, kbi * 128:kbi * 128 + nrows], in_=p_[:, 0:2, 0:nrows]), r=[p_], w=[KT])
                        c.op('act', lambda e: e.copy(out=KIT[:, kbi * 128:kbi * 128 + nrows], in_=p_[:, 2, 0:nrows]), r=[p_], w=[KIT])

                    for (kind, idx, tok0, slen) in seqs:
                        n = 128 if kind == 'p' else 64
                        topk = topk_p if kind == 'p' else topk_s
                        if kind == 's':
                            for kb_ in range(P // 128):
                                add_keys(ck[l, idx, kb_ * 128:(kb_ + 1) * 128, :], cv[l, idx, kb_ * 128:(kb_ + 1) * 128, :], cik[l, idx, kb_ * 128:(kb_ + 1) * 128, :], 128, kb_)
                        for t in range(slen // n):
                            ts_ = slice(tok0 + t * n, tok0 + (t + 1) * n)
                            pos0 = t * 128 if kind == 'p' else P
                            Lk = pos0 + n
                            add_keys(tm_d[ts_, 0:256], tm_d[ts_, 256:512], tm_d[ts_, 544:608], n, pos0 // 128)
                            c.dma('sp', qiT[:, :, 0:n], qiT_d[:, ts_].rearrange("(j p) t -> p j t", p=128), w=[qiT])
                            c.dma('sp', qbT[:, :, 0:n], qbT_d[:, ts_].rearrange("(h p) t -> p h t", p=128), w=[qbT])
                            c.dma('sp', zbT[:, :, 0:n], zbT_d[:, ts_].rearrange("(h p) t -> p h t", p=128), w=[zbT])
                            c.dma('sp', wi[0:n, :], tm_d[ts_, 608:624], w=[wi])
                            c.op('act', lambda e: e.activation(out=absw[0:n], in_=wi[0:n], func=AF.Abs), r=[wi], w=[absw])
                            c.op('act', lambda e: e.activation(out=sgn[0:n], in_=wi[0:n], func=AF.Sign), r=[wi], w=[sgn])
                            kchunks = [(c0, min(512, Lk - c0)) for c0 in range(0, Lk, 512)]
                            for (c0, wd) in kchunks:
                                for h in range(16):
                                    hb = 64 * (h % 2)
                                    p_ = psd[dctr[0] % 3]; r_ = rr[dctr[0] % 2]; dctr[0] += 1
                                    c.op('pe', lambda e: e.matmul(p_[0:n, 0:wd], qiT[hb:hb + 64, h // 2, 0:n], KIT[hb:hb + 64, c0:c0 + wd], start=True, stop=True), r=[qiT, KIT], w=[p_])
                                    c.op('act', lambda e: e.activation(out=r_[0:n, 0:wd], in_=p_[0:n, 0:wd], func=AF.Relu, scale=absw[0:n, h:h + 1]), r=[p_, absw], w=[r_])
                                    if h == 0:
                                        c.op('dve', lambda e: e.tensor_scalar(out=scb[0:n, c0:c0 + wd], in0=r_[0:n, 0:wd], scalar1=sgn[0:n, 0:1], scalar2=None, op0=ALU.mult), r=[r_, sgn], w=[scb])
                                    else:
                                        c.op('dve', lambda e: e.scalar_tensor_tensor(out=scb[0:n, c0:c0 + wd], in0=r_[0:n, 0:wd], scalar=sgn[0:n, h:h + 1], in1=scb[0:n, c0:c0 + wd],
                                                                                      op0=ALU.mult, op1=ALU.add), r=[r_, sgn, scb], w=[scb])
                            if kind == 'p':
                                c.op('dve', lambda e: e.tensor_reduce(out=lo[0:64], in_=scb[0:64, 0:Lk - 64], axis=AX.X, op=ALU.min), r=[scb], w=[lo])
                                c.op('dve', lambda e: e.tensor_reduce(out=lo[64:128], in_=scb[64:128, 0:Lk], axis=AX.X, op=ALU.min), r=[scb], w=[lo])
                                c.op('dve', lambda e: e.memset(scb[0:64, Lk - 64:Lk], -1.0e30), w=[scb])
                            else:
                                c.op('dve', lambda e: e.tensor_reduce(out=lo[0:n], in_=scb[0:n, 0:Lk], axis=AX.X, op=ALU.min), r=[scb], w=[lo])
                            c.op('dve', lambda e: e.tensor_reduce(out=hi[0:n], in_=scb[0:n, 0:Lk], axis=AX.X, op=ALU.max), r=[scb], w=[hi])
                            c.op('dve', lambda e: e.tensor_tensor(out=hi[0:n], in0=hi[0:n], in1=lo[0:n], op=ALU.subtract), r=[hi, lo], w=[hi])
                            c.op('dve', lambda e: e.tensor_scalar(out=wv[0:n], in0=pow2[0:n], scalar1=hi[0:n, 0:1], scalar2=None, op0=ALU.mult), r=[pow2, hi], w=[wv])
                            for k in range(NIT):
                                c.op('dve', lambda e: e.tensor_tensor(out=mid[0:n], in0=lo[0:n], in1=wv[0:n, k:k + 1], op=ALU.add), r=[lo, wv], w=[mid])
                                c.op('dve', lambda e: e.tensor_scalar(out=junk[0:n, 0:Lk], in0=scb[0:n, 0:Lk], scalar1=mid[0:n, 0:1], scalar2=0.0, op0=ALU.is_ge, op1=ALU.add, accum_out=cnt[0:n]),
                                     r=[scb, mid], w=[junk, cnt])
                                c.op('dve', lambda e: e.scalar_tensor_tensor(out=tmp1[0:n], in0=cnt[0:n], scalar=float(topk) - 0.5, in1=wv[0:n, k:k + 1], op0=ALU.is_ge, op1=ALU.mult),
                                     r=[cnt, wv], w=[tmp1])
                                c.op('dve', lambda e: e.tensor_tensor(out=lo[0:n], in0=lo[0:n], in1=tmp1[0:n], op=ALU.add), r=[lo, tmp1], w=[lo])
                            c.op('dve', lambda e: e.tensor_scalar(out=mb[0:n, 0:Lk], in0=scb[0:n, 0:Lk], scalar1=lo[0:n, 0:1], scalar2=-30000.0, op0=ALU.is_lt, op1=ALU.mult), r=[scb, lo], w=[mb])
                            nearw = min(256, Lk) if n == 128 else min(192, Lk)
                            near0 = Lk - nearw
                            nboff = (256 - nearw) if n == 128 else 0
                            if n == 64:
                                nboff = near0 - (pos0 - 128)
                            nkb = (Lk + 127) // 128
                            achunks = [(c0, min(512, near0 - c0), False) for c0 in range(0, near0, 512)] + [(near0, nearw, True)]
                            def att0(h):
                                g = h // 8
                                pb_ = pbuf[h % 2]; rs_ = rs[h % 2]; ri_ = rinv[h % 2]
                                for ci, (c0, wd, isnear) in enumerate(achunks):
                                    p_ = psd[dctr[0] % 3]; dctr[0] += 1

                                    def mm(e):
                                        e.matmul(p_[0:n, 0:wd], qbT[:, h, 0:n], KT[:, g, c0:c0 + wd], start=True, stop=False)
                                        ins = e.matmul(p_[0:n, 0:wd], identB[0:n, 0:n], mb[0:n, c0:c0 + wd], start=False, stop=(not isnear))
                                        if isnear:
                                            ins = e.matmul(p_[0:n, 0:wd], identB[0:n, 0:n], NBh[0:n, h, nboff:nboff + wd], start=False, stop=True)
                                        return ins
                                    c.op('pe', mm, r=[qbT, KT, mb, identB, NBh], w=[p_])
                                    c.op('act', lambda e: e.activation(out=pb_[0:n, c0:c0 + wd], in_=p_[0:n, 0:wd], func=AF.Exp, bias=CH[0:n, h:h + 1], accum_out=rs_[0:n, ci:ci + 1]),
                                         r=[p_, CH], w=[pb_, rs_])
                                c.op('dve', lambda e: e.tensor_reduce(out=ri_[0:n], in_=rs_[0:n, 0:len(achunks)], axis=AX.X, op=ALU.add), r=[rs_], w=[ri_])
                                c.op('dve', lambda e: e.reciprocal(out=ri_[0:n], in_=ri_[0:n]), r=[ri_], w=[ri_])
                                c.op('dve', lambda e: e.tensor_scalar(out=pb_[0:n, 0:Lk], in0=pb_[0:n, 0:Lk], scalar1=ri_[0:n, 0:1], scalar2=None, op0=ALU.mult), r=[pb_, ri_], w=[pb_])

                            def att1(h):
                                g = h // 8
                                pb_ = pbuf[h % 2]; pT_ = pT[h % 2]
                                for k4 in range(0, nkb, 4):
                                    ks = list(range(k4, min(nkb, k4 + 4)))
                                    p_ = pst[tctr[0] % 2]; tctr[0] += 1

                                    def mm(e):
                                        for j, kb_ in enumerate(ks):
                                            kw = min(128, Lk - kb_ * 128)
                                            ins = e.transpose(p_[0:kw, j, 0:n], pb_[0:n, kb_ * 128:kb_ * 128 + kw], identB[0:n, 0:n])
                                        return ins
                                    c.op('pe', mm, r=[pb_, identB], w=[p_])
                                    kwl = min(128, Lk - ks[-1] * 128)
                                    if kwl == 128:
                                        if (k4 // 4) % 2 == 0:
                                            c.op('act', lambda e: e.copy(out=pT_[:, k4:k4 + len(ks), 0:n], in_=p_[:, 0:len(ks), 0:n]), r=[p_], w=[pT_])
                                        else:
                                            c.op('dve', lambda e: e.tensor_copy(out=pT_[:, k4:k4 + len(ks), 0:n], in_=p_[:, 0:len(ks), 0:n]), r=[p_], w=[pT_])
                                    else:
                                        for j, kb_ in enumerate(ks):
                                            kw = min(128, Lk - kb_ * 128)
                                            c.op('act', lambda e: e.copy(out=pT_[0:kw, kb_, 0:n], in_=p_[0:kw, j, 0:n]), r=[p_], w=[pT_])
                                po = pso[(h // 4) % 2]

                                def mm(e):
                                    for kb_ in range(nkb):
                                        kw = min(128, Lk - kb_ * 128)
                                        ins = e.matmul(po[:, h % 4, 0:n], Vt[0:kw, kb_, g * 128:(g + 1) * 128], pT_[0:kw, kb_, 0:n], start=(kb_ == 0), stop=(kb_ == nkb - 1))
                                    return ins
                                c.op('pe', mm, r=[Vt, pT_], w=[po])
                                if h % 4 == 3:
                                    hg = h // 4
                                    o_ = obs[hg % 2]
                                    c.op('dve', lambda e: e.tensor_tensor(out=o_[:, :, 0:n], in0=po[:, :, 0:n], in1=zbT[:, hg * 4:hg * 4 + 4, 0:n], op=ALU.mult), r=[po, zbT], w=[o_])
                                    c.dma('pool', obT_d[hg * 512:(hg + 1) * 512, ts_].rearrange("(h p) t -> p h t", p=128), o_[:, :, 0:n], r=[o_])
                            for step in range(17):
                                if step < 16:
                                    att0(step)
                                if step >= 1:
                                    att1(step - 1)
                    c.barrier()

                chk('B')
                for (htok0, hn) in oparts:
                    with contextlib.ExitStack() as es:
                        m1 = SB(es, [128, KC, hn], BF16, "m1")
                        pp = [PS(es, [128, 512], F32) for _ in range(3)]
                        pctr = [0]
                        cchunks = [(c0, min(512, hn - c0)) for c0 in range(0, hn, 512)]
                        with contextlib.ExitStack() as es2:
                            big = SB(es2, [128, 16, hn], BF16, "big")
                            wstg = [SB(es2, [128, 16, 256], F32) for _ in range(2)]
                            wbf = [SB(es2, [128, 16, 256], BF16) for _ in range(2)]
                            sgs = [SB(es2, [128, 512], BF16) for _ in range(2)]
                            tmpb = [SB(es2, [128, 512], F32) for _ in range(2)]
                            wctr = [0]
                            for bi_, (srcd, wsrc, sgd) in enumerate(((ozT_d, w_a[l], sgaT_d), (obT_d, w_b[l], sgbT_d))):
                                for c0 in range(0, hn, 512):
                                    wd = min(512, hn - c0)
                                    c.dma('sp', big[:, :, c0:c0 + wd], srcd[:, htok0 + c0:htok0 + c0 + wd].rearrange("(k p) t -> p k t", p=128), w=[big])
                                for b2 in range(0, KC, 2):
                                    i = wctr[0] % 2; wctr[0] += 1
                                    st = wstg[i]; wb = wbf[i]
                                    c.dma('sp', st[:], wsrc[:, b2 * 128:b2 * 128 + 256].rearrange("(k p) n -> p k n", p=128), w=[st])
                                    c.op('dve', lambda e: e.tensor_copy(out=wb[:], in_=st[:]), r=[st], w=[wb])
                                    for bb in range(2):
                                        blk = b2 + bb
                                        for (c0, wd) in cchunks:
                                            p_ = pp[pctr[0] % 3]; s_ = sgs[pctr[0] % 2]; t_ = tmpb[pctr[0] % 2]; pctr[0] += 1
                                            c.dma('sp', s_[:, 0:wd], sgd[blk * 128:(blk + 1) * 128, htok0 + c0:htok0 + c0 + wd], w=[s_])

                                            def mm(e):
                                                for kc in range(16):
                                                    ins = e.matmul(p_[:, 0:wd], wb[:, kc, bb * 128:(bb + 1) * 128], big[:, kc, c0:c0 + wd], start=(kc == 0), stop=(kc == 15))
                                                return ins
                                            c.op('pe', mm, r=[wb, big], w=[p_])
                                            if bi_ == 0:
                                                c.op('dve', lambda e: e.tensor_tensor(out=m1[:, blk, c0:c0 + wd], in0=p_[:, 0:wd], in1=s_[:, 0:wd], op=ALU.mult), r=[p_, s_], w=[m1])
                                            else:
                                                c.op('dve', lambda e: e.tensor_tensor(out=t_[:, 0:wd], in0=p_[:, 0:wd], in1=s_[:, 0:wd], op=ALU.mult), r=[p_, s_], w=[t_])
                                                c.op('dve', lambda e: e.tensor_tensor(out=m1[:, blk, c0:c0 + wd], in0=m1[:, blk, c0:c0 + wd], in1=t_[:, 0:wd], op=ALU.add), r=[t_, m1], w=[m1])
                            c.barrier()
                        with contextlib.ExitStack() as es2:
                            Wo = SB(es2, [128, KC, D], BF16, "Wo")
                            wst2 = [SB(es2, [128, KC, 128], F32) for _ in range(2)]
                            for b1 in range(KC):
                                st = wst2[b1 % 2]
                                c.dma('sp', st[:], w_out[l, :, b1 * 128:b1 * 128 + 128].rearrange("(k p) n -> p k n", p=128), w=[st])
                                c.op('dve', lambda e: e.tensor_copy(out=Wo[:, :, b1 * 128:b1 * 128 + 128], in_=st[:]), r=[st], w=[Wo])
                            GATE = SB(es2, [128, D], F32, "GATE")
                            xt = [SB(es2, [128, D], F32) for _ in range(2)]
                            xo = [SB(es2, [128, D], F32) for _ in range(2)]
                            ssq = [SB(es2, [128, 1], F32) for _ in range(2)]
                            last_rows = None
                            ncg = max(1, D // 512)
                            cw = min(512, D)
                            for ti, (t0, parts) in enumerate(range_tiles(htok0, hn)):
                                x_ = xt[ti % 2]; xo_ = xo[ti % 2]; ss_ = ssq[ti % 2]
                                rows = tuple((idx if kind == 'p' else 2 + idx, p0, n) for (kind, idx, r0, n, p0) in parts)
                                if rows != last_rows:
                                    for (row, p0, n) in rows:
                                        c.dma('sp', GATE[p0:p0 + n, :], mod_d[row:row + 1, 2 * D:3 * D].partition_broadcast(n), w=[GATE])
                                    last_rows = rows
                                for (kind, idx, r0, n, p0) in parts:
                                    c.dma('sp', x_[p0:p0 + n, :], x_src(l, kind, idx, r0, n), w=[x_])
                                for cg in range(ncg):
                                    p_ = pp[pctr[0] % 3]; pctr[0] += 1
                                    cs = slice(cg * cw, (cg + 1) * cw)

                                    def mm(e):
                                        for kc in range(KC):
                                            ins = e.matmul(p_[:, 0:cw], m1[:, kc, t0:t0 + 128], Wo[:, kc, cs], start=(kc == 0), stop=(kc == KC - 1))
                                        return ins
                                    c.op('pe', mm, r=[m1, Wo], w=[p_])
                                    c.op('dve', lambda e: e.tensor_tensor(out=xo_[:, cs], in0=p_[:, 0:cw], in1=GATE[:, cs], op=ALU.mult), r=[p_, GATE], w=[xo_])
                                    c.op('dve', lambda e: e.tensor_tensor(out=xo_[:, cs], in0=xo_[:, cs], in1=x_[:, cs], op=ALU.add), r=[xo_, x_], w=[xo_])
                                if l < DEPTH - 1:
                                    c.dma('pool', xs_d[htok0 + t0:htok0 + t0 + 128, :], xo_[:], r=[xo_])
                                else:
                                    c.op('act', lambda e: e.activation(out=x_[:], in_=xo_[:], func=AF.Square, accum_out=ss_[:]), r=[xo_], w=[x_, ss_])
                                    c.op('act', lambda e: e.activation(out=ss_[:], in_=ss_[:], func=AF.Sqrt, scale=1.0 / D, bias=epsc[:]), r=[ss_, epsc], w=[ss_])
                                    c.op('dve', lambda e: e.reciprocal(out=ss_[:], in_=ss_[:]), r=[ss_], w=[ss_])
                                    c.op('dve', lambda e: e.scalar_tensor_tensor(out=xo_[:], in0=xo_[:], scalar=ss_[:, 0:1], in1=FNW[:], op0=ALU.mult, op1=ALU.mult), r=[xo_, ss_, FNW], w=[xo_])
                                    for (kind, idx, r0, n, p0) in parts:
                                        yo = yp if kind == 'p' else ys
                                        c.dma('pool', yo[idx, r0:r0 + n, :], xo_[p0:p0 + n, :], r=[xo_])
                        c.barrier()
        except _Stop:
            c.barrier()
            ges.pop_all()
            return nc
        c.barrier()
        c.es.close()
    return nc


def rel_bucket_np(rel):
    nb = 16
    max_exact = 8
    n = np.abs(rel)
    nf = np.maximum(n, 1).astype(np.float32)
    large = max_exact + (np.log(nf / max_exact) / np.float32(np.log(128 / max_exact)) * (nb - max_exact)).astype(np.int32)
    large = np.minimum(large, nb - 1)
    return np.where(rel > 0, nb, 0) + np.where(n < max_exact, n, large)


def make_ohr():
    r = np.arange(384) - 255
    b = rel_bucket_np(r)
    oh = np.zeros((32, 384), np.float32)
    oh[b, np.arange(384)] = 1.0
    return oh


_NC_CACHE = {}


def run(cfg, inputs, debug_outs=()):
    D, T, P, DEPTH = cfg['D'], cfg['T'], cfg['P'], cfg['DEPTH']
    key = (D, T, P, DEPTH, tuple(debug_outs))
    if key not in _NC_CACHE:
        _NC_CACHE[key] = build(cfg, debug_outs)
    nc = _NC_CACHE[key]
    f = lambda a: np.ascontiguousarray(np.asarray(a, dtype=np.float32))
    I = {k: f(v) for k, v in inputs.items()}
    ohr = make_ohr()
    in_maps = []
    for ci in range(NCORES):
        b = slice(2 * ci, 2 * ci + 2)
        m = {
            "xp": f(I['x_prompt'][b]), "xs": f(I['x_sample'][b]),
            "c4": f(np.concatenate([I['c_prompt'][b], I['c_sample'][b]], axis=0)),
            "ck": f(I['cache_k'][:, b].reshape(DEPTH, 2, P, 256)), "cv": f(I['cache_v'][:, b].reshape(DEPTH, 2, P, 256)),
            "cik": f(I['cache_idx_k'][:, b]), "sg": f(I['state_gdn'][:, b]), "scv": f(I['state_conv'][:, b]),
            "ohr": ohr,
        }
        for k in ("norm_w", "w_ada", "b_ada", "w_in", "w_conv", "a_log", "dt_bias", "gdn_norm_w", "w_branch_a", "w_branch_b", "w_out", "rel_bias", "final_norm_w"):
            m[k] = I[k]
        in_maps.append(m)
    res = run_bass_kernel_spmd(nc, in_maps, core_ids=list(range(NCORES)))
    R = res.results
    cat0 = lambda k: np.concatenate([r[k] for r in R], axis=0)
    cat1 = lambda k: np.concatenate([r[k] for r in R], axis=1)
    B = 2 * NCORES
    outs = (
        cat0("yp"), cat0("ys"),
        cat1("nkp").reshape(DEPTH, B, T, 2, 128), cat1("nvp").reshape(DEPTH, B, T, 2, 128), cat1("nikp"),
        cat1("ngp"), cat1("ncp"),
        cat1("nks").reshape(DEPTH, B, 64, 2, 128), cat1("nvs").reshape(DEPTH, B, 64, 2, 128), cat1("niks"),
        cat1("ngs"), cat1("ncs"),
    )
    if debug_outs:
        return outs, R
    return outs


def kernel(**inputs):
    cfg = dict(D=2048, T=2048, P=4096, DEPTH=2)
    return run(cfg, inputs)
```

```python
import contextlib
import numpy as np
import concourse.bass as bass
import concourse.mybir as mybir
from concourse.bass_utils import run_bass_kernel_spmd

F32 = mybir.dt.float32
BF16 = mybir.dt.bfloat16
AF = mybir.ActivationFunctionType
ALU = mybir.AluOpType
AX = mybir.AxisListType

NCORES = 8
GH = 16
HD = 128
CONV_CH = 6144
EPS = 1e-6
NIT = 14


def in_offsets(D):
    sizes = (6144, 2048, 16, 16, 2048, 256, 256, 2048, 1024, 64, 16, D, D)
    names = ("qkv", "za", "ba", "aa", "qb", "kb", "vb", "zb", "qi", "ki", "wi", "gla", "glb")
    off = {}
    o = 0
    for n, s in zip(names, sizes):
        off[n] = o
        o += s
    return off, o


class _Stop(Exception):
    pass


class Buf:
    def __init__(self, t, name):
        self.t = t
        self.name = name
        self.lw = None
        self.rd = []
        self.dsem = None

    def __getitem__(self, k):
        return self.t[k]


class Ctx:
    def __init__(self, nc, ndsem=40):
        self.nc = nc
        self.eng = {'pe': nc.tensor, 'act': nc.scalar, 'dve': nc.vector, 'pool': nc.gpsimd, 'sp': nc.sync}
        self.es = contextlib.ExitStack()
        self.sem = {}
        self.cnt = {}
        for e in self.eng:
            self.sem[e] = self.es.enter_context(nc.semaphore("s_" + e))
            self.cnt[e] = 0
        self.dsems = [self.es.enter_context(nc.semaphore("d%d" % i)) for i in range(ndsem)]
        self.dcnt = [0] * ndsem
        self.dnext = 0
        self.waited = {}

    def wait(self, e, tok):
        if tok is None:
            return
        sem, val = tok
        key = (e, id(sem))
        if self.waited.get(key, 0) >= val:
            return
        self.eng[e].wait_ge(sem, val)
        self.waited[key] = val

    def _deps(self, e, r, w):
        for b in r:
            self.wait(e, b.lw)
            if getattr(b, 'psum', False):
                for t in b.rd:
                    self.wait(e, t)
        for b in w:
            self.wait(e, b.lw)
            for t in b.rd:
                self.wait(e, t)

    def _commit(self, tok, r, w):
        for b in r:
            b.rd.append(tok)
            if len(b.rd) > 24:
                b.rd = b.rd[-24:]
        for b in w:
            b.lw = tok
            b.rd = []

    def op(self, e, fn, r=(), w=()):
        self._deps(e, r, w)
        ins = fn(self.eng[e])
        self.cnt[e] += 1
        ins.then_inc(self.sem[e], 1)
        tok = (self.sem[e], self.cnt[e])
        self._commit(tok, r, w)
        return tok

    def dma(self, q, out, in_, r=(), w=(), slow=False):
        self._deps(q, r, w)
        b = (list(w) + list(r))[0]
        if b.dsem is None:
            b.dsem = self.dnext % len(self.dsems)
            self.dnext += 1
        i = b.dsem
        kw = {}
        if slow:
            kw['allow_slow_non_contiguous'] = True
        self.eng[q].dma_start(out=out, in_=in_, **kw).then_inc(self.dsems[i], 16)
        self.dcnt[i] += 16
        tok = (self.dsems[i], self.dcnt[i])
        self._commit(tok, r, w)
        return tok

    def barrier(self):
        toks = [(self.sem[e], self.cnt[e]) for e in self.eng if self.cnt[e] > 0]
        toks += [(self.dsems[i], self.dcnt[i]) for i in range(len(self.dsems)) if self.dcnt[i] > 0]
        for e in self.eng:
            for t in toks:
                self.wait(e, t)


def build(cfg, debug_outs=()):
    D, T, P, DEPTH = cfg['D'], cfg['T'], cfg['P'], cfg['DEPTH']
    KC = D // 128
    NTP = T // 128
    NTOK = 2 * T + 128
    OFF, IN_DIM = in_offsets(D)
    LS = P + 64
    LMAX = max(T, LS)
    NKB = (LMAX + 127) // 128
    topk_p = min(256, T // 4)
    topk_s = min(256, LS // 4)

    nc = bass.Bass("TRN2", target_bir_lowering=False)

    def din(name, shape):
        return nc.dram_tensor(name, list(shape), F32, kind="ExternalInput").ap()

    def dout(name, shape):
        return nc.dram_tensor(name, list(shape), F32, kind="ExternalOutput").ap()

    def dscr(name, shape, dt):
        kind = "ExternalOutput" if name in debug_outs else "Internal"
        return nc.dram_tensor(name, list(shape), dt, kind=kind).ap()

    xp = din("xp", [2, T, D]); xs = din("xs", [2, 64, D]); c4 = din("c4", [4, D])
    ck = din("ck", [DEPTH, 2, P, 256]); cv = din("cv", [DEPTH, 2, P, 256]); cik = din("cik", [DEPTH, 2, P, 64])
    sg = din("sg", [DEPTH, 2, GH, 128, 128]); scv = din("scv", [DEPTH, 2, 3, CONV_CH])
    norm_w = din("norm_w", [DEPTH, D]); w_ada = din("w_ada", [DEPTH, D, 3 * D]); b_ada = din("b_ada", [DEPTH, 3 * D])
    w_in = din("w_in", [DEPTH, D, IN_DIM]); w_conv = din("w_conv", [DEPTH, 4, CONV_CH])
    a_log = din("a_log", [DEPTH, GH]); dt_bias = din("dt_bias", [DEPTH, GH]); gdn_norm_w = din("gdn_norm_w", [DEPTH, 128])
    w_a = din("w_branch_a", [DEPTH, 2048, D]); w_b = din("w_branch_b", [DEPTH, 2048, D]); w_out = din("w_out", [DEPTH, D, D])
    rel_bias = din("rel_bias", [32, 16]); fnw = din("final_norm_w", [D])
    ohr = din("ohr", [32, 384])

    yp = dout("yp", [2, T, D]); ys = dout("ys", [2, 64, D])
    nkp = dout("nkp", [DEPTH, 2, T, 256]); nvp = dout("nvp", [DEPTH, 2, T, 256]); nikp = dout("nikp", [DEPTH, 2, T, 64])
    ngp = dout("ngp", [DEPTH, 2, GH, 128, 128]); ncp = dout("ncp", [DEPTH, 2, 3, CONV_CH])
    nks = dout("nks", [DEPTH, 2, 64, 256]); nvs = dout("nvs", [DEPTH, 2, 64, 256]); niks = dout("niks", [DEPTH, 2, 64, 64])
    ngs = dout("ngs", [DEPTH, 2, GH, 128, 128]); ncs = dout("ncs", [DEPTH, 2, 3, CONV_CH])

    xs_d = dscr("xs_d", [NTOK, D], F32)
    mod_d = dscr("mod_d", [4, 3 * D], F32)
    tm_d = dscr("tm_d", [NTOK, 624], F32)
    tb_d = dscr("tb_d", [384, 16], F32)
    qT_d = dscr("qT_d", [2048, NTOK], BF16); kT_d = dscr("kT_d", [2048, NTOK], BF16); vT_d = dscr("vT_d", [2048, NTOK], BF16)
    zaT_d = dscr("zaT_d", [2048, NTOK], BF16); qbT_d = dscr("qbT_d", [2048, NTOK], BF16); zbT_d = dscr("zbT_d", [2048, NTOK], BF16)
    qiT_d = dscr("qiT_d", [1024, NTOK], BF16)
    sgaT_d = dscr("sgaT_d", [D, NTOK], BF16); sgbT_d = dscr("sgbT_d", [D, NTOK], BF16)
    ozT_d = dscr("ozT_d", [2048, NTOK], BF16); obT_d = dscr("obT_d", [2048, NTOK], BF16)

    c = Ctx(nc)
    uid = [0]

    def SB(es, shape, dt, name=None):
        uid[0] += 1
        nm = "%s_%d" % (name or "sb", uid[0])
        return Buf(es.enter_context(nc.sbuf_tensor(nm, list(shape), dt)), nm)

    def PS(es, shape, dt, name=None):
        uid[0] += 1
        nm = "%s_%d" % (name or "ps", uid[0])
        b = Buf(es.enter_context(nc.psum_tensor(nm, list(shape), dt)), nm)
        b.psum = True
        return b

    seqs = [('p', 0, 0, T), ('p', 1, T, T), ('s', 0, 2 * T, 64), ('s', 1, 2 * T + 64, 64)]
    halves = [(0, T), (T, T + 128)]

    with contextlib.ExitStack() as ges:
        identF = SB(ges, [128, 128], F32, "identF"); identB = SB(ges, [128, 128], BF16, "identB")
        onesB = SB(ges, [128, 128], BF16, "onesB"); onesF = SB(ges, [128, 128], F32, "onesF")
        Lm = SB(ges, [128, 128], F32, "Lm"); MLs = SB(ges, [128, 128], F32, "MLs")
        UP = SB(ges, [128, 128], F32, "UP"); LO = SB(ges, [128, 128], F32, "LO")
        NB = SB(ges, [128, 256, 16], F32, "NB"); NBh = SB(ges, [128, 16, 256], BF16, "NBh")
        CH = SB(ges, [128, 16], F32, "CH")
        epsc = SB(ges, [128, 1], F32, "epsc"); onec = SB(ges, [128, 1], F32, "onec"); zero3 = SB(ges, [128, 3], F32, "zero3")
        pow2 = SB(ges, [128, NIT], F32, "pow2")
        FNW = SB(ges, [128, D], F32, "FNW")

        def cmem(b, ap, v):
            c.op('pool', lambda e: e.memset(ap, v), w=[b])

        def csel(b, ap, pattern, op, fill, cm, base=0):
            c.op('pool', lambda e: e.affine_select(out=ap, in_=ap, pattern=pattern, compare_op=op, fill=fill,
                                                   base=base, channel_multiplier=cm), r=[b], w=[b])
        cmem(identF, identF[:], 1.0); csel(identF, identF[:], [[-1, 128]], ALU.is_equal, 0.0, 1)
        c.op('pool', lambda e: e.tensor_copy(out=identB[:], in_=identF[:]), r=[identF], w=[identB])
        cmem(onesB, onesB[:], 1.0); cmem(onesF, onesF[:], 1.0)
        cmem(Lm, Lm[:], 1.0); csel(Lm, Lm[:], [[1, 128]], ALU.is_ge, 0.0, -1)
        cmem(MLs, MLs[:], 1.0); csel(MLs, MLs[:], [[-1, 128]], ALU.is_gt, 0.0, 1)
        cmem(UP, UP[:], 1.0e4); csel(UP, UP[:], [[1, 128]], ALU.is_gt, 0.0, -1)
        cmem(LO, LO[:], 1.0e4); csel(LO, LO[:], [[-1, 128]], ALU.is_gt, 0.0, 1)
        cmem(epsc, epsc[:], EPS); cmem(onec, onec[:], 1.0); cmem(zero3, zero3[:], 0.0)
        for k in range(NIT):
            cmem(pow2, pow2[:, k:k + 1], 0.5 ** (k + 1))
        c.dma('sp', FNW[:], fnw.rearrange("(o d) -> o d", o=1).partition_broadcast(128), w=[FNW])
        c.dma('sp', CH[:], rel_bias[15:16, :].partition_broadcast(128), w=[CH])
        with contextlib.ExitStack() as es:
            oh = SB(es, [32, 384], F32); rb = SB(es, [32, 16], F32); tbs = SB(es, [128, 3, 16], F32)
            pt = PS(es, [128, 16], F32)
            c.dma('sp', oh[:], ohr[:, :], w=[oh]); c.dma('sp', rb[:], rel_bias[:, :], w=[rb])
            for j in range(3):
                c.op('pe', lambda e: e.matmul(pt[:], oh[:, j * 128:(j + 1) * 128], rb[:], start=True, stop=True), r=[oh, rb], w=[pt])
                c.op('dve', lambda e: e.tensor_tensor(out=tbs[:, j, :], in0=pt[:], in1=CH[:], op=ALU.subtract), r=[pt, CH], w=[tbs])
            c.dma('sp', tb_d.rearrange("(j p) h -> p j h", p=128), tbs[:], r=[tbs])
            c.barrier()
            for i in range(128):
                c.dma('sp', NB[i:i + 1, :, :], tb_d[127 - i:127 - i + 256, :].rearrange("(o c) h -> o c h", o=1), w=[NB])
            c.op('dve', lambda e: e.tensor_copy(out=NBh[:], in_=NB[:].rearrange("p c h -> p h c")), r=[NB], w=[NBh])
            c.barrier()

        def x_src(l, kind, idx, r0, n):
            if l == 0:
                return (xp if kind == 'p' else xs)[idx, r0:r0 + n, :]
            base = {('p', 0): 0, ('p', 1): T, ('s', 0): 2 * T, ('s', 1): 2 * T + 64}[(kind, idx)]
            return xs_d[base + r0:base + r0 + n, :]

        def range_tiles(tok0, ntok):
            tiles = []
            for a in range(tok0, tok0 + ntok, 128):
                if a < 2 * T:
                    parts = [('p', a // T, a % T, 128, 0)]
                else:
                    parts = [('s', 0, 0, 64, 0), ('s', 1, 0, 64, 64)]
                tiles.append((a - tok0, parts))
            return tiles

        def half_tiles(hf):
            return range_tiles(*halves[hf])
        oparts = [(0, T // 2), (T // 2, T // 2), (T, T // 2), (3 * T // 2, T // 2 + 128)]

        stop = cfg.get('stop')

        def chk(ph):
            if stop == ph:
                raise _Stop()
        try:
          chk('const')
          for l in range(DEPTH):
            with contextlib.ExitStack() as les:
                AT = SB(les, [128, KC, 4], F32, "AT"); BT = SB(les, [128, KC, 4], F32, "BT")
                wcT = SB(les, [128, 48, 4], F32, "wcT"); scT0 = SB(les, [128, 48, 3], F32, "scT0"); scT1 = SB(les, [128, 48, 3], F32, "scT1")
                gnw = SB(les, [128, 1], F32, "gnw"); dtb = SB(les, [128, 16], F32, "dtb"); nexpA = SB(les, [128, 16], F32, "nexpA")
                with contextlib.ExitStack() as es:
                    csb = SB(es, [4, D], F32); sc = SB(es, [4, D], F32); scT = SB(es, [128, KC, 4], F32)
                    mod = SB(es, [4, 3 * D], F32); bada = SB(es, [4, 3 * D], F32)
                    wst = [SB(es, [128, KC, 256], F32) for _ in range(2)]
                    modT = SB(es, [128, 2 * KC, 4], F32); nwT = SB(es, [128, KC], F32)
                    wc4 = SB(es, [4, CONV_CH], F32); sc3 = [SB(es, [3, CONV_CH], F32)] * 2
                    pt = PS(es, [128, 2 * KC, 4], F32); pm = [PS(es, [4, 512], F32) for _ in range(2)]
                    pw = PS(es, [128, 48, 4], F32)
                    c.dma('sp', csb[:], c4[:, :], w=[csb])
                    c.dma('sp', bada[:], b_ada[l:l + 1, :].partition_broadcast(4), w=[bada])
                    c.dma('sp', nwT[:], norm_w[l].rearrange("(k p) -> p k", p=128), w=[nwT], slow=True)
                    c.dma('sp', gnw[:], gdn_norm_w[l].rearrange("(p o) -> p o", o=1), w=[gnw], slow=True)
                    c.dma('sp', dtb[:], dt_bias[l:l + 1, :].partition_broadcast(128), w=[dtb])
                    c.dma('sp', nexpA[:], a_log[l:l + 1, :].partition_broadcast(128), w=[nexpA])
                    c.dma('sp', wc4[:], w_conv[l], w=[wc4])
                    c.op('act', lambda e: e.activation(out=nexpA[:], in_=nexpA[:], func=AF.Exp), r=[nexpA], w=[nexpA])
                    c.op('dve', lambda e: e.tensor_scalar(out=nexpA[:], in0=nexpA[:], scalar1=-1.0, scalar2=None, op0=ALU.mult), r=[nexpA], w=[nexpA])
                    c.op('act', lambda e: e.activation(out=sc[:], in_=csb[:], func=AF.Silu), r=[csb], w=[sc])
                    for kc in range(KC):
                        c.op('pe', lambda e: e.transpose(pt[:, kc, :], sc[:, kc * 128:(kc + 1) * 128], identF[0:4, 0:4]), r=[sc, identF], w=[pt])
                    c.op('dve', lambda e: e.tensor_copy(out=scT[:], in_=pt[:, 0:KC, :]), r=[pt], w=[scT])
                    for cb in range(48):
                        c.op('pe', lambda e: e.transpose(pw[:, cb, :], wc4[:, cb * 128:(cb + 1) * 128], identF[0:4, 0:4]), r=[wc4, identF], w=[pw])
                    c.op('dve', lambda e: e.tensor_copy(out=wcT[:], in_=pw[:]), r=[pw], w=[wcT])
                    for s, dst in ((0, scT0), (1, scT1)):
                        c.dma('sp', sc3[s][:], scv[l, s], w=[sc3[s]])
                        for cb in range(48):
                            c.op('pe', lambda e: e.transpose(pw[:, cb, 0:3], sc3[s][:, cb * 128:(cb + 1) * 128], identF[0:3, 0:3]), r=[sc3[s], identF], w=[pw])
                        c.op('dve', lambda e: e.tensor_copy(out=dst[:], in_=pw[:, :, 0:3]), r=[pw], w=[dst])
                    ncg = 3 * D // 256
                    for cg in range(ncg):
                        ws = wst[cg % 2]; pmm = pm[cg % 2]
                        c.dma('sp', ws[:], w_ada[l, :, cg * 256:(cg + 1) * 256].rearrange("(k p) n -> p k n", p=128), w=[ws])

                        def mm(e):
                            for kc in range(KC):
                                ins = e.matmul(pmm[:, 0:256], scT[:, kc, :], ws[:, kc, :], start=(kc == 0), stop=(kc == KC - 1))
                            return ins
                        c.op('pe', mm, r=[scT, ws], w=[pmm])
                        c.op('dve', lambda e: e.tensor_tensor(out=mod[:, cg * 256:(cg + 1) * 256], in0=pmm[:, 0:256], in1=bada[:, cg * 256:(cg + 1) * 256], op=ALU.add), r=[pmm, bada], w=[mod])
                    c.dma('pool', mod_d[:, :], mod[:], r=[mod])
                    for j in range(2 * KC):
                        c.op('pe', lambda e: e.transpose(pt[:, j, :], mod[:, j * 128:(j + 1) * 128], identF[0:4, 0:4]), r=[mod, identF], w=[pt])
                    c.op('dve', lambda e: e.tensor_copy(out=modT[:], in_=pt[:]), r=[pt], w=[modT])
                    c.op('dve', lambda e: e.tensor_copy(out=BT[:], in_=modT[:, 0:KC, :]), r=[modT], w=[BT])
                    c.op('dve', lambda e: e.scalar_tensor_tensor(out=AT[:], in0=modT[:, KC:2 * KC, :], scalar=1.0,
                                                                  in1=nwT[:].unsqueeze(2).to_broadcast([128, KC, 4]),
                                                                  op0=ALU.add, op1=ALU.mult), r=[modT, nwT], w=[AT])
                    c.barrier()

                chk('A')
                for hf, (htok0, hn) in enumerate(halves):
                    with contextlib.ExitStack() as es:
                        hT = SB(es, [128, KC, hn], BF16, "hT")
                        with contextlib.ExitStack() as es2:
                            xt = [SB(es2, [128, D], F32) for _ in range(2)]
                            junk = SB(es2, [128, D], F32)
                            xn = [SB(es2, [128, D], BF16) for _ in range(2)]
                            ssq = [SB(es2, [128, 1], F32) for _ in range(2)]
                            rstd = [SB(es2, [128, 1], F32) for _ in range(2)]
                            ptr = [PS(es2, [128, 4, 128], BF16) for _ in range(2)]
                            for ti, (t0, parts) in enumerate(half_tiles(hf)):
                                x_ = xt[ti % 2]; xn_ = xn[ti % 2]; ss_ = ssq[ti % 2]; rs_ = rstd[ti % 2]
                                for (kind, idx, r0, n, p0) in parts:
                                    c.dma('sp', x_[p0:p0 + n, :], x_src(l, kind, idx, r0, n), w=[x_])
                                c.op('act', lambda e: e.activation(out=junk[:], in_=x_[:], func=AF.Square, accum_out=ss_[:]), r=[x_], w=[junk, ss_])
                                c.op('act', lambda e: e.activation(out=ss_[:], in_=ss_[:], func=AF.Sqrt, scale=1.0 / D, bias=epsc[:]), r=[ss_, epsc], w=[ss_])
                                c.op('dve', lambda e: e.reciprocal(out=rs_[:], in_=ss_[:]), r=[ss_], w=[rs_])
                                c.op('dve', lambda e: e.tensor_scalar(out=xn_[:], in0=x_[:], scalar1=rs_[:], scalar2=None, op0=ALU.mult), r=[x_, rs_], w=[xn_])
                                for kg in range(KC // 4 if KC >= 4 else 1):
                                    kcs = list(range(kg * 4, min(KC, kg * 4 + 4)))
                                    p_ = ptr[kg % 2]

                                    def tr(e):
                                        for j, kc in enumerate(kcs):
                                            ins = e.transpose(p_[:, j, :], xn_[:, kc * 128:(kc + 1) * 128], identB[:])
                                        return ins
                                    c.op('pe', tr, r=[xn_, identB], w=[p_])
                                    for j, kc in enumerate(kcs):
                                        for (kind, idx, r0, n, p0) in parts:
                                            row = idx if kind == 'p' else 2 + idx
                                            c.op('act', lambda e: e.activation(out=hT[:, kc, t0 + p0:t0 + p0 + n], in_=p_[:, j, p0:p0 + n], func=AF.Identity,
                                                                               scale=AT[:, kc, row:row + 1], bias=BT[:, kc, row:row + 1]),
                                                 r=[p_, AT, BT], w=[hT])
                            c.barrier()
                        chk('N')
                        with contextlib.ExitStack() as es2:
                            wstg = [SB(es2, [128, KC, 256], F32) for _ in range(2)]
                            wbf = [SB(es2, [128, KC, 256], BF16) for _ in range(2)]
                            Wtm = SB(es2, [128, KC, 624], BF16)
                            wctr = [0]

                            def load_w(src, c0, ncol, dst=None, dcol=0):
                                i = wctr[0] % 2; wctr[0] += 1
                                st = wstg[i]
                                c.dma('sp', st[:, :, 0:ncol], src[:, c0:c0 + ncol].rearrange("(k p) n -> p k n", p=128), w=[st])
                                if dst is None:
                                    wb = wbf[i]
                                    c.op('dve', lambda e: e.tensor_copy(out=wb[:, :, 0:ncol], in_=st[:, :, 0:ncol]), r=[st], w=[wb])
                                    return wb
                                c.op('dve', lambda e: e.tensor_copy(out=dst[:, :, dcol:dcol + ncol], in_=st[:, :, 0:ncol]), r=[st], w=[dst])
                                return dst
                            W = w_in[l]
                            load_w(W, OFF['kb'], 256, Wtm, 0); load_w(W, OFF['vb'], 256, Wtm, 256)
                            load_w(W, OFF['ba'], 32, Wtm, 512); load_w(W, OFF['ki'], 80, Wtm, 544)
                            with contextlib.ExitStack() as es3:
                                ptm_a = [PS(es3, [128, 512], F32) for _ in range(2)]; ptm_b = [PS(es3, [128, 112], F32) for _ in range(2)]
                                ptm = [[ptm_a[0], ptm_b[0]], [ptm_a[1], ptm_b[1]]]
                                tms = [SB(es3, [128, 624], F32) for _ in range(2)]
                                for ti, (t0, parts) in enumerate(half_tiles(hf)):
                                    pa, pb = ptm[ti % 2]; tm_ = tms[ti % 2]

                                    def mm(e):
                                        for kc in range(KC):
                                            e.matmul(pa[:], hT[:, kc, t0:t0 + 128], Wtm[:, kc, 0:512], start=(kc == 0), stop=(kc == KC - 1))
                                        for kc in range(KC):
                                            ins = e.matmul(pb[:], hT[:, kc, t0:t0 + 128], Wtm[:, kc, 512:624], start=(kc == 0), stop=(kc == KC - 1))
                                        return ins
                                    c.op('pe', mm, r=[hT, Wtm], w=[pa, pb])
                                    c.op('act', lambda e: e.copy(out=tm_[:, 0:512], in_=pa[:]), r=[pa], w=[tm_])
                                    c.op('dve', lambda e: e.tensor_copy(out=tm_[:, 512:624], in_=pb[:]), r=[pb], w=[tm_])
                                    c.dma('pool', tm_d[htok0 + t0:htok0 + t0 + 128, :], tm_[:], r=[tm_])
                                    for (kind, idx, r0, n, p0) in parts:
                                        ok, ov, oi = (nkp, nvp, nikp) if kind == 'p' else (nks, nvs, niks)
                                        c.dma('pool', ok[l, idx, r0:r0 + n, :], tm_[p0:p0 + n, 0:256], r=[tm_])
                                        c.dma('pool', ov[l, idx, r0:r0 + n, :], tm_[p0:p0 + n, 256:512], r=[tm_])
                                        c.dma('pool', oi[l, idx, r0:r0 + n, :], tm_[p0:p0 + n, 544:608], r=[tm_])
                                c.barrier()
                            chunks = []
                            for t0 in range(0, T, 512):
                                wdt = min(512, T - t0)
                                chunks.append((t0, wdt, 'p', hf, t0 == 0, t0 + wdt == T))
                            if hf == 1:
                                chunks.append((T, 64, 's', 0, True, True)); chunks.append((T + 64, 64, 's', 1, True, True))
                            groups = [('conv', OFF['qkv'], 48, None), ('silu', OFF['za'], 16, zaT_d), ('qb', OFF['qb'], 16, qbT_d),
                                      ('silu', OFF['zb'], 16, zbT_d), ('copy', OFF['qi'], 8, qiT_d),
                                      ('sig', OFF['gla'], KC, sgaT_d), ('sig', OFF['glb'], KC, sgbT_d)]
                            with contextlib.ExitStack() as es3:
                                pp = [PS(es3, [128, 512], F32) for _ in range(3)]
                                pq = [PS(es3, [128, 512], F32) for _ in range(3)]
                                xpad = [SB(es3, [128, 3 + 512], F32) for _ in range(3)]
                                cvas = [SB(es3, [128, 512], F32) for _ in range(7)]; rnA = [SB(es3, [128, 512], F32) for _ in range(2)]
                                sqbs = [SB(es3, [128, 512], BF16) for _ in range(2)]; rns = [SB(es3, [128, 512], F32) for _ in range(2)]
                                ob = [SB(es3, [128, 512], BF16) for _ in range(3)]
                                pctr = [0]; octr = [0]
                                goff = OFF['qkv']
                                items = [(b2, bi, ch) for b2 in range(0, 48, 2) for bi in range(2) for ch in chunks]
                                NI = len(items)
                                wbd = {}

                                def st0_pe(i):
                                    b2, bi, (c0, wd, kind, idx, first, last) = items[i]
                                    if bi == 0 and (c0, wd) == chunks[0][0:2]:
                                        wbd[b2] = load_w(W, goff + b2 * 128, 256)
                                    wb = wbd[b2]; p_ = pp[i % 3]

                                    def mm(e):
                                        for kc in range(KC):
                                            ins = e.matmul(p_[:, 0:wd], wb[:, kc, bi * 128:(bi + 1) * 128], hT[:, kc, c0:c0 + wd], start=(kc == 0), stop=(kc == KC - 1))
                                        return ins
                                    c.op('pe', mm, r=[wb, hT], w=[p_])

                                def s1_copy(i):
                                    b2, bi, (c0, wd, kind, idx, first, last) = items[i]
                                    blk = b2 + bi
                                    p_ = pp[i % 3]; xq = xpad[i % 3]
                                    if first:
                                        if kind == 'p':
                                            c.op('act', lambda e: e.activation(out=xq[:, 0:3], in_=zero3[:], func=AF.Copy), r=[zero3], w=[xq])
                                        else:
                                            sct = scT0 if idx == 0 else scT1
                                            c.op('act', lambda e: e.copy(out=xq[:, 0:3], in_=sct[:, blk, :]), r=[sct], w=[xq])
                                    else:
                                        xprev = xpad[(i - 1) % 3]; pw_ = items[i - 1][2][1]
                                        c.op('act', lambda e: e.copy(out=xq[:, 0:3], in_=xprev[:, pw_:pw_ + 3]), r=[xprev], w=[xq])
                                    c.op('act', lambda e: e.copy(out=xq[:, 3:3 + wd], in_=p_[:, 0:wd]), r=[p_], w=[xq])
                                    if last:
                                        oc = ncp if kind == 'p' else ncs
                                        c.dma('pool', oc[l, idx, :, blk * 128:(blk + 1) * 128].rearrange("j p -> p j"), xq[:, wd:wd + 3], r=[xq], slow=True)

                                def s2_conv(i):
                                    b2, bi, (c0, wd, kind, idx, first, last) = items[i]
                                    blk = b2 + bi
                                    xq = xpad[i % 3]; cva = cvas[i % 7]
                                    c.op('dve', lambda e: e.tensor_scalar(out=cva[:, 0:wd], in0=xq[:, 0:wd], scalar1=wcT[:, blk, 0:1], scalar2=None, op0=ALU.mult), r=[xq, wcT], w=[cva])
                                    for j in range(1, 4):
                                        c.op('dve', lambda e: e.scalar_tensor_tensor(out=cva[:, 0:wd], in0=xq[:, j:j + wd], scalar=wcT[:, blk, j:j + 1], in1=cva[:, 0:wd],
                                                                                      op0=ALU.mult, op1=ALU.add), r=[xq, wcT, cva], w=[cva])

                                def s3_sig(i):
                                    b2, bi, (c0, wd, kind, idx, first, last) = items[i]
                                    cva = cvas[i % 7]; rn = rnA[i % 2]
                                    c.op('act', lambda e: e.activation(out=rn[:, 0:wd], in_=cva[:, 0:wd], func=AF.Exp, scale=-1.0), r=[cva], w=[rn])
                                    c.op('act', lambda e: e.activation(out=rn[:, 0:wd], in_=rn[:, 0:wd], func=AF.Ln, bias=onec[:]), r=[rn, onec], w=[rn])
                                    c.op('act', lambda e: e.activation(out=rn[:, 0:wd], in_=rn[:, 0:wd], func=AF.Exp, scale=-1.0), r=[rn], w=[rn])

                                def s4_silu(i):
                                    b2, bi, (c0, wd, kind, idx, first, last) = items[i]
                                    blk = b2 + bi
                                    cva = cvas[i % 7]; rn = rnA[i % 2]; o_ = ob[i % 3]
                                    tsl = slice(htok0 + c0, htok0 + c0 + wd)
                                    if blk >= 32:
                                        c.op('dve', lambda e: e.tensor_tensor(out=o_[:, 0:wd], in0=cva[:, 0:wd], in1=rn[:, 0:wd], op=ALU.mult), r=[cva, rn], w=[o_])
                                        c.dma('pool', vT_d[(blk - 32) * 128:(blk - 31) * 128, tsl], o_[:, 0:wd], r=[o_])
                                    else:
                                        c.op('dve', lambda e: e.tensor_tensor(out=cva[:, 0:wd], in0=cva[:, 0:wd], in1=rn[:, 0:wd], op=ALU.mult), r=[cva, rn], w=[cva])

                                def s5_sq(i):
                                    b2, bi, (c0, wd, kind, idx, first, last) = items[i]
                                    if b2 + bi >= 32:
                                        return
                                    cva = cvas[i % 7]; sqb = sqbs[i % 2]
                                    c.op('act', lambda e: e.activation(out=sqb[:, 0:wd], in_=cva[:, 0:wd], func=AF.Square), r=[cva], w=[sqb])

                                def s6_ones(i):
                                    b2, bi, (c0, wd, kind, idx, first, last) = items[i]
                                    if b2 + bi >= 32:
                                        return
                                    sqb = sqbs[i % 2]; q_ = pq[i % 3]
                                    c.op('pe', lambda e: e.matmul(q_[:, 0:wd], onesB[:], sqb[:, 0:wd], start=True, stop=True), r=[onesB, sqb], w=[q_])

                                def s7_rs(i):
                                    b2, bi, (c0, wd, kind, idx, first, last) = items[i]
                                    if b2 + bi >= 32:
                                        return
                                    rn = rns[i % 2]; q_ = pq[i % 3]
                                    c.op('act', lambda e: e.activation(out=rn[:, 0:wd], in_=q_[:, 0:wd], func=AF.Ln, bias=epsc[:]), r=[q_, epsc], w=[rn])
                                    c.op('act', lambda e: e.activation(out=rn[:, 0:wd], in_=rn[:, 0:wd], func=AF.Exp, scale=-0.5), r=[rn], w=[rn])

                                def s8_out(i):
                                    b2, bi, (c0, wd, kind, idx, first, last) = items[i]
                                    blk = b2 + bi
                                    if blk >= 32:
                                        return
                                    cva = cvas[i % 7]; rn = rns[i % 2]; o_ = ob[i % 3]
                                    tsl = slice(htok0 + c0, htok0 + c0 + wd)
                                    qs = float(HD) ** -0.5 if blk < 16 else 1.0
                                    c.op('dve', lambda e: e.scalar_tensor_tensor(out=o_[:, 0:wd], in0=cva[:, 0:wd], scalar=qs, in1=rn[:, 0:wd], op0=ALU.mult, op1=ALU.mult),
                                         r=[cva, rn], w=[o_])
                                    dst = qT_d if blk < 16 else kT_d; drow = (blk % 16) * 128
                                    c.dma('pool', dst[drow:drow + 128, tsl], o_[:, 0:wd], r=[o_])
                                stages = [st0_pe, s1_copy, s2_conv, s3_sig, s4_silu, s5_sq, s6_ones, s7_rs, s8_out]
                                for t in range(NI + len(stages) - 1):
                                    for k, fn in enumerate(stages):
                                        if 0 <= t - k < NI:
                                            fn(t - k)
                                for (gk, goff, nblk, dst) in groups[1:]:
                                    for b2 in range(0, nblk, 2):
                                        wb = load_w(W, goff + b2 * 128, 256)
                                        for bi in range(2):
                                            blk = b2 + bi
                                            for (c0, wd, kind, idx, first, last) in chunks:
                                                p_ = pp[pctr[0] % 3]; pctr[0] += 1
                                                o_ = ob[octr[0] % 3]; octr[0] += 1

                                                def mm(e):
                                                    for kc in range(KC):
                                                        ins = e.matmul(p_[:, 0:wd], wb[:, kc, bi * 128:(bi + 1) * 128], hT[:, kc, c0:c0 + wd], start=(kc == 0), stop=(kc == KC - 1))
                                                    return ins
                                                c.op('pe', mm, r=[wb, hT], w=[p_])
                                                tsl = slice(htok0 + c0, htok0 + c0 + wd)
                                                if gk == 'silu':
                                                    c.op('act', lambda e: e.activation(out=o_[:, 0:wd], in_=p_[:, 0:wd], func=AF.Silu), r=[p_], w=[o_])
                                                elif gk == 'sig':
                                                    c.op('act', lambda e: e.activation(out=o_[:, 0:wd], in_=p_[:, 0:wd], func=AF.Sigmoid), r=[p_], w=[o_])
                                                elif gk == 'qb':
                                                    c.op('act', lambda e: e.activation(out=o_[:, 0:wd], in_=p_[:, 0:wd], func=AF.Copy, scale=float(HD) ** -0.5), r=[p_], w=[o_])
                                                else:
                                                    c.op('act', lambda e: e.copy(out=o_[:, 0:wd], in_=p_[:, 0:wd]), r=[p_], w=[o_])
                                                c.dma('pool', dst[blk * 128:(blk + 1) * 128, tsl], o_[:, 0:wd], r=[o_])
                        c.barrier()

                chk('P')
                with contextlib.ExitStack() as es:
                    S = SB(es, [128, GH, 128], F32, "S"); Sb = SB(es, [128, GH, 128], BF16, "Sb")
                    Ibc = SB(es, [128, 4, 128], F32, "Ibc")
                    for hh in range(4):
                        c.op('pool', lambda e: e.tensor_copy(out=Ibc[:, hh, :], in_=identF[:]), r=[identF], w=[Ibc])
                    qT = SB(es, [128, GH, 128], BF16); kT = SB(es, [128, GH, 128], BF16); vT = SB(es, [128, GH, 128], BF16); zaT = SB(es, [128, GH, 128], BF16)
                    bgr = SB(es, [128, 32], F32); beta = SB(es, [128, 16], F32); g_ = SB(es, [128, 16], F32)
                    gc = SB(es, [128, 16], F32); gcT = SB(es, [16, 128], F32); egc = SB(es, [128, 16], F32)
                    kdsc = SB(es, [128, 16], F32); gle = SB(es, [128, 16], F32); bege = SB(es, [128, 16], F32)
                    MB = SB(es, [128, 16, 128], F32); GB = SB(es, [128, 16, 128], F32)
                    Xs = [SB(es, [128, 4, 128], F32) for _ in range(2)] * 2
                    Dm = [SB(es, [128, 4, 128], F32) for _ in range(2)] * 2
                    DTm = [SB(es, [128, 4, 128], F32) for _ in range(2)] * 2
                    egr = [SB(es, [128, 4, 128], F32) for _ in range(2)] * 2
                    Pa = [[SB(es, [128, 4, 128], F32) for _ in range(2)] for _ in range(4)]
                    PTa = [[SB(es, [128, 4, 128], F32) for _ in range(2)] for _ in range(4)]
                    Y = [SB(es, [128, 4, 128], F32) for _ in range(4)]
                    YB = [SB(es, [128, 4, 128], BF16) for _ in range(4)]
                    vb = [SB(es, [128, 4, 128], BF16) for _ in range(4)]
                    kbg = [SB(es, [128, 4, 128], BF16) for _ in range(4)]
                    kdec = [SB(es, [128, 4, 128], BF16) for _ in range(4)]
                    wTb = [SB(es, [128, 4, 128], BF16) for _ in range(4)]
                    us = [SB(es, [128, 4, 128], F32) for _ in range(4)]
                    QKm = [SB(es, [128, 4, 128], BF16) for _ in range(4)]
                    qdT = [SB(es, [128, 4, 128], BF16) for _ in range(4)]
                    vn = [SB(es, [128, 4, 128], BF16) for _ in range(4)]
                    oTs = [SB(es, [128, 4, 128], F32) for _ in range(2)]
                    osq = [SB(es, [128, 4, 128], BF16) for _ in range(2)]
                    rst = [SB(es, [128, 4, 128], F32) for _ in range(2)]
                    ozs = [SB(es, [128, 4, 128], BF16) for _ in range(2)]
                    psA = [PS(es, [128, 4, 128], F32) for _ in range(5)]
                    psB = [PS(es, [128, 4, 128], BF16) for _ in range(2)]
                    psS = PS(es, [128, 128], F32)
                    pctr = [0]

                    def nps():
                        p = psA[pctr[0] % 5]; pctr[0] += 1
                        return p
                    for (kind, idx, tok0, slen) in seqs:
                        n = 128 if kind == 'p' else 64
                        nst = 6 if n == 128 else 5
                        if kind == 'p':
                            c.op('pool', lambda e: e.memset(S[:], 0.0), w=[S])
                        else:
                            c.dma('sp', S[:], sg[l, idx].rearrange("h k v -> k h v"), w=[S])
                        c.op('act', lambda e: e.copy(out=Sb[:], in_=S[:]), r=[S], w=[Sb])
                        for t in range(slen // n):
                            ts_ = slice(tok0 + t * n, tok0 + (t + 1) * n)
                            for (dstb, srcd) in ((qT, qT_d), (kT, kT_d), (vT, vT_d), (zaT, zaT_d)):
                                c.dma('sp', dstb[:, :, 0:n], srcd[:, ts_].rearrange("(h p) t -> p h t", p=128), w=[dstb])
                            c.dma('sp', bgr[0:n, :], tm_d[ts_, 512:544], w=[bgr])
                            c.op('act', lambda e: e.activation(out=beta[0:n], in_=bgr[0:n, 0:16], func=AF.Sigmoid), r=[bgr], w=[beta])
                            c.op('dve', lambda e: e.tensor_tensor(out=g_[0:n], in0=bgr[0:n, 16:32], in1=dtb[0:n], op=ALU.add), r=[bgr, dtb], w=[g_])
                            c.op('act', lambda e: e.activation(out=gc[0:n], in_=g_[0:n], func=AF.Abs), r=[g_], w=[gc])
                            c.op('act', lambda e: e.activation(out=gc[0:n], in_=gc[0:n], func=AF.Exp, scale=-1.0), r=[gc], w=[gc])
                            c.op('act', lambda e: e.activation(out=gc[0:n], in_=gc[0:n], func=AF.Ln, bias=onec[0:n]), r=[gc, onec], w=[gc])
                            c.op('dve', lambda e: e.scalar_tensor_tensor(out=g_[0:n], in0=g_[0:n], scalar=0.0, in1=gc[0:n], op0=ALU.max, op1=ALU.add), r=[g_, gc], w=[g_])
                            c.op('dve', lambda e: e.tensor_tensor(out=g_[0:n], in0=g_[0:n], in1=nexpA[0:n], op=ALU.mult), r=[g_, nexpA], w=[g_])
                            c.op('pe', lambda e: e.matmul(psS[0:n, 0:16], Lm[0:n, 0:n], g_[0:n, :], start=True, stop=True), r=[Lm, g_], w=[psS])
                            c.op('dve', lambda e: e.tensor_copy(out=gc[0:n], in_=psS[0:n, 0:16]), r=[psS], w=[gc])
                            c.op('pe', lambda e: e.matmul(psS[0:16, 0:n], g_[0:n, :], Lm[0:n, 0:n], start=True, stop=True), r=[Lm, g_], w=[psS])
                            c.op('dve', lambda e: e.tensor_copy(out=gcT[:, 0:n], in_=psS[0:16, 0:n]), r=[psS], w=[gcT])
                            c.op('pe', lambda e: e.matmul(psS[:, 0:16], onesF[0:n, :], g_[0:n, :], start=True, stop=True), r=[onesF, g_], w=[psS])
                            c.op('act', lambda e: e.activation(out=gle[:], in_=psS[:, 0:16], func=AF.Exp), r=[psS], w=[gle])
                            c.op('dve', lambda e: e.tensor_tensor(out=kdsc[0:n], in0=psS[0:n, 0:16], in1=gc[0:n], op=ALU.subtract), r=[psS, gc], w=[kdsc])
                            c.op('act', lambda e: e.activation(out=kdsc[0:n], in_=kdsc[0:n], func=AF.Exp), r=[kdsc], w=[kdsc])
                            c.op('act', lambda e: e.activation(out=egc[0:n], in_=gc[0:n], func=AF.Exp), r=[gc], w=[egc])
                            c.op('dve', lambda e: e.tensor_tensor(out=bege[0:n], in0=egc[0:n], in1=beta[0:n], op=ALU.mult), r=[egc, beta], w=[bege])
                            c.op('dve', lambda e: e.tensor_copy(out=GB[0:n], in_=g_[0:n, :].unsqueeze(2).to_broadcast([n, 16, 128])), r=[g_], w=[GB])
                            c.op('dve', lambda e: e.tensor_tensor(out=MB[0:n, :, 0:n], in0=MLs[0:n, 0:n].unsqueeze(1).to_broadcast([n, 16, n]),
                                                                   in1=beta[0:n, :].unsqueeze(2).to_broadcast([n, 16, n]), op=ALU.mult), r=[MLs, beta], w=[MB])
                            chk('G1')
                            for hg in range(4):
                                hs = slice(hg * 4, hg * 4 + 4)
                                pg = nps()

                                def mm(e):
                                    for hh in range(4):
                                        ins = e.matmul(pg[:, hh, 0:n], GB[0:n, hg * 4 + hh, :], Lm[0:n, 0:n], start=True, stop=True)
                                    return ins
                                c.op('pe', mm, r=[GB, Lm], w=[pg])
                                chk('G1a')
                                c.op('act', lambda e: e.activation(out=egr[hg][:, :, 0:n], in_=pg[:, :, 0:n], func=AF.Exp), r=[pg], w=[egr[hg]])
                                chk('G1b')
                                c.op('dve', lambda e: e.tensor_tensor(out=Xs[hg][0:n, :, 0:n], in0=pg[0:n, :, 0:n], in1=gc[0:n, hs].unsqueeze(2).to_broadcast([n, 4, n]), op=ALU.subtract),
                                     r=[pg, gc], w=[Xs[hg]])
                                chk('Ga')
                                c.op('dve', lambda e: e.tensor_tensor(out=Dm[hg][0:n, :, 0:n], in0=Xs[hg][0:n, :, 0:n], in1=UP[0:n, 0:n].unsqueeze(1).to_broadcast([n, 4, n]), op=ALU.add),
                                     r=[Xs[hg], UP], w=[Dm[hg]])
                                chk('H1')
                                c.op('act', lambda e: e.activation(out=Dm[hg][0:n, :, 0:n], in_=Dm[hg][0:n, :, 0:n], func=AF.Exp, scale=-1.0), r=[Dm[hg]], w=[Dm[hg]])
                                chk('H2')
                                c.op('pool', lambda e: e.tensor_tensor(out=DTm[hg][0:n, :, 0:n], in0=Xs[hg][0:n, :, 0:n], in1=LO[0:n, 0:n].unsqueeze(1).to_broadcast([n, 4, n]), op=ALU.subtract),
                                     r=[Xs[hg], LO], w=[DTm[hg]])
                                c.op('act', lambda e: e.activation(out=DTm[hg][0:n, :, 0:n], in_=DTm[hg][0:n, :, 0:n], func=AF.Exp), r=[DTm[hg]], w=[DTm[hg]])
                                chk('H3')
                                c.op('dve', lambda e: e.tensor_tensor(out=Dm[hg][0:n, :, 0:n], in0=Dm[hg][0:n, :, 0:n], in1=MB[0:n, hs, 0:n], op=ALU.mult), r=[Dm[hg], MB], w=[Dm[hg]])
                                chk('Gb')
                                pk = nps()

                                def mm(e):
                                    for hh in range(4):
                                        h = hg * 4 + hh
                                        ins = e.matmul(pk[0:n, hh, 0:n], kT[:, h, 0:n], kT[:, h, 0:n], start=True, stop=True)
                                    return ins
                                c.op('pe', mm, r=[kT], w=[pk])
                                A_ = PTa[hg][0]; B_ = Pa[hg][0]
                                c.op('dve', lambda e: e.tensor_tensor(out=A_[0:n, :, 0:n], in0=pk[0:n, :, 0:n], in1=Dm[hg][0:n, :, 0:n], op=ALU.mult), r=[pk, Dm[hg]], w=[A_])
                                pb_ = nps()

                                def mm(e):
                                    for hh in range(4):
                                        ins = e.transpose(pb_[0:n, hh, 0:n], A_[0:n, hh, 0:n], identF[0:n, 0:n])
                                    return ins
                                c.op('pe', mm, r=[A_, identF], w=[pb_])
                                c.op('act', lambda e: e.copy(out=B_[0:n, :, 0:n], in_=pb_[0:n, :, 0:n]), r=[pb_], w=[B_])
                                c.op('dve', lambda e: e.tensor_tensor(out=Y[hg][0:n, :, 0:n], in0=Ibc[0:n, :, 0:n], in1=pb_[0:n, :, 0:n], op=ALU.subtract), r=[pb_, Ibc], w=[Y[hg]])
                                chk('Gc')
                                ptk = psB[0]

                                def mm(e):
                                    for hh in range(4):
                                        ins = e.transpose(ptk[0:n, hh, :], kT[:, hg * 4 + hh, 0:n], identB[:])
                                    return ins
                                c.op('pe', mm, r=[kT, identB], w=[ptk])
                                c.op('dve', lambda e: e.tensor_tensor(out=kbg[hg][0:n], in0=ptk[0:n], in1=bege[0:n, hs].unsqueeze(2).to_broadcast([n, 4, 128]), op=ALU.mult), r=[ptk, bege], w=[kbg[hg]])
                                c.op('dve', lambda e: e.tensor_tensor(out=kdec[hg][0:n], in0=ptk[0:n], in1=kdsc[0:n, hs].unsqueeze(2).to_broadcast([n, 4, 128]), op=ALU.mult), r=[ptk, kdsc], w=[kdec[hg]])
                                ptv = psB[1]

                                def mm(e):
                                    for hh in range(4):
                                        ins = e.transpose(ptv[0:n, hh, :], vT[:, hg * 4 + hh, 0:n], identB[:])
                                    return ins
                                c.op('pe', mm, r=[vT, identB], w=[ptv])
                                c.op('dve', lambda e: e.tensor_tensor(out=vb[hg][0:n], in0=ptv[0:n], in1=beta[0:n, hs].unsqueeze(2).to_broadcast([n, 4, 128]), op=ALU.mult), r=[ptv, beta], w=[vb[hg]])
                                chk('Gd')
                                pqk = nps()

                                def mm(e):
                                    for hh in range(4):
                                        h = hg * 4 + hh
                                        ins = e.matmul(pqk[0:n, hh, 0:n], kT[:, h, 0:n], qT[:, h, 0:n], start=True, stop=True)
                                    return ins
                                c.op('pe', mm, r=[kT, qT], w=[pqk])
                                c.op('dve', lambda e: e.tensor_tensor(out=QKm[hg][0:n, :, 0:n], in0=pqk[0:n, :, 0:n], in1=DTm[hg][0:n, :, 0:n], op=ALU.mult), r=[pqk, DTm[hg]], w=[QKm[hg]])
                                c.op('pool', lambda e: e.tensor_tensor(out=qdT[hg][:, :, 0:n], in0=qT[:, hs, 0:n], in1=egr[hg][:, :, 0:n], op=ALU.mult), r=[qT, egr[hg]], w=[qdT[hg]])
                            chk('G2')
                            cur = 0
                            for m in range(nst):
                                lastm = (m == nst - 1)
                                for hg in range(4):
                                    P_ = Pa[hg][cur]; PT_ = PTa[hg][cur]; Pn = Pa[hg][1 - cur]; PTn = PTa[hg][1 - cur]
                                    if not lastm:
                                        p1 = nps()

                                        def mm(e):
                                            for hh in range(4):
                                                ins = e.matmul(p1[0:n, hh, 0:n], PT_[0:n, hh, 0:n], P_[0:n, hh, 0:n], start=True, stop=True)
                                            return ins
                                        c.op('pe', mm, r=[PT_, P_], w=[p1])
                                        c.op('act', lambda e: e.copy(out=Pn[0:n, :, 0:n], in_=p1[0:n, :, 0:n]), r=[p1], w=[Pn])
                                    p2 = nps()

                                    def mm(e):
                                        for hh in range(4):
                                            ins = e.matmul(p2[0:n, hh, 0:n], P_[0:n, hh, 0:n], PT_[0:n, hh, 0:n], start=True, stop=True)
                                        return ins
                                    c.op('pe', mm, r=[PT_, P_], w=[p2])
                                    c.op('dve', lambda e: e.tensor_copy(out=PTn[0:n, :, 0:n], in_=p2[0:n, :, 0:n]), r=[p2], w=[PTn])
                                    p3 = nps()

                                    def mm(e):
                                        for hh in range(4):
                                            ins = e.matmul(p3[0:n, hh, 0:n], PTn[0:n, hh, 0:n], Y[hg][0:n, hh, 0:n], start=True, stop=True)
                                        return ins
                                    c.op('pe', mm, r=[PTn, Y[hg]], w=[p3])
                                    c.op('dve', lambda e: e.tensor_tensor(out=Y[hg][0:n, :, 0:n], in0=Y[hg][0:n, :, 0:n], in1=p3[0:n, :, 0:n], op=ALU.add), r=[p3, Y[hg]], w=[Y[hg]])
                                cur = 1 - cur
                            chk('G3')
                            for hg in range(4):
                                hs = slice(hg * 4, hg * 4 + 4)
                                c.op('act', lambda e: e.copy(out=YB[hg][0:n, :, 0:n], in_=Y[hg][0:n, :, 0:n]), r=[Y[hg]], w=[YB[hg]])
                                pu = nps()

                                def mm(e):
                                    for hh in range(4):
                                        ins = e.matmul(pu[0:n, hh, :], YB[hg][0:n, hh, 0:n], vb[hg][0:n, hh, :], start=True, stop=True)
                                    return ins
                                c.op('pe', mm, r=[YB[hg], vb[hg]], w=[pu])
                                c.op('act', lambda e: e.copy(out=us[hg][0:n], in_=pu[0:n]), r=[pu], w=[us[hg]])
                                pw_ = nps()

                                def mm(e):
                                    for hh in range(4):
                                        ins = e.matmul(pw_[:, hh, 0:n], kbg[hg][0:n, hh, :], YB[hg][0:n, hh, 0:n], start=True, stop=True)
                                    return ins
                                c.op('pe', mm, r=[YB[hg], kbg[hg]], w=[pw_])
                                c.op('dve', lambda e: e.tensor_copy(out=wTb[hg][:, :, 0:n], in_=pw_[:, :, 0:n]), r=[pw_], w=[wTb[hg]])
                            chk('G4')
                            for hg in range(4):
                                hs = slice(hg * 4, hg * 4 + 4)
                                pv_ = nps()

                                def mm(e):
                                    for hh in range(4):
                                        ins = e.matmul(pv_[0:n, hh, :], wTb[hg][:, hh, 0:n], Sb[:, hg * 4 + hh, :], start=True, stop=True)
                                    return ins
                                c.op('pe', mm, r=[wTb[hg], Sb], w=[pv_])
                                c.op('dve', lambda e: e.tensor_tensor(out=vn[hg][0:n], in0=us[hg][0:n], in1=pv_[0:n], op=ALU.subtract), r=[us[hg], pv_], w=[vn[hg]])
                                po = nps()

                                def mm(e):
                                    for hh in range(4):
                                        e.matmul(po[:, hh, 0:n], Sb[:, hg * 4 + hh, :], qdT[hg][:, hh, 0:n], start=True, stop=False)
                                        ins = e.matmul(po[:, hh, 0:n], vn[hg][0:n, hh, :], QKm[hg][0:n, hh, 0:n], start=False, stop=True)
                                    return ins
                                c.op('pe', mm, r=[Sb, qdT[hg], vn[hg], QKm[hg]], w=[po])
                                o_ = oTs[hg % 2]; q_ = osq[hg % 2]; r_ = rst[hg % 2]; z_ = ozs[hg % 2]
                                c.op('act', lambda e: e.copy(out=o_[:, :, 0:n], in_=po[:, :, 0:n]), r=[po], w=[o_])
                                c.op('act', lambda e: e.activation(out=q_[:, :, 0:n], in_=po[:, :, 0:n], func=AF.Square), r=[po], w=[q_])
                                pss = nps()
                                if n == 128:
                                    c.op('pe', lambda e: e.matmul(pss[:].rearrange("p a b -> p (a b)"), onesB[:], q_[:].rearrange("p a b -> p (a b)"), start=True, stop=True), r=[onesB, q_], w=[pss])
                                else:
                                    def mm(e):
                                        for hh in range(4):
                                            ins = e.matmul(pss[:, hh, 0:n], onesB[:], q_[:, hh, 0:n], start=True, stop=True)
                                        return ins
                                    c.op('pe', mm, r=[onesB, q_], w=[pss])
                                c.op('act', lambda e: e.activation(out=r_[:, :, 0:n], in_=pss[:, :, 0:n], func=AF.Sqrt, scale=1.0 / 128, bias=epsc[:]), r=[pss, epsc], w=[r_])
                                c.op('dve', lambda e: e.reciprocal(out=r_[:, :, 0:n], in_=r_[:, :, 0:n]), r=[r_], w=[r_])
                                c.op('dve', lambda e: e.tensor_tensor(out=o_[:, :, 0:n], in0=o_[:, :, 0:n], in1=r_[:, :, 0:n], op=ALU.mult), r=[o_, r_], w=[o_])
                                c.op('dve', lambda e: e.scalar_tensor_tensor(out=z_[:, :, 0:n], in0=o_[:, :, 0:n], scalar=gnw[:, 0:1], in1=zaT[:, hs, 0:n], op0=ALU.mult, op1=ALU.mult),
                                     r=[o_, gnw, zaT], w=[z_])
                                c.dma('pool', ozT_d[hg * 512:(hg + 1) * 512, ts_].rearrange("(h p) t -> p h t", p=128), z_[:, :, 0:n], r=[z_])
                                pS = nps()

                                def mm(e):
                                    for hh in range(4):
                                        ins = e.matmul(pS[:, hh, :], kdec[hg][0:n, hh, :], vn[hg][0:n, hh, :], start=True, stop=True)
                                    return ins
                                c.op('pe', mm, r=[kdec[hg], vn[hg]], w=[pS])
                                c.op('pool', lambda e: e.tensor_tensor(out=S[:, hs, :], in0=S[:, hs, :], in1=gle[:, hs].unsqueeze(2).to_broadcast([128, 4, 128]), op=ALU.mult), r=[S, gle, Sb], w=[S])
                                c.op('dve', lambda e: e.tensor_tensor(out=S[:, hs, :], in0=S[:, hs, :], in1=pS[:], op=ALU.add), r=[S, pS], w=[S])
                            c.op('act', lambda e: e.copy(out=Sb[:], in_=S[:]), r=[S], w=[Sb])
                            chk('G5')
                        og = ngp if kind == 'p' else ngs
                        c.dma('pool', og[l, idx].rearrange("h k v -> k h v"), S[:], r=[S])
                    c.barrier()

                chk('G')
                with contextlib.ExitStack() as es:
                    KT = SB(es, [128, 2, NKB * 128], BF16, "KT"); Vt = SB(es, [128, NKB, 256], BF16, "Vt"); KIT = SB(es, [128, NKB * 128], BF16, "KIT")
                    kst = [SB(es, [128, 576], F32) for _ in range(2)]
                    kbf = [SB(es, [128, 384], BF16) for _ in range(2)]
                    qiT = SB(es, [128, 8, 128], BF16); qbT = SB(es, [128, 16, 128], BF16); zbT = SB(es, [128, 16, 128], BF16)
                    wi = SB(es, [128, 16], F32); absw = SB(es, [128, 16], F32); sgn = SB(es, [128, 16], F32)
                    scb = SB(es, [128, LMAX], F32, "scb"); mb = SB(es, [128, LMAX], BF16, "mb"); junk = SB(es, [128, LMAX], F32, "junkb")
                    rr = [SB(es, [128, 512], F32) for _ in range(2)]
                    hi = SB(es, [128, 1], F32); lo = SB(es, [128, 1], F32); lo2 = SB(es, [128, 1], F32); mid = SB(es, [128, 1], F32)
                    cnt = SB(es, [128, 1], F32); tmp1 = SB(es, [128, 1], F32); wv = SB(es, [128, NIT], F32)
                    pbuf = [SB(es, [128, LMAX], BF16) for _ in range(2)]
                    rs = [SB(es, [128, 16], F32) for _ in range(2)]
                    rinv = [SB(es, [128, 1], F32) for _ in range(2)]
                    pT = [SB(es, [128, NKB, 128], BF16) for _ in range(2)]
                    obs = [SB(es, [128, 4, 128], BF16) for _ in range(2)]
                    psd = [PS(es, [128, 512], F32) for _ in range(3)]
                    pst = [PS(es, [128, 4, 128], BF16) for _ in range(2)]
                    pso = [PS(es, [128, 4, 128], F32) for _ in range(2)]
                    dctr = [0]; tctr = [0]; kctr = [0]

                    def add_keys(src_k, src_v, src_ki, nrows, kbi):
                        st = kst[kctr[0] % 2]; bf = kbf[kctr[0] % 2]; kctr[0] += 1
                        c.dma('sp', st[0:nrows, 0:256], src_k, w=[st])
                        c.dma('sp', st[0:nrows, 256:512], src_v, w=[st])
                        c.dma('sp', st[0:nrows, 512:576], src_ki, w=[st])
                        c.op('pool', lambda e: e.tensor_copy(out=bf[0:nrows, 0:256], in_=st[0:nrows, 0:256]), r=[st], w=[bf])
                        c.op('pool', lambda e: e.tensor_copy(out=bf[0:nrows, 256:320], in_=st[0:nrows, 512:576]), r=[st], w=[bf])
                        c.op('pool', lambda e: e.tensor_copy(out=bf[0:nrows, 320:384], in_=st[0:nrows, 512:576]), r=[st], w=[bf])
                        c.op('pool', lambda e: e.tensor_copy(out=Vt[0:nrows, kbi, :], in_=st[0:nrows, 256:512]), r=[st], w=[Vt])
                        p_ = pst[tctr[0] % 2]; tctr[0] += 1

                        def mm(e):
                            for j in range(3):
                                ins = e.transpose(p_[:, j, 0:nrows], bf[0:nrows, j * 128:(j + 1) * 128], identB[0:nrows, 0:nrows])
                            return ins
                        c.op('pe', mm, r=[bf, identB], w=[p_])
                        c.op('act', lambda e: e.copy(out=KT[:, :, kbi * 128:kbi * 128 + nrows], in_=p_[:, 0:2, 0:nrows]), r=[p_], w=[KT])
                        c.op('act', lambda e: e.copy(out=KIT[:, kbi * 128:kbi * 128 + nrows], in_=p_[:, 2, 0:nrows]), r=[p_], w=[KIT])

                    for (kind, idx, tok0, slen) in seqs:
                        n = 128 if kind == 'p' else 64
                        topk = topk_p if kind == 'p' else topk_s
                        if kind == 's':
                            for kb_ in range(P // 128):
                                add_keys(ck[l, idx, kb_ * 128:(kb_ + 1) * 128, :], cv[l, idx, kb_ * 128:(kb_ + 1) * 128, :], cik[l, idx, kb_ * 128:(kb_ + 1) * 128, :], 128, kb_)
                        for t in range(slen // n):
                            ts_ = slice(tok0 + t * n, tok0 + (t + 1) * n)
                            pos0 = t * 128 if kind == 'p' else P
                            Lk = pos0 + n
                            add_keys(tm_d[ts_, 0:256], tm_d[ts_, 256:512], tm_d[ts_, 544:608], n, pos0 // 128)
                            c.dma('sp', qiT[:, :, 0:n], qiT_d[:, ts_].rearrange("(j p) t -> p j t", p=128), w=[qiT])
                            c.dma('sp', qbT[:, :, 0:n], qbT_d[:, ts_].rearrange("(h p) t -> p h t", p=128), w=[qbT])
                            c.dma('sp', zbT[:, :, 0:n], zbT_d[:, ts_].rearrange("(h p) t -> p h t", p=128), w=[zbT])
                            c.dma('sp', wi[0:n, :], tm_d[ts_, 608:624], w=[wi])
                            c.op('act', lambda e: e.activation(out=absw[0:n], in_=wi[0:n], func=AF.Abs), r=[wi], w=[absw])
                            c.op('act', lambda e: e.activation(out=sgn[0:n], in_=wi[0:n], func=AF.Sign), r=[wi], w=[sgn])
                            kchunks = [(c0, min(512, Lk - c0)) for c0 in range(0, Lk, 512)]
                            for (c0, wd) in kchunks:
                                for h in range(16):
                                    hb = 64 * (h % 2)
                                    p_ = psd[dctr[0] % 3]; r_ = rr[dctr[0] % 2]; dctr[0] += 1
                                    c.op('pe', lambda e: e.matmul(p_[0:n, 0:wd], qiT[hb:hb + 64, h // 2, 0:n], KIT[hb:hb + 64, c0:c0 + wd], start=True, stop=True), r=[qiT, KIT], w=[p_])
                                    c.op('act', lambda e: e.activation(out=r_[0:n, 0:wd], in_=p_[0:n, 0:wd], func=AF.Relu, scale=absw[0:n, h:h + 1]), r=[p_, absw], w=[r_])
                                    if h == 0:
                                        c.op('dve', lambda e: e.tensor_scalar(out=scb[0:n, c0:c0 + wd], in0=r_[0:n, 0:wd], scalar1=sgn[0:n, 0:1], scalar2=None, op0=ALU.mult), r=[r_, sgn], w=[scb])
                                    else:
                                        c.op('dve', lambda e: e.scalar_tensor_tensor(out=scb[0:n, c0:c0 + wd], in0=r_[0:n, 0:wd], scalar=sgn[0:n, h:h + 1], in1=scb[0:n, c0:c0 + wd],
                                                                                      op0=ALU.mult, op1=ALU.add), r=[r_, sgn, scb], w=[scb])
                            if kind == 'p':
                                c.op('dve', lambda e: e.tensor_reduce(out=lo[0:64], in_=scb[0:64, 0:Lk - 64], axis=AX.X, op=ALU.min), r=[scb], w=[lo])
                                c.op('dve', lambda e: e.tensor_reduce(out=lo[64:128], in_=scb[64:128, 0:Lk], axis=AX.X, op=ALU.min), r=[scb], w=[lo])
                                c.op('dve', lambda e: e.memset(scb[0:64, Lk - 64:Lk], -1.0e30), w=[scb])
                            else:
                                c.op('dve', lambda e: e.tensor_reduce(out=lo[0:n], in_=scb[0:n, 0:Lk], axis=AX.X, op=ALU.min), r=[scb], w=[lo])
                            c.op('dve', lambda e: e.tensor_reduce(out=hi[0:n], in_=scb[0:n, 0:Lk], axis=AX.X, op=ALU.max), r=[scb], w=[hi])
                            c.op('dve', lambda e: e.tensor_tensor(out=hi[0:n], in0=hi[0:n], in1=lo[0:n], op=ALU.subtract), r=[hi, lo], w=[hi])
                            c.op('dve', lambda e: e.tensor_scalar(out=wv[0:n], in0=pow2[0:n], scalar1=hi[0:n, 0:1], scalar2=None, op0=ALU.mult), r=[pow2, hi], w=[wv])
                            for k in range(NIT):
                                c.op('dve', lambda e: e.tensor_tensor(out=mid[0:n], in0=lo[0:n], in1=wv[0:n, k:k + 1], op=ALU.add), r=[lo, wv], w=[mid])
                                c.op('dve', lambda e: e.tensor_scalar(out=junk[0:n, 0:Lk], in0=scb[0:n, 0:Lk], scalar1=mid[0:n, 0:1], scalar2=0.0, op0=ALU.is_ge, op1=ALU.add, accum_out=cnt[0:n]),
                                     r=[scb, mid], w=[junk, cnt])
                                c.op('dve', lambda e: e.scalar_tensor_tensor(out=tmp1[0:n], in0=cnt[0:n], scalar=float(topk) - 0.5, in1=wv[0:n, k:k + 1], op0=ALU.is_ge, op1=ALU.mult),
                                     r=[cnt, wv], w=[tmp1])
                                c.op('dve', lambda e: e.tensor_tensor(out=lo[0:n], in0=lo[0:n], in1=tmp1[0:n], op=ALU.add), r=[lo, tmp1], w=[lo])
                            c.op('dve', lambda e: e.tensor_scalar(out=mb[0:n, 0:Lk], in0=scb[0:n, 0:Lk], scalar1=lo[0:n, 0:1], scalar2=-30000.0, op0=ALU.is_lt, op1=ALU.mult), r=[scb, lo], w=[mb])
                            nearw = min(256, Lk) if n == 128 else min(192, Lk)
                            near0 = Lk - nearw
                            nboff = (256 - nearw) if n == 128 else 0
                            if n == 64:
                                nboff = near0 - (pos0 - 128)
                            nkb = (Lk + 127) // 128
                            achunks = [(c0, min(512, near0 - c0), False) for c0 in range(0, near0, 512)] + [(near0, nearw, True)]
                            def att0(h):
                                g = h // 8
                                pb_ = pbuf[h % 2]; rs_ = rs[h % 2]; ri_ = rinv[h % 2]
                                for ci, (c0, wd, isnear) in enumerate(achunks):
                                    p_ = psd[dctr[0] % 3]; dctr[0] += 1

                                    def mm(e):
                                        e.matmul(p_[0:n, 0:wd], qbT[:, h, 0:n], KT[:, g, c0:c0 + wd], start=True, stop=False)
                                        ins = e.matmul(p_[0:n, 0:wd], identB[0:n, 0:n], mb[0:n, c0:c0 + wd], start=False, stop=(not isnear))
                                        if isnear:
                                            ins = e.matmul(p_[0:n, 0:wd], identB[0:n, 0:n], NBh[0:n, h, nboff:nboff + wd], start=False, stop=True)
                                        return ins
                                    c.op('pe', mm, r=[qbT, KT, mb, identB, NBh], w=[p_])
                                    c.op('act', lambda e: e.activation(out=pb_[0:n, c0:c0 + wd], in_=p_[0:n, 0:wd], func=AF.Exp, bias=CH[0:n, h:h + 1], accum_out=rs_[0:n, ci:ci + 1]),
                                         r=[p_, CH], w=[pb_, rs_])
                                c.op('dve', lambda e: e.tensor_reduce(out=ri_[0:n], in_=rs_[0:n, 0:len(achunks)], axis=AX.X, op=ALU.add), r=[rs_], w=[ri_])
                                c.op('dve', lambda e: e.reciprocal(out=ri_[0:n], in_=ri_[0:n]), r=[ri_], w=[ri_])
                                c.op('dve', lambda e: e.tensor_scalar(out=pb_[0:n, 0:Lk], in0=pb_[0:n, 0:Lk], scalar1=ri_[0:n, 0:1], scalar2=None, op0=ALU.mult), r=[pb_, ri_], w=[pb_])

                            def att1(h):
                                g = h // 8
                                pb_ = pbuf[h % 2]; pT_ = pT[h % 2]
                                for k4 in range(0, nkb, 4):
                                    ks = list(range(k4, min(nkb, k4 + 4)))
                                    p_ = pst[tctr[0] % 2]; tctr[0] += 1

                                    def mm(e):
                                        for j, kb_ in enumerate(ks):
                                            kw = min(128, Lk - kb_ * 128)
                                            ins = e.transpose(p_[0:kw, j, 0:n], pb_[0:n, kb_ * 128:kb_ * 128 + kw], identB[0:n, 0:n])
                                        return ins
                                    c.op('pe', mm, r=[pb_, identB], w=[p_])
                                    kwl = min(128, Lk - ks[-1] * 128)
                                    if kwl == 128:
                                        if (k4 // 4) % 2 == 0:
                                            c.op('act', lambda e: e.copy(out=pT_[:, k4:k4 + len(ks), 0:n], in_=p_[:, 0:len(ks), 0:n]), r=[p_], w=[pT_])
                                        else:
                                            c.op('dve', lambda e: e.tensor_copy(out=pT_[:, k4:k4 + len(ks), 0:n], in_=p_[:, 0:len(ks), 0:n]), r=[p_], w=[pT_])
                                    else:
                                        for j, kb_ in enumerate(ks):
                                            kw = min(128, Lk - kb_ * 128)
                                            c.op('act', lambda e: e.copy(out=pT_[0:kw, kb_, 0:n], in_=p_[0:kw, j, 0:n]), r=[p_], w=[pT_])
                                po = pso[(h // 4) % 2]

                                def mm(e):
                                    for kb_ in range(nkb):
                                        kw = min(128, Lk - kb_ * 128)
                                        ins = e.matmul(po[:, h % 4, 0:n], Vt[0:kw, kb_, g * 128:(g + 1) * 128], pT_[0:kw, kb_, 0:n], start=(kb_ == 0), stop=(kb_ == nkb - 1))
                                    return ins
                                c.op('pe', mm, r=[Vt, pT_], w=[po])
                                if h % 4 == 3:
                                    hg = h // 4
                                    o_ = obs[hg % 2]
                                    c.op('dve', lambda e: e.tensor_tensor(out=o_[:, :, 0:n], in0=po[:, :, 0:n], in1=zbT[:, hg * 4:hg * 4 + 4, 0:n], op=ALU.mult), r=[po, zbT], w=[o_])
                                    c.dma('pool', obT_d[hg * 512:(hg + 1) * 512, ts_].rearrange("(h p) t -> p h t", p=128), o_[:, :, 0:n], r=[o_])
                            for step in range(17):
                                if step < 16:
                                    att0(step)
                                if step >= 1:
                                    att1(step - 1)
                    c.barrier()

                chk('B')
                for (htok0, hn) in oparts:
                    with contextlib.ExitStack() as es:
                        m1 = SB(es, [128, KC, hn], BF16, "m1")
                        pp = [PS(es, [128, 512], F32) for _ in range(3)]
                        pctr = [0]
                        cchunks = [(c0, min(512, hn - c0)) for c0 in range(0, hn, 512)]
                        with contextlib.ExitStack() as es2:
                            big = SB(es2, [128, 16, hn], BF16, "big")
                            wstg = [SB(es2, [128, 16, 256], F32) for _ in range(2)]
                            wbf = [SB(es2, [128, 16, 256], BF16) for _ in range(2)]
                            sgs = [SB(es2, [128, 512], BF16) for _ in range(2)]
                            tmpb = [SB(es2, [128, 512], F32) for _ in range(2)]
                            wctr = [0]
                            for bi_, (srcd, wsrc, sgd) in enumerate(((ozT_d, w_a[l], sgaT_d), (obT_d, w_b[l], sgbT_d))):
                                for c0 in range(0, hn, 512):
                                    wd = min(512, hn - c0)
                                    c.dma('sp', big[:, :, c0:c0 + wd], srcd[:, htok0 + c0:htok0 + c0 + wd].rearrange("(k p) t -> p k t", p=128), w=[big])
                                for b2 in range(0, KC, 2):
                                    i = wctr[0] % 2; wctr[0] += 1
                                    st = wstg[i]; wb = wbf[i]
                                    c.dma('sp', st[:], wsrc[:, b2 * 128:b2 * 128 + 256].rearrange("(k p) n -> p k n", p=128), w=[st])
                                    c.op('dve', lambda e: e.tensor_copy(out=wb[:], in_=st[:]), r=[st], w=[wb])
                                    for bb in range(2):
                                        blk = b2 + bb
                                        for (c0, wd) in cchunks:
                                            p_ = pp[pctr[0] % 3]; s_ = sgs[pctr[0] % 2]; t_ = tmpb[pctr[0] % 2]; pctr[0] += 1
                                            c.dma('sp', s_[:, 0:wd], sgd[blk * 128:(blk + 1) * 128, htok0 + c0:htok0 + c0 + wd], w=[s_])

                                            def mm(e):
                                                for kc in range(16):
                                                    ins = e.matmul(p_[:, 0:wd], wb[:, kc, bb * 128:(bb + 1) * 128], big[:, kc, c0:c0 + wd], start=(kc == 0), stop=(kc == 15))
                                                return ins
                                            c.op('pe', mm, r=[wb, big], w=[p_])
                                            if bi_ == 0:
                                                c.op('dve', lambda e: e.tensor_tensor(out=m1[:, blk, c0:c0 + wd], in0=p_[:, 0:wd], in1=s_[:, 0:wd], op=ALU.mult), r=[p_, s_], w=[m1])
                                            else:
                                                c.op('dve', lambda e: e.tensor_tensor(out=t_[:, 0:wd], in0=p_[:, 0:wd], in1=s_[:, 0:wd], op=ALU.mult), r=[p_, s_], w=[t_])
                                                c.op('dve', lambda e: e.tensor_tensor(out=m1[:, blk, c0:c0 + wd], in0=m1[:, blk, c0:c0 + wd], in1=t_[:, 0:wd], op=ALU.add), r=[t_, m1], w=[m1])
                            c.barrier()
                        with contextlib.ExitStack() as es2:
                            Wo = SB(es2, [128, KC, D], BF16, "Wo")
                            wst2 = [SB(es2, [128, KC, 128], F32) for _ in range(2)]
                            for b1 in range(KC):
                                st = wst2[b1 % 2]
                                c.dma('sp', st[:], w_out[l, :, b1 * 128:b1 * 128 + 128].rearrange("(k p) n -> p k n", p=128), w=[st])
                                c.op('dve', lambda e: e.tensor_copy(out=Wo[:, :, b1 * 128:b1 * 128 + 128], in_=st[:]), r=[st], w=[Wo])
                            GATE = SB(es2, [128, D], F32, "GATE")
                            xt = [SB(es2, [128, D], F32) for _ in range(2)]
                            xo = [SB(es2, [128, D], F32) for _ in range(2)]
                            ssq = [SB(es2, [128, 1], F32) for _ in range(2)]
                            last_rows = None
                            ncg = max(1, D // 512)
                            cw = min(512, D)
                            for ti, (t0, parts) in enumerate(range_tiles(htok0, hn)):
                                x_ = xt[ti % 2]; xo_ = xo[ti % 2]; ss_ = ssq[ti % 2]
                                rows = tuple((idx if kind == 'p' else 2 + idx, p0, n) for (kind, idx, r0, n, p0) in parts)
                                if rows != last_rows:
                                    for (row, p0, n) in rows:
                                        c.dma('sp', GATE[p0:p0 + n, :], mod_d[row:row + 1, 2 * D:3 * D].partition_broadcast(n), w=[GATE])
                                    last_rows = rows
                                for (kind, idx, r0, n, p0) in parts:
                                    c.dma('sp', x_[p0:p0 + n, :], x_src(l, kind, idx, r0, n), w=[x_])
                                for cg in range(ncg):
                                    p_ = pp[pctr[0] % 3]; pctr[0] += 1
                                    cs = slice(cg * cw, (cg + 1) * cw)

                                    def mm(e):
                                        for kc in range(KC):
                                            ins = e.matmul(p_[:, 0:cw], m1[:, kc, t0:t0 + 128], Wo[:, kc, cs], start=(kc == 0), stop=(kc == KC - 1))
                                        return ins
                                    c.op('pe', mm, r=[m1, Wo], w=[p_])
                                    c.op('dve', lambda e: e.tensor_tensor(out=xo_[:, cs], in0=p_[:, 0:cw], in1=GATE[:, cs], op=ALU.mult), r=[p_, GATE], w=[xo_])
                                    c.op('dve', lambda e: e.tensor_tensor(out=xo_[:, cs], in0=xo_[:, cs], in1=x_[:, cs], op=ALU.add), r=[xo_, x_], w=[xo_])
                                if l < DEPTH - 1:
                                    c.dma('pool', xs_d[htok0 + t0:htok0 + t0 + 128, :], xo_[:], r=[xo_])
                                else:
                                    c.op('act', lambda e: e.activation(out=x_[:], in_=xo_[:], func=AF.Square, accum_out=ss_[:]), r=[xo_], w=[x_, ss_])
                                    c.op('act', lambda e: e.activation(out=ss_[:], in_=ss_[:], func=AF.Sqrt, scale=1.0 / D, bias=epsc[:]), r=[ss_, epsc], w=[ss_])
                                    c.op('dve', lambda e: e.reciprocal(out=ss_[:], in_=ss_[:]), r=[ss_], w=[ss_])
                                    c.op('dve', lambda e: e.scalar_tensor_tensor(out=xo_[:], in0=xo_[:], scalar=ss_[:, 0:1], in1=FNW[:], op0=ALU.mult, op1=ALU.mult), r=[xo_, ss_, FNW], w=[xo_])
                                    for (kind, idx, r0, n, p0) in parts:
                                        yo = yp if kind == 'p' else ys
                                        c.dma('pool', yo[idx, r0:r0 + n, :], xo_[p0:p0 + n, :], r=[xo_])
                        c.barrier()
        except _Stop:
            c.barrier()
            ges.pop_all()
            return nc
        c.barrier()
        c.es.close()
    return nc


def rel_bucket_np(rel):
    nb = 16
    max_exact = 8
    n = np.abs(rel)
    nf = np.maximum(n, 1).astype(np.float32)
    large = max_exact + (np.log(nf / max_exact) / np.float32(np.log(128 / max_exact)) * (nb - max_exact)).astype(np.int32)
    large = np.minimum(large, nb - 1)
    return np.where(rel > 0, nb, 0) + np.where(n < max_exact, n, large)


def make_ohr():
    r = np.arange(384) - 255
    b = rel_bucket_np(r)
    oh = np.zeros((32, 384), np.float32)
    oh[b, np.arange(384)] = 1.0
    return oh


_NC_CACHE = {}


def run(cfg, inputs, debug_outs=()):
    D, T, P, DEPTH = cfg['D'], cfg['T'], cfg['P'], cfg['DEPTH']
    key = (D, T, P, DEPTH, tuple(debug_outs))
    if key not in _NC_CACHE:
        _NC_CACHE[key] = build(cfg, debug_outs)
    nc = _NC_CACHE[key]
    f = lambda a: np.ascontiguousarray(np.asarray(a, dtype=np.float32))
    I = {k: f(v) for k, v in inputs.items()}
    ohr = make_ohr()
    in_maps = []
    for ci in range(NCORES):
        b = slice(2 * ci, 2 * ci + 2)
        m = {
            "xp": f(I['x_prompt'][b]), "xs": f(I['x_sample'][b]),
            "c4": f(np.concatenate([I['c_prompt'][b], I['c_sample'][b]], axis=0)),
            "ck": f(I['cache_k'][:, b].reshape(DEPTH, 2, P, 256)), "cv": f(I['cache_v'][:, b].reshape(DEPTH, 2, P, 256)),
            "cik": f(I['cache_idx_k'][:, b]), "sg": f(I['state_gdn'][:, b]), "scv": f(I['state_conv'][:, b]),
            "ohr": ohr,
        }
        for k in ("norm_w", "w_ada", "b_ada", "w_in", "w_conv", "a_log", "dt_bias", "gdn_norm_w", "w_branch_a", "w_branch_b", "w_out", "rel_bias", "final_norm_w"):
            m[k] = I[k]
        in_maps.append(m)
    res = run_bass_kernel_spmd(nc, in_maps, core_ids=list(range(NCORES)))
    R = res.results
    cat0 = lambda k: np.concatenate([r[k] for r in R], axis=0)
    cat1 = lambda k: np.concatenate([r[k] for r in R], axis=1)
    B = 2 * NCORES
    outs = (
        cat0("yp"), cat0("ys"),
        cat1("nkp").reshape(DEPTH, B, T, 2, 128), cat1("nvp").reshape(DEPTH, B, T, 2, 128), cat1("nikp"),
        cat1("ngp"), cat1("ncp"),
        cat1("nks").reshape(DEPTH, B, 64, 2, 128), cat1("nvs").reshape(DEPTH, B, 64, 2, 128), cat1("niks"),
        cat1("ngs"), cat1("ncs"),
    )
    if debug_outs:
        return outs, R
    return outs


def kernel(**inputs):
    cfg = dict(D=2048, T=2048, P=4096, DEPTH=2)
    return run(cfg, inputs)
```

```python
import contextlib
import numpy as np
import concourse.bass as bass
import concourse.mybir as mybir
from concourse.bass_utils import run_bass_kernel_spmd

F32 = mybir.dt.float32
BF16 = mybir.dt.bfloat16
AF = mybir.ActivationFunctionType
ALU = mybir.AluOpType
AX = mybir.AxisListType

NCORES = 8
GH = 16
HD = 128
CONV_CH = 6144
EPS = 1e-6
NIT = 14


def in_offsets(D):
    sizes = (6144, 2048, 16, 16, 2048, 256, 256, 2048, 1024, 64, 16, D, D)
    names = ("qkv", "za", "ba", "aa", "qb", "kb", "vb", "zb", "qi", "ki", "wi", "gla", "glb")
    off = {}
    o = 0
    for n, s in zip(names, sizes):
        off[n] = o
        o += s
    return off, o


class _Stop(Exception):
    pass


class Buf:
    def __init__(self, t, name):
        self.t = t
        self.name = name
        self.lw = None
        self.rd = []
        self.dsem = None

    def __getitem__(self, k):
        return self.t[k]


class Ctx:
    def __init__(self, nc, ndsem=40):
        self.nc = nc
        self.eng = {'pe': nc.tensor, 'act': nc.scalar, 'dve': nc.vector, 'pool': nc.gpsimd, 'sp': nc.sync}
        self.es = contextlib.ExitStack()
        self.sem = {}
        self.cnt = {}
        for e in self.eng:
            self.sem[e] = self.es.enter_context(nc.semaphore("s_" + e))
            self.cnt[e] = 0
        self.dsems = [self.es.enter_context(nc.semaphore("d%d" % i)) for i in range(ndsem)]
        self.dcnt = [0] * ndsem
        self.dnext = 0
        self.waited = {}

    def wait(self, e, tok):
        if tok is None:
            return
        sem, val = tok
        key = (e, id(sem))
        if self.waited.get(key, 0) >= val:
            return
        self.eng[e].wait_ge(sem, val)
        self.waited[key] = val

    def _deps(self, e, r, w):
        for b in r:
            self.wait(e, b.lw)
            if getattr(b, 'psum', False):
                for t in b.rd:
                    self.wait(e, t)
        for b in w:
            self.wait(e, b.lw)
            for t in b.rd:
                self.wait(e, t)

    def _commit(self, tok, r, w):
        for b in r:
            b.rd.append(tok)
            if len(b.rd) > 24:
                b.rd = b.rd[-24:]
        for b in w:
            b.lw = tok
            b.rd = []

    def op(self, e, fn, r=(), w=()):
        self._deps(e, r, w)
        ins = fn(self.eng[e])
        self.cnt[e] += 1
        ins.then_inc(self.sem[e], 1)
        tok = (self.sem[e], self.cnt[e])
        self._commit(tok, r, w)
        return tok

    def dma(self, q, out, in_, r=(), w=(), slow=False):
        self._deps(q, r, w)
        b = (list(w) + list(r))[0]
        if b.dsem is None:
            b.dsem = self.dnext % len(self.dsems)
            self.dnext += 1
        i = b.dsem
        kw = {}
        if slow:
            kw['allow_slow_non_contiguous'] = True
        self.eng[q].dma_start(out=out, in_=in_, **kw).then_inc(self.dsems[i], 16)
        self.dcnt[i] += 16
        tok = (self.dsems[i], self.dcnt[i])
        self._commit(tok, r, w)
        return tok

    def barrier(self):
        toks = [(self.sem[e], self.cnt[e]) for e in self.eng if self.cnt[e] > 0]
        toks += [(self.dsems[i], self.dcnt[i]) for i in range(len(self.dsems)) if self.dcnt[i] > 0]
        for e in self.eng:
            for t in toks:
                self.wait(e, t)


def build(cfg, debug_outs=()):
    D, T, P, DEPTH = cfg['D'], cfg['T'], cfg['P'], cfg['DEPTH']
    KC = D // 128
    NTP = T // 128
    NTOK = 2 * T + 128
    OFF, IN_DIM = in_offsets(D)
    LS = P + 64
    LMAX = max(T, LS)
    NKB = (LMAX + 127) // 128
    topk_p = min(256, T // 4)
    topk_s = min(256, LS // 4)

    nc = bass.Bass("TRN2", target_bir_lowering=False)

    def din(name, shape):
        return nc.dram_tensor(name, list(shape), F32, kind="ExternalInput").ap()

    def dout(name, shape):
        return nc.dram_tensor(name, list(shape), F32, kind="ExternalOutput").ap()

    def dscr(name, shape, dt):
        kind = "ExternalOutput" if name in debug_outs else "Internal"
        return nc.dram_tensor(name, list(shape), dt, kind=kind).ap()

    xp = din("xp", [2, T, D]); xs = din("xs", [2, 64, D]); c4 = din("c4", [4, D])
    ck = din("ck", [DEPTH, 2, P, 256]); cv = din("cv", [DEPTH, 2, P, 256]); cik = din("cik", [DEPTH, 2, P, 64])
    sg = din("sg", [DEPTH, 2, GH, 128, 128]); scv = din("scv", [DEPTH, 2, 3, CONV_CH])
    norm_w = din("norm_w", [DEPTH, D]); w_ada = din("w_ada", [DEPTH, D, 3 * D]); b_ada = din("b_ada", [DEPTH, 3 * D])
    w_in = din("w_in", [DEPTH, D, IN_DIM]); w_conv = din("w_conv", [DEPTH, 4, CONV_CH])
    a_log = din("a_log", [DEPTH, GH]); dt_bias = din("dt_bias", [DEPTH, GH]); gdn_norm_w = din("gdn_norm_w", [DEPTH, 128])
    w_a = din("w_branch_a", [DEPTH, 2048, D]); w_b = din("w_branch_b", [DEPTH, 2048, D]); w_out = din("w_out", [DEPTH, D, D])
    rel_bias = din("rel_bias", [32, 16]); fnw = din("final_norm_w", [D])
    ohr = din("ohr", [32, 384])

    yp = dout("yp", [2, T, D]); ys = dout("ys", [2, 64, D])
    nkp = dout("nkp", [DEPTH, 2, T, 256]); nvp = dout("nvp", [DEPTH, 2, T, 256]); nikp = dout("nikp", [DEPTH, 2, T, 64])
    ngp = dout("ngp", [DEPTH, 2, GH, 128, 128]); ncp = dout("ncp", [DEPTH, 2, 3, CONV_CH])
    nks = dout("nks", [DEPTH, 2, 64, 256]); nvs = dout("nvs", [DEPTH, 2, 64, 256]); niks = dout("niks", [DEPTH, 2, 64, 64])
    ngs = dout("ngs", [DEPTH, 2, GH, 128, 128]); ncs = dout("ncs", [DEPTH, 2, 3, CONV_CH])

    xs_d = dscr("xs_d", [NTOK, D], F32)
    mod_d = dscr("mod_d", [4, 3 * D], F32)
    tm_d = dscr("tm_d", [NTOK, 624], F32)
    tb_d = dscr("tb_d", [384, 16], F32)
    qT_d = dscr("qT_d", [2048, NTOK], BF16); kT_d = dscr("kT_d", [2048, NTOK], BF16); vT_d = dscr("vT_d", [2048, NTOK], BF16)
    zaT_d = dscr("zaT_d", [2048, NTOK], BF16); qbT_d = dscr("qbT_d", [2048, NTOK], BF16); zbT_d = dscr("zbT_d", [2048, NTOK], BF16)
    qiT_d = dscr("qiT_d", [1024, NTOK], BF16)
    sgaT_d = dscr("sgaT_d", [D, NTOK], BF16); sgbT_d = dscr("sgbT_d", [D, NTOK], BF16)
    ozT_d = dscr("ozT_d", [2048, NTOK], BF16); obT_d = dscr("obT_d", [2048, NTOK], BF16)

    c = Ctx(nc)
    uid = [0]

    def SB(es, shape, dt, name=None):
        uid[0] += 1
        nm = "%s_%d" % (name or "sb", uid[0])
        return Buf(es.enter_context(nc.sbuf_tensor(nm, list(shape), dt)), nm)

    def PS(es, shape, dt, name=None):
        uid[0] += 1
        nm = "%s_%d" % (name or "ps", uid[0])
        b = Buf(es.enter_context(nc.psum_tensor(nm, list(shape), dt)), nm)
        b.psum = True
        return b

    seqs = [('p', 0, 0, T), ('p', 1, T, T), ('s', 0, 2 * T, 64), ('s', 1, 2 * T + 64, 64)]
    halves = [(0, T), (T, T + 128)]

    with contextlib.ExitStack() as ges:
        identF = SB(ges, [128, 128], F32, "identF"); identB = SB(ges, [128, 128], BF16, "identB")
        onesB = SB(ges, [128, 128], BF16, "onesB"); onesF = SB(ges, [128, 128], F32, "onesF")
        Lm = SB(ges, [128, 128], F32, "Lm"); MLs = SB(ges, [128, 128], F32, "MLs")
        UP = SB(ges, [128, 128], F32, "UP"); LO = SB(ges, [128, 128], F32, "LO")
        NB = SB(ges, [128, 256, 16], F32, "NB"); NBh = SB(ges, [128, 16, 256], BF16, "NBh")
        CH = SB(ges, [128, 16], F32, "CH")
        epsc = SB(ges, [128, 1], F32, "epsc"); onec = SB(ges, [128, 1], F32, "onec"); zero3 = SB(ges, [128, 3], F32, "zero3")
        pow2 = SB(ges, [128, NIT], F32, "pow2")
        FNW = SB(ges, [128, D], F32, "FNW")

        def cmem(b, ap, v):
            c.op('pool', lambda e: e.memset(ap, v), w=[b])

        def csel(b, ap, pattern, op, fill, cm, base=0):
            c.op('pool', lambda e: e.affine_select(out=ap, in_=ap, pattern=pattern, compare_op=op, fill=fill,
                                                   base=base, channel_multiplier=cm), r=[b], w=[b])
        cmem(identF, identF[:], 1.0); csel(identF, identF[:], [[-1, 128]], ALU.is_equal, 0.0, 1)
        c.op('pool', lambda e: e.tensor_copy(out=identB[:], in_=identF[:]), r=[identF], w=[identB])
        cmem(onesB, onesB[:], 1.0); cmem(onesF, onesF[:], 1.0)
        cmem(Lm, Lm[:], 1.0); csel(Lm, Lm[:], [[1, 128]], ALU.is_ge, 0.0, -1)
        cmem(MLs, MLs[:], 1.0); csel(MLs, MLs[:], [[-1, 128]], ALU.is_gt, 0.0, 1)
        cmem(UP, UP[:], 1.0e4); csel(UP, UP[:], [[1, 128]], ALU.is_gt, 0.0, -1)
        cmem(LO, LO[:], 1.0e4); csel(LO, LO[:], [[-1, 128]], ALU.is_gt, 0.0, 1)
        cmem(epsc, epsc[:], EPS); cmem(onec, onec[:], 1.0); cmem(zero3, zero3[:], 0.0)
        for k in range(NIT):
            cmem(pow2, pow2[:, k:k + 1], 0.5 ** (k + 1))
        c.dma('sp', FNW[:], fnw.rearrange("(o d) -> o d", o=1).partition_broadcast(128), w=[FNW])
        c.dma('sp', CH[:], rel_bias[15:16, :].partition_broadcast(128), w=[CH])
        with contextlib.ExitStack() as es:
            oh = SB(es, [32, 384], F32); rb = SB(es, [32, 16], F32); tbs = SB(es, [128, 3, 16], F32)
            pt = PS(es, [128, 16], F32)
            c.dma('sp', oh[:], ohr[:, :], w=[oh]); c.dma('sp', rb[:], rel_bias[:, :], w=[rb])
            for j in range(3):
                c.op('pe', lambda e: e.matmul(pt[:], oh[:, j * 128:(j + 1) * 128], rb[:], start=True, stop=True), r=[oh, rb], w=[pt])
                c.op('dve', lambda e: e.tensor_tensor(out=tbs[:, j, :], in0=pt[:], in1=CH[:], op=ALU.subtract), r=[pt, CH], w=[tbs])
            c.dma('sp', tb_d.rearrange("(j p) h -> p j h", p=128), tbs[:], r=[tbs])
            c.barrier()
            for i in range(128):
                c.dma('sp', NB[i:i + 1, :, :], tb_d[127 - i:127 - i + 256, :].rearrange("(o c) h -> o c h", o=1), w=[NB])
            c.op('dve', lambda e: e.tensor_copy(out=NBh[:], in_=NB[:].rearrange("p c h -> p h c")), r=[NB], w=[NBh])
            c.barrier()

        def x_src(l, kind, idx, r0, n):
            if l == 0:
                return (xp if kind == 'p' else xs)[idx, r0:r0 + n, :]
            base = {('p', 0): 0, ('p', 1): T, ('s', 0): 2 * T, ('s', 1): 2 * T + 64}[(kind, idx)]
            return xs_d[base + r0:base + r0 + n, :]

        def range_tiles(tok0, ntok):
            tiles = []
            for a in range(tok0, tok0 + ntok, 128):
                if a < 2 * T:
                    parts = [('p', a // T, a % T, 128, 0)]
                else:
                    parts = [('s', 0, 0, 64, 0), ('s', 1, 0, 64, 64)]
                tiles.append((a - tok0, parts))
            return tiles

        def half_tiles(hf):
            return range_tiles(*halves[hf])
        oparts = [(0, T // 2), (T // 2, T // 2), (T, T // 2), (3 * T // 2, T // 2 + 128)]

        stop = cfg.get('stop')

        def chk(ph):
            if stop == ph:
                raise _Stop()
        try:
          chk('const')
          for l in range(DEPTH):
            with contextlib.ExitStack() as les:
                AT = SB(les, [128, KC, 4], F32, "AT"); BT = SB(les, [128, KC, 4], F32, "BT")
                wcT = SB(les, [128, 48, 4], F32, "wcT"); scT0 = SB(les, [128, 48, 3], F32, "scT0"); scT1 = SB(les, [128, 48, 3], F32, "scT1")
                gnw = SB(les, [128, 1], F32, "gnw"); dtb = SB(les, [128, 16], F32, "dtb"); nexpA = SB(les, [128, 16], F32, "nexpA")
                with contextlib.ExitStack() as es:
                    csb = SB(es, [4, D], F32); sc = SB(es, [4, D], F32); scT = SB(es, [128, KC, 4], F32)
                    mod = SB(es, [4, 3 * D], F32); bada = SB(es, [4, 3 * D], F32)
                    wst = [SB(es, [128, KC, 256], F32) for _ in range(2)]
                    modT = SB(es, [128, 2 * KC, 4], F32); nwT = SB(es, [128, KC], F32)
                    wc4 = SB(es, [4, CONV_CH], F32); sc3 = [SB(es, [3, CONV_CH], F32)] * 2
                    pt = PS(es, [128, 2 * KC, 4], F32); pm = [PS(es, [4, 512], F32) for _ in range(2)]
                    pw = PS(es, [128, 48, 4], F32)
                    c.dma('sp', csb[:], c4[:, :], w=[csb])
                    c.dma('sp', bada[:], b_ada[l:l + 1, :].partition_broadcast(4), w=[bada])
                    c.dma('sp', nwT[:], norm_w[l].rearrange("(k p) -> p k", p=128), w=[nwT], slow=True)
                    c.dma('sp', gnw[:], gdn_norm_w[l].rearrange("(p o) -> p o", o=1), w=[gnw], slow=True)
                    c.dma('sp', dtb[:], dt_bias[l:l + 1, :].partition_broadcast(128), w=[dtb])
                    c.dma('sp', nexpA[:], a_log[l:l + 1, :].partition_broadcast(128), w=[nexpA])
                    c.dma('sp', wc4[:], w_conv[l], w=[wc4])
                    c.op('act', lambda e: e.activation(out=nexpA[:], in_=nexpA[:], func=AF.Exp), r=[nexpA], w=[nexpA])
                    c.op('dve', lambda e: e.tensor_scalar(out=nexpA[:], in0=nexpA[:], scalar1=-1.0, scalar2=None, op0=ALU.mult), r=[nexpA], w=[nexpA])
                    c.op('act', lambda e: e.activation(out=sc[:], in_=csb[:], func=AF.Silu), r=[csb], w=[sc])
                    for kc in range(KC):
                        c.op('pe', lambda e: e.transpose(pt[:, kc, :], sc[:, kc * 128:(kc + 1) * 128], identF[0:4, 0:4]), r=[sc, identF], w=[pt])
                    c.op('dve', lambda e: e.tensor_copy(out=scT[:], in_=pt[:, 0:KC, :]), r=[pt], w=[scT])
                    for cb in range(48):
                        c.op('pe', lambda e: e.transpose(pw[:, cb, :], wc4[:, cb * 128:(cb + 1) * 128], identF[0:4, 0:4]), r=[wc4, identF], w=[pw])
                    c.op('dve', lambda e: e.tensor_copy(out=wcT[:], in_=pw[:]), r=[pw], w=[wcT])
                    for s, dst in ((0, scT0), (1, scT1)):
                        c.dma('sp', sc3[s][:], scv[l, s], w=[sc3[s]])
                        for cb in range(48):
                            c.op('pe', lambda e: e.transpose(pw[:, cb, 0:3], sc3[s][:, cb * 128:(cb + 1) * 128], identF[0:3, 0:3]), r=[sc3[s], identF], w=[pw])
                        c.op('dve', lambda e: e.tensor_copy(out=dst[:], in_=pw[:, :, 0:3]), r=[pw], w=[dst])
                    ncg = 3 * D // 256
                    for cg in range(ncg):
                        ws = wst[cg % 2]; pmm = pm[cg % 2]
                        c.dma('sp', ws[:], w_ada[l, :, cg * 256:(cg + 1) * 256].rearrange("(k p) n -> p k n", p=128), w=[ws])

                        def mm(e):
                            for kc in range(KC):
                                ins = e.matmul(pmm[:, 0:256], scT[:, kc, :], ws[:, kc, :], start=(kc == 0), stop=(kc == KC - 1))
                            return ins
                        c.op('pe', mm, r=[scT, ws], w=[pmm])
                        c.op('dve', lambda e: e.tensor_tensor(out=mod[:, cg * 256:(cg + 1) * 256], in0=pmm[:, 0:256], in1=bada[:, cg * 256:(cg + 1) * 256], op=ALU.add), r=[pmm, bada], w=[mod])
                    c.dma('pool', mod_d[:, :], mod[:], r=[mod])
                    for j in range(2 * KC):
                        c.op('pe', lambda e: e.transpose(pt[:, j, :], mod[:, j * 128:(j + 1) * 128], identF[0:4, 0:4]), r=[mod, identF], w=[pt])
                    c.op('dve', lambda e: e.tensor_copy(out=modT[:], in_=pt[:]), r=[pt], w=[modT])
                    c.op('dve', lambda e: e.tensor_copy(out=BT[:], in_=modT[:, 0:KC, :]), r=[modT], w=[BT])
                    c.op('dve', lambda e: e.scalar_tensor_tensor(out=AT[:], in0=modT[:, KC:2 * KC, :], scalar=1.0,
                                                                  in1=nwT[:].unsqueeze(2).to_broadcast([128, KC, 4]),
                                                                  op0=ALU.add, op1=ALU.mult), r=[modT, nwT], w=[AT])
                    c.barrier()

                chk('A')
                for hf, (htok0, hn) in enumerate(halves):
                    with contextlib.ExitStack() as es:
                        hT = SB(es, [128, KC, hn], BF16, "hT")
                        with contextlib.ExitStack() as es2:
                            xt = [SB(es2, [128, D], F32) for _ in range(2)]
                            junk = SB(es2, [128, D], F32)
                            xn = [SB(es2, [128, D], BF16) for _ in range(2)]
                            ssq = [SB(es2, [128, 1], F32) for _ in range(2)]
                            rstd = [SB(es2, [128, 1], F32) for _ in range(2)]
                            ptr = [PS(es2, [128, 4, 128], BF16) for _ in range(2)]
                            for ti, (t0, parts) in enumerate(half_tiles(hf)):
                                x_ = xt[ti % 2]; xn_ = xn[ti % 2]; ss_ = ssq[ti % 2]; rs_ = rstd[ti % 2]
                                for (kind, idx, r0, n, p0) in parts:
                                    c.dma('sp', x_[p0:p0 + n, :], x_src(l, kind, idx, r0, n), w=[x_])
                                c.op('act', lambda e: e.activation(out=junk[:], in_=x_[:], func=AF.Square, accum_out=ss_[:]), r=[x_], w=[junk, ss_])
                                c.op('act', lambda e: e.activation(out=ss_[:], in_=ss_[:], func=AF.Sqrt, scale=1.0 / D, bias=epsc[:]), r=[ss_, epsc], w=[ss_])
                                c.op('dve', lambda e: e.reciprocal(out=rs_[:], in_=ss_[:]), r=[ss_], w=[rs_])
                                c.op('dve', lambda e: e.tensor_scalar(out=xn_[:], in0=x_[:], scalar1=rs_[:], scalar2=None, op0=ALU.mult), r=[x_, rs_], w=[xn_])
                                for kg in range(KC // 4 if KC >= 4 else 1):
                                    kcs = list(range(kg * 4, min(KC, kg * 4 + 4)))
                                    p_ = ptr[kg % 2]

                                    def tr(e):
                                        for j, kc in enumerate(kcs):
                                            ins = e.transpose(p_[:, j, :], xn_[:, kc * 128:(kc + 1) * 128], identB[:])
                                        return ins
                                    c.op('pe', tr, r=[xn_, identB], w=[p_])
                                    for j, kc in enumerate(kcs):
                                        for (kind, idx, r0, n, p0) in parts:
                                            row = idx if kind == 'p' else 2 + idx
                                            c.op('act', lambda e: e.activation(out=hT[:, kc, t0 + p0:t0 + p0 + n], in_=p_[:, j, p0:p0 + n], func=AF.Identity,
                                                                               scale=AT[:, kc, row:row + 1], bias=BT[:, kc, row:row + 1]),
                                                 r=[p_, AT, BT], w=[hT])
                            c.barrier()
                        chk('N')
                        with contextlib.ExitStack() as es2:
                            wstg = [SB(es2, [128, KC, 256], F32) for _ in range(2)]
                            wbf = [SB(es2, [128, KC, 256], BF16) for _ in range(2)]
                            Wtm = SB(es2, [128, KC, 624], BF16)
                            wctr = [0]

                            def load_w(src, c0, ncol, dst=None, dcol=0):
                                i = wctr[0] % 2; wctr[0] += 1
                                st = wstg[i]
                                c.dma('sp', st[:, :, 0:ncol], src[:, c0:c0 + ncol].rearrange("(k p) n -> p k n", p=128), w=[st])
                                if dst is None:
                                    wb = wbf[i]
                                    c.op('dve', lambda e: e.tensor_copy(out=wb[:, :, 0:ncol], in_=st[:, :, 0:ncol]), r=[st], w=[wb])
                                    return wb
                                c.op('dve', lambda e: e.tensor_copy(out=dst[:, :, dcol:dcol + ncol], in_=st[:, :, 0:ncol]), r=[st], w=[dst])
                                return dst
                            W = w_in[l]
                            load_w(W, OFF['kb'], 256, Wtm, 0); load_w(W, OFF['vb'], 256, Wtm, 256)
                            load_w(W, OFF['ba'], 32, Wtm, 512); load_w(W, OFF['ki'], 80, Wtm, 544)
                            with contextlib.ExitStack() as es3:
                                ptm_a = [PS(es3, [128, 512], F32) for _ in range(2)]; ptm_b = [PS(es3, [128, 112], F32) for _ in range(2)]
                                ptm = [[ptm_a[0], ptm_b[0]], [ptm_a[1], ptm_b[1]]]
                                tms = [SB(es3, [128, 624], F32) for _ in range(2)]
                                for ti, (t0, parts) in enumerate(half_tiles(hf)):
                                    pa, pb = ptm[ti % 2]; tm_ = tms[ti % 2]

                                    def mm(e):
                                        for kc in range(KC):
                                            e.matmul(pa[:], hT[:, kc, t0:t0 + 128], Wtm[:, kc, 0:512], start=(kc == 0), stop=(kc == KC - 1))
                                        for kc in range(KC):
                                            ins = e.matmul(pb[:], hT[:, kc, t0:t0 + 128], Wtm[:, kc, 512:624], start=(kc == 0), stop=(kc == KC - 1))
                                        return ins
                                    c.op('pe', mm, r=[hT, Wtm], w=[pa, pb])
                                    c.op('act', lambda e: e.copy(out=tm_[:, 0:512], in_=pa[:]), r=[pa], w=[tm_])
                                    c.op('dve', lambda e: e.tensor_copy(out=tm_[:, 512:624], in_=pb[:]), r=[pb], w=[tm_])
                                    c.dma('pool', tm_d[htok0 + t0:htok0 + t0 + 128, :], tm_[:], r=[tm_])
                                    for (kind, idx, r0, n, p0) in parts:
                                        ok, ov, oi = (nkp, nvp, nikp) if kind == 'p' else (nks, nvs, niks)
                                        c.dma('pool', ok[l, idx, r0:r0 + n, :], tm_[p0:p0 + n, 0:256], r=[tm_])
                                        c.dma('pool', ov[l, idx, r0:r0 + n, :], tm_[p0:p0 + n, 256:512], r=[tm_])
                                        c.dma('pool', oi[l, idx, r0:r0 + n, :], tm_[p0:p0 + n, 544:608], r=[tm_])
                                c.barrier()
                            chunks = []
                            for t0 in range(0, T, 512):
                                wdt = min(512, T - t0)
                                chunks.append((t0, wdt, 'p', hf, t0 == 0, t0 + wdt == T))
                            if hf == 1:
                                chunks.append((T, 64, 's', 0, True, True)); chunks.append((T + 64, 64, 's', 1, True, True))
                            groups = [('conv', OFF['qkv'], 48, None), ('silu', OFF['za'], 16, zaT_d), ('qb', OFF['qb'], 16, qbT_d),
                                      ('silu', OFF['zb'], 16, zbT_d), ('copy', OFF['qi'], 8, qiT_d),
                                      ('sig', OFF['gla'], KC, sgaT_d), ('sig', OFF['glb'], KC, sgbT_d)]
                            with contextlib.ExitStack() as es3:
                                pp = [PS(es3, [128, 512], F32) for _ in range(3)]
                                pq = [PS(es3, [128, 512], F32) for _ in range(3)]
                                xpad = [SB(es3, [128, 3 + 512], F32) for _ in range(3)]
                                cvas = [SB(es3, [128, 512], F32) for _ in range(7)]; rnA = [SB(es3, [128, 512], F32) for _ in range(2)]
                                sqbs = [SB(es3, [128, 512], BF16) for _ in range(2)]; rns = [SB(es3, [128, 512], F32) for _ in range(2)]
                                ob = [SB(es3, [128, 512], BF16) for _ in range(3)]
                                pctr = [0]; octr = [0]
                                goff = OFF['qkv']
                                items = [(b2, bi, ch) for b2 in range(0, 48, 2) for bi in range(2) for ch in chunks]
                                NI = len(items)
                                wbd = {}

                                def st0_pe(i):
                                    b2, bi, (c0, wd, kind, idx, first, last) = items[i]
                                    if bi == 0 and (c0, wd) == chunks[0][0:2]:
                                        wbd[b2] = load_w(W, goff + b2 * 128, 256)
                                    wb = wbd[b2]; p_ = pp[i % 3]

                                    def mm(e):
                                        for kc in range(KC):
                                            ins = e.matmul(p_[:, 0:wd], wb[:, kc, bi * 128:(bi + 1) * 128], hT[:, kc, c0:c0 + wd], start=(kc == 0), stop=(kc == KC - 1))
                                        return ins
                                    c.op('pe', mm, r=[wb, hT], w=[p_])

                                def s1_copy(i):
                                    b2, bi, (c0, wd, kind, idx, first, last) = items[i]
                                    blk = b2 + bi
                                    p_ = pp[i % 3]; xq = xpad[i % 3]
                                    if first:
                                        if kind == 'p':
                                            c.op('act', lambda e: e.activation(out=xq[:, 0:3], in_=zero3[:], func=AF.Copy), r=[zero3], w=[xq])
                                        else:
                                            sct = scT0 if idx == 0 else scT1
                                            c.op('act', lambda e: e.copy(out=xq[:, 0:3], in_=sct[:, blk, :]), r=[sct], w=[xq])
                                    else:
                                        xprev = xpad[(i - 1) % 3]; pw_ = items[i - 1][2][1]
                                        c.op('act', lambda e: e.copy(out=xq[:, 0:3], in_=xprev[:, pw_:pw_ + 3]), r=[xprev], w=[xq])
                                    c.op('act', lambda e: e.copy(out=xq[:, 3:3 + wd], in_=p_[:, 0:wd]), r=[p_], w=[xq])
                                    if last:
                                        oc = ncp if kind == 'p' else ncs
                                        c.dma('pool', oc[l, idx, :, blk * 128:(blk + 1) * 128].rearrange("j p -> p j"), xq[:, wd:wd + 3], r=[xq], slow=True)

                                def s2_conv(i):
                                    b2, bi, (c0, wd, kind, idx, first, last) = items[i]
                                    blk = b2 + bi
                                    xq = xpad[i % 3]; cva = cvas[i % 7]
                                    c.op('dve', lambda e: e.tensor_scalar(out=cva[:, 0:wd], in0=xq[:, 0:wd], scalar1=wcT[:, blk, 0:1], scalar2=None, op0=ALU.mult), r=[xq, wcT], w=[cva])
                                    for j in range(1, 4):
                                        c.op('dve', lambda e: e.scalar_tensor_tensor(out=cva[:, 0:wd], in0=xq[:, j:j + wd], scalar=wcT[:, blk, j:j + 1], in1=cva[:, 0:wd],
                                                                                      op0=ALU.mult, op1=ALU.add), r=[xq, wcT, cva], w=[cva])

                                def s3_sig(i):
                                    b2, bi, (c0, wd, kind, idx, first, last) = items[i]
                                    cva = cvas[i % 7]; rn = rnA[i % 2]
                                    c.op('act', lambda e: e.activation(out=rn[:, 0:wd], in_=cva[:, 0:wd], func=AF.Exp, scale=-1.0), r=[cva], w=[rn])
                                    c.op('act', lambda e: e.activation(out=rn[:, 0:wd], in_=rn[:, 0:wd], func=AF.Ln, bias=onec[:]), r=[rn, onec], w=[rn])
                                    c.op('act', lambda e: e.activation(out=rn[:, 0:wd], in_=rn[:, 0:wd], func=AF.Exp, scale=-1.0), r=[rn], w=[rn])

                                def s4_silu(i):
                                    b2, bi, (c0, wd, kind, idx, first, last) = items[i]
                                    blk = b2 + bi
                                    cva = cvas[i % 7]; rn = rnA[i % 2]; o_ = ob[i % 3]
                                    tsl = slice(htok0 + c0, htok0 + c0 + wd)
                                    if blk >= 32:
                                        c.op('dve', lambda e: e.tensor_tensor(out=o_[:, 0:wd], in0=cva[:, 0:wd], in1=rn[:, 0:wd], op=ALU.mult), r=[cva, rn], w=[o_])
                                        c.dma('pool', vT_d[(blk - 32) * 128:(blk - 31) * 128, tsl], o_[:, 0:wd], r=[o_])
                                    else:
                                        c.op('dve', lambda e: e.tensor_tensor(out=cva[:, 0:wd], in0=cva[:, 0:wd], in1=rn[:, 0:wd], op=ALU.mult), r=[cva, rn], w=[cva])

                                def s5_sq(i):
                                    b2, bi, (c0, wd, kind, idx, first, last) = items[i]
                                    if b2 + bi >= 32:
                                        return
                                    cva = cvas[i % 7]; sqb = sqbs[i % 2]
                                    c.op('act', lambda e: e.activation(out=sqb[:, 0:wd], in_=cva[:, 0:wd], func=AF.Square), r=[cva], w=[sqb])

                                def s6_ones(i):
                                    b2, bi, (c0, wd, kind, idx, first, last) = items[i]
                                    if b2 + bi >= 32:
                                        return
                                    sqb = sqbs[i % 2]; q_ = pq[i % 3]
                                    c.op('pe', lambda e: e.matmul(q_[:, 0:wd], onesB[:], sqb[:, 0:wd], start=True, stop=True), r=[onesB, sqb], w=[q_])

                                def s7_rs(i):
                                    b2, bi, (c0, wd, kind, idx, first, last) = items[i]
                                    if b2 + bi >= 32:
                                        return
                                    rn = rns[i % 2]; q_ = pq[i % 3]
                                    c.op('act', lambda e: e.activation(out=rn[:, 0:wd], in_=q_[:, 0:wd], func=AF.Ln, bias=epsc[:]), r=[q_, epsc], w=[rn])
                                    c.op('act', lambda e: e.activation(out=rn[:, 0:wd], in_=rn[:, 0:wd], func=AF.Exp, scale=-0.5), r=[rn], w=[rn])

                                def s8_out(i):
                                    b2, bi, (c0, wd, kind, idx, first, last) = items[i]
                                    blk = b2 + bi
                                    if blk >= 32:
                                        return
                                    cva = cvas[i % 7]; rn = rns[i % 2]; o_ = ob[i % 3]
                                    tsl = slice(htok0 + c0, htok0 + c0 + wd)
                                    qs = float(HD) ** -0.5 if blk < 16 else 1.0
                                    c.op('dve', lambda e: e.scalar_tensor_tensor(out=o_[:, 0:wd], in0=cva[:, 0:wd], scalar=qs, in1=rn[:, 0:wd], op0=ALU.mult, op1=ALU.mult),
                                         r=[cva, rn], w=[o_])
                                    dst = qT_d if blk < 16 else kT_d; drow = (blk % 16) * 128
                                    c.dma('pool', dst[drow:drow + 128, tsl], o_[:, 0:wd], r=[o_])
                                stages = [st0_pe, s1_copy, s2_conv, s3_sig, s4_silu, s5_sq, s6_ones, s7_rs, s8_out]
                                for t in range(NI + len(stages) - 1):
                                    for k, fn in enumerate(stages):
                                        if 0 <= t - k < NI:
                                            fn(t - k)
                                for (gk, goff, nblk, dst) in groups[1:]:
                                    for b2 in range(0, nblk, 2):
                                        wb = load_w(W, goff + b2 * 128, 256)
                                        for bi in range(2):
                                            blk = b2 + bi
                                            for (c0, wd, kind, idx, first, last) in chunks:
                                                p_ = pp[pctr[0] % 3]; pctr[0] += 1
                                                o_ = ob[octr[0] % 3]; octr[0] += 1

                                                def mm(e):
                                                    for kc in range(KC):
                                                        ins = e.matmul(p_[:, 0:wd], wb[:, kc, bi * 128:(bi + 1) * 128], hT[:, kc, c0:c0 + wd], start=(kc == 0), stop=(kc == KC - 1))
                                                    return ins
                                                c.op('pe', mm, r=[wb, hT], w=[p_])
                                                tsl = slice(htok0 + c0, htok0 + c0 + wd)
                                                if gk == 'silu':
                                                    c.op('act', lambda e: e.activation(out=o_[:, 0:wd], in_=p_[:, 0:wd], func=AF.Silu), r=[p_], w=[o_])
                                                elif gk == 'sig':
                                                    c.op('act', lambda e: e.activation(out=o_[:, 0:wd], in_=p_[:, 0:wd], func=AF.Sigmoid), r=[p_], w=[o_])
                                                elif gk == 'qb':
                                                    c.op('act', lambda e: e.activation(out=o_[:, 0:wd], in_=p_[:, 0:wd], func=AF.Copy, scale=float(HD) ** -0.5), r=[p_], w=[o_])
                                                else:
                                                    c.op('act', lambda e: e.copy(out=o_[:, 0:wd], in_=p_[:, 0:wd]), r=[p_], w=[o_])
                                                c.dma('pool', dst[blk * 128:(blk + 1) * 128, tsl], o_[:, 0:wd], r=[o_])
                        c.barrier()

                chk('P')
                with contextlib.ExitStack() as es:
                    S = SB(es, [128, GH, 128], F32, "S"); Sb = SB(es, [128, GH, 128], BF16, "Sb")
                    Ibc = SB(es, [128, 4, 128], F32, "Ibc")
                    for hh in range(4):
                        c.op('pool', lambda e: e.tensor_copy(out=Ibc[:, hh, :], in_=identF[:]), r=[identF], w=[Ibc])
                    qT = SB(es, [128, GH, 128], BF16); kT = SB(es, [128, GH, 128], BF16); vT = SB(es, [128, GH, 128], BF16); zaT = SB(es, [128, GH, 128], BF16)
                    bgr = SB(es, [128, 32], F32); beta = SB(es, [128, 16], F32); g_ = SB(es, [128, 16], F32)
                    gc = SB(es, [128, 16], F32); gcT = SB(es, [16, 128], F32); egc = SB(es, [128, 16], F32)
                    kdsc = SB(es, [128, 16], F32); gle = SB(es, [128, 16], F32); bege = SB(es, [128, 16], F32)
                    MB = SB(es, [128, 16, 128], F32); GB = SB(es, [128, 16, 128], F32)
                    Xs = [SB(es, [128, 4, 128], F32) for _ in range(2)] * 2
                    Dm = [SB(es, [128, 4, 128], F32) for _ in range(2)] * 2
                    DTm = [SB(es, [128, 4, 128], F32) for _ in range(2)] * 2
                    egr = [SB(es, [128, 4, 128], F32) for _ in range(2)] * 2
                    Pa = [[SB(es, [128, 4, 128], F32) for _ in range(2)] for _ in range(4)]
                    PTa = [[SB(es, [128, 4, 128], F32) for _ in range(2)] for _ in range(4)]
                    Y = [SB(es, [128, 4, 128], F32) for _ in range(4)]
                    YB = [SB(es, [128, 4, 128], BF16) for _ in range(4)]
                    vb = [SB(es, [128, 4, 128], BF16) for _ in range(4)]
                    kbg = [SB(es, [128, 4, 128], BF16) for _ in range(4)]
                    kdec = [SB(es, [128, 4, 128], BF16) for _ in range(4)]
                    wTb = [SB(es, [128, 4, 128], BF16) for _ in range(4)]
                    us = [SB(es, [128, 4, 128], F32) for _ in range(4)]
                    QKm = [SB(es, [128, 4, 128], BF16) for _ in range(4)]
                    qdT = [SB(es, [128, 4, 128], BF16) for _ in range(4)]
                    vn = [SB(es, [128, 4, 128], BF16) for _ in range(4)]
                    oTs = [SB(es, [128, 4, 128], F32) for _ in range(4)]
                    osq = [SB(es, [128, 4, 128], BF16) for _ in range(4)]
                    rst = [SB(es, [128, 4, 128], F32) for _ in range(4)]
                    ozs = [SB(es, [128, 4, 128], BF16) for _ in range(4)]
                    psA = [PS(es, [128, 4, 128], F32) for _ in range(5)]
                    psB = [PS(es, [128, 4, 128], BF16) for _ in range(2)]
                    psS = PS(es, [128, 128], F32)
                    pctr = [0]

                    def nps():
                        p = psA[pctr[0] % 5]; pctr[0] += 1
                        return p
                    for (kind, idx, tok0, slen) in seqs:
                        n = 128 if kind == 'p' else 64
                        nst = 6 if n == 128 else 5
                        if kind == 'p':
                            c.op('pool', lambda e: e.memset(S[:], 0.0), w=[S])
                        else:
                            c.dma('sp', S[:], sg[l, idx].rearrange("h k v -> k h v"), w=[S])
                        c.op('act', lambda e: e.copy(out=Sb[:], in_=S[:]), r=[S], w=[Sb])
                        for t in range(slen // n):
                            ts_ = slice(tok0 + t * n, tok0 + (t + 1) * n)
                            for (dstb, srcd) in ((qT, qT_d), (kT, kT_d), (vT, vT_d), (zaT, zaT_d)):
                                c.dma('sp', dstb[:, :, 0:n], srcd[:, ts_].rearrange("(h p) t -> p h t", p=128), w=[dstb])
                            c.dma('sp', bgr[0:n, :], tm_d[ts_, 512:544], w=[bgr])
                            c.op('act', lambda e: e.activation(out=beta[0:n], in_=bgr[0:n, 0:16], func=AF.Sigmoid), r=[bgr], w=[beta])
                            c.op('dve', lambda e: e.tensor_tensor(out=g_[0:n], in0=bgr[0:n, 16:32], in1=dtb[0:n], op=ALU.add), r=[bgr, dtb], w=[g_])
                            c.op('act', lambda e: e.activation(out=gc[0:n], in_=g_[0:n], func=AF.Abs), r=[g_], w=[gc])
                            c.op('act', lambda e: e.activation(out=gc[0:n], in_=gc[0:n], func=AF.Exp, scale=-1.0), r=[gc], w=[gc])
                            c.op('act', lambda e: e.activation(out=gc[0:n], in_=gc[0:n], func=AF.Ln, bias=onec[0:n]), r=[gc, onec], w=[gc])
                            c.op('dve', lambda e: e.scalar_tensor_tensor(out=g_[0:n], in0=g_[0:n], scalar=0.0, in1=gc[0:n], op0=ALU.max, op1=ALU.add), r=[g_, gc], w=[g_])
                            c.op('dve', lambda e: e.tensor_tensor(out=g_[0:n], in0=g_[0:n], in1=nexpA[0:n], op=ALU.mult), r=[g_, nexpA], w=[g_])
                            c.op('pe', lambda e: e.matmul(psS[0:n, 0:16], Lm[0:n, 0:n], g_[0:n, :], start=True, stop=True), r=[Lm, g_], w=[psS])
                            c.op('dve', lambda e: e.tensor_copy(out=gc[0:n], in_=psS[0:n, 0:16]), r=[psS], w=[gc])
                            c.op('pe', lambda e: e.matmul(psS[0:16, 0:n], g_[0:n, :], Lm[0:n, 0:n], start=True, stop=True), r=[Lm, g_], w=[psS])
                            c.op('dve', lambda e: e.tensor_copy(out=gcT[:, 0:n], in_=psS[0:16, 0:n]), r=[psS], w=[gcT])
                            c.op('pe', lambda e: e.matmul(psS[:, 0:16], onesF[0:n, :], g_[0:n, :], start=True, stop=True), r=[onesF, g_], w=[psS])
                            c.op('act', lambda e: e.activation(out=gle[:], in_=psS[:, 0:16], func=AF.Exp), r=[psS], w=[gle])
                            c.op('dve', lambda e: e.tensor_tensor(out=kdsc[0:n], in0=psS[0:n, 0:16], in1=gc[0:n], op=ALU.subtract), r=[psS, gc], w=[kdsc])
                            c.op('act', lambda e: e.activation(out=kdsc[0:n], in_=kdsc[0:n], func=AF.Exp), r=[kdsc], w=[kdsc])
                            c.op('act', lambda e: e.activation(out=egc[0:n], in_=gc[0:n], func=AF.Exp), r=[gc], w=[egc])
                            c.op('dve', lambda e: e.tensor_tensor(out=bege[0:n], in0=egc[0:n], in1=beta[0:n], op=ALU.mult), r=[egc, beta], w=[bege])
                            c.op('dve', lambda e: e.tensor_copy(out=GB[0:n], in_=g_[0:n, :].unsqueeze(2).to_broadcast([n, 16, 128])), r=[g_], w=[GB])
                            c.op('dve', lambda e: e.tensor_tensor(out=MB[0:n, :, 0:n], in0=MLs[0:n, 0:n].unsqueeze(1).to_broadcast([n, 16, n]),
                                                                   in1=beta[0:n, :].unsqueeze(2).to_broadcast([n, 16, n]), op=ALU.mult), r=[MLs, beta], w=[MB])
                            chk('G1')
                            for hg in range(4):
                                hs = slice(hg * 4, hg * 4 + 4)
                                pg = nps()

                                def mm(e):
                                    for hh in range(4):
                                        ins = e.matmul(pg[:, hh, 0:n], GB[0:n, hg * 4 + hh, :], Lm[0:n, 0:n], start=True, stop=True)
                                    return ins
                                c.op('pe', mm, r=[GB, Lm], w=[pg])
                                chk('G1a')
                                c.op('act', lambda e: e.activation(out=egr[hg][:, :, 0:n], in_=pg[:, :, 0:n], func=AF.Exp), r=[pg], w=[egr[hg]])
                                chk('G1b')
                                c.op('dve', lambda e: e.tensor_tensor(out=Xs[hg][0:n, :, 0:n], in0=pg[0:n, :, 0:n], in1=gc[0:n, hs].unsqueeze(2).to_broadcast([n, 4, n]), op=ALU.subtract),
                                     r=[pg, gc], w=[Xs[hg]])
                                chk('Ga')
                                c.op('dve', lambda e: e.tensor_tensor(out=Dm[hg][0:n, :, 0:n], in0=Xs[hg][0:n, :, 0:n], in1=UP[0:n, 0:n].unsqueeze(1).to_broadcast([n, 4, n]), op=ALU.add),
                                     r=[Xs[hg], UP], w=[Dm[hg]])
                                chk('H1')
                                c.op('act', lambda e: e.activation(out=Dm[hg][0:n, :, 0:n], in_=Dm[hg][0:n, :, 0:n], func=AF.Exp, scale=-1.0), r=[Dm[hg]], w=[Dm[hg]])
                                chk('H2')
                                c.op('pool', lambda e: e.tensor_tensor(out=DTm[hg][0:n, :, 0:n], in0=Xs[hg][0:n, :, 0:n], in1=LO[0:n, 0:n].unsqueeze(1).to_broadcast([n, 4, n]), op=ALU.subtract),
                                     r=[Xs[hg], LO], w=[DTm[hg]])
                                c.op('act', lambda e: e.activation(out=DTm[hg][0:n, :, 0:n], in_=DTm[hg][0:n, :, 0:n], func=AF.Exp), r=[DTm[hg]], w=[DTm[hg]])
                                chk('H3')
                                c.op('dve', lambda e: e.tensor_tensor(out=Dm[hg][0:n, :, 0:n], in0=Dm[hg][0:n, :, 0:n], in1=MB[0:n, hs, 0:n], op=ALU.mult), r=[Dm[hg], MB], w=[Dm[hg]])
                                chk('Gb')
                                pk = nps()

                                def mm(e):
                                    for hh in range(4):
                                        h = hg * 4 + hh
                                        ins = e.matmul(pk[0:n, hh, 0:n], kT[:, h, 0:n], kT[:, h, 0:n], start=True, stop=True)
                                    return ins
                                c.op('pe', mm, r=[kT], w=[pk])
                                A_ = PTa[hg][0]; B_ = Pa[hg][0]
                                c.op('dve', lambda e: e.tensor_tensor(out=A_[0:n, :, 0:n], in0=pk[0:n, :, 0:n], in1=Dm[hg][0:n, :, 0:n], op=ALU.mult), r=[pk, Dm[hg]], w=[A_])
                                pb_ = nps()

                                def mm(e):
                                    for hh in range(4):
                                        ins = e.transpose(pb_[0:n, hh, 0:n], A_[0:n, hh, 0:n], identF[0:n, 0:n])
                                    return ins
                                c.op('pe', mm, r=[A_, identF], w=[pb_])
                                c.op('act', lambda e: e.copy(out=B_[0:n, :, 0:n], in_=pb_[0:n, :, 0:n]), r=[pb_], w=[B_])
                                c.op('dve', lambda e: e.tensor_tensor(out=Y[hg][0:n, :, 0:n], in0=Ibc[0:n, :, 0:n], in1=pb_[0:n, :, 0:n], op=ALU.subtract), r=[pb_, Ibc], w=[Y[hg]])
                                chk('Gc')
                                ptk = psB[0]

                                def mm(e):
                                    for hh in range(4):
                                        ins = e.transpose(ptk[0:n, hh, :], kT[:, hg * 4 + hh, 0:n], identB[:])
                                    return ins
                                c.op('pe', mm, r=[kT, identB], w=[ptk])
                                c.op('dve', lambda e: e.tensor_tensor(out=kbg[hg][0:n], in0=ptk[0:n], in1=bege[0:n, hs].unsqueeze(2).to_broadcast([n, 4, 128]), op=ALU.mult), r=[ptk, bege], w=[kbg[hg]])
                                c.op('dve', lambda e: e.tensor_tensor(out=kdec[hg][0:n], in0=ptk[0:n], in1=kdsc[0:n, hs].unsqueeze(2).to_broadcast([n, 4, 128]), op=ALU.mult), r=[ptk, kdsc], w=[kdec[hg]])
                                ptv = psB[1]

                                def mm(e):
                                    for hh in range(4):
                                        ins = e.transpose(ptv[0:n, hh, :], vT[:, hg * 4 + hh, 0:n], identB[:])
                                    return ins
                                c.op('pe', mm, r=[vT, identB], w=[ptv])
                                c.op('dve', lambda e: e.tensor_tensor(out=vb[hg][0:n], in0=ptv[0:n], in1=beta[0:n, hs].unsqueeze(2).to_broadcast([n, 4, 128]), op=ALU.mult), r=[ptv, beta], w=[vb[hg]])
                                chk('Gd')
                                pqk = nps()

                                def mm(e):
                                    for hh in range(4):
                                        h = hg * 4 + hh
                                        ins = e.matmul(pqk[0:n, hh, 0:n], kT[:, h, 0:n], qT[:, h, 0:n], start=True, stop=True)
                                    return ins
                                c.op('pe', mm, r=[kT, qT], w=[pqk])
                                c.op('dve', lambda e: e.tensor_tensor(out=QKm[hg][0:n, :, 0:n], in0=pqk[0:n, :, 0:n], in1=DTm[hg][0:n, :, 0:n], op=ALU.mult), r=[pqk, DTm[hg]], w=[QKm[hg]])
                                c.op('pool', lambda e: e.tensor_tensor(out=qdT[hg][:, :, 0:n], in0=qT[:, hs, 0:n], in1=egr[hg][:, :, 0:n], op=ALU.mult), r=[qT, egr[hg]], w=[qdT[hg]])
                            chk('G2')
                            cur = 0
                            for m in range(nst):
                                lastm = (m == nst - 1)
                                for hg in range(4):
                                    P_ = Pa[hg][cur]; PT_ = PTa[hg][cur]; PTn = PTa[hg][1 - cur]
                                    p2 = nps()

                                    def mm(e):
                                        for hh in range(4):
                                            ins = e.matmul(p2[0:n, hh, 0:n], P_[0:n, hh, 0:n], PT_[0:n, hh, 0:n], start=True, stop=True)
                                        return ins
                                    c.op('pe', mm, r=[PT_, P_], w=[p2])
                                    c.op('dve', lambda e: e.tensor_copy(out=PTn[0:n, :, 0:n], in_=p2[0:n, :, 0:n]), r=[p2], w=[PTn])
                                if not lastm:
                                    for hg in range(4):
                                        P_ = Pa[hg][cur]; PT_ = PTa[hg][cur]; Pn = Pa[hg][1 - cur]
                                        p1 = nps()

                                        def mm(e):
                                            for hh in range(4):
                                                ins = e.matmul(p1[0:n, hh, 0:n], PT_[0:n, hh, 0:n], P_[0:n, hh, 0:n], start=True, stop=True)
                                            return ins
                                        c.op('pe', mm, r=[PT_, P_], w=[p1])
                                        c.op('act', lambda e: e.copy(out=Pn[0:n, :, 0:n], in_=p1[0:n, :, 0:n]), r=[p1], w=[Pn])
                                for hg in range(4):
                                    PTn = PTa[hg][1 - cur]
                                    p3 = nps()

                                    def mm(e):
                                        for hh in range(4):
                                            ins = e.matmul(p3[0:n, hh, 0:n], PTn[0:n, hh, 0:n], Y[hg][0:n, hh, 0:n], start=True, stop=True)
                                        return ins
                                    c.op('pe', mm, r=[PTn, Y[hg]], w=[p3])
                                    c.op('dve', lambda e: e.tensor_tensor(out=Y[hg][0:n, :, 0:n], in0=Y[hg][0:n, :, 0:n], in1=p3[0:n, :, 0:n], op=ALU.add), r=[p3, Y[hg]], w=[Y[hg]])
                                cur = 1 - cur
                            chk('G3')
                            for hg in range(4):
                                hs = slice(hg * 4, hg * 4 + 4)
                                c.op('act', lambda e: e.copy(out=YB[hg][0:n, :, 0:n], in_=Y[hg][0:n, :, 0:n]), r=[Y[hg]], w=[YB[hg]])
                                pu = nps()

                                def mm(e):
                                    for hh in range(4):
                                        ins = e.matmul(pu[0:n, hh, :], YB[hg][0:n, hh, 0:n], vb[hg][0:n, hh, :], start=True, stop=True)
                                    return ins
                                c.op('pe', mm, r=[YB[hg], vb[hg]], w=[pu])
                                c.op('act', lambda e: e.copy(out=us[hg][0:n], in_=pu[0:n]), r=[pu], w=[us[hg]])
                                pw_ = nps()

                                def mm(e):
                                    for hh in range(4):
                                        ins = e.matmul(pw_[:, hh, 0:n], kbg[hg][0:n, hh, :], YB[hg][0:n, hh, 0:n], start=True, stop=True)
                                    return ins
                                c.op('pe', mm, r=[YB[hg], kbg[hg]], w=[pw_])
                                c.op('dve', lambda e: e.tensor_copy(out=wTb[hg][:, :, 0:n], in_=pw_[:, :, 0:n]), r=[pw_], w=[wTb[hg]])
                            chk('G4')
                            for hg in range(4):
                                pv_ = nps()

                                def mm(e):
                                    for hh in range(4):
                                        ins = e.matmul(pv_[0:n, hh, :], wTb[hg][:, hh, 0:n], Sb[:, hg * 4 + hh, :], start=True, stop=True)
                                    return ins
                                c.op('pe', mm, r=[wTb[hg], Sb], w=[pv_])
                                c.op('dve', lambda e: e.tensor_tensor(out=vn[hg][0:n], in0=us[hg][0:n], in1=pv_[0:n], op=ALU.subtract), r=[us[hg], pv_], w=[vn[hg]])
                            for hg in range(4):
                                po = nps()

                                def mm(e):
                                    for hh in range(4):
                                        e.matmul(po[:, hh, 0:n], Sb[:, hg * 4 + hh, :], qdT[hg][:, hh, 0:n], start=True, stop=False)
                                        ins = e.matmul(po[:, hh, 0:n], vn[hg][0:n, hh, :], QKm[hg][0:n, hh, 0:n], start=False, stop=True)
                                    return ins
                                c.op('pe', mm, r=[Sb, qdT[hg], vn[hg], QKm[hg]], w=[po])
                                o_ = oTs[hg]; q_ = osq[hg]
                                c.op('act', lambda e: e.copy(out=o_[:, :, 0:n], in_=po[:, :, 0:n]), r=[po], w=[o_])
                                c.op('act', lambda e: e.activation(out=q_[:, :, 0:n], in_=po[:, :, 0:n], func=AF.Square), r=[po], w=[q_])
                            for hg in range(4):
                                hs = slice(hg * 4, hg * 4 + 4)
                                pS = nps()

                                def mm(e):
                                    for hh in range(4):
                                        ins = e.matmul(pS[:, hh, :], kdec[hg][0:n, hh, :], vn[hg][0:n, hh, :], start=True, stop=True)
                                    return ins
                                c.op('pe', mm, r=[kdec[hg], vn[hg]], w=[pS])
                                c.op('pool', lambda e: e.tensor_tensor(out=S[:, hs, :], in0=S[:, hs, :], in1=gle[:, hs].unsqueeze(2).to_broadcast([128, 4, 128]), op=ALU.mult), r=[S, gle, Sb], w=[S])
                                c.op('dve', lambda e: e.tensor_tensor(out=S[:, hs, :], in0=S[:, hs, :], in1=pS[:], op=ALU.add), r=[S, pS], w=[S])
                            for hg in range(4):
                                hs = slice(hg * 4, hg * 4 + 4)
                                o_ = oTs[hg]; q_ = osq[hg]; r_ = rst[hg]; z_ = ozs[hg]
                                pss = nps()
                                if n == 128:
                                    c.op('pe', lambda e: e.matmul(pss[:].rearrange("p a b -> p (a b)"), onesB[:], q_[:].rearrange("p a b -> p (a b)"), start=True, stop=True), r=[onesB, q_], w=[pss])
                                else:
                                    def mm(e):
                                        for hh in range(4):
                                            ins = e.matmul(pss[:, hh, 0:n], onesB[:], q_[:, hh, 0:n], start=True, stop=True)
                                        return ins
                                    c.op('pe', mm, r=[onesB, q_], w=[pss])
                                c.op('act', lambda e: e.activation(out=r_[:, :, 0:n], in_=pss[:, :, 0:n], func=AF.Ln, scale=1.0 / 128, bias=epsc[:]), r=[pss, epsc], w=[r_])
                                c.op('act', lambda e: e.activation(out=r_[:, :, 0:n], in_=r_[:, :, 0:n], func=AF.Exp, scale=-0.5), r=[r_], w=[r_])
                                c.op('dve', lambda e: e.tensor_tensor(out=o_[:, :, 0:n], in0=o_[:, :, 0:n], in1=r_[:, :, 0:n], op=ALU.mult), r=[o_, r_], w=[o_])
                                c.op('dve', lambda e: e.scalar_tensor_tensor(out=z_[:, :, 0:n], in0=o_[:, :, 0:n], scalar=gnw[:, 0:1], in1=zaT[:, hs, 0:n], op0=ALU.mult, op1=ALU.mult),
                                     r=[o_, gnw, zaT], w=[z_])
                                c.dma('pool', ozT_d[hg * 512:(hg + 1) * 512, ts_].rearrange("(h p) t -> p h t", p=128), z_[:, :, 0:n], r=[z_])
                            c.op('act', lambda e: e.copy(out=Sb[:], in_=S[:]), r=[S], w=[Sb])
                            chk('G5')
                        og = ngp if kind == 'p' else ngs
                        c.dma('pool', og[l, idx].rearrange("h k v -> k h v"), S[:], r=[S])
                    c.barrier()

                chk('G')
                with contextlib.ExitStack() as es:
                    KT = SB(es, [128, 2, NKB * 128], BF16, "KT"); Vt = SB(es, [128, NKB, 256], BF16, "Vt"); KIT = SB(es, [128, NKB * 128], BF16, "KIT")
                    kst = [SB(es, [128, 576], F32) for _ in range(2)]
                    kbf = [SB(es, [128, 384], BF16) for _ in range(2)]
                    qiT = SB(es, [128, 8, 128], BF16); qbT = SB(es, [128, 16, 128], BF16); zbT = SB(es, [128, 16, 128], BF16)
                    wi = SB(es, [128, 16], F32); absw = SB(es, [128, 16], F32); sgn = SB(es, [128, 16], F32)
                    scb = SB(es, [128, LMAX], F32, "scb"); mb = SB(es, [128, LMAX], BF16, "mb"); junk = SB(es, [128, LMAX], F32, "junkb")
                    rr = [SB(es, [128, 512], F32) for _ in range(2)]
                    hi = SB(es, [128, 1], F32); lo = SB(es, [128, 1], F32); lo2 = SB(es, [128, 1], F32); mid = SB(es, [128, 1], F32)
                    cnt = SB(es, [128, 1], F32); tmp1 = SB(es, [128, 1], F32); wv = SB(es, [128, NIT], F32)
                    pbuf = [SB(es, [128, LMAX], BF16) for _ in range(2)]
                    rs = [SB(es, [128, 16], F32) for _ in range(2)]
                    rinv = [SB(es, [128, 1], F32) for _ in range(2)]
                    pT = [SB(es, [128, NKB, 128], BF16) for _ in range(2)]
                    obs = [SB(es, [128, 4, 128], BF16) for _ in range(2)]
                    psd = [PS(es, [128, 512], F32) for _ in range(3)]
                    pst = [PS(es, [128, 4, 128], BF16) for _ in range(2)]
                    pso = [PS(es, [128, 4, 128], F32) for _ in range(2)]
                    dctr = [0]; tctr = [0]; kctr = [0]

                    def add_keys(src_k, src_v, src_ki, nrows, kbi):
                        st = kst[kctr[0] % 2]; bf = kbf[kctr[0] % 2]; kctr[0] += 1
                        c.dma('sp', st[0:nrows, 0:256], src_k, w=[st])
                        c.dma('sp', st[0:nrows, 256:512], src_v, w=[st])
                        c.dma('sp', st[0:nrows, 512:576], src_ki, w=[st])
                        c.op('pool', lambda e: e.tensor_copy(out=bf[0:nrows, 0:256], in_=st[0:nrows, 0:256]), r=[st], w=[bf])
                        c.op('pool', lambda e: e.tensor_copy(out=bf[0:nrows, 256:320], in_=st[0:nrows, 512:576]), r=[st], w=[bf])
                        c.op('pool', lambda e: e.tensor_copy(out=bf[0:nrows, 320:384], in_=st[0:nrows, 512:576]), r=[st], w=[bf])
                        c.op('pool', lambda e: e.tensor_copy(out=Vt[0:nrows, kbi, :], in_=st[0:nrows, 256:512]), r=[st], w=[Vt])
                        p_ = pst[tctr[0] % 2]; tctr[0] += 1

                        def mm(e):
                            for j in range(3):
                                ins = e.transpose(p_[:, j, 0:nrows], bf[0:nrows, j * 128:(j + 1) * 128], identB[0:nrows, 0:nrows])
                            return ins
                        c.op('pe', mm, r=[bf, identB], w=[p_])
                        c.op('act', lambda e: e.copy(out=KT[:, :, kbi * 128:kbi * 128 + nrows], in_=p_[:, 0:2, 0:nrows]), r=[p_], w=[KT])
                        c.op('act', lambda e: e.copy(out=KIT[:, kbi * 128:kbi * 128 + nrows], in_=p_[:, 2, 0:nrows]), r=[p_], w=[KIT])

                    for (kind, idx, tok0, slen) in seqs:
                        n = 128 if kind == 'p' else 64
                        topk = topk_p if kind == 'p' else topk_s
                        if kind == 's':
                            for kb_ in range(P // 128):
                                add_keys(ck[l, idx, kb_ * 128:(kb_ + 1) * 128, :], cv[l, idx, kb_ * 128:(kb_ + 1) * 128, :], cik[l, idx, kb_ * 128:(kb_ + 1) * 128, :], 128, kb_)
                        for t in range(slen // n):
                            ts_ = slice(tok0 + t * n, tok0 + (t + 1) * n)
                            pos0 = t * 128 if kind == 'p' else P
                            Lk = pos0 + n
                            add_keys(tm_d[ts_, 0:256], tm_d[ts_, 256:512], tm_d[ts_, 544:608], n, pos0 // 128)
                            c.dma('sp', qiT[:, :, 0:n], qiT_d[:, ts_].rearrange("(j p) t -> p j t", p=128), w=[qiT])
                            c.dma('sp', qbT[:, :, 0:n], qbT_d[:, ts_].rearrange("(h p) t -> p h t", p=128), w=[qbT])
                            c.dma('sp', zbT[:, :, 0:n], zbT_d[:, ts_].rearrange("(h p) t -> p h t", p=128), w=[zbT])
                            c.dma('sp', wi[0:n, :], tm_d[ts_, 608:624], w=[wi])
                            c.op('act', lambda e: e.activation(out=absw[0:n], in_=wi[0:n], func=AF.Abs), r=[wi], w=[absw])
                            c.op('act', lambda e: e.activation(out=sgn[0:n], in_=wi[0:n], func=AF.Sign), r=[wi], w=[sgn])
                            kchunks = [(c0, min(512, Lk - c0)) for c0 in range(0, Lk, 512)]
                            for (c0, wd) in kchunks:
                                for h in range(16):
                                    hb = 64 * (h % 2)
                                    p_ = psd[dctr[0] % 3]; r_ = rr[dctr[0] % 2]; dctr[0] += 1
                                    c.op('pe', lambda e: e.matmul(p_[0:n, 0:wd], qiT[hb:hb + 64, h // 2, 0:n], KIT[hb:hb + 64, c0:c0 + wd], start=True, stop=True), r=[qiT, KIT], w=[p_])
                                    c.op('act', lambda e: e.activation(out=r_[0:n, 0:wd], in_=p_[0:n, 0:wd], func=AF.Relu, scale=absw[0:n, h:h + 1]), r=[p_, absw], w=[r_])
                                    if h == 0:
                                        c.op('dve', lambda e: e.tensor_scalar(out=scb[0:n, c0:c0 + wd], in0=r_[0:n, 0:wd], scalar1=sgn[0:n, 0:1], scalar2=None, op0=ALU.mult), r=[r_, sgn], w=[scb])
                                    else:
                                        c.op('dve', lambda e: e.scalar_tensor_tensor(out=scb[0:n, c0:c0 + wd], in0=r_[0:n, 0:wd], scalar=sgn[0:n, h:h + 1], in1=scb[0:n, c0:c0 + wd],
                                                                                      op0=ALU.mult, op1=ALU.add), r=[r_, sgn, scb], w=[scb])
                            if kind == 'p':
                                c.op('dve', lambda e: e.tensor_reduce(out=lo[0:64], in_=scb[0:64, 0:Lk - 64], axis=AX.X, op=ALU.min), r=[scb], w=[lo])
                                c.op('dve', lambda e: e.tensor_reduce(out=lo[64:128], in_=scb[64:128, 0:Lk], axis=AX.X, op=ALU.min), r=[scb], w=[lo])
                                c.op('dve', lambda e: e.memset(scb[0:64, Lk - 64:Lk], -1.0e30), w=[scb])
                            else:
                                c.op('dve', lambda e: e.tensor_reduce(out=lo[0:n], in_=scb[0:n, 0:Lk], axis=AX.X, op=ALU.min), r=[scb], w=[lo])
                            c.op('dve', lambda e: e.tensor_reduce(out=hi[0:n], in_=scb[0:n, 0:Lk], axis=AX.X, op=ALU.max), r=[scb], w=[hi])
                            c.op('dve', lambda e: e.tensor_tensor(out=hi[0:n], in0=hi[0:n], in1=lo[0:n], op=ALU.subtract), r=[hi, lo], w=[hi])
                            c.op('dve', lambda e: e.tensor_scalar(out=wv[0:n], in0=pow2[0:n], scalar1=hi[0:n, 0:1], scalar2=None, op0=ALU.mult), r=[pow2, hi], w=[wv])
                            for k in range(NIT):
                                c.op('dve', lambda e: e.tensor_tensor(out=mid[0:n], in0=lo[0:n], in1=wv[0:n, k:k + 1], op=ALU.add), r=[lo, wv], w=[mid])
                                c.op('dve', lambda e: e.tensor_scalar(out=junk[0:n, 0:Lk], in0=scb[0:n, 0:Lk], scalar1=mid[0:n, 0:1], scalar2=0.0, op0=ALU.is_ge, op1=ALU.add, accum_out=cnt[0:n]),
                                     r=[scb, mid], w=[junk, cnt])
                                c.op('dve', lambda e: e.scalar_tensor_tensor(out=tmp1[0:n], in0=cnt[0:n], scalar=float(topk) - 0.5, in1=wv[0:n, k:k + 1], op0=ALU.is_ge, op1=ALU.mult),
                                     r=[cnt, wv], w=[tmp1])
                                c.op('dve', lambda e: e.tensor_tensor(out=lo[0:n], in0=lo[0:n], in1=tmp1[0:n], op=ALU.add), r=[lo, tmp1], w=[lo])
                            c.op('dve', lambda e: e.tensor_scalar(out=mb[0:n, 0:Lk], in0=scb[0:n, 0:Lk], scalar1=lo[0:n, 0:1], scalar2=-30000.0, op0=ALU.is_lt, op1=ALU.mult), r=[scb, lo], w=[mb])
                            nearw = min(256, Lk) if n == 128 else min(192, Lk)
                            near0 = Lk - nearw
                            nboff = (256 - nearw) if n == 128 else 0
                            if n == 64:
                                nboff = near0 - (pos0 - 128)
                            nkb = (Lk + 127) // 128
                            achunks = [(c0, min(512, near0 - c0), False) for c0 in range(0, near0, 512)] + [(near0, nearw, True)]
                            def att0(h):
                                g = h // 8
                                pb_ = pbuf[h % 2]; rs_ = rs[h % 2]; ri_ = rinv[h % 2]
                                for ci, (c0, wd, isnear) in enumerate(achunks):
                                    p_ = psd[dctr[0] % 3]; dctr[0] += 1

                                    def mm(e):
                                        e.matmul(p_[0:n, 0:wd], qbT[:, h, 0:n], KT[:, g, c0:c0 + wd], start=True, stop=False)
                                        ins = e.matmul(p_[0:n, 0:wd], identB[0:n, 0:n], mb[0:n, c0:c0 + wd], start=False, stop=(not isnear))
                                        if isnear:
                                            ins = e.matmul(p_[0:n, 0:wd], identB[0:n, 0:n], NBh[0:n, h, nboff:nboff + wd], start=False, stop=True)
                                        return ins
                                    c.op('pe', mm, r=[qbT, KT, mb, identB, NBh], w=[p_])
                                    c.op('act', lambda e: e.activation(out=pb_[0:n, c0:c0 + wd], in_=p_[0:n, 0:wd], func=AF.Exp, bias=CH[0:n, h:h + 1], accum_out=rs_[0:n, ci:ci + 1]),
                                         r=[p_, CH], w=[pb_, rs_])
                                c.op('dve', lambda e: e.tensor_reduce(out=ri_[0:n], in_=rs_[0:n, 0:len(achunks)], axis=AX.X, op=ALU.add), r=[rs_], w=[ri_])
                                c.op('dve', lambda e: e.reciprocal(out=ri_[0:n], in_=ri_[0:n]), r=[ri_], w=[ri_])
                                c.op('dve', lambda e: e.tensor_scalar(out=pb_[0:n, 0:Lk], in0=pb_[0:n, 0:Lk], scalar1=ri_[0:n, 0:1], scalar2=None, op0=ALU.mult), r=[pb_, ri_], w=[pb_])

                            def att1(h):
                                g = h // 8
                                pb_ = pbuf[h % 2]; pT_ = pT[h % 2]
                                for k4 in range(0, nkb, 4):
                                    ks = list(range(k4, min(nkb, k4 + 4)))
                                    p_ = pst[tctr[0] % 2]; tctr[0] += 1

                                    def mm(e):
                                        for j, kb_ in enumerate(ks):
                                            kw = min(128, Lk - kb_ * 128)
                                            ins = e.transpose(p_[0:kw, j, 0:n], pb_[0:n, kb_ * 128:kb_ * 128 + kw], identB[0:n, 0:n])
                                        return ins
                                    c.op('pe', mm, r=[pb_, identB], w=[p_])
                                    kwl = min(128, Lk - ks[-1] * 128)
                                    if kwl == 128:
                                        if (k4 // 4) % 2 == 0:
                                            c.op('act', lambda e: e.copy(out=pT_[:, k4:k4 + len(ks), 0:n], in_=p_[:, 0:len(ks), 0:n]), r=[p_], w=[pT_])
                                        else:
                                            c.op('dve', lambda e: e.tensor_copy(out=pT_[:, k4:k4 + len(ks), 0:n], in_=p_[:, 0:len(ks), 0:n]), r=[p_], w=[pT_])
                                    else:
                                        for j, kb_ in enumerate(ks):
                                            kw = min(128, Lk - kb_ * 128)
                                            c.op('act', lambda e: e.copy(out=pT_[0:kw, kb_, 0:n], in_=p_[0:kw, j, 0:n]), r=[p_], w=[pT_])
                                po = pso[(h // 4) % 2]

                                def mm(e):
                                    for kb_ in range(nkb):
                                        kw = min(128, Lk - kb_ * 128)
                                        ins = e.matmul(po[:, h % 4, 0:n], Vt[0:kw, kb_, g * 128:(g + 1) * 128], pT_[0:kw, kb_, 0:n], start=(kb_ == 0), stop=(kb_ == nkb - 1))
                                    return ins
                                c.op('pe', mm, r=[Vt, pT_], w=[po])
                                if h % 4 == 3:
                                    hg = h // 4
                                    o_ = obs[hg % 2]
                                    c.op('dve', lambda e: e.tensor_tensor(out=o_[:, :, 0:n], in0=po[:, :, 0:n], in1=zbT[:, hg * 4:hg * 4 + 4, 0:n], op=ALU.mult), r=[po, zbT], w=[o_])
                                    c.dma('pool', obT_d[hg * 512:(hg + 1) * 512, ts_].rearrange("(h p) t -> p h t", p=128), o_[:, :, 0:n], r=[o_])
                            for step in range(17):
                                if step < 16:
                                    att0(step)
                                if step >= 1:
                                    att1(step - 1)
                    c.barrier()

                chk('B')
                for (htok0, hn) in oparts:
                    with contextlib.ExitStack() as es:
                        m1 = SB(es, [128, KC, hn], BF16, "m1")
                        pp = [PS(es, [128, 512], F32) for _ in range(3)]
                        pctr = [0]
                        cchunks = [(c0, min(512, hn - c0)) for c0 in range(0, hn, 512)]
                        with contextlib.ExitStack() as es2:
                            big = SB(es2, [128, 16, hn], BF16, "big")
                            wstg = [SB(es2, [128, 16, 256], F32) for _ in range(2)]
                            wbf = [SB(es2, [128, 16, 256], BF16) for _ in range(2)]
                            sgs = [SB(es2, [128, 512], BF16) for _ in range(2)]
                            tmpb = [SB(es2, [128, 512], F32) for _ in range(2)]
                            wctr = [0]
                            for bi_, (srcd, wsrc, sgd) in enumerate(((ozT_d, w_a[l], sgaT_d), (obT_d, w_b[l], sgbT_d))):
                                for c0 in range(0, hn, 512):
                                    wd = min(512, hn - c0)
                                    c.dma('sp', big[:, :, c0:c0 + wd], srcd[:, htok0 + c0:htok0 + c0 + wd].rearrange("(k p) t -> p k t", p=128), w=[big])
                                for b2 in range(0, KC, 2):
                                    i = wctr[0] % 2; wctr[0] += 1
                                    st = wstg[i]; wb = wbf[i]
                                    c.dma('sp', st[:], wsrc[:, b2 * 128:b2 * 128 + 256].rearrange("(k p) n -> p k n", p=128), w=[st])
                                    c.op('dve', lambda e: e.tensor_copy(out=wb[:], in_=st[:]), r=[st], w=[wb])
                                    for bb in range(2):
                                        blk = b2 + bb
                                        for (c0, wd) in cchunks:
                                            p_ = pp[pctr[0] % 3]; s_ = sgs[pctr[0] % 2]; t_ = tmpb[pctr[0] % 2]; pctr[0] += 1
                                            c.dma('sp', s_[:, 0:wd], sgd[blk * 128:(blk + 1) * 128, htok0 + c0:htok0 + c0 + wd], w=[s_])

                                            def mm(e):
                                                for kc in range(16):
                                                    ins = e.matmul(p_[:, 0:wd], wb[:, kc, bb * 128:(bb + 1) * 128], big[:, kc, c0:c0 + wd], start=(kc == 0), stop=(kc == 15))
                                                return ins
                                            c.op('pe', mm, r=[wb, big], w=[p_])
                                            if bi_ == 0:
                                                c.op('dve', lambda e: e.tensor_tensor(out=m1[:, blk, c0:c0 + wd], in0=p_[:, 0:wd], in1=s_[:, 0:wd], op=ALU.mult), r=[p_, s_], w=[m1])
                                            else:
                                                c.op('dve', lambda e: e.tensor_tensor(out=t_[:, 0:wd], in0=p_[:, 0:wd], in1=s_[:, 0:wd], op=ALU.mult), r=[p_, s_], w=[t_])
                                                c.op('dve', lambda e: e.tensor_tensor(out=m1[:, blk, c0:c0 + wd], in0=m1[:, blk, c0:c0 + wd], in1=t_[:, 0:wd], op=ALU.add), r=[t_, m1], w=[m1])
                            c.barrier()
                        with contextlib.ExitStack() as es2:
                            Wo = SB(es2, [128, KC, D], BF16, "Wo")
                            wst2 = [SB(es2, [128, KC, 128], F32) for _ in range(2)]
                            for b1 in range(KC):
                                st = wst2[b1 % 2]
                                c.dma('sp', st[:], w_out[l, :, b1 * 128:b1 * 128 + 128].rearrange("(k p) n -> p k n", p=128), w=[st])
                                c.op('dve', lambda e: e.tensor_copy(out=Wo[:, :, b1 * 128:b1 * 128 + 128], in_=st[:]), r=[st], w=[Wo])
                            GATE = SB(es2, [128, D], F32, "GATE")
                            xt = [SB(es2, [128, D], F32) for _ in range(2)]
                            xo = [SB(es2, [128, D], F32) for _ in range(2)]
                            ssq = [SB(es2, [128, 1], F32) for _ in range(2)]
                            last_rows = None
                            ncg = max(1, D // 512)
                            cw = min(512, D)
                            for ti, (t0, parts) in enumerate(range_tiles(htok0, hn)):
                                x_ = xt[ti % 2]; xo_ = xo[ti % 2]; ss_ = ssq[ti % 2]
                                rows = tuple((idx if kind == 'p' else 2 + idx, p0, n) for (kind, idx, r0, n, p0) in parts)
                                if rows != last_rows:
                                    for (row, p0, n) in rows:
                                        c.dma('sp', GATE[p0:p0 + n, :], mod_d[row:row + 1, 2 * D:3 * D].partition_broadcast(n), w=[GATE])
                                    last_rows = rows
                                for (kind, idx, r0, n, p0) in parts:
                                    c.dma('sp', x_[p0:p0 + n, :], x_src(l, kind, idx, r0, n), w=[x_])
                                for cg in range(ncg):
                                    p_ = pp[pctr[0] % 3]; pctr[0] += 1
                                    cs = slice(cg * cw, (cg + 1) * cw)

                                    def mm(e):
                                        for kc in range(KC):
                                            ins = e.matmul(p_[:, 0:cw], m1[:, kc, t0:t0 + 128], Wo[:, kc, cs], start=(kc == 0), stop=(kc == KC - 1))
                                        return ins
                                    c.op('pe', mm, r=[m1, Wo], w=[p_])
                                    c.op('dve', lambda e: e.tensor_tensor(out=xo_[:, cs], in0=p_[:, 0:cw], in1=GATE[:, cs], op=ALU.mult), r=[p_, GATE], w=[xo_])
                                    c.op('dve', lambda e: e.tensor_tensor(out=xo_[:, cs], in0=xo_[:, cs], in1=x_[:, cs], op=ALU.add), r=[xo_, x_], w=[xo_])
                                if l < DEPTH - 1:
                                    c.dma('pool', xs_d[htok0 + t0:htok0 + t0 + 128, :], xo_[:], r=[xo_])
                                else:
                                    c.op('act', lambda e: e.activation(out=x_[:], in_=xo_[:], func=AF.Square, accum_out=ss_[:]), r=[xo_], w=[x_, ss_])
                                    c.op('act', lambda e: e.activation(out=ss_[:], in_=ss_[:], func=AF.Sqrt, scale=1.0 / D, bias=epsc[:]), r=[ss_, epsc], w=[ss_])
                                    c.op('dve', lambda e: e.reciprocal(out=ss_[:], in_=ss_[:]), r=[ss_], w=[ss_])
                                    c.op('dve', lambda e: e.scalar_tensor_tensor(out=xo_[:], in0=xo_[:], scalar=ss_[:, 0:1], in1=FNW[:], op0=ALU.mult, op1=ALU.mult), r=[xo_, ss_, FNW], w=[xo_])
                                    for (kind, idx, r0, n, p0) in parts:
                                        yo = yp if kind == 'p' else ys
                                        c.dma('pool', yo[idx, r0:r0 + n, :], xo_[p0:p0 + n, :], r=[xo_])
                        c.barrier()
        except _Stop:
            c.barrier()
            ges.pop_all()
            return nc
        c.barrier()
        c.es.close()
    return nc


def rel_bucket_np(rel):
    nb = 16
    max_exact = 8
    n = np.abs(rel)
    nf = np.maximum(n, 1).astype(np.float32)
    large = max_exact + (np.log(nf / max_exact) / np.float32(np.log(128 / max_exact)) * (nb - max_exact)).astype(np.int32)
    large = np.minimum(large, nb - 1)
    return np.where(rel > 0, nb, 0) + np.where(n < max_exact, n, large)


def make_ohr():
    r = np.arange(384) - 255
    b = rel_bucket_np(r)
    oh = np.zeros((32, 384), np.float32)
    oh[b, np.arange(384)] = 1.0
    return oh


_NC_CACHE = {}


def run(cfg, inputs, debug_outs=()):
    D, T, P, DEPTH = cfg['D'], cfg['T'], cfg['P'], cfg['DEPTH']
    key = (D, T, P, DEPTH, tuple(debug_outs))
    if key not in _NC_CACHE:
        _NC_CACHE[key] = build(cfg, debug_outs)
    nc = _NC_CACHE[key]
    f = lambda a: np.ascontiguousarray(np.asarray(a, dtype=np.float32))
    I = {k: f(v) for k, v in inputs.items()}
    ohr = make_ohr()
    in_maps = []
    for ci in range(NCORES):
        b = slice(2 * ci, 2 * ci + 2)
        m = {
            "xp": f(I['x_prompt'][b]), "xs": f(I['x_sample'][b]),
            "c4": f(np.concatenate([I['c_prompt'][b], I['c_sample'][b]], axis=0)),
            "ck": f(I['cache_k'][:, b].reshape(DEPTH, 2, P, 256)), "cv": f(I['cache_v'][:, b].reshape(DEPTH, 2, P, 256)),
            "cik": f(I['cache_idx_k'][:, b]), "sg": f(I['state_gdn'][:, b]), "scv": f(I['state_conv'][:, b]),
            "ohr": ohr,
        }
        for k in ("norm_w", "w_ada", "b_ada", "w_in", "w_conv", "a_log", "dt_bias", "gdn_norm_w", "w_branch_a", "w_branch_b", "w_out", "rel_bias", "final_norm_w"):
            m[k] = I[k]
        in_maps.append(m)
    res = run_bass_kernel_spmd(nc, in_maps, core_ids=list(range(NCORES)))
    R = res.results
    cat0 = lambda k: np.concatenate([r[k] for r in R], axis=0)
    cat1 = lambda k: np.concatenate([r[k] for r in R], axis=1)
    B = 2 * NCORES
    outs = (
        cat0("yp"), cat0("ys"),
        cat1("nkp").reshape(DEPTH, B, T, 2, 128), cat1("nvp").reshape(DEPTH, B, T, 2, 128), cat1("nikp"),
        cat1("ngp"), cat1("ncp"),
        cat1("nks").reshape(DEPTH, B, 64, 2, 128), cat1("nvs").reshape(DEPTH, B, 64, 2, 128), cat1("niks"),
        cat1("ngs"), cat1("ncs"),
    )
    if debug_outs:
        return outs, R
    return outs


def kernel(**inputs):
    cfg = dict(D=2048, T=2048, P=4096, DEPTH=2)
    return run(cfg, inputs)
```

```python
import contextlib
import numpy as np
import concourse.bass as bass
import concourse.mybir as mybir
from concourse.bass_utils import run_bass_kernel_spmd

F32 = mybir.dt.float32
BF16 = mybir.dt.bfloat16
AF = mybir.ActivationFunctionType
ALU = mybir.AluOpType
AX = mybir.AxisListType

NCORES = 8
GH = 16
HD = 128
CONV_CH = 6144
EPS = 1e-6
NIT = 14


def in_offsets(D):
    sizes = (6144, 2048, 16, 16, 2048, 256, 256, 2048, 1024, 64, 16, D, D)
    names = ("qkv", "za", "ba", "aa", "qb", "kb", "vb", "zb", "qi", "ki", "wi", "gla", "glb")
    off = {}
    o = 0
    for n, s in zip(names, sizes):
        off[n] = o
        o += s
    return off, o


class _Stop(Exception):
    pass


class Buf:
    def __init__(self, t, name):
        self.t = t
        self.name = name
        self.lw = None
        self.rd = []
        self.dsem = None

    def __getitem__(self, k):
        return self.t[k]


class Ctx:
    def __init__(self, nc, ndsem=40):
        self.nc = nc
        self.eng = {'pe': nc.tensor, 'act': nc.scalar, 'dve': nc.vector, 'pool': nc.gpsimd, 'sp': nc.sync}
        self.es = contextlib.ExitStack()
        self.sem = {}
        self.cnt = {}
        for e in self.eng:
            self.sem[e] = self.es.enter_context(nc.semaphore("s_" + e))
            self.cnt[e] = 0
        self.dsems = [self.es.enter_context(nc.semaphore("d%d" % i)) for i in range(ndsem)]
        self.dcnt = [0] * ndsem
        self.dnext = 0
        self.waited = {}

    def wait(self, e, tok):
        if tok is None:
            return
        sem, val = tok
        key = (e, id(sem))
        if self.waited.get(key, 0) >= val:
            return
        self.eng[e].wait_ge(sem, val)
        self.waited[key] = val

    def _deps(self, e, r, w):
        for b in r:
            self.wait(e, b.lw)
            if getattr(b, 'psum', False):
                for t in b.rd:
                    self.wait(e, t)
        for b in w:
            self.wait(e, b.lw)
            for t in b.rd:
                self.wait(e, t)

    def _commit(self, tok, r, w):
        for b in r:
            b.rd.append(tok)
            if len(b.rd) > 24:
                b.rd = b.rd[-24:]
        for b in w:
            b.lw = tok
            b.rd = []

    def op(self, e, fn, r=(), w=()):
        self._deps(e, r, w)
        ins = fn(self.eng[e])
        self.cnt[e] += 1
        ins.then_inc(self.sem[e], 1)
        tok = (self.sem[e], self.cnt[e])
        self._commit(tok, r, w)
        return tok

    def dma(self, q, out, in_, r=(), w=(), slow=False):
        self._deps(q, r, w)
        b = (list(w) + list(r))[0]
        if b.dsem is None:
            b.dsem = self.dnext % len(self.dsems)
            self.dnext += 1
        i = b.dsem
        kw = {}
        if slow:
            kw['allow_slow_non_contiguous'] = True
        self.eng[q].dma_start(out=out, in_=in_, **kw).then_inc(self.dsems[i], 16)
        self.dcnt[i] += 16
        tok = (self.dsems[i], self.dcnt[i])
        self._commit(tok, r, w)
        return tok

    def barrier(self):
        toks = [(self.sem[e], self.cnt[e]) for e in self.eng if self.cnt[e] > 0]
        toks += [(self.dsems[i], self.dcnt[i]) for i in range(len(self.dsems)) if self.dcnt[i] > 0]
        for e in self.eng:
            for t in toks:
                self.wait(e, t)


def build(cfg, debug_outs=()):
    D, T, P, DEPTH = cfg['D'], cfg['T'], cfg['P'], cfg['DEPTH']
    KC = D // 128
    NTP = T // 128
    NTOK = 2 * T + 128
    OFF, IN_DIM = in_offsets(D)
    LS = P + 64
    LMAX = max(T, LS)
    NKB = (LMAX + 127) // 128
    topk_p = min(256, T // 4)
    topk_s = min(256, LS // 4)

    nc = bass.Bass("TRN2", target_bir_lowering=False)

    def din(name, shape):
        return nc.dram_tensor(name, list(shape), F32, kind="ExternalInput").ap()

    def dout(name, shape):
        return nc.dram_tensor(name, list(shape), F32, kind="ExternalOutput").ap()

    def dscr(name, shape, dt):
        kind = "ExternalOutput" if name in debug_outs else "Internal"
        return nc.dram_tensor(name, list(shape), dt, kind=kind).ap()

    xp = din("xp", [2, T, D]); xs = din("xs", [2, 64, D]); c4 = din("c4", [4, D])
    ck = din("ck", [DEPTH, 2, P, 256]); cv = din("cv", [DEPTH, 2, P, 256]); cik = din("cik", [DEPTH, 2, P, 64])
    sg = din("sg", [DEPTH, 2, GH, 128, 128]); scv = din("scv", [DEPTH, 2, 3, CONV_CH])
    norm_w = din("norm_w", [DEPTH, D]); w_ada = din("w_ada", [DEPTH, D, 3 * D]); b_ada = din("b_ada", [DEPTH, 3 * D])
    w_in = din("w_in", [DEPTH, D, IN_DIM]); w_conv = din("w_conv", [DEPTH, 4, CONV_CH])
    a_log = din("a_log", [DEPTH, GH]); dt_bias = din("dt_bias", [DEPTH, GH]); gdn_norm_w = din("gdn_norm_w", [DEPTH, 128])
    w_a = din("w_branch_a", [DEPTH, 2048, D]); w_b = din("w_branch_b", [DEPTH, 2048, D]); w_out = din("w_out", [DEPTH, D, D])
    rel_bias = din("rel_bias", [32, 16]); fnw = din("final_norm_w", [D])
    ohr = din("ohr", [32, 384])

    yp = dout("yp", [2, T, D]); ys = dout("ys", [2, 64, D])
    nkp = dout("nkp", [DEPTH, 2, T, 256]); nvp = dout("nvp", [DEPTH, 2, T, 256]); nikp = dout("nikp", [DEPTH, 2, T, 64])
    ngp = dout("ngp", [DEPTH, 2, GH, 128, 128]); ncp = dout("ncp", [DEPTH, 2, 3, CONV_CH])
    nks = dout("nks", [DEPTH, 2, 64, 256]); nvs = dout("nvs", [DEPTH, 2, 64, 256]); niks = dout("niks", [DEPTH, 2, 64, 64])
    ngs = dout("ngs", [DEPTH, 2, GH, 128, 128]); ncs = dout("ncs", [DEPTH, 2, 3, CONV_CH])

    xs_d = dscr("xs_d", [NTOK, D], F32)
    mod_d = dscr("mod_d", [4, 3 * D], F32)
    tm_d = dscr("tm_d", [NTOK, 624], F32)
    tb_d = dscr("tb_d", [384, 16], F32)
    qT_d = dscr("qT_d", [2048, NTOK], BF16); kT_d = dscr("kT_d", [2048, NTOK], BF16); vT_d = dscr("vT_d", [2048, NTOK], BF16)
    zaT_d = dscr("zaT_d", [2048, NTOK], BF16); qbT_d = dscr("qbT_d", [2048, NTOK], BF16); zbT_d = dscr("zbT_d", [2048, NTOK], BF16)
    qiT_d = dscr("qiT_d", [1024, NTOK], BF16)
    sgaT_d = dscr("sgaT_d", [D, NTOK], BF16); sgbT_d = dscr("sgbT_d", [D, NTOK], BF16)
    ozT_d = dscr("ozT_d", [2048, NTOK], BF16); obT_d = dscr("obT_d", [2048, NTOK], BF16)

    c = Ctx(nc)
    uid = [0]

    def SB(es, shape, dt, name=None):
        uid[0] += 1
        nm = "%s_%d" % (name or "sb", uid[0])
        return Buf(es.enter_context(nc.sbuf_tensor(nm, list(shape), dt)), nm)

    def PS(es, shape, dt, name=None):
        uid[0] += 1
        nm = "%s_%d" % (name or "ps", uid[0])
        b = Buf(es.enter_context(nc.psum_tensor(nm, list(shape), dt)), nm)
        b.psum = True
        return b

    seqs = [('p', 0, 0, T), ('p', 1, T, T), ('s', 0, 2 * T, 64), ('s', 1, 2 * T + 64, 64)]
    halves = [(0, T), (T, T + 128)]

    with contextlib.ExitStack() as ges:
        identF = SB(ges, [128, 128], F32, "identF"); identB = SB(ges, [128, 128], BF16, "identB")
        onesB = SB(ges, [128, 128], BF16, "onesB"); onesF = SB(ges, [128, 128], F32, "onesF")
        Lm = SB(ges, [128, 128], F32, "Lm"); MLs = SB(ges, [128, 128], F32, "MLs")
        UP = SB(ges, [128, 128], F32, "UP"); LO = SB(ges, [128, 128], F32, "LO")
        NB = SB(ges, [128, 256, 16], F32, "NB"); NBh = SB(ges, [128, 16, 256], BF16, "NBh")
        CH = SB(ges, [128, 16], F32, "CH")
        epsc = SB(ges, [128, 1], F32, "epsc"); onec = SB(ges, [128, 1], F32, "onec"); zero3 = SB(ges, [128, 3], F32, "zero3")
        pow2 = SB(ges, [128, NIT], F32, "pow2")
        FNW = SB(ges, [128, D], F32, "FNW")

        def cmem(b, ap, v):
            c.op('pool', lambda e: e.memset(ap, v), w=[b])

        def csel(b, ap, pattern, op, fill, cm, base=0):
            c.op('pool', lambda e: e.affine_select(out=ap, in_=ap, pattern=pattern, compare_op=op, fill=fill,
                                                   base=base, channel_multiplier=cm), r=[b], w=[b])
        cmem(identF, identF[:], 1.0); csel(identF, identF[:], [[-1, 128]], ALU.is_equal, 0.0, 1)
        c.op('pool', lambda e: e.tensor_copy(out=identB[:], in_=identF[:]), r=[identF], w=[identB])
        cmem(onesB, onesB[:], 1.0); cmem(onesF, onesF[:], 1.0)
        cmem(Lm, Lm[:], 1.0); csel(Lm, Lm[:], [[1, 128]], ALU.is_ge, 0.0, -1)
        cmem(MLs, MLs[:], 1.0); csel(MLs, MLs[:], [[-1, 128]], ALU.is_gt, 0.0, 1)
        cmem(UP, UP[:], 1.0e4); csel(UP, UP[:], [[1, 128]], ALU.is_gt, 0.0, -1)
        cmem(LO, LO[:], 1.0e4); csel(LO, LO[:], [[-1, 128]], ALU.is_gt, 0.0, 1)
        cmem(epsc, epsc[:], EPS); cmem(onec, onec[:], 1.0); cmem(zero3, zero3[:], 0.0)
        for k in range(NIT):
            cmem(pow2, pow2[:, k:k + 1], 0.5 ** (k + 1))
        c.dma('sp', FNW[:], fnw.rearrange("(o d) -> o d", o=1).partition_broadcast(128), w=[FNW])
        c.dma('sp', CH[:], rel_bias[15:16, :].partition_broadcast(128), w=[CH])
        with contextlib.ExitStack() as es:
            oh = SB(es, [32, 384], F32); rb = SB(es, [32, 16], F32); tbs = SB(es, [128, 3, 16], F32)
            pt = PS(es, [128, 16], F32)
            c.dma('sp', oh[:], ohr[:, :], w=[oh]); c.dma('sp', rb[:], rel_bias[:, :], w=[rb])
            for j in range(3):
                c.op('pe', lambda e: e.matmul(pt[:], oh[:, j * 128:(j + 1) * 128], rb[:], start=True, stop=True), r=[oh, rb], w=[pt])
                c.op('dve', lambda e: e.tensor_tensor(out=tbs[:, j, :], in0=pt[:], in1=CH[:], op=ALU.subtract), r=[pt, CH], w=[tbs])
            c.dma('sp', tb_d.rearrange("(j p) h -> p j h", p=128), tbs[:], r=[tbs])
            c.barrier()
            for i in range(128):
                c.dma('sp', NB[i:i + 1, :, :], tb_d[127 - i:127 - i + 256, :].rearrange("(o c) h -> o c h", o=1), w=[NB])
            c.op('dve', lambda e: e.tensor_copy(out=NBh[:], in_=NB[:].rearrange("p c h -> p h c")), r=[NB], w=[NBh])
            c.barrier()

        def x_src(l, kind, idx, r0, n):
            if l == 0:
                return (xp if kind == 'p' else xs)[idx, r0:r0 + n, :]
            base = {('p', 0): 0, ('p', 1): T, ('s', 0): 2 * T, ('s', 1): 2 * T + 64}[(kind, idx)]
            return xs_d[base + r0:base + r0 + n, :]

        def range_tiles(tok0, ntok):
            tiles = []
            for a in range(tok0, tok0 + ntok, 128):
                if a < 2 * T:
                    parts = [('p', a // T, a % T, 128, 0)]
                else:
                    parts = [('s', 0, 0, 64, 0), ('s', 1, 0, 64, 64)]
                tiles.append((a - tok0, parts))
            return tiles

        def half_tiles(hf):
            return range_tiles(*halves[hf])
        oparts = [(0, T // 2), (T // 2, T // 2), (T, T // 2), (3 * T // 2, T // 2 + 128)]

        stop = cfg.get('stop')

        def chk(ph):
            if stop == ph:
                raise _Stop()
        try:
          chk('const')
          for l in range(DEPTH):
            with contextlib.ExitStack() as les:
                AT = SB(les, [128, KC, 4], F32, "AT"); BT = SB(les, [128, KC, 4], F32, "BT")
                wcT = SB(les, [128, 48, 4], F32, "wcT"); scT0 = SB(les, [128, 48, 3], F32, "scT0"); scT1 = SB(les, [128, 48, 3], F32, "scT1")
                gnw = SB(les, [128, 1], F32, "gnw"); dtb = SB(les, [128, 16], F32, "dtb"); nexpA = SB(les, [128, 16], F32, "nexpA")
                with contextlib.ExitStack() as es:
                    csb = SB(es, [4, D], F32); sc = SB(es, [4, D], F32); scT = SB(es, [128, KC, 4], F32)
                    mod = SB(es, [4, 3 * D], F32); bada = SB(es, [4, 3 * D], F32)
                    wst = [SB(es, [128, KC, 256], F32) for _ in range(2)]
                    modT = SB(es, [128, 2 * KC, 4], F32); nwT = SB(es, [128, KC], F32)
                    wc4 = SB(es, [4, CONV_CH], F32); sc3 = [SB(es, [3, CONV_CH], F32)] * 2
                    pt = PS(es, [128, 2 * KC, 4], F32); pm = [PS(es, [4, 512], F32) for _ in range(2)]
                    pw = PS(es, [128, 48, 4], F32)
                    c.dma('sp', csb[:], c4[:, :], w=[csb])
                    c.dma('sp', bada[:], b_ada[l:l + 1, :].partition_broadcast(4), w=[bada])
                    c.dma('sp', nwT[:], norm_w[l].rearrange("(k p) -> p k", p=128), w=[nwT], slow=True)
                    c.dma('sp', gnw[:], gdn_norm_w[l].rearrange("(p o) -> p o", o=1), w=[gnw], slow=True)
                    c.dma('sp', dtb[:], dt_bias[l:l + 1, :].partition_broadcast(128), w=[dtb])
                    c.dma('sp', nexpA[:], a_log[l:l + 1, :].partition_broadcast(128), w=[nexpA])
                    c.dma('sp', wc4[:], w_conv[l], w=[wc4])
                    c.op('act', lambda e: e.activation(out=nexpA[:], in_=nexpA[:], func=AF.Exp), r=[nexpA], w=[nexpA])
                    c.op('dve', lambda e: e.tensor_scalar(out=nexpA[:], in0=nexpA[:], scalar1=-1.0, scalar2=None, op0=ALU.mult), r=[nexpA], w=[nexpA])
                    c.op('act', lambda e: e.activation(out=sc[:], in_=csb[:], func=AF.Silu), r=[csb], w=[sc])
                    for kc in range(KC):
                        c.op('pe', lambda e: e.transpose(pt[:, kc, :], sc[:, kc * 128:(kc + 1) * 128], identF[0:4, 0:4]), r=[sc, identF], w=[pt])
                    c.op('dve', lambda e: e.tensor_copy(out=scT[:], in_=pt[:, 0:KC, :]), r=[pt], w=[scT])
                    for cb in range(48):
                        c.op('pe', lambda e: e.transpose(pw[:, cb, :], wc4[:, cb * 128:(cb + 1) * 128], identF[0:4, 0:4]), r=[wc4, identF], w=[pw])
                    c.op('dve', lambda e: e.tensor_copy(out=wcT[:], in_=pw[:]), r=[pw], w=[wcT])
                    for s, dst in ((0, scT0), (1, scT1)):
                        c.dma('sp', sc3[s][:], scv[l, s], w=[sc3[s]])
                        for cb in range(48):
                            c.op('pe', lambda e: e.transpose(pw[:, cb, 0:3], sc3[s][:, cb * 128:(cb + 1) * 128], identF[0:3, 0:3]), r=[sc3[s], identF], w=[pw])
                        c.op('dve', lambda e: e.tensor_copy(out=dst[:], in_=pw[:, :, 0:3]), r=[pw], w=[dst])
                    ncg = 3 * D // 256
                    for cg in range(ncg):
                        ws = wst[cg % 2]; pmm = pm[cg % 2]
                        c.dma('sp', ws[:], w_ada[l, :, cg * 256:(cg + 1) * 256].rearrange("(k p) n -> p k n", p=128), w=[ws])

                        def mm(e):
                            for kc in range(KC):
                                ins = e.matmul(pmm[:, 0:256], scT[:, kc, :], ws[:, kc, :], start=(kc == 0), stop=(kc == KC - 1))
                            return ins
                        c.op('pe', mm, r=[scT, ws], w=[pmm])
                        c.op('dve', lambda e: e.tensor_tensor(out=mod[:, cg * 256:(cg + 1) * 256], in0=pmm[:, 0:256], in1=bada[:, cg * 256:(cg + 1) * 256], op=ALU.add), r=[pmm, bada], w=[mod])
                    c.dma('pool', mod_d[:, :], mod[:], r=[mod])
                    for j in range(2 * KC):
                        c.op('pe', lambda e: e.transpose(pt[:, j, :], mod[:, j * 128:(j + 1) * 128], identF[0:4, 0:4]), r=[mod, identF], w=[pt])
                    c.op('dve', lambda e: e.tensor_copy(out=modT[:], in_=pt[:]), r=[pt], w=[modT])
                    c.op('dve', lambda e: e.tensor_copy(out=BT[:], in_=modT[:, 0:KC, :]), r=[modT], w=[BT])
                    c.op('dve', lambda e: e.scalar_tensor_tensor(out=AT[:], in0=modT[:, KC:2 * KC, :], scalar=1.0,
                                                                  in1=nwT[:].unsqueeze(2).to_broadcast([128, KC, 4]),
                                                                  op0=ALU.add, op1=ALU.mult), r=[modT, nwT], w=[AT])
                    c.barrier()

                chk('A')
                for hf, (htok0, hn) in enumerate(halves):
                    with contextlib.ExitStack() as es:
                        hT = SB(es, [128, KC, hn], BF16, "hT")
                        with contextlib.ExitStack() as es2:
                            xt = [SB(es2, [128, D], F32) for _ in range(2)]
                            junk = SB(es2, [128, D], F32)
                            xn = [SB(es2, [128, D], BF16) for _ in range(2)]
                            ssq = [SB(es2, [128, 1], F32) for _ in range(2)]
                            rstd = [SB(es2, [128, 1], F32) for _ in range(2)]
                            ptr = [PS(es2, [128, 4, 128], BF16) for _ in range(2)]
                            for ti, (t0, parts) in enumerate(half_tiles(hf)):
                                x_ = xt[ti % 2]; xn_ = xn[ti % 2]; ss_ = ssq[ti % 2]; rs_ = rstd[ti % 2]
                                for (kind, idx, r0, n, p0) in parts:
                                    c.dma('sp', x_[p0:p0 + n, :], x_src(l, kind, idx, r0, n), w=[x_])
                                c.op('act', lambda e: e.activation(out=junk[:], in_=x_[:], func=AF.Square, accum_out=ss_[:]), r=[x_], w=[junk, ss_])
                                c.op('act', lambda e: e.activation(out=ss_[:], in_=ss_[:], func=AF.Sqrt, scale=1.0 / D, bias=epsc[:]), r=[ss_, epsc], w=[ss_])
                                c.op('dve', lambda e: e.reciprocal(out=rs_[:], in_=ss_[:]), r=[ss_], w=[rs_])
                                c.op('dve', lambda e: e.tensor_scalar(out=xn_[:], in0=x_[:], scalar1=rs_[:], scalar2=None, op0=ALU.mult), r=[x_, rs_], w=[xn_])
                                for kg in range(KC // 4 if KC >= 4 else 1):
                                    kcs = list(range(kg * 4, min(KC, kg * 4 + 4)))
                                    p_ = ptr[kg % 2]

                                    def tr(e):
                                        for j, kc in enumerate(kcs):
                                            ins = e.transpose(p_[:, j, :], xn_[:, kc * 128:(kc + 1) * 128], identB[:])
                                        return ins
                                    c.op('pe', tr, r=[xn_, identB], w=[p_])
                                    for j, kc in enumerate(kcs):
                                        for (kind, idx, r0, n, p0) in parts:
                                            row = idx if kind == 'p' else 2 + idx
                                            c.op('act', lambda e: e.activation(out=hT[:, kc, t0 + p0:t0 + p0 + n], in_=p_[:, j, p0:p0 + n], func=AF.Identity,
                                                                               scale=AT[:, kc, row:row + 1], bias=BT[:, kc, row:row + 1]),
                                                 r=[p_, AT, BT], w=[hT])
                            c.barrier()
                        chk('N')
                        with contextlib.ExitStack() as es2:
                            wstg = [SB(es2, [128, KC, 256], F32) for _ in range(2)]
                            wbf = [SB(es2, [128, KC, 256], BF16) for _ in range(2)]
                            Wtm = SB(es2, [128, KC, 624], BF16)
                            wctr = [0]

                            def load_w(src, c0, ncol, dst=None, dcol=0):
                                i = wctr[0] % 2; wctr[0] += 1
                                st = wstg[i]
                                c.dma('sp', st[:, :, 0:ncol], src[:, c0:c0 + ncol].rearrange("(k p) n -> p k n", p=128), w=[st])
                                if dst is None:
                                    wb = wbf[i]
                                    c.op('dve', lambda e: e.tensor_copy(out=wb[:, :, 0:ncol], in_=st[:, :, 0:ncol]), r=[st], w=[wb])
                                    return wb
                                c.op('dve', lambda e: e.tensor_copy(out=dst[:, :, dcol:dcol + ncol], in_=st[:, :, 0:ncol]), r=[st], w=[dst])
                                return dst
                            W = w_in[l]
                            load_w(W, OFF['kb'], 256, Wtm, 0); load_w(W, OFF['vb'], 256, Wtm, 256)
                            load_w(W, OFF['ba'], 32, Wtm, 512); load_w(W, OFF['ki'], 80, Wtm, 544)
                            with contextlib.ExitStack() as es3:
                                ptm_a = [PS(es3, [128, 512], F32) for _ in range(2)]; ptm_b = [PS(es3, [128, 112], F32) for _ in range(2)]
                                ptm = [[ptm_a[0], ptm_b[0]], [ptm_a[1], ptm_b[1]]]
                                tms = [SB(es3, [128, 624], F32) for _ in range(2)]
                                for ti, (t0, parts) in enumerate(half_tiles(hf)):
                                    pa, pb = ptm[ti % 2]; tm_ = tms[ti % 2]

                                    def mm(e):
                                        for kc in range(KC):
                                            e.matmul(pa[:], hT[:, kc, t0:t0 + 128], Wtm[:, kc, 0:512], start=(kc == 0), stop=(kc == KC - 1))
                                        for kc in range(KC):
                                            ins = e.matmul(pb[:], hT[:, kc, t0:t0 + 128], Wtm[:, kc, 512:624], start=(kc == 0), stop=(kc == KC - 1))
                                        return ins
                                    c.op('pe', mm, r=[hT, Wtm], w=[pa, pb])
                                    c.op('act', lambda e: e.copy(out=tm_[:, 0:512], in_=pa[:]), r=[pa], w=[tm_])
                                    c.op('dve', lambda e: e.tensor_copy(out=tm_[:, 512:624], in_=pb[:]), r=[pb], w=[tm_])
                                    c.dma('pool', tm_d[htok0 + t0:htok0 + t0 + 128, :], tm_[:], r=[tm_])
                                    for (kind, idx, r0, n, p0) in parts:
                                        ok, ov, oi = (nkp, nvp, nikp) if kind == 'p' else (nks, nvs, niks)
                                        c.dma('pool', ok[l, idx, r0:r0 + n, :], tm_[p0:p0 + n, 0:256], r=[tm_])
                                        c.dma('pool', ov[l, idx, r0:r0 + n, :], tm_[p0:p0 + n, 256:512], r=[tm_])
                                        c.dma('pool', oi[l, idx, r0:r0 + n, :], tm_[p0:p0 + n, 544:608], r=[tm_])
                                c.barrier()
                            chunks = []
                            for t0 in range(0, T, 512):
                                wdt = min(512, T - t0)
                                chunks.append((t0, wdt, 'p', hf, t0 == 0, t0 + wdt == T))
                            if hf == 1:
                                chunks.append((T, 64, 's', 0, True, True)); chunks.append((T + 64, 64, 's', 1, True, True))
                            groups = [('conv', OFF['qkv'], 48, None), ('silu', OFF['za'], 16, zaT_d), ('qb', OFF['qb'], 16, qbT_d),
                                      ('silu', OFF['zb'], 16, zbT_d), ('copy', OFF['qi'], 8, qiT_d),
                                      ('sig', OFF['gla'], KC, sgaT_d), ('sig', OFF['glb'], KC, sgbT_d)]
                            with contextlib.ExitStack() as es3:
                                pp = [PS(es3, [128, 512], F32) for _ in range(3)]
                                pq = [PS(es3, [128, 512], F32) for _ in range(3)]
                                xpad = [SB(es3, [128, 3 + 512], F32) for _ in range(3)]
                                cvas = [SB(es3, [128, 512], F32) for _ in range(7)]; rnA = [SB(es3, [128, 512], F32) for _ in range(2)]
                                sqbs = [SB(es3, [128, 512], BF16) for _ in range(2)]; rns = [SB(es3, [128, 512], F32) for _ in range(2)]
                                ob = [SB(es3, [128, 512], BF16) for _ in range(3)]
                                pctr = [0]; octr = [0]
                                goff = OFF['qkv']
                                items = [(b2, bi, ch) for b2 in range(0, 48, 2) for bi in range(2) for ch in chunks]
                                NI = len(items)
                                wbd = {}

                                def st0_pe(i):
                                    b2, bi, (c0, wd, kind, idx, first, last) = items[i]
                                    if bi == 0 and (c0, wd) == chunks[0][0:2]:
                                        wbd[b2] = load_w(W, goff + b2 * 128, 256)
                                    wb = wbd[b2]; p_ = pp[i % 3]

                                    def mm(e):
                                        for kc in range(KC):
                                            ins = e.matmul(p_[:, 0:wd], wb[:, kc, bi * 128:(bi + 1) * 128], hT[:, kc, c0:c0 + wd], start=(kc == 0), stop=(kc == KC - 1))
                                        return ins
                                    c.op('pe', mm, r=[wb, hT], w=[p_])

                                def s1_copy(i):
                                    b2, bi, (c0, wd, kind, idx, first, last) = items[i]
                                    blk = b2 + bi
                                    p_ = pp[i % 3]; xq = xpad[i % 3]
                                    if first:
                                        if kind == 'p':
                                            c.op('act', lambda e: e.activation(out=xq[:, 0:3], in_=zero3[:], func=AF.Copy), r=[zero3], w=[xq])
                                        else:
                                            sct = scT0 if idx == 0 else scT1
                                            c.op('act', lambda e: e.copy(out=xq[:, 0:3], in_=sct[:, blk, :]), r=[sct], w=[xq])
                                    else:
                                        xprev = xpad[(i - 1) % 3]; pw_ = items[i - 1][2][1]
                                        c.op('act', lambda e: e.copy(out=xq[:, 0:3], in_=xprev[:, pw_:pw_ + 3]), r=[xprev], w=[xq])
                                    c.op('act', lambda e: e.copy(out=xq[:, 3:3 + wd], in_=p_[:, 0:wd]), r=[p_], w=[xq])
                                    if last:
                                        oc = ncp if kind == 'p' else ncs
                                        c.dma('pool', oc[l, idx, :, blk * 128:(blk + 1) * 128].rearrange("j p -> p j"), xq[:, wd:wd + 3], r=[xq], slow=True)

                                def s2_conv(i):
                                    b2, bi, (c0, wd, kind, idx, first, last) = items[i]
                                    blk = b2 + bi
                                    xq = xpad[i % 3]; cva = cvas[i % 7]
                                    c.op('dve', lambda e: e.tensor_scalar(out=cva[:, 0:wd], in0=xq[:, 0:wd], scalar1=wcT[:, blk, 0:1], scalar2=None, op0=ALU.mult), r=[xq, wcT], w=[cva])
                                    for j in range(1, 4):
                                        c.op('dve', lambda e: e.scalar_tensor_tensor(out=cva[:, 0:wd], in0=xq[:, j:j + wd], scalar=wcT[:, blk, j:j + 1], in1=cva[:, 0:wd],
                                                                                      op0=ALU.mult, op1=ALU.add), r=[xq, wcT, cva], w=[cva])

                                def s3_sig(i):
                                    b2, bi, (c0, wd, kind, idx, first, last) = items[i]
                                    cva = cvas[i % 7]; rn = rnA[i % 2]
                                    c.op('act', lambda e: e.activation(out=rn[:, 0:wd], in_=cva[:, 0:wd], func=AF.Exp, scale=-1.0), r=[cva], w=[rn])
                                    c.op('act', lambda e: e.activation(out=rn[:, 0:wd], in_=rn[:, 0:wd], func=AF.Ln, bias=onec[:]), r=[rn, onec], w=[rn])
                                    c.op('act', lambda e: e.activation(out=rn[:, 0:wd], in_=rn[:, 0:wd], func=AF.Exp, scale=-1.0), r=[rn], w=[rn])

                                def s4_silu(i):
                                    b2, bi, (c0, wd, kind, idx, first, last) = items[i]
                                    blk = b2 + bi
                                    cva = cvas[i % 7]; rn = rnA[i % 2]; o_ = ob[i % 3]
                                    tsl = slice(htok0 + c0, htok0 + c0 + wd)
                                    if blk >= 32:
                                        c.op('dve', lambda e: e.tensor_tensor(out=o_[:, 0:wd], in0=cva[:, 0:wd], in1=rn[:, 0:wd], op=ALU.mult), r=[cva, rn], w=[o_])
                                        c.dma('pool', vT_d[(blk - 32) * 128:(blk - 31) * 128, tsl], o_[:, 0:wd], r=[o_])
                                    else:
                                        c.op('dve', lambda e: e.tensor_tensor(out=cva[:, 0:wd], in0=cva[:, 0:wd], in1=rn[:, 0:wd], op=ALU.mult), r=[cva, rn], w=[cva])

                                def s5_sq(i):
                                    b2, bi, (c0, wd, kind, idx, first, last) = items[i]
                                    if b2 + bi >= 32:
                                        return
                                    cva = cvas[i % 7]; sqb = sqbs[i % 2]
                                    c.op('act', lambda e: e.activation(out=sqb[:, 0:wd], in_=cva[:, 0:wd], func=AF.Square), r=[cva], w=[sqb])

                                def s6_ones(i):
                                    b2, bi, (c0, wd, kind, idx, first, last) = items[i]
                                    if b2 + bi >= 32:
                                        return
                                    sqb = sqbs[i % 2]; q_ = pq[i % 3]
                                    c.op('pe', lambda e: e.matmul(q_[:, 0:wd], onesB[:], sqb[:, 0:wd], start=True, stop=True), r=[onesB, sqb], w=[q_])

                                def s7_rs(i):
                                    b2, bi, (c0, wd, kind, idx, first, last) = items[i]
                                    if b2 + bi >= 32:
                                        return
                                    rn = rns[i % 2]; q_ = pq[i % 3]
                                    c.op('act', lambda e: e.activation(out=rn[:, 0:wd], in_=q_[:, 0:wd], func=AF.Ln, bias=epsc[:]), r=[q_, epsc], w=[rn])
                                    c.op('act', lambda e: e.activation(out=rn[:, 0:wd], in_=rn[:, 0:wd], func=AF.Exp, scale=-0.5), r=[rn], w=[rn])

                                def s8_out(i):
                                    b2, bi, (c0, wd, kind, idx, first, last) = items[i]
                                    blk = b2 + bi
                                    if blk >= 32:
                                        return
                                    cva = cvas[i % 7]; rn = rns[i % 2]; o_ = ob[i % 3]
                                    tsl = slice(htok0 + c0, htok0 + c0 + wd)
                                    qs = float(HD) ** -0.5 if blk < 16 else 1.0
                                    c.op('dve', lambda e: e.scalar_tensor_tensor(out=o_[:, 0:wd], in0=cva[:, 0:wd], scalar=qs, in1=rn[:, 0:wd], op0=ALU.mult, op1=ALU.mult),
                                         r=[cva, rn], w=[o_])
                                    dst = qT_d if blk < 16 else kT_d; drow = (blk % 16) * 128
                                    c.dma('pool', dst[drow:drow + 128, tsl], o_[:, 0:wd], r=[o_])
                                stages = [st0_pe, s1_copy, s2_conv, s3_sig, s4_silu, s5_sq, s6_ones, s7_rs, s8_out]
                                for t in range(NI + len(stages) - 1):
                                    for k, fn in enumerate(stages):
                                        if 0 <= t - k < NI:
                                            fn(t - k)
                                for (gk, goff, nblk, dst) in groups[1:]:
                                    for b2 in range(0, nblk, 2):
                                        wb = load_w(W, goff + b2 * 128, 256)
                                        for bi in range(2):
                                            blk = b2 + bi
                                            for (c0, wd, kind, idx, first, last) in chunks:
                                                p_ = pp[pctr[0] % 3]; pctr[0] += 1
                                                o_ = ob[octr[0] % 3]; octr[0] += 1

                                                def mm(e):
                                                    for kc in range(KC):
                                                        ins = e.matmul(p_[:, 0:wd], wb[:, kc, bi * 128:(bi + 1) * 128], hT[:, kc, c0:c0 + wd], start=(kc == 0), stop=(kc == KC - 1))
                                                    return ins
                                                c.op('pe', mm, r=[wb, hT], w=[p_])
                                                tsl = slice(htok0 + c0, htok0 + c0 + wd)
                                                if gk == 'silu':
                                                    c.op('act', lambda e: e.activation(out=o_[:, 0:wd], in_=p_[:, 0:wd], func=AF.Silu), r=[p_], w=[o_])
                                                elif gk == 'sig':
                                                    c.op('act', lambda e: e.activation(out=o_[:, 0:wd], in_=p_[:, 0:wd], func=AF.Sigmoid), r=[p_], w=[o_])
                                                elif gk == 'qb':
                                                    c.op('act', lambda e: e.activation(out=o_[:, 0:wd], in_=p_[:, 0:wd], func=AF.Copy, scale=float(HD) ** -0.5), r=[p_], w=[o_])
                                                else:
                                                    c.op('act', lambda e: e.copy(out=o_[:, 0:wd], in_=p_[:, 0:wd]), r=[p_], w=[o_])
                                                c.dma('pool', dst[blk * 128:(blk + 1) * 128, tsl], o_[:, 0:wd], r=[o_])
                        c.barrier()

                chk('P')
                with contextlib.ExitStack() as es:
                    S = SB(es, [128, GH, 128], F32, "S"); Sb = SB(es, [128, GH, 128], BF16, "Sb")
                    Ibc = SB(es, [128, 4, 128], F32, "Ibc")
                    for hh in range(4):
                        c.op('pool', lambda e: e.tensor_copy(out=Ibc[:, hh, :], in_=identF[:]), r=[identF], w=[Ibc])
                    qT = SB(es, [128, GH, 128], BF16); kT = SB(es, [128, GH, 128], BF16); vT = SB(es, [128, GH, 128], BF16); zaT = SB(es, [128, GH, 128], BF16)
                    bgr = SB(es, [128, 32], F32); beta = SB(es, [128, 16], F32); g_ = SB(es, [128, 16], F32)
                    gc = SB(es, [128, 16], F32); gcT = SB(es, [16, 128], F32); egc = SB(es, [128, 16], F32)
                    kdsc = SB(es, [128, 16], F32); gle = SB(es, [128, 16], F32); bege = SB(es, [128, 16], F32)
                    MB = SB(es, [128, 16, 128], F32); GB = SB(es, [128, 16, 128], F32)
                    Xs = [SB(es, [128, 4, 128], F32) for _ in range(2)] * 2
                    Dm = [SB(es, [128, 4, 128], F32) for _ in range(2)] * 2
                    DTm = [SB(es, [128, 4, 128], F32) for _ in range(2)] * 2
                    egr = [SB(es, [128, 4, 128], F32) for _ in range(2)] * 2
                    Pa = [[SB(es, [128, 4, 128], F32) for _ in range(2)] for _ in range(4)]
                    PTa = [[SB(es, [128, 4, 128], F32) for _ in range(2)] for _ in range(4)]
                    Y = [SB(es, [128, 4, 128], F32) for _ in range(4)]
                    YB = [SB(es, [128, 4, 128], BF16) for _ in range(4)]
                    vb = [SB(es, [128, 4, 128], BF16) for _ in range(4)]
                    kbg = [SB(es, [128, 4, 128], BF16) for _ in range(4)]
                    kdec = [SB(es, [128, 4, 128], BF16) for _ in range(4)]
                    wTb = [SB(es, [128, 4, 128], BF16) for _ in range(4)]
                    us = [SB(es, [128, 4, 128], F32) for _ in range(4)]
                    QKm = [SB(es, [128, 4, 128], BF16) for _ in range(4)]
                    qdT = [SB(es, [128, 4, 128], BF16) for _ in range(4)]
                    vn = [SB(es, [128, 4, 128], BF16) for _ in range(4)]
                    oTs = [SB(es, [128, 4, 128], F32) for _ in range(4)]
                    osq = [SB(es, [128, 4, 128], BF16) for _ in range(4)]
                    rst = [SB(es, [128, 4, 128], F32) for _ in range(4)]
                    ozs = [SB(es, [128, 4, 128], BF16) for _ in range(4)]
                    psA = [PS(es, [128, 4, 128], F32) for _ in range(5)]
                    psB = [PS(es, [128, 4, 128], BF16) for _ in range(2)]
                    psS = PS(es, [128, 128], F32)
                    pctr = [0]

                    def nps():
                        p = psA[pctr[0] % 5]; pctr[0] += 1
                        return p
                    for (kind, idx, tok0, slen) in seqs:
                        n = 128 if kind == 'p' else 64
                        nst = 6 if n == 128 else 5
                        if kind == 'p':
                            c.op('pool', lambda e: e.memset(S[:], 0.0), w=[S])
                        else:
                            c.dma('sp', S[:], sg[l, idx].rearrange("h k v -> k h v"), w=[S])
                        c.op('act', lambda e: e.copy(out=Sb[:], in_=S[:]), r=[S], w=[Sb])
                        for t in range(slen // n):
                            ts_ = slice(tok0 + t * n, tok0 + (t + 1) * n)
                            for (dstb, srcd) in ((qT, qT_d), (kT, kT_d), (vT, vT_d), (zaT, zaT_d)):
                                c.dma('sp', dstb[:, :, 0:n], srcd[:, ts_].rearrange("(h p) t -> p h t", p=128), w=[dstb])
                            c.dma('sp', bgr[0:n, :], tm_d[ts_, 512:544], w=[bgr])
                            c.op('act', lambda e: e.activation(out=beta[0:n], in_=bgr[0:n, 0:16], func=AF.Sigmoid), r=[bgr], w=[beta])
                            c.op('dve', lambda e: e.tensor_tensor(out=g_[0:n], in0=bgr[0:n, 16:32], in1=dtb[0:n], op=ALU.add), r=[bgr, dtb], w=[g_])
                            c.op('act', lambda e: e.activation(out=gc[0:n], in_=g_[0:n], func=AF.Abs), r=[g_], w=[gc])
                            c.op('act', lambda e: e.activation(out=gc[0:n], in_=gc[0:n], func=AF.Exp, scale=-1.0), r=[gc], w=[gc])
                            c.op('act', lambda e: e.activation(out=gc[0:n], in_=gc[0:n], func=AF.Ln, bias=onec[0:n]), r=[gc, onec], w=[gc])
                            c.op('dve', lambda e: e.scalar_tensor_tensor(out=g_[0:n], in0=g_[0:n], scalar=0.0, in1=gc[0:n], op0=ALU.max, op1=ALU.add), r=[g_, gc], w=[g_])
                            c.op('dve', lambda e: e.tensor_tensor(out=g_[0:n], in0=g_[0:n], in1=nexpA[0:n], op=ALU.mult), r=[g_, nexpA], w=[g_])
                            c.op('pe', lambda e: e.matmul(psS[0:n, 0:16], Lm[0:n, 0:n], g_[0:n, :], start=True, stop=True), r=[Lm, g_], w=[psS])
                            c.op('dve', lambda e: e.tensor_copy(out=gc[0:n], in_=psS[0:n, 0:16]), r=[psS], w=[gc])
                            c.op('pe', lambda e: e.matmul(psS[0:16, 0:n], g_[0:n, :], Lm[0:n, 0:n], start=True, stop=True), r=[Lm, g_], w=[psS])
                            c.op('dve', lambda e: e.tensor_copy(out=gcT[:, 0:n], in_=psS[0:16, 0:n]), r=[psS], w=[gcT])
                            c.op('pe', lambda e: e.matmul(psS[:, 0:16], onesF[0:n, :], g_[0:n, :], start=True, stop=True), r=[onesF, g_], w=[psS])
                            c.op('act', lambda e: e.activation(out=gle[:], in_=psS[:, 0:16], func=AF.Exp), r=[psS], w=[gle])
                            c.op('dve', lambda e: e.tensor_tensor(out=kdsc[0:n], in0=psS[0:n, 0:16], in1=gc[0:n], op=ALU.subtract), r=[psS, gc], w=[kdsc])
                            c.op('act', lambda e: e.activation(out=kdsc[0:n], in_=kdsc[0:n], func=AF.Exp), r=[kdsc], w=[kdsc])
                            c.op('act', lambda e: e.activation(out=egc[0:n], in_=gc[0:n], func=AF.Exp), r=[gc], w=[egc])
                            c.op('dve', lambda e: e.tensor_tensor(out=bege[0:n], in0=egc[0:n], in1=beta[0:n], op=ALU.mult), r=[egc, beta], w=[bege])
                            c.op('dve', lambda e: e.tensor_copy(out=GB[0:n], in_=g_[0:n, :].unsqueeze(2).to_broadcast([n, 16, 128])), r=[g_], w=[GB])
                            c.op('dve', lambda e: e.tensor_tensor(out=MB[0:n, :, 0:n], in0=MLs[0:n, 0:n].unsqueeze(1).to_broadcast([n, 16, n]),
                                                                   in1=beta[0:n, :].unsqueeze(2).to_broadcast([n, 16, n]), op=ALU.mult), r=[MLs, beta], w=[MB])
                            chk('G1')
                            for hg in range(4):
                                hs = slice(hg * 4, hg * 4 + 4)
                                pg = nps()

                                def mm(e):
                                    for hh in range(4):
                                        ins = e.matmul(pg[:, hh, 0:n], GB[0:n, hg * 4 + hh, :], Lm[0:n, 0:n], start=True, stop=True)
                                    return ins
                                c.op('pe', mm, r=[GB, Lm], w=[pg])
                                chk('G1a')
                                c.op('act', lambda e: e.activation(out=egr[hg][:, :, 0:n], in_=pg[:, :, 0:n], func=AF.Exp), r=[pg], w=[egr[hg]])
                                chk('G1b')
                                c.op('dve', lambda e: e.tensor_tensor(out=Xs[hg][0:n, :, 0:n], in0=pg[0:n, :, 0:n], in1=gc[0:n, hs].unsqueeze(2).to_broadcast([n, 4, n]), op=ALU.subtract),
                                     r=[pg, gc], w=[Xs[hg]])
                                chk('Ga')
                                c.op('dve', lambda e: e.tensor_tensor(out=Dm[hg][0:n, :, 0:n], in0=Xs[hg][0:n, :, 0:n], in1=UP[0:n, 0:n].unsqueeze(1).to_broadcast([n, 4, n]), op=ALU.add),
                                     r=[Xs[hg], UP], w=[Dm[hg]])
                                chk('H1')
                                c.op('act', lambda e: e.activation(out=Dm[hg][0:n, :, 0:n], in_=Dm[hg][0:n, :, 0:n], func=AF.Exp, scale=-1.0), r=[Dm[hg]], w=[Dm[hg]])
                                chk('H2')
                                c.op('pool', lambda e: e.tensor_tensor(out=DTm[hg][0:n, :, 0:n], in0=Xs[hg][0:n, :, 0:n], in1=LO[0:n, 0:n].unsqueeze(1).to_broadcast([n, 4, n]), op=ALU.subtract),
                                     r=[Xs[hg], LO], w=[DTm[hg]])
                                c.op('act', lambda e: e.activation(out=DTm[hg][0:n, :, 0:n], in_=DTm[hg][0:n, :, 0:n], func=AF.Exp), r=[DTm[hg]], w=[DTm[hg]])
                                chk('H3')
                                c.op('dve', lambda e: e.tensor_tensor(out=Dm[hg][0:n, :, 0:n], in0=Dm[hg][0:n, :, 0:n], in1=MB[0:n, hs, 0:n], op=ALU.mult), r=[Dm[hg], MB], w=[Dm[hg]])
                                chk('Gb')
                                pk = nps()

                                def mm(e):
                                    for hh in range(4):
                                        h = hg * 4 + hh
                                        ins = e.matmul(pk[0:n, hh, 0:n], kT[:, h, 0:n], kT[:, h, 0:n], start=True, stop=True)
                                    return ins
                                c.op('pe', mm, r=[kT], w=[pk])
                                A_ = PTa[hg][0]; B_ = Pa[hg][0]
                                c.op('dve', lambda e: e.tensor_tensor(out=A_[0:n, :, 0:n], in0=pk[0:n, :, 0:n], in1=Dm[hg][0:n, :, 0:n], op=ALU.mult), r=[pk, Dm[hg]], w=[A_])
                                chk('Gc')
                                ptk = psB[0]

                                def mm(e):
                                    for hh in range(4):
                                        ins = e.transpose(ptk[0:n, hh, :], kT[:, hg * 4 + hh, 0:n], identB[:])
                                    return ins
                                c.op('pe', mm, r=[kT, identB], w=[ptk])
                                c.op('dve', lambda e: e.tensor_tensor(out=kbg[hg][0:n], in0=ptk[0:n], in1=bege[0:n, hs].unsqueeze(2).to_broadcast([n, 4, 128]), op=ALU.mult), r=[ptk, bege], w=[kbg[hg]])
                                c.op('dve', lambda e: e.tensor_tensor(out=kdec[hg][0:n], in0=ptk[0:n], in1=kdsc[0:n, hs].unsqueeze(2).to_broadcast([n, 4, 128]), op=ALU.mult), r=[ptk, kdsc], w=[kdec[hg]])
                                ptv = psB[1]

                                def mm(e):
                                    for hh in range(4):
                                        ins = e.transpose(ptv[0:n, hh, :], vT[:, hg * 4 + hh, 0:n], identB[:])
                                    return ins
                                c.op('pe', mm, r=[vT, identB], w=[ptv])
                                c.op('dve', lambda e: e.tensor_tensor(out=vb[hg][0:n], in0=ptv[0:n], in1=beta[0:n, hs].unsqueeze(2).to_broadcast([n, 4, 128]), op=ALU.mult), r=[ptv, beta], w=[vb[hg]])
                                chk('Gd')
                                pqk = nps()

                                def mm(e):
                                    for hh in range(4):
                                        h = hg * 4 + hh
                                        ins = e.matmul(pqk[0:n, hh, 0:n], kT[:, h, 0:n], qT[:, h, 0:n], start=True, stop=True)
                                    return ins
                                c.op('pe', mm, r=[kT, qT], w=[pqk])
                                c.op('dve', lambda e: e.tensor_tensor(out=QKm[hg][0:n, :, 0:n], in0=pqk[0:n, :, 0:n], in1=DTm[hg][0:n, :, 0:n], op=ALU.mult), r=[pqk, DTm[hg]], w=[QKm[hg]])
                                c.op('pool', lambda e: e.tensor_tensor(out=qdT[hg][:, :, 0:n], in0=qT[:, hs, 0:n], in1=egr[hg][:, :, 0:n], op=ALU.mult), r=[qT, egr[hg]], w=[qdT[hg]])
                                pb_ = nps()

                                def mm(e):
                                    for hh in range(4):
                                        ins = e.transpose(pb_[0:n, hh, 0:n], A_[0:n, hh, 0:n], identF[0:n, 0:n])
                                    return ins
                                c.op('pe', mm, r=[A_, identF], w=[pb_])
                                c.op('act', lambda e: e.copy(out=B_[0:n, :, 0:n], in_=pb_[0:n, :, 0:n]), r=[pb_], w=[B_])
                                c.op('dve', lambda e: e.tensor_tensor(out=Y[hg][0:n, :, 0:n], in0=Ibc[0:n, :, 0:n], in1=pb_[0:n, :, 0:n], op=ALU.subtract), r=[pb_, Ibc], w=[Y[hg]])
                            chk('G2')
                            cur = 0
                            for m in range(nst):
                                lastm = (m == nst - 1)
                                for hg in range(4):
                                    P_ = Pa[hg][cur]; PT_ = PTa[hg][cur]; PTn = PTa[hg][1 - cur]
                                    p2 = nps()

                                    def mm(e):
                                        for hh in range(4):
                                            ins = e.matmul(p2[0:n, hh, 0:n], P_[0:n, hh, 0:n], PT_[0:n, hh, 0:n], start=True, stop=True)
                                        return ins
                                    c.op('pe', mm, r=[PT_, P_], w=[p2])
                                    c.op('dve', lambda e: e.tensor_copy(out=PTn[0:n, :, 0:n], in_=p2[0:n, :, 0:n]), r=[p2], w=[PTn])
                                if not lastm:
                                    for hg in range(4):
                                        P_ = Pa[hg][cur]; PT_ = PTa[hg][cur]; Pn = Pa[hg][1 - cur]
                                        p1 = nps()

                                        def mm(e):
                                            for hh in range(4):
                                                ins = e.matmul(p1[0:n, hh, 0:n], PT_[0:n, hh, 0:n], P_[0:n, hh, 0:n], start=True, stop=True)
                                            return ins
                                        c.op('pe', mm, r=[PT_, P_], w=[p1])
                                        c.op('act', lambda e: e.copy(out=Pn[0:n, :, 0:n], in_=p1[0:n, :, 0:n]), r=[p1], w=[Pn])
                                for hg in range(4):
                                    PTn = PTa[hg][1 - cur]
                                    p3 = nps()

                                    def mm(e):
                                        for hh in range(4):
                                            ins = e.matmul(p3[0:n, hh, 0:n], PTn[0:n, hh, 0:n], Y[hg][0:n, hh, 0:n], start=True, stop=True)
                                        return ins
                                    c.op('pe', mm, r=[PTn, Y[hg]], w=[p3])
                                    c.op('dve', lambda e: e.tensor_tensor(out=Y[hg][0:n, :, 0:n], in0=Y[hg][0:n, :, 0:n], in1=p3[0:n, :, 0:n], op=ALU.add), r=[p3, Y[hg]], w=[Y[hg]])
                                cur = 1 - cur
                            chk('G3')
                            for hg in range(4):
                                hs = slice(hg * 4, hg * 4 + 4)
                                c.op('act', lambda e: e.copy(out=YB[hg][0:n, :, 0:n], in_=Y[hg][0:n, :, 0:n]), r=[Y[hg]], w=[YB[hg]])
                                pu = nps()

                                def mm(e):
                                    for hh in range(4):
                                        ins = e.matmul(pu[0:n, hh, :], YB[hg][0:n, hh, 0:n], vb[hg][0:n, hh, :], start=True, stop=True)
                                    return ins
                                c.op('pe', mm, r=[YB[hg], vb[hg]], w=[pu])
                                c.op('act', lambda e: e.copy(out=us[hg][0:n], in_=pu[0:n]), r=[pu], w=[us[hg]])
                                pw_ = nps()

                                def mm(e):
                                    for hh in range(4):
                                        ins = e.matmul(pw_[:, hh, 0:n], kbg[hg][0:n, hh, :], YB[hg][0:n, hh, 0:n], start=True, stop=True)
                                    return ins
                                c.op('pe', mm, r=[YB[hg], kbg[hg]], w=[pw_])
                                c.op('dve', lambda e: e.tensor_copy(out=wTb[hg][:, :, 0:n], in_=pw_[:, :, 0:n]), r=[pw_], w=[wTb[hg]])
                            chk('G4')
                            for hg in range(4):
                                pv_ = nps()

                                def mm(e):
                                    for hh in range(4):
                                        ins = e.matmul(pv_[0:n, hh, :], wTb[hg][:, hh, 0:n], Sb[:, hg * 4 + hh, :], start=True, stop=True)
                                    return ins
                                c.op('pe', mm, r=[wTb[hg], Sb], w=[pv_])
                                c.op('dve', lambda e: e.tensor_tensor(out=vn[hg][0:n], in0=us[hg][0:n], in1=pv_[0:n], op=ALU.subtract), r=[us[hg], pv_], w=[vn[hg]])
                            for hg in range(4):
                                po = nps()

                                def mm(e):
                                    for hh in range(4):
                                        e.matmul(po[:, hh, 0:n], Sb[:, hg * 4 + hh, :], qdT[hg][:, hh, 0:n], start=True, stop=False)
                                        ins = e.matmul(po[:, hh, 0:n], vn[hg][0:n, hh, :], QKm[hg][0:n, hh, 0:n], start=False, stop=True)
                                    return ins
                                c.op('pe', mm, r=[Sb, qdT[hg], vn[hg], QKm[hg]], w=[po])
                                o_ = oTs[hg]; q_ = osq[hg]
                                c.op('act', lambda e: e.copy(out=o_[:, :, 0:n], in_=po[:, :, 0:n]), r=[po], w=[o_])
                                c.op('act', lambda e: e.activation(out=q_[:, :, 0:n], in_=po[:, :, 0:n], func=AF.Square), r=[po], w=[q_])
                            for hg in range(4):
                                hs = slice(hg * 4, hg * 4 + 4)
                                pS = nps()

                                def mm(e):
                                    for hh in range(4):
                                        ins = e.matmul(pS[:, hh, :], kdec[hg][0:n, hh, :], vn[hg][0:n, hh, :], start=True, stop=True)
                                    return ins
                                c.op('pe', mm, r=[kdec[hg], vn[hg]], w=[pS])
                                c.op('pool', lambda e: e.tensor_tensor(out=S[:, hs, :], in0=S[:, hs, :], in1=gle[:, hs].unsqueeze(2).to_broadcast([128, 4, 128]), op=ALU.mult), r=[S, gle, Sb], w=[S])
                                c.op('dve', lambda e: e.tensor_tensor(out=S[:, hs, :], in0=S[:, hs, :], in1=pS[:], op=ALU.add), r=[S, pS], w=[S])
                            for hg in range(4):
                                hs = slice(hg * 4, hg * 4 + 4)
                                o_ = oTs[hg]; q_ = osq[hg]; r_ = rst[hg]; z_ = ozs[hg]
                                pss = nps()
                                if n == 128:
                                    c.op('pe', lambda e: e.matmul(pss[:].rearrange("p a b -> p (a b)"), onesB[:], q_[:].rearrange("p a b -> p (a b)"), start=True, stop=True), r=[onesB, q_], w=[pss])
                                else:
                                    def mm(e):
                                        for hh in range(4):
                                            ins = e.matmul(pss[:, hh, 0:n], onesB[:], q_[:, hh, 0:n], start=True, stop=True)
                                        return ins
                                    c.op('pe', mm, r=[onesB, q_], w=[pss])
                                c.op('act', lambda e: e.activation(out=r_[:, :, 0:n], in_=pss[:, :, 0:n], func=AF.Ln, scale=1.0 / 128, bias=epsc[:]), r=[pss, epsc], w=[r_])
                                c.op('act', lambda e: e.activation(out=r_[:, :, 0:n], in_=r_[:, :, 0:n], func=AF.Exp, scale=-0.5), r=[r_], w=[r_])
                                c.op('dve', lambda e: e.tensor_tensor(out=o_[:, :, 0:n], in0=o_[:, :, 0:n], in1=r_[:, :, 0:n], op=ALU.mult), r=[o_, r_], w=[o_])
                                c.op('dve', lambda e: e.scalar_tensor_tensor(out=z_[:, :, 0:n], in0=o_[:, :, 0:n], scalar=gnw[:, 0:1], in1=zaT[:, hs, 0:n], op0=ALU.mult, op1=ALU.mult),
                                     r=[o_, gnw, zaT], w=[z_])
                                c.dma('pool', ozT_d[hg * 512:(hg + 1) * 512, ts_].rearrange("(h p) t -> p h t", p=128), z_[:, :, 0:n], r=[z_])
                            c.op('act', lambda e: e.copy(out=Sb[:], in_=S[:]), r=[S], w=[Sb])
                            chk('G5')
                        og = ngp if kind == 'p' else ngs
                        c.dma('pool', og[l, idx].rearrange("h k v -> k h v"), S[:], r=[S])
                    c.barrier()

                chk('G')
                with contextlib.ExitStack() as es:
                    KT = SB(es, [128, 2, NKB * 128], BF16, "KT"); Vt = SB(es, [128, NKB, 256], BF16, "Vt"); KIT = SB(es, [128, NKB * 128], BF16, "KIT")
                    kst = [SB(es, [128, 576], F32) for _ in range(2)]
                    kbf = [SB(es, [128, 384], BF16) for _ in range(2)]
                    qiT = SB(es, [128, 8, 128], BF16); qbT = SB(es, [128, 16, 128], BF16); zbT = SB(es, [128, 16, 128], BF16)
                    wi = SB(es, [128, 16], F32); absw = SB(es, [128, 16], F32); sgn = SB(es, [128, 16], F32)
                    scb = SB(es, [128, LMAX], F32, "scb"); mb = SB(es, [128, LMAX], BF16, "mb"); junk = SB(es, [128, LMAX], F32, "junkb")
                    rr = [SB(es, [128, 512], F32) for _ in range(2)]
                    hi = SB(es, [128, 1], F32); lo = SB(es, [128, 1], F32); lo2 = SB(es, [128, 1], F32); mid = SB(es, [128, 1], F32)
                    cnt = SB(es, [128, 1], F32); tmp1 = SB(es, [128, 1], F32); wv = SB(es, [128, NIT], F32)
                    pbuf = [SB(es, [128, LMAX], BF16) for _ in range(2)]
                    rs = [SB(es, [128, 16], F32) for _ in range(2)]
                    rinv = [SB(es, [128, 1], F32) for _ in range(2)]
                    pT = [SB(es, [128, NKB, 128], BF16) for _ in range(2)]
                    obs = [SB(es, [128, 4, 128], BF16) for _ in range(2)]
                    psd = [PS(es, [128, 512], F32) for _ in range(3)]
                    pst = [PS(es, [128, 4, 128], BF16) for _ in range(2)]
                    pso = [PS(es, [128, 4, 128], F32) for _ in range(2)]
                    dctr = [0]; tctr = [0]; kctr = [0]

                    def add_keys(src_k, src_v, src_ki, nrows, kbi):
                        st = kst[kctr[0] % 2]; bf = kbf[kctr[0] % 2]; kctr[0] += 1
                        c.dma('sp', st[0:nrows, 0:256], src_k, w=[st])
                        c.dma('sp', st[0:nrows, 256:512], src_v, w=[st])
                        c.dma('sp', st[0:nrows, 512:576], src_ki, w=[st])
                        c.op('pool', lambda e: e.tensor_copy(out=bf[0:nrows, 0:256], in_=st[0:nrows, 0:256]), r=[st], w=[bf])
                        c.op('pool', lambda e: e.tensor_copy(out=bf[0:nrows, 256:320], in_=st[0:nrows, 512:576]), r=[st], w=[bf])
                        c.op('pool', lambda e: e.tensor_copy(out=bf[0:nrows, 320:384], in_=st[0:nrows, 512:576]), r=[st], w=[bf])
                        c.op('pool', lambda e: e.tensor_copy(out=Vt[0:nrows, kbi, :], in_=st[0:nrows, 256:512]), r=[st], w=[Vt])
                        p_ = pst[tctr[0] % 2]; tctr[0] += 1

                        def mm(e):
                            for j in range(3):
                                ins = e.transpose(p_[:, j, 0:nrows], bf[0:nrows, j * 128:(j + 1) * 128], identB[0:nrows, 0:nrows])
                            return ins
                        c.op('pe', mm, r=[bf, identB], w=[p_])
                        c.op('act', lambda e: e.copy(out=KT[:, :, kbi * 128:kbi * 128 + nrows], in_=p_[:, 0:2, 0:nrows]), r=[p_], w=[KT])
                        c.op('act', lambda e: e.copy(out=KIT[:, kbi * 128:kbi * 128 + nrows], in_=p_[:, 2, 0:nrows]), r=[p_], w=[KIT])

                    for (kind, idx, tok0, slen) in seqs:
                        n = 128 if kind == 'p' else 64
                        topk = topk_p if kind == 'p' else topk_s
                        if kind == 's':
                            for kb_ in range(P // 128):
                                add_keys(ck[l, idx, kb_ * 128:(kb_ + 1) * 128, :], cv[l, idx, kb_ * 128:(kb_ + 1) * 128, :], cik[l, idx, kb_ * 128:(kb_ + 1) * 128, :], 128, kb_)
                        for t in range(slen // n):
                            ts_ = slice(tok0 + t * n, tok0 + (t + 1) * n)
                            pos0 = t * 128 if kind == 'p' else P
                            Lk = pos0 + n
                            add_keys(tm_d[ts_, 0:256], tm_d[ts_, 256:512], tm_d[ts_, 544:608], n, pos0 // 128)
                            c.dma('sp', qiT[:, :, 0:n], qiT_d[:, ts_].rearrange("(j p) t -> p j t", p=128), w=[qiT])
                            c.dma('sp', qbT[:, :, 0:n], qbT_d[:, ts_].rearrange("(h p) t -> p h t", p=128), w=[qbT])
                            c.dma('sp', zbT[:, :, 0:n], zbT_d[:, ts_].rearrange("(h p) t -> p h t", p=128), w=[zbT])
                            c.dma('sp', wi[0:n, :], tm_d[ts_, 608:624], w=[wi])
                            c.op('act', lambda e: e.activation(out=absw[0:n], in_=wi[0:n], func=AF.Abs), r=[wi], w=[absw])
                            c.op('act', lambda e: e.activation(out=sgn[0:n], in_=wi[0:n], func=AF.Sign), r=[wi], w=[sgn])
                            kchunks = [(c0, min(512, Lk - c0)) for c0 in range(0, Lk, 512)]
                            for (c0, wd) in kchunks:
                                for h in range(16):
                                    hb = 64 * (h % 2)
                                    p_ = psd[dctr[0] % 3]; r_ = rr[dctr[0] % 2]; dctr[0] += 1
                                    c.op('pe', lambda e: e.matmul(p_[0:n, 0:wd], qiT[hb:hb + 64, h // 2, 0:n], KIT[hb:hb + 64, c0:c0 + wd], start=True, stop=True), r=[qiT, KIT], w=[p_])
                                    c.op('act', lambda e: e.activation(out=r_[0:n, 0:wd], in_=p_[0:n, 0:wd], func=AF.Relu, scale=absw[0:n, h:h + 1]), r=[p_, absw], w=[r_])
                                    if h == 0:
                                        c.op('dve', lambda e: e.tensor_scalar(out=scb[0:n, c0:c0 + wd], in0=r_[0:n, 0:wd], scalar1=sgn[0:n, 0:1], scalar2=None, op0=ALU.mult), r=[r_, sgn], w=[scb])
                                    else:
                                        c.op('dve', lambda e: e.scalar_tensor_tensor(out=scb[0:n, c0:c0 + wd], in0=r_[0:n, 0:wd], scalar=sgn[0:n, h:h + 1], in1=scb[0:n, c0:c0 + wd],
                                                                                      op0=ALU.mult, op1=ALU.add), r=[r_, sgn, scb], w=[scb])
                            if kind == 'p':
                                c.op('dve', lambda e: e.tensor_reduce(out=lo[0:64], in_=scb[0:64, 0:Lk - 64], axis=AX.X, op=ALU.min), r=[scb], w=[lo])
                                c.op('dve', lambda e: e.tensor_reduce(out=lo[64:128], in_=scb[64:128, 0:Lk], axis=AX.X, op=ALU.min), r=[scb], w=[lo])
                                c.op('dve', lambda e: e.memset(scb[0:64, Lk - 64:Lk], -1.0e30), w=[scb])
                            else:
                                c.op('dve', lambda e: e.tensor_reduce(out=lo[0:n], in_=scb[0:n, 0:Lk], axis=AX.X, op=ALU.min), r=[scb], w=[lo])
                            c.op('dve', lambda e: e.tensor_reduce(out=hi[0:n], in_=scb[0:n, 0:Lk], axis=AX.X, op=ALU.max), r=[scb], w=[hi])
                            c.op('dve', lambda e: e.tensor_tensor(out=hi[0:n], in0=hi[0:n], in1=lo[0:n], op=ALU.subtract), r=[hi, lo], w=[hi])
                            c.op('dve', lambda e: e.tensor_scalar(out=wv[0:n], in0=pow2[0:n], scalar1=hi[0:n, 0:1], scalar2=None, op0=ALU.mult), r=[pow2, hi], w=[wv])
                            for k in range(NIT):
                                c.op('dve', lambda e: e.tensor_tensor(out=mid[0:n], in0=lo[0:n], in1=wv[0:n, k:k + 1], op=ALU.add), r=[lo, wv], w=[mid])
                                c.op('dve', lambda e: e.tensor_scalar(out=junk[0:n, 0:Lk], in0=scb[0:n, 0:Lk], scalar1=mid[0:n, 0:1], scalar2=0.0, op0=ALU.is_ge, op1=ALU.add, accum_out=cnt[0:n]),
                                     r=[scb, mid], w=[junk, cnt])
                                c.op('dve', lambda e: e.scalar_tensor_tensor(out=tmp1[0:n], in0=cnt[0:n], scalar=float(topk) - 0.5, in1=wv[0:n, k:k + 1], op0=ALU.is_ge, op1=ALU.mult),
                                     r=[cnt, wv], w=[tmp1])
                                c.op('dve', lambda e: e.tensor_tensor(out=lo[0:n], in0=lo[0:n], in1=tmp1[0:n], op=ALU.add), r=[lo, tmp1], w=[lo])
                            c.op('dve', lambda e: e.tensor_scalar(out=mb[0:n, 0:Lk], in0=scb[0:n, 0:Lk], scalar1=lo[0:n, 0:1], scalar2=-30000.0, op0=ALU.is_lt, op1=ALU.mult), r=[scb, lo], w=[mb])
                            nearw = min(256, Lk) if n == 128 else min(192, Lk)
                            near0 = Lk - nearw
                            nboff = (256 - nearw) if n == 128 else 0
                            if n == 64:
                                nboff = near0 - (pos0 - 128)
                            nkb = (Lk + 127) // 128
                            achunks = [(c0, min(512, near0 - c0), False) for c0 in range(0, near0, 512)] + [(near0, nearw, True)]
                            def att0(h):
                                g = h // 8
                                pb_ = pbuf[h % 2]; rs_ = rs[h % 2]; ri_ = rinv[h % 2]
                                for ci, (c0, wd, isnear) in enumerate(achunks):
                                    p_ = psd[dctr[0] % 3]; dctr[0] += 1

                                    def mm(e):
                                        e.matmul(p_[0:n, 0:wd], qbT[:, h, 0:n], KT[:, g, c0:c0 + wd], start=True, stop=False)
                                        ins = e.matmul(p_[0:n, 0:wd], identB[0:n, 0:n], mb[0:n, c0:c0 + wd], start=False, stop=(not isnear))
                                        if isnear:
                                            ins = e.matmul(p_[0:n, 0:wd], identB[0:n, 0:n], NBh[0:n, h, nboff:nboff + wd], start=False, stop=True)
                                        return ins
                                    c.op('pe', mm, r=[qbT, KT, mb, identB, NBh], w=[p_])
                                    c.op('act', lambda e: e.activation(out=pb_[0:n, c0:c0 + wd], in_=p_[0:n, 0:wd], func=AF.Exp, bias=CH[0:n, h:h + 1], accum_out=rs_[0:n, ci:ci + 1]),
                                         r=[p_, CH], w=[pb_, rs_])
                                c.op('dve', lambda e: e.tensor_reduce(out=ri_[0:n], in_=rs_[0:n, 0:len(achunks)], axis=AX.X, op=ALU.add), r=[rs_], w=[ri_])
                                c.op('dve', lambda e: e.reciprocal(out=ri_[0:n], in_=ri_[0:n]), r=[ri_], w=[ri_])
                                c.op('dve', lambda e: e.tensor_scalar(out=pb_[0:n, 0:Lk], in0=pb_[0:n, 0:Lk], scalar1=ri_[0:n, 0:1], scalar2=None, op0=ALU.mult), r=[pb_, ri_], w=[pb_])

                            def att1(h):
                                g = h // 8
                                pb_ = pbuf[h % 2]; pT_ = pT[h % 2]
                                for k4 in range(0, nkb, 4):
                                    ks = list(range(k4, min(nkb, k4 + 4)))
                                    p_ = pst[tctr[0] % 2]; tctr[0] += 1

                                    def mm(e):
                                        for j, kb_ in enumerate(ks):
                                            kw = min(128, Lk - kb_ * 128)
                                            ins = e.transpose(p_[0:kw, j, 0:n], pb_[0:n, kb_ * 128:kb_ * 128 + kw], identB[0:n, 0:n])
                                        return ins
                                    c.op('pe', mm, r=[pb_, identB], w=[p_])
                                    kwl = min(128, Lk - ks[-1] * 128)
                                    if kwl == 128:
                                        if (k4 // 4) % 2 == 0:
                                            c.op('act', lambda e: e.copy(out=pT_[:, k4:k4 + len(ks), 0:n], in_=p_[:, 0:len(ks), 0:n]), r=[p_], w=[pT_])
                                        else:
                                            c.op('dve', lambda e: e.tensor_copy(out=pT_[:, k4:k4 + len(ks), 0:n], in_=p_[:, 0:len(ks), 0:n]), r=[p_], w=[pT_])
                                    else:
                                        for j, kb_ in enumerate(ks):
                                            kw = min(128, Lk - kb_ * 128)
                                            c.op('act', lambda e: e.copy(out=pT_[0:kw, kb_, 0:n], in_=p_[0:kw, j, 0:n]), r=[p_], w=[pT_])
                                po = pso[(h // 4) % 2]

                                def mm(e):
                                    for kb_ in range(nkb):
                                        kw = min(128, Lk - kb_ * 128)
                                        ins = e.matmul(po[:, h % 4, 0:n], Vt[0:kw, kb_, g * 128:(g + 1) * 128], pT_[0:kw, kb_, 0:n], start=(kb_ == 0), stop=(kb_ == nkb - 1))
                                    return ins
                                c.op('pe', mm, r=[Vt, pT_], w=[po])
                                if h % 4 == 3:
                                    hg = h // 4
                                    o_ = obs[hg % 2]
                                    c.op('dve', lambda e: e.tensor_tensor(out=o_[:, :, 0:n], in0=po[:, :, 0:n], in1=zbT[:, hg * 4:hg * 4 + 4, 0:n], op=ALU.mult), r=[po, zbT], w=[o_])
                                    c.dma('pool', obT_d[hg * 512:(hg + 1) * 512, ts_].rearrange("(h p) t -> p h t", p=128), o_[:, :, 0:n], r=[o_])
                            for step in range(17):
                                if step < 16:
                                    att0(step)
                                if step >= 1:
                                    att1(step - 1)
                    c.barrier()

                chk('B')
                for (htok0, hn) in oparts:
                    with contextlib.ExitStack() as es:
                        m1 = SB(es, [128, KC, hn], BF16, "m1")
                        pp = [PS(es, [128, 512], F32) for _ in range(3)]
                        pctr = [0]
                        cchunks = [(c0, min(512, hn - c0)) for c0 in range(0, hn, 512)]
                        with contextlib.ExitStack() as es2:
                            big = SB(es2, [128, 16, hn], BF16, "big")
                            wstg = [SB(es2, [128, 16, 256], F32) for _ in range(2)]
                            wbf = [SB(es2, [128, 16, 256], BF16) for _ in range(2)]
                            sgs = [SB(es2, [128, 512], BF16) for _ in range(2)]
                            tmpb = [SB(es2, [128, 512], F32) for _ in range(2)]
                            wctr = [0]
                            for bi_, (srcd, wsrc, sgd) in enumerate(((ozT_d, w_a[l], sgaT_d), (obT_d, w_b[l], sgbT_d))):
                                for c0 in range(0, hn, 512):
                                    wd = min(512, hn - c0)
                                    c.dma('sp', big[:, :, c0:c0 + wd], srcd[:, htok0 + c0:htok0 + c0 + wd].rearrange("(k p) t -> p k t", p=128), w=[big])
                                for b2 in range(0, KC, 2):
                                    i = wctr[0] % 2; wctr[0] += 1
                                    st = wstg[i]; wb = wbf[i]
                                    c.dma('sp', st[:], wsrc[:, b2 * 128:b2 * 128 + 256].rearrange("(k p) n -> p k n", p=128), w=[st])
                                    c.op('dve', lambda e: e.tensor_copy(out=wb[:], in_=st[:]), r=[st], w=[wb])
                                    for bb in range(2):
                                        blk = b2 + bb
                                        for (c0, wd) in cchunks:
                                            p_ = pp[pctr[0] % 3]; s_ = sgs[pctr[0] % 2]; t_ = tmpb[pctr[0] % 2]; pctr[0] += 1
                                            c.dma('sp', s_[:, 0:wd], sgd[blk * 128:(blk + 1) * 128, htok0 + c0:htok0 + c0 + wd], w=[s_])

                                            def mm(e):
                                                for kc in range(16):
                                                    ins = e.matmul(p_[:, 0:wd], wb[:, kc, bb * 128:(bb + 1) * 128], big[:, kc, c0:c0 + wd], start=(kc == 0), stop=(kc == 15))
                                                return ins
                                            c.op('pe', mm, r=[wb, big], w=[p_])
                                            if bi_ == 0:
                                                c.op('dve', lambda e: e.tensor_tensor(out=m1[:, blk, c0:c0 + wd], in0=p_[:, 0:wd], in1=s_[:, 0:wd], op=ALU.mult), r=[p_, s_], w=[m1])
                                            else:
                                                c.op('dve', lambda e: e.tensor_tensor(out=t_[:, 0:wd], in0=p_[:, 0:wd], in1=s_[:, 0:wd], op=ALU.mult), r=[p_, s_], w=[t_])
                                                c.op('dve', lambda e: e.tensor_tensor(out=m1[:, blk, c0:c0 + wd], in0=m1[:, blk, c0:c0 + wd], in1=t_[:, 0:wd], op=ALU.add), r=[t_, m1], w=[m1])
                            c.barrier()
                        with contextlib.ExitStack() as es2:
                            Wo = SB(es2, [128, KC, D], BF16, "Wo")
                            wst2 = [SB(es2, [128, KC, 128], F32) for _ in range(2)]
                            for b1 in range(KC):
                                st = wst2[b1 % 2]
                                c.dma('sp', st[:], w_out[l, :, b1 * 128:b1 * 128 + 128].rearrange("(k p) n -> p k n", p=128), w=[st])
                                c.op('dve', lambda e: e.tensor_copy(out=Wo[:, :, b1 * 128:b1 * 128 + 128], in_=st[:]), r=[st], w=[Wo])
                            GATE = SB(es2, [128, D], F32, "GATE")
                            xt = [SB(es2, [128, D], F32) for _ in range(2)]
                            xo = [SB(es2, [128, D], F32) for _ in range(2)]
                            ssq = [SB(es2, [128, 1], F32) for _ in range(2)]
                            last_rows = None
                            ncg = max(1, D // 512)
                            cw = min(512, D)
                            for ti, (t0, parts) in enumerate(range_tiles(htok0, hn)):
                                x_ = xt[ti % 2]; xo_ = xo[ti % 2]; ss_ = ssq[ti % 2]
                                rows = tuple((idx if kind == 'p' else 2 + idx, p0, n) for (kind, idx, r0, n, p0) in parts)
                                if rows != last_rows:
                                    for (row, p0, n) in rows:
                                        c.dma('sp', GATE[p0:p0 + n, :], mod_d[row:row + 1, 2 * D:3 * D].partition_broadcast(n), w=[GATE])
                                    last_rows = rows
                                for (kind, idx, r0, n, p0) in parts:
                                    c.dma('sp', x_[p0:p0 + n, :], x_src(l, kind, idx, r0, n), w=[x_])
                                for cg in range(ncg):
                                    p_ = pp[pctr[0] % 3]; pctr[0] += 1
                                    cs = slice(cg * cw, (cg + 1) * cw)

                                    def mm(e):
                                        for kc in range(KC):
                                            ins = e.matmul(p_[:, 0:cw], m1[:, kc, t0:t0 + 128], Wo[:, kc, cs], start=(kc == 0), stop=(kc == KC - 1))
                                        return ins
                                    c.op('pe', mm, r=[m1, Wo], w=[p_])
                                    c.op('dve', lambda e: e.tensor_tensor(out=xo_[:, cs], in0=p_[:, 0:cw], in1=GATE[:, cs], op=ALU.mult), r=[p_, GATE], w=[xo_])
                                    c.op('dve', lambda e: e.tensor_tensor(out=xo_[:, cs], in0=xo_[:, cs], in1=x_[:, cs], op=ALU.add), r=[xo_, x_], w=[xo_])
                                if l < DEPTH - 1:
                                    c.dma('pool', xs_d[htok0 + t0:htok0 + t0 + 128, :], xo_[:], r=[xo_])
                                else:
                                    c.op('act', lambda e: e.activation(out=x_[:], in_=xo_[:], func=AF.Square, accum_out=ss_[:]), r=[xo_], w=[x_, ss_])
                                    c.op('act', lambda e: e.activation(out=ss_[:], in_=ss_[:], func=AF.Sqrt, scale=1.0 / D, bias=epsc[:]), r=[ss_, epsc], w=[ss_])
                                    c.op('dve', lambda e: e.reciprocal(out=ss_[:], in_=ss_[:]), r=[ss_], w=[ss_])
                                    c.op('dve', lambda e: e.scalar_tensor_tensor(out=xo_[:], in0=xo_[:], scalar=ss_[:, 0:1], in1=FNW[:], op0=ALU.mult, op1=ALU.mult), r=[xo_, ss_, FNW], w=[xo_])
                                    for (kind, idx, r0, n, p0) in parts:
                                        yo = yp if kind == 'p' else ys
                                        c.dma('pool', yo[idx, r0:r0 + n, :], xo_[p0:p0 + n, :], r=[xo_])
                        c.barrier()
        except _Stop:
            c.barrier()
            ges.pop_all()
            return nc
        c.barrier()
        c.es.close()
    return nc


def rel_bucket_np(rel):
    nb = 16
    max_exact = 8
    n = np.abs(rel)
    nf = np.maximum(n, 1).astype(np.float32)
    large = max_exact + (np.log(nf / max_exact) / np.float32(np.log(128 / max_exact)) * (nb - max_exact)).astype(np.int32)
    large = np.minimum(large, nb - 1)
    return np.where(rel > 0, nb, 0) + np.where(n < max_exact, n, large)


def make_ohr():
    r = np.arange(384) - 255
    b = rel_bucket_np(r)
    oh = np.zeros((32, 384), np.float32)
    oh[b, np.arange(384)] = 1.0
    return oh


_NC_CACHE = {}


def run(cfg, inputs, debug_outs=()):
    D, T, P, DEPTH = cfg['D'], cfg['T'], cfg['P'], cfg['DEPTH']
    key = (D, T, P, DEPTH, tuple(debug_outs))
    if key not in _NC_CACHE:
        _NC_CACHE[key] = build(cfg, debug_outs)
    nc = _NC_CACHE[key]
    f = lambda a: np.ascontiguousarray(np.asarray(a, dtype=np.float32))
    I = {k: f(v) for k, v in inputs.items()}
    ohr = make_ohr()
    in_maps = []
    for ci in range(NCORES):
        b = slice(2 * ci, 2 * ci + 2)
        m = {
            "xp": f(I['x_prompt'][b]), "xs": f(I['x_sample'][b]),
            "c4": f(np.concatenate([I['c_prompt'][b], I['c_sample'][b]], axis=0)),
            "ck": f(I['cache_k'][:, b].reshape(DEPTH, 2, P, 256)), "cv": f(I['cache_v'][:, b].reshape(DEPTH, 2, P, 256)),
            "cik": f(I['cache_idx_k'][:, b]), "sg": f(I['state_gdn'][:, b]), "scv": f(I['state_conv'][:, b]),
            "ohr": ohr,
        }
        for k in ("norm_w", "w_ada", "b_ada", "w_in", "w_conv", "a_log", "dt_bias", "gdn_norm_w", "w_branch_a", "w_branch_b", "w_out", "rel_bias", "final_norm_w"):
            m[k] = I[k]
        in_maps.append(m)
    res = run_bass_kernel_spmd(nc, in_maps, core_ids=list(range(NCORES)))
    R = res.results
    cat0 = lambda k: np.concatenate([r[k] for r in R], axis=0)
    cat1 = lambda k: np.concatenate([r[k] for r in R], axis=1)
    B = 2 * NCORES
    outs = (
        cat0("yp"), cat0("ys"),
        cat1("nkp").reshape(DEPTH, B, T, 2, 128), cat1("nvp").reshape(DEPTH, B, T, 2, 128), cat1("nikp"),
        cat1("ngp"), cat1("ncp"),
        cat1("nks").reshape(DEPTH, B, 64, 2, 128), cat1("nvs").reshape(DEPTH, B, 64, 2, 128), cat1("niks"),
        cat1("ngs"), cat1("ncs"),
    )
    if debug_outs:
        return outs, R
    return outs


def kernel(**inputs):
    cfg = dict(D=2048, T=2048, P=4096, DEPTH=2)
    return run(cfg, inputs)
```

```python
import contextlib
import numpy as np
import concourse.bass as bass
import concourse.mybir as mybir
from concourse.bass_utils import run_bass_kernel_spmd

F32 = mybir.dt.float32
BF16 = mybir.dt.bfloat16
AF = mybir.ActivationFunctionType
ALU = mybir.AluOpType
AX = mybir.AxisListType

NCORES = 8
GH = 16
HD = 128
CONV_CH = 6144
EPS = 1e-6
NIT = 14


def in_offsets(D):
    sizes = (6144, 2048, 16, 16, 2048, 256, 256, 2048, 1024, 64, 16, D, D)
    names = ("qkv", "za", "ba", "aa", "qb", "kb", "vb", "zb", "qi", "ki", "wi", "gla", "glb")
    off = {}
    o = 0
    for n, s in zip(names, sizes):
        off[n] = o
        o += s
    return off, o


class _Stop(Exception):
    pass


class Buf:
    def __init__(self, t, name):
        self.t = t
        self.name = name
        self.lw = None
        self.rd = []
        self.dsem = None

    def __getitem__(self, k):
        return self.t[k]


class Ctx:
    def __init__(self, nc, ndsem=40):
        self.nc = nc
        self.eng = {'pe': nc.tensor, 'act': nc.scalar, 'dve': nc.vector, 'pool': nc.gpsimd, 'sp': nc.sync}
        self.es = contextlib.ExitStack()
        self.sem = {}
        self.cnt = {}
        for e in self.eng:
            self.sem[e] = self.es.enter_context(nc.semaphore("s_" + e))
            self.cnt[e] = 0
        self.dsems = [self.es.enter_context(nc.semaphore("d%d" % i)) for i in range(ndsem)]
        self.dcnt = [0] * ndsem
        self.dnext = 0
        self.waited = {}

    def wait(self, e, tok):
        if tok is None:
            return
        sem, val = tok
        key = (e, id(sem))
        if self.waited.get(key, 0) >= val:
            return
        self.eng[e].wait_ge(sem, val)
        self.waited[key] = val

    def _deps(self, e, r, w):
        for b in r:
            self.wait(e, b.lw)
            if getattr(b, 'psum', False):
                for t in b.rd:
                    self.wait(e, t)
        for b in w:
            self.wait(e, b.lw)
            for t in b.rd:
                self.wait(e, t)

    def _commit(self, tok, r, w):
        for b in r:
            b.rd.append(tok)
            if len(b.rd) > 24:
                b.rd = b.rd[-24:]
        for b in w:
            b.lw = tok
            b.rd = []

    def op(self, e, fn, r=(), w=()):
        self._deps(e, r, w)
        ins = fn(self.eng[e])
        self.cnt[e] += 1
        ins.then_inc(self.sem[e], 1)
        tok = (self.sem[e], self.cnt[e])
        self._commit(tok, r, w)
        return tok

    def dma(self, q, out, in_, r=(), w=(), slow=False):
        self._deps(q, r, w)
        b = (list(w) + list(r))[0]
        if b.dsem is None:
            b.dsem = self.dnext % len(self.dsems)
            self.dnext += 1
        i = b.dsem
        kw = {}
        if slow:
            kw['allow_slow_non_contiguous'] = True
        self.eng[q].dma_start(out=out, in_=in_, **kw).then_inc(self.dsems[i], 16)
        self.dcnt[i] += 16
        tok = (self.dsems[i], self.dcnt[i])
        self._commit(tok, r, w)
        return tok

    def barrier(self):
        toks = [(self.sem[e], self.cnt[e]) for e in self.eng if self.cnt[e] > 0]
        toks += [(self.dsems[i], self.dcnt[i]) for i in range(len(self.dsems)) if self.dcnt[i] > 0]
        for e in self.eng:
            for t in toks:
                self.wait(e, t)


def build(cfg, debug_outs=()):
    D, T, P, DEPTH = cfg['D'], cfg['T'], cfg['P'], cfg['DEPTH']
    KC = D // 128
    NTP = T // 128
    NTOK = 2 * T + 128
    OFF, IN_DIM = in_offsets(D)
    LS = P + 64
    LMAX = max(T, LS)
    NKB = (LMAX + 127) // 128
    topk_p = min(256, T // 4)
    topk_s = min(256, LS // 4)

    nc = bass.Bass("TRN2", target_bir_lowering=False)

    def din(name, shape):
        return nc.dram_tensor(name, list(shape), F32, kind="ExternalInput").ap()

    def dout(name, shape):
        return nc.dram_tensor(name, list(shape), F32, kind="ExternalOutput").ap()

    def dscr(name, shape, dt):
        kind = "ExternalOutput" if name in debug_outs else "Internal"
        return nc.dram_tensor(name, list(shape), dt, kind=kind).ap()

    xp = din("xp", [2, T, D]); xs = din("xs", [2, 64, D]); c4 = din("c4", [4, D])
    ck = din("ck", [DEPTH, 2, P, 256]); cv = din("cv", [DEPTH, 2, P, 256]); cik = din("cik", [DEPTH, 2, P, 64])
    sg = din("sg", [DEPTH, 2, GH, 128, 128]); scv = din("scv", [DEPTH, 2, 3, CONV_CH])
    norm_w = din("norm_w", [DEPTH, D]); w_ada = din("w_ada", [DEPTH, D, 3 * D]); b_ada = din("b_ada", [DEPTH, 3 * D])
    w_in = din("w_in", [DEPTH, D, IN_DIM]); w_conv = din("w_conv", [DEPTH, 4, CONV_CH])
    a_log = din("a_log", [DEPTH, GH]); dt_bias = din("dt_bias", [DEPTH, GH]); gdn_norm_w = din("gdn_norm_w", [DEPTH, 128])
    w_a = din("w_branch_a", [DEPTH, 2048, D]); w_b = din("w_branch_b", [DEPTH, 2048, D]); w_out = din("w_out", [DEPTH, D, D])
    rel_bias = din("rel_bias", [32, 16]); fnw = din("final_norm_w", [D])
    ohr = din("ohr", [32, 384])

    yp = dout("yp", [2, T, D]); ys = dout("ys", [2, 64, D])
    nkp = dout("nkp", [DEPTH, 2, T, 256]); nvp = dout("nvp", [DEPTH, 2, T, 256]); nikp = dout("nikp", [DEPTH, 2, T, 64])
    ngp = dout("ngp", [DEPTH, 2, GH, 128, 128]); ncp = dout("ncp", [DEPTH, 2, 3, CONV_CH])
    nks = dout("nks", [DEPTH, 2, 64, 256]); nvs = dout("nvs", [DEPTH, 2, 64, 256]); niks = dout("niks", [DEPTH, 2, 64, 64])
    ngs = dout("ngs", [DEPTH, 2, GH, 128, 128]); ncs = dout("ncs", [DEPTH, 2, 3, CONV_CH])

    xs_d = dscr("xs_d", [NTOK, D], F32)
    mod_d = dscr("mod_d", [4, 3 * D], F32)
    tm_d = dscr("tm_d", [NTOK, 624], F32)
    tb_d = dscr("tb_d", [384, 16], F32)
    qT_d = dscr("qT_d", [2048, NTOK], BF16); kT_d = dscr("kT_d", [2048, NTOK], BF16); vT_d = dscr("vT_d", [2048, NTOK], BF16)
    zaT_d = dscr("zaT_d", [2048, NTOK], BF16); qbT_d = dscr("qbT_d", [2048, NTOK], BF16); zbT_d = dscr("zbT_d", [2048, NTOK], BF16)
    qiT_d = dscr("qiT_d", [1024, NTOK], BF16)
    sgaT_d = dscr("sgaT_d", [D, NTOK], BF16); sgbT_d = dscr("sgbT_d", [D, NTOK], BF16)
    ozT_d = dscr("ozT_d", [2048, NTOK], BF16); obT_d = dscr("obT_d", [2048, NTOK], BF16)

    c = Ctx(nc)
    uid = [0]

    def SB(es, shape, dt, name=None):
        uid[0] += 1
        nm = "%s_%d" % (name or "sb", uid[0])
        return Buf(es.enter_context(nc.sbuf_tensor(nm, list(shape), dt)), nm)

    def PS(es, shape, dt, name=None):
        uid[0] += 1
        nm = "%s_%d" % (name or "ps", uid[0])
        b = Buf(es.enter_context(nc.psum_tensor(nm, list(shape), dt)), nm)
        b.psum = True
        return b

    seqs = [('p', 0, 0, T), ('p', 1, T, T), ('s', 0, 2 * T, 64), ('s', 1, 2 * T + 64, 64)]
    halves = [(0, T), (T, T + 128)]

    with contextlib.ExitStack() as ges:
        identF = SB(ges, [128, 128], F32, "identF"); identB = SB(ges, [128, 128], BF16, "identB")
        onesB = SB(ges, [128, 128], BF16, "onesB"); onesF = SB(ges, [128, 128], F32, "onesF")
        Lm = SB(ges, [128, 128], F32, "Lm"); MLs = SB(ges, [128, 128], F32, "MLs")
        UP = SB(ges, [128, 128], F32, "UP"); LO = SB(ges, [128, 128], F32, "LO")
        NB = SB(ges, [128, 256, 16], F32, "NB"); NBh = SB(ges, [128, 16, 256], BF16, "NBh")
        CH = SB(ges, [128, 16], F32, "CH")
        epsc = SB(ges, [128, 1], F32, "epsc"); onec = SB(ges, [128, 1], F32, "onec"); zero3 = SB(ges, [128, 3], F32, "zero3")
        pow2 = SB(ges, [128, NIT], F32, "pow2")
        FNW = SB(ges, [128, D], F32, "FNW")

        def cmem(b, ap, v):
            c.op('pool', lambda e: e.memset(ap, v), w=[b])

        def csel(b, ap, pattern, op, fill, cm, base=0):
            c.op('pool', lambda e: e.affine_select(out=ap, in_=ap, pattern=pattern, compare_op=op, fill=fill,
                                                   base=base, channel_multiplier=cm), r=[b], w=[b])
        cmem(identF, identF[:], 1.0); csel(identF, identF[:], [[-1, 128]], ALU.is_equal, 0.0, 1)
        c.op('pool', lambda e: e.tensor_copy(out=identB[:], in_=identF[:]), r=[identF], w=[identB])
        cmem(onesB, onesB[:], 1.0); cmem(onesF, onesF[:], 1.0)
        cmem(Lm, Lm[:], 1.0); csel(Lm, Lm[:], [[1, 128]], ALU.is_ge, 0.0, -1)
        cmem(MLs, MLs[:], 1.0); csel(MLs, MLs[:], [[-1, 128]], ALU.is_gt, 0.0, 1)
        cmem(UP, UP[:], 1.0e4); csel(UP, UP[:], [[1, 128]], ALU.is_gt, 0.0, -1)
        cmem(LO, LO[:], 1.0e4); csel(LO, LO[:], [[-1, 128]], ALU.is_gt, 0.0, 1)
        cmem(epsc, epsc[:], EPS); cmem(onec, onec[:], 1.0); cmem(zero3, zero3[:], 0.0)
        for k in range(NIT):
            cmem(pow2, pow2[:, k:k + 1], 0.5 ** (k + 1))
        c.dma('sp', FNW[:], fnw.rearrange("(o d) -> o d", o=1).partition_broadcast(128), w=[FNW])
        c.dma('sp', CH[:], rel_bias[15:16, :].partition_broadcast(128), w=[CH])
        with contextlib.ExitStack() as es:
            oh = SB(es, [32, 384], F32); rb = SB(es, [32, 16], F32); tbs = SB(es, [128, 3, 16], F32)
            pt = PS(es, [128, 16], F32)
            c.dma('sp', oh[:], ohr[:, :], w=[oh]); c.dma('sp', rb[:], rel_bias[:, :], w=[rb])
            for j in range(3):
                c.op('pe', lambda e: e.matmul(pt[:], oh[:, j * 128:(j + 1) * 128], rb[:], start=True, stop=True), r=[oh, rb], w=[pt])
                c.op('dve', lambda e: e.tensor_tensor(out=tbs[:, j, :], in0=pt[:], in1=CH[:], op=ALU.subtract), r=[pt, CH], w=[tbs])
            c.dma('sp', tb_d.rearrange("(j p) h -> p j h", p=128), tbs[:], r=[tbs])
            c.barrier()
            for i in range(128):
                c.dma('sp', NB[i:i + 1, :, :], tb_d[127 - i:127 - i + 256, :].rearrange("(o c) h -> o c h", o=1), w=[NB])
            c.op('dve', lambda e: e.tensor_copy(out=NBh[:], in_=NB[:].rearrange("p c h -> p h c")), r=[NB], w=[NBh])
            c.barrier()

        def x_src(l, kind, idx, r0, n):
            if l == 0:
                return (xp if kind == 'p' else xs)[idx, r0:r0 + n, :]
            base = {('p', 0): 0, ('p', 1): T, ('s', 0): 2 * T, ('s', 1): 2 * T + 64}[(kind, idx)]
            return xs_d[base + r0:base + r0 + n, :]

        def range_tiles(tok0, ntok):
            tiles = []
            for a in range(tok0, tok0 + ntok, 128):
                if a < 2 * T:
                    parts = [('p', a // T, a % T, 128, 0)]
                else:
                    parts = [('s', 0, 0, 64, 0), ('s', 1, 0, 64, 64)]
                tiles.append((a - tok0, parts))
            return tiles

        def half_tiles(hf):
            return range_tiles(*halves[hf])
        oparts = [(0, T // 2), (T // 2, T // 2), (T, T // 2), (3 * T // 2, T // 2 + 128)]

        stop = cfg.get('stop')

        def chk(ph):
            if stop == ph:
                raise _Stop()
        try:
          chk('const')
          for l in range(DEPTH):
            with contextlib.ExitStack() as les:
                AT = SB(les, [128, KC, 4], F32, "AT"); BT = SB(les, [128, KC, 4], F32, "BT")
                wcT = SB(les, [128, 48, 4], F32, "wcT"); scT0 = SB(les, [128, 48, 3], F32, "scT0"); scT1 = SB(les, [128, 48, 3], F32, "scT1")
                gnw = SB(les, [128, 1], F32, "gnw"); dtb = SB(les, [128, 16], F32, "dtb"); nexpA = SB(les, [128, 16], F32, "nexpA")
                with contextlib.ExitStack() as es:
                    csb = SB(es, [4, D], F32); sc = SB(es, [4, D], F32); scT = SB(es, [128, KC, 4], F32)
                    mod = SB(es, [4, 3 * D], F32); bada = SB(es, [4, 3 * D], F32)
                    wst = [SB(es, [128, KC, 256], F32) for _ in range(2)]
                    modT = SB(es, [128, 2 * KC, 4], F32); nwT = SB(es, [128, KC], F32)
                    wc4 = SB(es, [4, CONV_CH], F32); sc3 = [SB(es, [3, CONV_CH], F32)] * 2
                    pt = PS(es, [128, 2 * KC, 4], F32); pm = [PS(es, [4, 512], F32) for _ in range(2)]
                    pw = PS(es, [128, 48, 4], F32)
                    c.dma('sp', csb[:], c4[:, :], w=[csb])
                    c.dma('sp', bada[:], b_ada[l:l + 1, :].partition_broadcast(4), w=[bada])
                    c.dma('sp', nwT[:], norm_w[l].rearrange("(k p) -> p k", p=128), w=[nwT], slow=True)
                    c.dma('sp', gnw[:], gdn_norm_w[l].rearrange("(p o) -> p o", o=1), w=[gnw], slow=True)
                    c.dma('sp', dtb[:], dt_bias[l:l + 1, :].partition_broadcast(128), w=[dtb])
                    c.dma('sp', nexpA[:], a_log[l:l + 1, :].partition_broadcast(128), w=[nexpA])
                    c.dma('sp', wc4[:], w_conv[l], w=[wc4])
                    c.op('act', lambda e: e.activation(out=nexpA[:], in_=nexpA[:], func=AF.Exp), r=[nexpA], w=[nexpA])
                    c.op('dve', lambda e: e.tensor_scalar(out=nexpA[:], in0=nexpA[:], scalar1=-1.0, scalar2=None, op0=ALU.mult), r=[nexpA], w=[nexpA])
                    c.op('act', lambda e: e.activation(out=sc[:], in_=csb[:], func=AF.Silu), r=[csb], w=[sc])
                    for kc in range(KC):
                        c.op('pe', lambda e: e.transpose(pt[:, kc, :], sc[:, kc * 128:(kc + 1) * 128], identF[0:4, 0:4]), r=[sc, identF], w=[pt])
                    c.op('dve', lambda e: e.tensor_copy(out=scT[:], in_=pt[:, 0:KC, :]), r=[pt], w=[scT])
                    for cb in range(48):
                        c.op('pe', lambda e: e.transpose(pw[:, cb, :], wc4[:, cb * 128:(cb + 1) * 128], identF[0:4, 0:4]), r=[wc4, identF], w=[pw])
                    c.op('dve', lambda e: e.tensor_copy(out=wcT[:], in_=pw[:]), r=[pw], w=[wcT])
                    for s, dst in ((0, scT0), (1, scT1)):
                        c.dma('sp', sc3[s][:], scv[l, s], w=[sc3[s]])
                        for cb in range(48):
                            c.op('pe', lambda e: e.transpose(pw[:, cb, 0:3], sc3[s][:, cb * 128:(cb + 1) * 128], identF[0:3, 0:3]), r=[sc3[s], identF], w=[pw])
                        c.op('dve', lambda e: e.tensor_copy(out=dst[:], in_=pw[:, :, 0:3]), r=[pw], w=[dst])
                    ncg = 3 * D // 256
                    for cg in range(ncg):
                        ws = wst[cg % 2]; pmm = pm[cg % 2]
                        c.dma('sp', ws[:], w_ada[l, :, cg * 256:(cg + 1) * 256].rearrange("(k p) n -> p k n", p=128), w=[ws])

                        def mm(e):
                            for kc in range(KC):
                                ins = e.matmul(pmm[:, 0:256], scT[:, kc, :], ws[:, kc, :], start=(kc == 0), stop=(kc == KC - 1))
                            return ins
                        c.op('pe', mm, r=[scT, ws], w=[pmm])
                        c.op('dve', lambda e: e.tensor_tensor(out=mod[:, cg * 256:(cg + 1) * 256], in0=pmm[:, 0:256], in1=bada[:, cg * 256:(cg + 1) * 256], op=ALU.add), r=[pmm, bada], w=[mod])
                    c.dma('pool', mod_d[:, :], mod[:], r=[mod])
                    for j in range(2 * KC):
                        c.op('pe', lambda e: e.transpose(pt[:, j, :], mod[:, j * 128:(j + 1) * 128], identF[0:4, 0:4]), r=[mod, identF], w=[pt])
                    c.op('dve', lambda e: e.tensor_copy(out=modT[:], in_=pt[:]), r=[pt], w=[modT])
                    c.op('dve', lambda e: e.tensor_copy(out=BT[:], in_=modT[:, 0:KC, :]), r=[modT], w=[BT])
                    c.op('dve', lambda e: e.scalar_tensor_tensor(out=AT[:], in0=modT[:, KC:2 * KC, :], scalar=1.0,
                                                                  in1=nwT[:].unsqueeze(2).to_broadcast([128, KC, 4]),
                                                                  op0=ALU.add, op1=ALU.mult), r=[modT, nwT], w=[AT])
                    c.barrier()

                chk('A')
                for hf, (htok0, hn) in enumerate(halves):
                    with contextlib.ExitStack() as es:
                        hT = SB(es, [128, KC, hn], BF16, "hT")
                        with contextlib.ExitStack() as es2:
                            xt = [SB(es2, [128, D], F32) for _ in range(2)]
                            junk = SB(es2, [128, D], F32)
                            xn = [SB(es2, [128, D], BF16) for _ in range(2)]
                            ssq = [SB(es2, [128, 1], F32) for _ in range(2)]
                            rstd = [SB(es2, [128, 1], F32) for _ in range(2)]
                            ptr = [PS(es2, [128, 4, 128], BF16) for _ in range(2)]
                            for ti, (t0, parts) in enumerate(half_tiles(hf)):
                                x_ = xt[ti % 2]; xn_ = xn[ti % 2]; ss_ = ssq[ti % 2]; rs_ = rstd[ti % 2]
                                for (kind, idx, r0, n, p0) in parts:
                                    c.dma('sp', x_[p0:p0 + n, :], x_src(l, kind, idx, r0, n), w=[x_])
                                c.op('act', lambda e: e.activation(out=junk[:], in_=x_[:], func=AF.Square, accum_out=ss_[:]), r=[x_], w=[junk, ss_])
                                c.op('act', lambda e: e.activation(out=ss_[:], in_=ss_[:], func=AF.Sqrt, scale=1.0 / D, bias=epsc[:]), r=[ss_, epsc], w=[ss_])
                                c.op('dve', lambda e: e.reciprocal(out=rs_[:], in_=ss_[:]), r=[ss_], w=[rs_])
                                c.op('dve', lambda e: e.tensor_scalar(out=xn_[:], in0=x_[:], scalar1=rs_[:], scalar2=None, op0=ALU.mult), r=[x_, rs_], w=[xn_])
                                for kg in range(KC // 4 if KC >= 4 else 1):
                                    kcs = list(range(kg * 4, min(KC, kg * 4 + 4)))
                                    p_ = ptr[kg % 2]

                                    def tr(e):
                                        for j, kc in enumerate(kcs):
                                            ins = e.transpose(p_[:, j, :], xn_[:, kc * 128:(kc + 1) * 128], identB[:])
                                        return ins
                                    c.op('pe', tr, r=[xn_, identB], w=[p_])
                                    for j, kc in enumerate(kcs):
                                        for (kind, idx, r0, n, p0) in parts:
                                            row = idx if kind == 'p' else 2 + idx
                                            c.op('act', lambda e: e.activation(out=hT[:, kc, t0 + p0:t0 + p0 + n], in_=p_[:, j, p0:p0 + n], func=AF.Identity,
                                                                               scale=AT[:, kc, row:row + 1], bias=BT[:, kc, row:row + 1]),
                                                 r=[p_, AT, BT], w=[hT])
                            c.barrier()
                        chk('N')
                        with contextlib.ExitStack() as es2:
                            wstg = [SB(es2, [128, KC, 256], F32) for _ in range(2)]
                            wbf = [SB(es2, [128, KC, 256], BF16) for _ in range(2)]
                            Wtm = SB(es2, [128, KC, 624], BF16)
                            wctr = [0]

                            def load_w(src, c0, ncol, dst=None, dcol=0):
                                i = wctr[0] % 2; wctr[0] += 1
                                st = wstg[i]
                                c.dma('sp', st[:, :, 0:ncol], src[:, c0:c0 + ncol].rearrange("(k p) n -> p k n", p=128), w=[st])
                                if dst is None:
                                    wb = wbf[i]
                                    c.op('dve', lambda e: e.tensor_copy(out=wb[:, :, 0:ncol], in_=st[:, :, 0:ncol]), r=[st], w=[wb])
                                    return wb
                                c.op('dve', lambda e: e.tensor_copy(out=dst[:, :, dcol:dcol + ncol], in_=st[:, :, 0:ncol]), r=[st], w=[dst])
                                return dst
                            W = w_in[l]
                            load_w(W, OFF['kb'], 256, Wtm, 0); load_w(W, OFF['vb'], 256, Wtm, 256)
                            load_w(W, OFF['ba'], 32, Wtm, 512); load_w(W, OFF['ki'], 80, Wtm, 544)
                            with contextlib.ExitStack() as es3:
                                ptm_a = [PS(es3, [128, 512], F32) for _ in range(2)]; ptm_b = [PS(es3, [128, 112], F32) for _ in range(2)]
                                ptm = [[ptm_a[0], ptm_b[0]], [ptm_a[1], ptm_b[1]]]
                                tms = [SB(es3, [128, 624], F32) for _ in range(2)]
                                for ti, (t0, parts) in enumerate(half_tiles(hf)):
                                    pa, pb = ptm[ti % 2]; tm_ = tms[ti % 2]

                                    def mm(e):
                                        for kc in range(KC):
                                            e.matmul(pa[:], hT[:, kc, t0:t0 + 128], Wtm[:, kc, 0:512], start=(kc == 0), stop=(kc == KC - 1))
                                        for kc in range(KC):
                                            ins = e.matmul(pb[:], hT[:, kc, t0:t0 + 128], Wtm[:, kc, 512:624], start=(kc == 0), stop=(kc == KC - 1))
                                        return ins
                                    c.op('pe', mm, r=[hT, Wtm], w=[pa, pb])
                                    c.op('act', lambda e: e.copy(out=tm_[:, 0:512], in_=pa[:]), r=[pa], w=[tm_])
                                    c.op('dve', lambda e: e.tensor_copy(out=tm_[:, 512:624], in_=pb[:]), r=[pb], w=[tm_])
                                    c.dma('pool', tm_d[htok0 + t0:htok0 + t0 + 128, :], tm_[:], r=[tm_])
                                    for (kind, idx, r0, n, p0) in parts:
                                        ok, ov, oi = (nkp, nvp, nikp) if kind == 'p' else (nks, nvs, niks)
                                        c.dma('pool', ok[l, idx, r0:r0 + n, :], tm_[p0:p0 + n, 0:256], r=[tm_])
                                        c.dma('pool', ov[l, idx, r0:r0 + n, :], tm_[p0:p0 + n, 256:512], r=[tm_])
                                        c.dma('pool', oi[l, idx, r0:r0 + n, :], tm_[p0:p0 + n, 544:608], r=[tm_])
                                c.barrier()
                            chunks = []
                            for t0 in range(0, T, 512):
                                wdt = min(512, T - t0)
                                chunks.append((t0, wdt, 'p', hf, t0 == 0, t0 + wdt == T))
                            if hf == 1:
                                chunks.append((T, 64, 's', 0, True, True)); chunks.append((T + 64, 64, 's', 1, True, True))
                            groups = [('conv', OFF['qkv'], 48, None), ('silu', OFF['za'], 16, zaT_d), ('qb', OFF['qb'], 16, qbT_d),
                                      ('silu', OFF['zb'], 16, zbT_d), ('copy', OFF['qi'], 8, qiT_d),
                                      ('sig', OFF['gla'], KC, sgaT_d), ('sig', OFF['glb'], KC, sgbT_d)]
                            with contextlib.ExitStack() as es3:
                                pp = [PS(es3, [128, 512], F32) for _ in range(3)]
                                pq = [PS(es3, [128, 512], F32) for _ in range(3)]
                                xpad = [SB(es3, [128, 3 + 512], F32) for _ in range(3)]
                                cvas = [SB(es3, [128, 512], F32) for _ in range(7)]; rnA = [SB(es3, [128, 512], F32) for _ in range(2)]
                                sqbs = [SB(es3, [128, 512], BF16) for _ in range(2)]; rns = [SB(es3, [128, 512], F32) for _ in range(2)]
                                ob = [SB(es3, [128, 512], BF16) for _ in range(3)]
                                pctr = [0]; octr = [0]
                                goff = OFF['qkv']
                                items = [(b2, bi, ch) for b2 in range(0, 48, 2) for bi in range(2) for ch in chunks]
                                NI = len(items)
                                wbd = {}

                                def st0_pe(i):
                                    b2, bi, (c0, wd, kind, idx, first, last) = items[i]
                                    if bi == 0 and (c0, wd) == chunks[0][0:2]:
                                        wbd[b2] = load_w(W, goff + b2 * 128, 256)
                                    wb = wbd[b2]; p_ = pp[i % 3]

                                    def mm(e):
                                        for kc in range(KC):
                                            ins = e.matmul(p_[:, 0:wd], wb[:, kc, bi * 128:(bi + 1) * 128], hT[:, kc, c0:c0 + wd], start=(kc == 0), stop=(kc == KC - 1))
                                        return ins
                                    c.op('pe', mm, r=[wb, hT], w=[p_])

                                def s1_copy(i):
                                    b2, bi, (c0, wd, kind, idx, first, last) = items[i]
                                    blk = b2 + bi
                                    p_ = pp[i % 3]; xq = xpad[i % 3]
                                    if first:
                                        if kind == 'p':
                                            c.op('act', lambda e: e.activation(out=xq[:, 0:3], in_=zero3[:], func=AF.Copy), r=[zero3], w=[xq])
                                        else:
                                            sct = scT0 if idx == 0 else scT1
                                            c.op('act', lambda e: e.copy(out=xq[:, 0:3], in_=sct[:, blk, :]), r=[sct], w=[xq])
                                    else:
                                        xprev = xpad[(i - 1) % 3]; pw_ = items[i - 1][2][1]
                                        c.op('act', lambda e: e.copy(out=xq[:, 0:3], in_=xprev[:, pw_:pw_ + 3]), r=[xprev], w=[xq])
                                    c.op('act', lambda e: e.copy(out=xq[:, 3:3 + wd], in_=p_[:, 0:wd]), r=[p_], w=[xq])
                                    if last:
                                        oc = ncp if kind == 'p' else ncs
                                        c.dma('pool', oc[l, idx, :, blk * 128:(blk + 1) * 128].rearrange("j p -> p j"), xq[:, wd:wd + 3], r=[xq], slow=True)

                                def s2_conv(i):
                                    b2, bi, (c0, wd, kind, idx, first, last) = items[i]
                                    blk = b2 + bi
                                    xq = xpad[i % 3]; cva = cvas[i % 7]
                                    c.op('dve', lambda e: e.tensor_scalar(out=cva[:, 0:wd], in0=xq[:, 0:wd], scalar1=wcT[:, blk, 0:1], scalar2=None, op0=ALU.mult), r=[xq, wcT], w=[cva])
                                    for j in range(1, 4):
                                        c.op('dve', lambda e: e.scalar_tensor_tensor(out=cva[:, 0:wd], in0=xq[:, j:j + wd], scalar=wcT[:, blk, j:j + 1], in1=cva[:, 0:wd],
                                                                                      op0=ALU.mult, op1=ALU.add), r=[xq, wcT, cva], w=[cva])

                                def s3_sig(i):
                                    b2, bi, (c0, wd, kind, idx, first, last) = items[i]
                                    cva = cvas[i % 7]; rn = rnA[i % 2]
                                    c.op('act', lambda e: e.activation(out=rn[:, 0:wd], in_=cva[:, 0:wd], func=AF.Exp, scale=-1.0), r=[cva], w=[rn])
                                    c.op('act', lambda e: e.activation(out=rn[:, 0:wd], in_=rn[:, 0:wd], func=AF.Ln, bias=onec[:]), r=[rn, onec], w=[rn])
                                    c.op('act', lambda e: e.activation(out=rn[:, 0:wd], in_=rn[:, 0:wd], func=AF.Exp, scale=-1.0), r=[rn], w=[rn])

                                def s4_silu(i):
                                    b2, bi, (c0, wd, kind, idx, first, last) = items[i]
                                    blk = b2 + bi
                                    cva = cvas[i % 7]; rn = rnA[i % 2]; o_ = ob[i % 3]
                                    tsl = slice(htok0 + c0, htok0 + c0 + wd)
                                    if blk >= 32:
                                        c.op('dve', lambda e: e.tensor_tensor(out=o_[:, 0:wd], in0=cva[:, 0:wd], in1=rn[:, 0:wd], op=ALU.mult), r=[cva, rn], w=[o_])
                                        c.dma('pool', vT_d[(blk - 32) * 128:(blk - 31) * 128, tsl], o_[:, 0:wd], r=[o_])
                                    else:
                                        c.op('dve', lambda e: e.tensor_tensor(out=cva[:, 0:wd], in0=cva[:, 0:wd], in1=rn[:, 0:wd], op=ALU.mult), r=[cva, rn], w=[cva])

                                def s5_sq(i):
                                    b2, bi, (c0, wd, kind, idx, first, last) = items[i]
                                    if b2 + bi >= 32:
                                        return
                                    cva = cvas[i % 7]; sqb = sqbs[i % 2]
                                    c.op('act', lambda e: e.activation(out=sqb[:, 0:wd], in_=cva[:, 0:wd], func=AF.Square), r=[cva], w=[sqb])

                                def s6_ones(i):
                                    b2, bi, (c0, wd, kind, idx, first, last) = items[i]
                                    if b2 + bi >= 32:
                                        return
                                    sqb = sqbs[i % 2]; q_ = pq[i % 3]
                                    c.op('pe', lambda e: e.matmul(q_[:, 0:wd], onesB[:], sqb[:, 0:wd], start=True, stop=True), r=[onesB, sqb], w=[q_])

                                def s7_rs(i):
                                    b2, bi, (c0, wd, kind, idx, first, last) = items[i]
                                    if b2 + bi >= 32:
                                        return
                                    rn = rns[i % 2]; q_ = pq[i % 3]
                                    c.op('act', lambda e: e.activation(out=rn[:, 0:wd], in_=q_[:, 0:wd], func=AF.Ln, bias=epsc[:]), r=[q_, epsc], w=[rn])
                                    c.op('act', lambda e: e.activation(out=rn[:, 0:wd], in_=rn[:, 0:wd], func=AF.Exp, scale=-0.5), r=[rn], w=[rn])

                                def s8_out(i):
                                    b2, bi, (c0, wd, kind, idx, first, last) = items[i]
                                    blk = b2 + bi
                                    if blk >= 32:
                                        return
                                    cva = cvas[i % 7]; rn = rns[i % 2]; o_ = ob[i % 3]
                                    tsl = slice(htok0 + c0, htok0 + c0 + wd)
                                    qs = float(HD) ** -0.5 if blk < 16 else 1.0
                                    c.op('dve', lambda e: e.scalar_tensor_tensor(out=o_[:, 0:wd], in0=cva[:, 0:wd], scalar=qs, in1=rn[:, 0:wd], op0=ALU.mult, op1=ALU.mult),
                                         r=[cva, rn], w=[o_])
                                    dst = qT_d if blk < 16 else kT_d; drow = (blk % 16) * 128
                                    c.dma('pool', dst[drow:drow + 128, tsl], o_[:, 0:wd], r=[o_])
                                stages = [st0_pe, s1_copy, s2_conv, s3_sig, s4_silu, s5_sq, s6_ones, s7_rs, s8_out]
                                for t in range(NI + len(stages) - 1):
                                    for k, fn in enumerate(stages):
                                        if 0 <= t - k < NI:
                                            fn(t - k)
                                for (gk, goff, nblk, dst) in groups[1:]:
                                    for b2 in range(0, nblk, 2):
                                        wb = load_w(W, goff + b2 * 128, 256)
                                        for bi in range(2):
                                            blk = b2 + bi
                                            for (c0, wd, kind, idx, first, last) in chunks:
                                                p_ = pp[pctr[0] % 3]; pctr[0] += 1
                                                o_ = ob[octr[0] % 3]; octr[0] += 1

                                                def mm(e):
                                                    for kc in range(KC):
                                                        ins = e.matmul(p_[:, 0:wd], wb[:, kc, bi * 128:(bi + 1) * 128], hT[:, kc, c0:c0 + wd], start=(kc == 0), stop=(kc == KC - 1))
                                                    return ins
                                                c.op('pe', mm, r=[wb, hT], w=[p_])
                                                tsl = slice(htok0 + c0, htok0 + c0 + wd)
                                                if gk == 'silu':
                                                    c.op('act', lambda e: e.activation(out=o_[:, 0:wd], in_=p_[:, 0:wd], func=AF.Silu), r=[p_], w=[o_])
                                                elif gk == 'sig':
                                                    c.op('act', lambda e: e.activation(out=o_[:, 0:wd], in_=p_[:, 0:wd], func=AF.Sigmoid), r=[p_], w=[o_])
                                                elif gk == 'qb':
                                                    c.op('act', lambda e: e.activation(out=o_[:, 0:wd], in_=p_[:, 0:wd], func=AF.Copy, scale=float(HD) ** -0.5), r=[p_], w=[o_])
                                                else:
                                                    c.op('act', lambda e: e.copy(out=o_[:, 0:wd], in_=p_[:, 0:wd]), r=[p_], w=[o_])
                                                c.dma('pool', dst[blk * 128:(blk + 1) * 128, tsl], o_[:, 0:wd], r=[o_])
                        c.barrier()

                chk('P')
                with contextlib.ExitStack() as es:
                    S = SB(es, [128, GH, 128], F32, "S"); Sb = SB(es, [128, GH, 128], BF16, "Sb")
                    Ibc = SB(es, [128, 4, 128], F32, "Ibc")
                    for hh in range(4):
                        c.op('pool', lambda e: e.tensor_copy(out=Ibc[:, hh, :], in_=identF[:]), r=[identF], w=[Ibc])
                    qT = SB(es, [128, GH, 128], BF16); kT = SB(es, [128, GH, 128], BF16); vT = SB(es, [128, GH, 128], BF16); zaT = SB(es, [128, GH, 128], BF16)
                    bgr = SB(es, [128, 32], F32); beta = SB(es, [128, 16], F32); g_ = SB(es, [128, 16], F32)
                    gc = SB(es, [128, 16], F32); gcT = SB(es, [16, 128], F32); egc = SB(es, [128, 16], F32)
                    kdsc = SB(es, [128, 16], F32); gle = SB(es, [128, 16], F32); bege = SB(es, [128, 16], F32)
                    MB = SB(es, [128, 16, 128], F32); GB = SB(es, [128, 16, 128], F32)
                    Xs = [SB(es, [128, 4, 128], F32) for _ in range(2)] * 2
                    Dm = [SB(es, [128, 4, 128], F32) for _ in range(2)] * 2
                    DTm = [SB(es, [128, 4, 128], F32) for _ in range(2)] * 2
                    egr = [SB(es, [128, 4, 128], F32) for _ in range(2)] * 2
                    Pa = [[SB(es, [128, 4, 128], F32) for _ in range(2)] for _ in range(4)]
                    PTa = [[SB(es, [128, 4, 128], F32) for _ in range(2)] for _ in range(4)]
                    Y = [SB(es, [128, 4, 128], F32) for _ in range(4)]
                    YB = [SB(es, [128, 4, 128], BF16) for _ in range(4)]
                    vb = [SB(es, [128, 4, 128], BF16) for _ in range(4)]
                    kbg = [SB(es, [128, 4, 128], BF16) for _ in range(4)]
                    kdec = [SB(es, [128, 4, 128], BF16) for _ in range(4)]
                    wTb = [SB(es, [128, 4, 128], BF16) for _ in range(4)]
                    us = [SB(es, [128, 4, 128], F32) for _ in range(4)]
                    QKm = [SB(es, [128, 4, 128], BF16) for _ in range(4)]
                    qdT = [SB(es, [128, 4, 128], BF16) for _ in range(4)]
                    vn = [SB(es, [128, 4, 128], BF16) for _ in range(4)]
                    oTs = [SB(es, [128, 4, 128], F32) for _ in range(4)]
                    osq = [SB(es, [128, 4, 128], BF16) for _ in range(4)]
                    rst = [SB(es, [128, 4, 128], F32) for _ in range(4)]
                    ozs = [SB(es, [128, 4, 128], BF16) for _ in range(4)]
                    psA = [PS(es, [128, 4, 128], F32) for _ in range(5)]
                    psB = [PS(es, [128, 4, 128], BF16) for _ in range(2)]
                    psS = PS(es, [128, 128], F32)
                    pctr = [0]

                    def nps():
                        p = psA[pctr[0] % 5]; pctr[0] += 1
                        return p
                    for (kind, idx, tok0, slen) in seqs:
                        n = 128 if kind == 'p' else 64
                        nst = 6 if n == 128 else 5
                        if kind == 'p':
                            c.op('pool', lambda e: e.memset(S[:], 0.0), w=[S])
                        else:
                            c.dma('sp', S[:], sg[l, idx].rearrange("h k v -> k h v"), w=[S])
                        c.op('act', lambda e: e.copy(out=Sb[:], in_=S[:]), r=[S], w=[Sb])
                        for t in range(slen // n):
                            ts_ = slice(tok0 + t * n, tok0 + (t + 1) * n)
                            for (dstb, srcd) in ((qT, qT_d), (kT, kT_d), (vT, vT_d), (zaT, zaT_d)):
                                c.dma('sp', dstb[:, :, 0:n], srcd[:, ts_].rearrange("(h p) t -> p h t", p=128), w=[dstb])
                            c.dma('sp', bgr[0:n, :], tm_d[ts_, 512:544], w=[bgr])
                            c.op('act', lambda e: e.activation(out=beta[0:n], in_=bgr[0:n, 0:16], func=AF.Sigmoid), r=[bgr], w=[beta])
                            c.op('dve', lambda e: e.tensor_tensor(out=g_[0:n], in0=bgr[0:n, 16:32], in1=dtb[0:n], op=ALU.add), r=[bgr, dtb], w=[g_])
                            c.op('act', lambda e: e.activation(out=gc[0:n], in_=g_[0:n], func=AF.Abs), r=[g_], w=[gc])
                            c.op('act', lambda e: e.activation(out=gc[0:n], in_=gc[0:n], func=AF.Exp, scale=-1.0), r=[gc], w=[gc])
                            c.op('act', lambda e: e.activation(out=gc[0:n], in_=gc[0:n], func=AF.Ln, bias=onec[0:n]), r=[gc, onec], w=[gc])
                            c.op('dve', lambda e: e.scalar_tensor_tensor(out=g_[0:n], in0=g_[0:n], scalar=0.0, in1=gc[0:n], op0=ALU.max, op1=ALU.add), r=[g_, gc], w=[g_])
                            c.op('dve', lambda e: e.tensor_tensor(out=g_[0:n], in0=g_[0:n], in1=nexpA[0:n], op=ALU.mult), r=[g_, nexpA], w=[g_])
                            c.op('pe', lambda e: e.matmul(psS[0:n, 0:16], Lm[0:n, 0:n], g_[0:n, :], start=True, stop=True), r=[Lm, g_], w=[psS])
                            c.op('dve', lambda e: e.tensor_copy(out=gc[0:n], in_=psS[0:n, 0:16]), r=[psS], w=[gc])
                            c.op('pe', lambda e: e.matmul(psS[0:16, 0:n], g_[0:n, :], Lm[0:n, 0:n], start=True, stop=True), r=[Lm, g_], w=[psS])
                            c.op('dve', lambda e: e.tensor_copy(out=gcT[:, 0:n], in_=psS[0:16, 0:n]), r=[psS], w=[gcT])
                            c.op('pe', lambda e: e.matmul(psS[:, 0:16], onesF[0:n, :], g_[0:n, :], start=True, stop=True), r=[onesF, g_], w=[psS])
                            c.op('act', lambda e: e.activation(out=gle[:], in_=psS[:, 0:16], func=AF.Exp), r=[psS], w=[gle])
                            c.op('dve', lambda e: e.tensor_tensor(out=kdsc[0:n], in0=psS[0:n, 0:16], in1=gc[0:n], op=ALU.subtract), r=[psS, gc], w=[kdsc])
                            c.op('act', lambda e: e.activation(out=kdsc[0:n], in_=kdsc[0:n], func=AF.Exp), r=[kdsc], w=[kdsc])
                            c.op('act', lambda e: e.activation(out=egc[0:n], in_=gc[0:n], func=AF.Exp), r=[gc], w=[egc])
                            c.op('dve', lambda e: e.tensor_tensor(out=bege[0:n], in0=egc[0:n], in1=beta[0:n], op=ALU.mult), r=[egc, beta], w=[bege])
                            c.op('dve', lambda e: e.tensor_copy(out=GB[0:n], in_=g_[0:n, :].unsqueeze(2).to_broadcast([n, 16, 128])), r=[g_], w=[GB])
                            c.op('dve', lambda e: e.tensor_tensor(out=MB[0:n, :, 0:n], in0=MLs[0:n, 0:n].unsqueeze(1).to_broadcast([n, 16, n]),
                                                                   in1=beta[0:n, :].unsqueeze(2).to_broadcast([n, 16, n]), op=ALU.mult), r=[MLs, beta], w=[MB])
                            chk('G1')
                            for hg in range(4):
                                hs = slice(hg * 4, hg * 4 + 4)
                                pg = nps()

                                def mm(e):
                                    for hh in range(4):
                                        ins = e.matmul(pg[:, hh, 0:n], GB[0:n, hg * 4 + hh, :], Lm[0:n, 0:n], start=True, stop=True)
                                    return ins
                                c.op('pe', mm, r=[GB, Lm], w=[pg])
                                chk('G1a')
                                c.op('act', lambda e: e.activation(out=egr[hg][:, :, 0:n], in_=pg[:, :, 0:n], func=AF.Exp), r=[pg], w=[egr[hg]])
                                chk('G1b')
                                c.op('dve', lambda e: e.tensor_tensor(out=Xs[hg][0:n, :, 0:n], in0=pg[0:n, :, 0:n], in1=gc[0:n, hs].unsqueeze(2).to_broadcast([n, 4, n]), op=ALU.subtract),
                                     r=[pg, gc], w=[Xs[hg]])
                                chk('Ga')
                                c.op('dve', lambda e: e.tensor_tensor(out=Dm[hg][0:n, :, 0:n], in0=Xs[hg][0:n, :, 0:n], in1=UP[0:n, 0:n].unsqueeze(1).to_broadcast([n, 4, n]), op=ALU.add),
                                     r=[Xs[hg], UP], w=[Dm[hg]])
                                chk('H1')
                                c.op('act', lambda e: e.activation(out=Dm[hg][0:n, :, 0:n], in_=Dm[hg][0:n, :, 0:n], func=AF.Exp, scale=-1.0), r=[Dm[hg]], w=[Dm[hg]])
                                chk('H2')
                                c.op('pool', lambda e: e.tensor_tensor(out=DTm[hg][0:n, :, 0:n], in0=Xs[hg][0:n, :, 0:n], in1=LO[0:n, 0:n].unsqueeze(1).to_broadcast([n, 4, n]), op=ALU.subtract),
                                     r=[Xs[hg], LO], w=[DTm[hg]])
                                c.op('act', lambda e: e.activation(out=DTm[hg][0:n, :, 0:n], in_=DTm[hg][0:n, :, 0:n], func=AF.Exp), r=[DTm[hg]], w=[DTm[hg]])
                                chk('H3')
                                c.op('dve', lambda e: e.tensor_tensor(out=Dm[hg][0:n, :, 0:n], in0=Dm[hg][0:n, :, 0:n], in1=MB[0:n, hs, 0:n], op=ALU.mult), r=[Dm[hg], MB], w=[Dm[hg]])
                                chk('Gb')
                                pk = nps()

                                def mm(e):
                                    for hh in range(4):
                                        h = hg * 4 + hh
                                        ins = e.matmul(pk[0:n, hh, 0:n], kT[:, h, 0:n], kT[:, h, 0:n], start=True, stop=True)
                                    return ins
                                c.op('pe', mm, r=[kT], w=[pk])
                                A_ = PTa[hg][0]; B_ = Pa[hg][0]
                                c.op('dve', lambda e: e.tensor_tensor(out=A_[0:n, :, 0:n], in0=pk[0:n, :, 0:n], in1=Dm[hg][0:n, :, 0:n], op=ALU.mult), r=[pk, Dm[hg]], w=[A_])
                                chk('Gc')
                                ptk = psB[0]

                                def mm(e):
                                    for hh in range(4):
                                        ins = e.transpose(ptk[0:n, hh, :], kT[:, hg * 4 + hh, 0:n], identB[:])
                                    return ins
                                c.op('pe', mm, r=[kT, identB], w=[ptk])
                                c.op('dve', lambda e: e.tensor_tensor(out=kbg[hg][0:n], in0=ptk[0:n], in1=bege[0:n, hs].unsqueeze(2).to_broadcast([n, 4, 128]), op=ALU.mult), r=[ptk, bege], w=[kbg[hg]])
                                c.op('dve', lambda e: e.tensor_tensor(out=kdec[hg][0:n], in0=ptk[0:n], in1=kdsc[0:n, hs].unsqueeze(2).to_broadcast([n, 4, 128]), op=ALU.mult), r=[ptk, kdsc], w=[kdec[hg]])
                                ptv = psB[1]

                                def mm(e):
                                    for hh in range(4):
                                        ins = e.transpose(ptv[0:n, hh, :], vT[:, hg * 4 + hh, 0:n], identB[:])
                                    return ins
                                c.op('pe', mm, r=[vT, identB], w=[ptv])
                                c.op('dve', lambda e: e.tensor_tensor(out=vb[hg][0:n], in0=ptv[0:n], in1=beta[0:n, hs].unsqueeze(2).to_broadcast([n, 4, 128]), op=ALU.mult), r=[ptv, beta], w=[vb[hg]])
                                chk('Gd')
                                pqk = nps()

                                def mm(e):
                                    for hh in range(4):
                                        h = hg * 4 + hh
                                        ins = e.matmul(pqk[0:n, hh, 0:n], kT[:, h, 0:n], qT[:, h, 0:n], start=True, stop=True)
                                    return ins
                                c.op('pe', mm, r=[kT, qT], w=[pqk])
                                c.op('dve', lambda e: e.tensor_tensor(out=QKm[hg][0:n, :, 0:n], in0=pqk[0:n, :, 0:n], in1=DTm[hg][0:n, :, 0:n], op=ALU.mult), r=[pqk, DTm[hg]], w=[QKm[hg]])
                                c.op('pool', lambda e: e.tensor_tensor(out=qdT[hg][:, :, 0:n], in0=qT[:, hs, 0:n], in1=egr[hg][:, :, 0:n], op=ALU.mult), r=[qT, egr[hg]], w=[qdT[hg]])
                                pb_ = nps()

                                def mm(e):
                                    for hh in range(4):
                                        ins = e.transpose(pb_[0:n, hh, 0:n], A_[0:n, hh, 0:n], identF[0:n, 0:n])
                                    return ins
                                c.op('pe', mm, r=[A_, identF], w=[pb_])
                                c.op('act', lambda e: e.copy(out=B_[0:n, :, 0:n], in_=pb_[0:n, :, 0:n]), r=[pb_], w=[B_])
                                c.op('dve', lambda e: e.tensor_tensor(out=Y[hg][0:n, :, 0:n], in0=Ibc[0:n, :, 0:n], in1=pb_[0:n, :, 0:n], op=ALU.subtract), r=[pb_, Ibc], w=[Y[hg]])
                            chk('G2')
                            cur = 0
                            for m in range(nst):
                                lastm = (m == nst - 1)
                                for hg in range(4):
                                    P_ = Pa[hg][cur]; PT_ = PTa[hg][cur]; PTn = PTa[hg][1 - cur]
                                    p2 = nps()

                                    def mm(e):
                                        for hh in range(4):
                                            ins = e.matmul(p2[0:n, hh, 0:n], P_[0:n, hh, 0:n], PT_[0:n, hh, 0:n], start=True, stop=True)
                                        return ins
                                    c.op('pe', mm, r=[PT_, P_], w=[p2])
                                    c.op('dve', lambda e: e.tensor_copy(out=PTn[0:n, :, 0:n], in_=p2[0:n, :, 0:n]), r=[p2], w=[PTn])
                                if not lastm:
                                    for hg in range(4):
                                        P_ = Pa[hg][cur]; PT_ = PTa[hg][cur]; Pn = Pa[hg][1 - cur]
                                        p1 = nps()

                                        def mm(e):
                                            for hh in range(4):
                                                ins = e.matmul(p1[0:n, hh, 0:n], PT_[0:n, hh, 0:n], P_[0:n, hh, 0:n], start=True, stop=True)
                                            return ins
                                        c.op('pe', mm, r=[PT_, P_], w=[p1])
                                        c.op('act', lambda e: e.copy(out=Pn[0:n, :, 0:n], in_=p1[0:n, :, 0:n]), r=[p1], w=[Pn])
                                for hg in range(4):
                                    PTn = PTa[hg][1 - cur]
                                    p3 = nps()

                                    def mm(e):
                                        for hh in range(4):
                                            ins = e.matmul(p3[0:n, hh, 0:n], PTn[0:n, hh, 0:n], Y[hg][0:n, hh, 0:n], start=True, stop=True)
                                        return ins
                                    c.op('pe', mm, r=[PTn, Y[hg]], w=[p3])
                                    c.op('dve', lambda e: e.tensor_tensor(out=Y[hg][0:n, :, 0:n], in0=Y[hg][0:n, :, 0:n], in1=p3[0:n, :, 0:n], op=ALU.add), r=[p3, Y[hg]], w=[Y[hg]])
                                cur = 1 - cur
                            chk('G3')
                            for hg in range(4):
                                hs = slice(hg * 4, hg * 4 + 4)
                                c.op('act', lambda e: e.copy(out=YB[hg][0:n, :, 0:n], in_=Y[hg][0:n, :, 0:n]), r=[Y[hg]], w=[YB[hg]])
                                pu = nps()

                                def mm(e):
                                    for hh in range(4):
                                        ins = e.matmul(pu[0:n, hh, :], YB[hg][0:n, hh, 0:n], vb[hg][0:n, hh, :], start=True, stop=True)
                                    return ins
                                c.op('pe', mm, r=[YB[hg], vb[hg]], w=[pu])
                                c.op('act', lambda e: e.copy(out=us[hg][0:n], in_=pu[0:n]), r=[pu], w=[us[hg]])
                                pw_ = nps()

                                def mm(e):
                                    for hh in range(4):
                                        ins = e.matmul(pw_[:, hh, 0:n], kbg[hg][0:n, hh, :], YB[hg][0:n, hh, 0:n], start=True, stop=True)
                                    return ins
                                c.op('pe', mm, r=[YB[hg], kbg[hg]], w=[pw_])
                                c.op('dve', lambda e: e.tensor_copy(out=wTb[hg][:, :, 0:n], in_=pw_[:, :, 0:n]), r=[pw_], w=[wTb[hg]])
                            chk('G4')
                            for hg in range(4):
                                pv_ = nps()

                                def mm(e):
                                    for hh in range(4):
                                        ins = e.matmul(pv_[0:n, hh, :], wTb[hg][:, hh, 0:n], Sb[:, hg * 4 + hh, :], start=True, stop=True)
                                    return ins
                                c.op('pe', mm, r=[wTb[hg], Sb], w=[pv_])
                                c.op('dve', lambda e: e.tensor_tensor(out=vn[hg][0:n], in0=us[hg][0:n], in1=pv_[0:n], op=ALU.subtract), r=[us[hg], pv_], w=[vn[hg]])
                            for hg in range(4):
                                po = nps()

                                def mm(e):
                                    for hh in range(4):
                                        e.matmul(po[:, hh, 0:n], Sb[:, hg * 4 + hh, :], qdT[hg][:, hh, 0:n], start=True, stop=False)
                                        ins = e.matmul(po[:, hh, 0:n], vn[hg][0:n, hh, :], QKm[hg][0:n, hh, 0:n], start=False, stop=True)
                                    return ins
                                c.op('pe', mm, r=[Sb, qdT[hg], vn[hg], QKm[hg]], w=[po])
                                o_ = oTs[hg]; q_ = osq[hg]
                                c.op('act', lambda e: e.copy(out=o_[:, :, 0:n], in_=po[:, :, 0:n]), r=[po], w=[o_])
                                c.op('act', lambda e: e.activation(out=q_[:, :, 0:n], in_=po[:, :, 0:n], func=AF.Square), r=[po], w=[q_])
                            for hg in range(4):
                                hs = slice(hg * 4, hg * 4 + 4)
                                pS = nps()

                                def mm(e):
                                    for hh in range(4):
                                        ins = e.matmul(pS[:, hh, :], kdec[hg][0:n, hh, :], vn[hg][0:n, hh, :], start=True, stop=True)
                                    return ins
                                c.op('pe', mm, r=[kdec[hg], vn[hg]], w=[pS])
                                c.op('pool', lambda e: e.tensor_tensor(out=S[:, hs, :], in0=S[:, hs, :], in1=gle[:, hs].unsqueeze(2).to_broadcast([128, 4, 128]), op=ALU.mult), r=[S, gle, Sb], w=[S])
                                c.op('dve', lambda e: e.tensor_tensor(out=S[:, hs, :], in0=S[:, hs, :], in1=pS[:], op=ALU.add), r=[S, pS], w=[S])
                            for hg in range(4):
                                hs = slice(hg * 4, hg * 4 + 4)
                                o_ = oTs[hg]; q_ = osq[hg]; r_ = rst[hg]; z_ = ozs[hg]
                                pss = nps()
                                if n == 128:
                                    c.op('pe', lambda e: e.matmul(pss[:].rearrange("p a b -> p (a b)"), onesB[:], q_[:].rearrange("p a b -> p (a b)"), start=True, stop=True), r=[onesB, q_], w=[pss])
                                else:
                                    def mm(e):
                                        for hh in range(4):
                                            ins = e.matmul(pss[:, hh, 0:n], onesB[:], q_[:, hh, 0:n], start=True, stop=True)
                                        return ins
                                    c.op('pe', mm, r=[onesB, q_], w=[pss])
                                c.op('act', lambda e: e.activation(out=r_[:, :, 0:n], in_=pss[:, :, 0:n], func=AF.Ln, scale=1.0 / 128, bias=epsc[:]), r=[pss, epsc], w=[r_])
                                c.op('act', lambda e: e.activation(out=r_[:, :, 0:n], in_=r_[:, :, 0:n], func=AF.Exp, scale=-0.5), r=[r_], w=[r_])
                                c.op('dve', lambda e: e.tensor_tensor(out=o_[:, :, 0:n], in0=o_[:, :, 0:n], in1=r_[:, :, 0:n], op=ALU.mult), r=[o_, r_], w=[o_])
                                c.op('dve', lambda e: e.scalar_tensor_tensor(out=z_[:, :, 0:n], in0=o_[:, :, 0:n], scalar=gnw[:, 0:1], in1=zaT[:, hs, 0:n], op0=ALU.mult, op1=ALU.mult),
                                     r=[o_, gnw, zaT], w=[z_])
                                c.dma('pool', ozT_d[hg * 512:(hg + 1) * 512, ts_].rearrange("(h p) t -> p h t", p=128), z_[:, :, 0:n], r=[z_])
                            c.op('act', lambda e: e.copy(out=Sb[:], in_=S[:]), r=[S], w=[Sb])
                            chk('G5')
                        og = ngp if kind == 'p' else ngs
                        c.dma('pool', og[l, idx].rearrange("h k v -> k h v"), S[:], r=[S])
                    c.barrier()

                chk('G')
                with contextlib.ExitStack() as es:
                    KT = SB(es, [128, 2, NKB * 128], BF16, "KT"); Vt = SB(es, [128, NKB, 256], BF16, "Vt"); KIT = SB(es, [128, NKB * 128], BF16, "KIT")
                    kst = [SB(es, [128, 576], F32) for _ in range(2)]
                    kbf = [SB(es, [128, 384], BF16) for _ in range(2)]
                    qiT = SB(es, [128, 8, 128], BF16); qbT = SB(es, [128, 16, 128], BF16); zbT = SB(es, [128, 16, 128], BF16)
                    wi = SB(es, [128, 16], F32); absw = SB(es, [128, 16], F32); sgn = SB(es, [128, 16], F32)
                    scb = SB(es, [128, LMAX], F32, "scb"); mb = SB(es, [128, LMAX], BF16, "mb"); junk = SB(es, [128, LMAX], F32, "junkb")
                    rr = [SB(es, [128, 512], F32) for _ in range(2)]
                    hi = SB(es, [128, 1], F32); lo = SB(es, [128, 1], F32); lo2 = SB(es, [128, 1], F32); mid = SB(es, [128, 1], F32)
                    cnt = SB(es, [128, 1], F32); tmp1 = SB(es, [128, 1], F32); wv = SB(es, [128, NIT], F32)
                    pbuf = [SB(es, [128, LMAX], BF16) for _ in range(2)]
                    rs = [SB(es, [128, 16], F32) for _ in range(2)]
                    rinv = [SB(es, [128, 1], F32) for _ in range(2)]
                    pT = [SB(es, [128, NKB, 128], BF16) for _ in range(2)]
                    obs = [SB(es, [128, 4, 128], BF16) for _ in range(2)]
                    psd = [PS(es, [128, 512], F32) for _ in range(4)]
                    pst = [PS(es, [128, 4, 128], BF16) for _ in range(2)]
                    pso = [PS(es, [128, 4, 128], F32) for _ in range(2)]
                    dctr = [0]; tctr = [0]; kctr = [0]

                    def add_keys(src_k, src_v, src_ki, nrows, kbi):
                        st = kst[kctr[0] % 2]; bf = kbf[kctr[0] % 2]; kctr[0] += 1
                        c.dma('sp', st[0:nrows, 0:256], src_k, w=[st])
                        c.dma('sp', st[0:nrows, 256:512], src_v, w=[st])
                        c.dma('sp', st[0:nrows, 512:576], src_ki, w=[st])
                        c.op('pool', lambda e: e.tensor_copy(out=bf[0:nrows, 0:256], in_=st[0:nrows, 0:256]), r=[st], w=[bf])
                        c.op('pool', lambda e: e.tensor_copy(out=bf[0:nrows, 256:320], in_=st[0:nrows, 512:576]), r=[st], w=[bf])
                        c.op('pool', lambda e: e.tensor_copy(out=bf[0:nrows, 320:384], in_=st[0:nrows, 512:576]), r=[st], w=[bf])
                        c.op('pool', lambda e: e.tensor_copy(out=Vt[0:nrows, kbi, :], in_=st[0:nrows, 256:512]), r=[st], w=[Vt])
                        p_ = pst[tctr[0] % 2]; tctr[0] += 1

                        def mm(e):
                            for j in range(3):
                                ins = e.transpose(p_[:, j, 0:nrows], bf[0:nrows, j * 128:(j + 1) * 128], identB[0:nrows, 0:nrows])
                            return ins
                        c.op('pe', mm, r=[bf, identB], w=[p_])
                        c.op('act', lambda e: e.copy(out=KT[:, :, kbi * 128:kbi * 128 + nrows], in_=p_[:, 0:2, 0:nrows]), r=[p_], w=[KT])
                        c.op('act', lambda e: e.copy(out=KIT[:, kbi * 128:kbi * 128 + nrows], in_=p_[:, 2, 0:nrows]), r=[p_], w=[KIT])

                    for (kind, idx, tok0, slen) in seqs:
                        n = 128 if kind == 'p' else 64
                        topk = topk_p if kind == 'p' else topk_s
                        if kind == 's':
                            for kb_ in range(P // 128):
                                add_keys(ck[l, idx, kb_ * 128:(kb_ + 1) * 128, :], cv[l, idx, kb_ * 128:(kb_ + 1) * 128, :], cik[l, idx, kb_ * 128:(kb_ + 1) * 128, :], 128, kb_)
                        for t in range(slen // n):
                            ts_ = slice(tok0 + t * n, tok0 + (t + 1) * n)
                            pos0 = t * 128 if kind == 'p' else P
                            Lk = pos0 + n
                            add_keys(tm_d[ts_, 0:256], tm_d[ts_, 256:512], tm_d[ts_, 544:608], n, pos0 // 128)
                            c.dma('sp', qiT[:, :, 0:n], qiT_d[:, ts_].rearrange("(j p) t -> p j t", p=128), w=[qiT])
                            c.dma('sp', qbT[:, :, 0:n], qbT_d[:, ts_].rearrange("(h p) t -> p h t", p=128), w=[qbT])
                            c.dma('sp', zbT[:, :, 0:n], zbT_d[:, ts_].rearrange("(h p) t -> p h t", p=128), w=[zbT])
                            c.dma('sp', wi[0:n, :], tm_d[ts_, 608:624], w=[wi])
                            c.op('act', lambda e: e.activation(out=absw[0:n], in_=wi[0:n], func=AF.Abs), r=[wi], w=[absw])
                            c.op('act', lambda e: e.activation(out=sgn[0:n], in_=wi[0:n], func=AF.Sign), r=[wi], w=[sgn])
                            kchunks = [(c0, min(512, Lk - c0)) for c0 in range(0, Lk, 512)]
                            for (c0, wd) in kchunks:
                                for h in range(16):
                                    hb = 64 * (h % 2)
                                    p_ = psd[dctr[0] % 4]; r_ = rr[dctr[0] % 2]; dctr[0] += 1
                                    c.op('pe', lambda e: e.matmul(p_[0:n, 0:wd], qiT[hb:hb + 64, h // 2, 0:n], KIT[hb:hb + 64, c0:c0 + wd], start=True, stop=True), r=[qiT, KIT], w=[p_])
                                    c.op('act', lambda e: e.activation(out=r_[0:n, 0:wd], in_=p_[0:n, 0:wd], func=AF.Relu, scale=absw[0:n, h:h + 1]), r=[p_, absw], w=[r_])
                                    if h == 0:
                                        c.op('dve', lambda e: e.tensor_scalar(out=scb[0:n, c0:c0 + wd], in0=r_[0:n, 0:wd], scalar1=sgn[0:n, 0:1], scalar2=None, op0=ALU.mult), r=[r_, sgn], w=[scb])
                                    else:
                                        c.op('dve', lambda e: e.scalar_tensor_tensor(out=scb[0:n, c0:c0 + wd], in0=r_[0:n, 0:wd], scalar=sgn[0:n, h:h + 1], in1=scb[0:n, c0:c0 + wd],
                                                                                      op0=ALU.mult, op1=ALU.add), r=[r_, sgn, scb], w=[scb])
                            if kind == 'p':
                                c.op('dve', lambda e: e.tensor_reduce(out=lo[0:64], in_=scb[0:64, 0:Lk - 64], axis=AX.X, op=ALU.min), r=[scb], w=[lo])
                                c.op('dve', lambda e: e.tensor_reduce(out=lo[64:128], in_=scb[64:128, 0:Lk], axis=AX.X, op=ALU.min), r=[scb], w=[lo])
                                c.op('dve', lambda e: e.memset(scb[0:64, Lk - 64:Lk], -1.0e30), w=[scb])
                            else:
                                c.op('dve', lambda e: e.tensor_reduce(out=lo[0:n], in_=scb[0:n, 0:Lk], axis=AX.X, op=ALU.min), r=[scb], w=[lo])
                            c.op('dve', lambda e: e.tensor_reduce(out=hi[0:n], in_=scb[0:n, 0:Lk], axis=AX.X, op=ALU.max), r=[scb], w=[hi])
                            c.op('dve', lambda e: e.tensor_tensor(out=hi[0:n], in0=hi[0:n], in1=lo[0:n], op=ALU.subtract), r=[hi, lo], w=[hi])
                            c.op('dve', lambda e: e.tensor_scalar(out=wv[0:n], in0=pow2[0:n], scalar1=hi[0:n, 0:1], scalar2=None, op0=ALU.mult), r=[pow2, hi], w=[wv])
                            for k in range(NIT):
                                c.op('dve', lambda e: e.tensor_tensor(out=mid[0:n], in0=lo[0:n], in1=wv[0:n, k:k + 1], op=ALU.add), r=[lo, wv], w=[mid])
                                c.op('dve', lambda e: e.tensor_scalar(out=junk[0:n, 0:Lk], in0=scb[0:n, 0:Lk], scalar1=mid[0:n, 0:1], scalar2=0.0, op0=ALU.is_ge, op1=ALU.add, accum_out=cnt[0:n]),
                                     r=[scb, mid], w=[junk, cnt])
                                c.op('dve', lambda e: e.scalar_tensor_tensor(out=tmp1[0:n], in0=cnt[0:n], scalar=float(topk) - 0.5, in1=wv[0:n, k:k + 1], op0=ALU.is_ge, op1=ALU.mult),
                                     r=[cnt, wv], w=[tmp1])
                                c.op('dve', lambda e: e.tensor_tensor(out=lo[0:n], in0=lo[0:n], in1=tmp1[0:n], op=ALU.add), r=[lo, tmp1], w=[lo])
                            c.op('dve', lambda e: e.tensor_scalar(out=mb[0:n, 0:Lk], in0=scb[0:n, 0:Lk], scalar1=lo[0:n, 0:1], scalar2=-30000.0, op0=ALU.is_lt, op1=ALU.mult), r=[scb, lo], w=[mb])
                            nearw = min(256, Lk) if n == 128 else min(192, Lk)
                            near0 = Lk - nearw
                            nboff = (256 - nearw) if n == 128 else 0
                            if n == 64:
                                nboff = near0 - (pos0 - 128)
                            nkb = (Lk + 127) // 128
                            achunks = [(c0, min(512, near0 - c0), False) for c0 in range(0, near0, 512)] + [(near0, nearw, True)]
                            def att0(h):
                                g = h // 8
                                pb_ = pbuf[h % 2]; rs_ = rs[h % 2]; ri_ = rinv[h % 2]
                                for ci, (c0, wd, isnear) in enumerate(achunks):
                                    p_ = psd[dctr[0] % 4]; dctr[0] += 1

                                    def mm(e):
                                        e.matmul(p_[0:n, 0:wd], qbT[:, h, 0:n], KT[:, g, c0:c0 + wd], start=True, stop=False)
                                        ins = e.matmul(p_[0:n, 0:wd], identB[0:n, 0:n], mb[0:n, c0:c0 + wd], start=False, stop=(not isnear))
                                        if isnear:
                                            ins = e.matmul(p_[0:n, 0:wd], identB[0:n, 0:n], NBh[0:n, h, nboff:nboff + wd], start=False, stop=True)
                                        return ins
                                    c.op('pe', mm, r=[qbT, KT, mb, identB, NBh], w=[p_])
                                    c.op('act', lambda e: e.activation(out=pb_[0:n, c0:c0 + wd], in_=p_[0:n, 0:wd], func=AF.Exp, bias=CH[0:n, h:h + 1], accum_out=rs_[0:n, ci:ci + 1]),
                                         r=[p_, CH], w=[pb_, rs_])
                                c.op('dve', lambda e: e.tensor_reduce(out=ri_[0:n], in_=rs_[0:n, 0:len(achunks)], axis=AX.X, op=ALU.add), r=[rs_], w=[ri_])
                                c.op('dve', lambda e: e.reciprocal(out=ri_[0:n], in_=ri_[0:n]), r=[ri_], w=[ri_])
                                c.op('dve', lambda e: e.tensor_scalar(out=pb_[0:n, 0:Lk], in0=pb_[0:n, 0:Lk], scalar1=ri_[0:n, 0:1], scalar2=None, op0=ALU.mult), r=[pb_, ri_], w=[pb_])

                            def att1(h):
                                g = h // 8
                                pb_ = pbuf[h % 2]; pT_ = pT[h % 2]
                                for k4 in range(0, nkb, 4):
                                    ks = list(range(k4, min(nkb, k4 + 4)))
                                    p_ = pst[tctr[0] % 2]; tctr[0] += 1

                                    def mm(e):
                                        for j, kb_ in enumerate(ks):
                                            kw = min(128, Lk - kb_ * 128)
                                            ins = e.transpose(p_[0:kw, j, 0:n], pb_[0:n, kb_ * 128:kb_ * 128 + kw], identB[0:n, 0:n])
                                        return ins
                                    c.op('pe', mm, r=[pb_, identB], w=[p_])
                                    kwl = min(128, Lk - ks[-1] * 128)
                                    if kwl == 128:
                                        if (k4 // 4) % 2 == 0:
                                            c.op('act', lambda e: e.copy(out=pT_[:, k4:k4 + len(ks), 0:n], in_=p_[:, 0:len(ks), 0:n]), r=[p_], w=[pT_])
                                        else:
                                            c.op('dve', lambda e: e.tensor_copy(out=pT_[:, k4:k4 + len(ks), 0:n], in_=p_[:, 0:len(ks), 0:n]), r=[p_], w=[pT_])
                                    else:
                                        for j, kb_ in enumerate(ks):
                                            kw = min(128, Lk - kb_ * 128)
                                            c.op('act', lambda e: e.copy(out=pT_[0:kw, kb_, 0:n], in_=p_[0:kw, j, 0:n]), r=[p_], w=[pT_])
                                po = pso[(h // 4) % 2]

                                def mm(e):
                                    for kb_ in range(nkb):
                                        kw = min(128, Lk - kb_ * 128)
                                        ins = e.matmul(po[:, h % 4, 0:n], Vt[0:kw, kb_, g * 128:(g + 1) * 128], pT_[0:kw, kb_, 0:n], start=(kb_ == 0), stop=(kb_ == nkb - 1))
                                    return ins
                                c.op('pe', mm, r=[Vt, pT_], w=[po])
                                if h % 4 == 3:
                                    hg = h // 4
                                    o_ = obs[hg % 2]
                                    c.op('dve', lambda e: e.tensor_tensor(out=o_[:, :, 0:n], in0=po[:, :, 0:n], in1=zbT[:, hg * 4:hg * 4 + 4, 0:n], op=ALU.mult), r=[po, zbT], w=[o_])
                                    c.dma('pool', obT_d[hg * 512:(hg + 1) * 512, ts_].rearrange("(h p) t -> p h t", p=128), o_[:, :, 0:n], r=[o_])
                            for step in range(17):
                                if step < 16:
                                    att0(step)
                                if step >= 1:
                                    att1(step - 1)
                    c.barrier()

                chk('B')
                for (htok0, hn) in oparts:
                    with contextlib.ExitStack() as es:
                        m1 = SB(es, [128, KC, hn], BF16, "m1")
                        pp = [PS(es, [128, 512], F32) for _ in range(3)]
                        pctr = [0]
                        cchunks = [(c0, min(512, hn - c0)) for c0 in range(0, hn, 512)]
                        with contextlib.ExitStack() as es2:
                            big = SB(es2, [128, 16, hn], BF16, "big")
                            wstg = [SB(es2, [128, 16, 256], F32) for _ in range(2)]
                            wbf = [SB(es2, [128, 16, 256], BF16) for _ in range(2)]
                            sgs = [SB(es2, [128, 512], BF16) for _ in range(2)]
                            tmpb = [SB(es2, [128, 512], F32) for _ in range(2)]
                            wctr = [0]
                            for bi_, (srcd, wsrc, sgd) in enumerate(((ozT_d, w_a[l], sgaT_d), (obT_d, w_b[l], sgbT_d))):
                                for c0 in range(0, hn, 512):
                                    wd = min(512, hn - c0)
                                    c.dma('sp', big[:, :, c0:c0 + wd], srcd[:, htok0 + c0:htok0 + c0 + wd].rearrange("(k p) t -> p k t", p=128), w=[big])
                                for b2 in range(0, KC, 2):
                                    i = wctr[0] % 2; wctr[0] += 1
                                    st = wstg[i]; wb = wbf[i]
                                    c.dma('sp', st[:], wsrc[:, b2 * 128:b2 * 128 + 256].rearrange("(k p) n -> p k n", p=128), w=[st])
                                    c.op('dve', lambda e: e.tensor_copy(out=wb[:], in_=st[:]), r=[st], w=[wb])
                                    for bb in range(2):
                                        blk = b2 + bb
                                        for (c0, wd) in cchunks:
                                            p_ = pp[pctr[0] % 3]; s_ = sgs[pctr[0] % 2]; t_ = tmpb[pctr[0] % 2]; pctr[0] += 1
                                            c.dma('sp', s_[:, 0:wd], sgd[blk * 128:(blk + 1) * 128, htok0 + c0:htok0 + c0 + wd], w=[s_])

                                            def mm(e):
                                                for kc in range(16):
                                                    ins = e.matmul(p_[:, 0:wd], wb[:, kc, bb * 128:(bb + 1) * 128], big[:, kc, c0:c0 + wd], start=(kc == 0), stop=(kc == 15))
                                                return ins
                                            c.op('pe', mm, r=[wb, big], w=[p_])
                                            if bi_ == 0:
                                                c.op('dve', lambda e: e.tensor_tensor(out=m1[:, blk, c0:c0 + wd], in0=p_[:, 0:wd], in1=s_[:, 0:wd], op=ALU.mult), r=[p_, s_], w=[m1])
                                            else:
                                                c.op('dve', lambda e: e.tensor_tensor(out=t_[:, 0:wd], in0=p_[:, 0:wd], in1=s_[:, 0:wd], op=ALU.mult), r=[p_, s_], w=[t_])
                                                c.op('dve', lambda e: e.tensor_tensor(out=m1[:, blk, c0:c0 + wd], in0=m1[:, blk, c0:c0 + wd], in1=t_[:, 0:wd], op=ALU.add), r=[t_, m1], w=[m1])
                            c.barrier()
                        with contextlib.ExitStack() as es2:
                            Wo = SB(es2, [128, KC, D], BF16, "Wo")
                            wst2 = [SB(es2, [128, KC, 128], F32) for _ in range(2)]
                            for b1 in range(KC):
                                st = wst2[b1 % 2]
                                c.dma('sp', st[:], w_out[l, :, b1 * 128:b1 * 128 + 128].rearrange("(k p) n -> p k n", p=128), w=[st])
                                c.op('dve', lambda e: e.tensor_copy(out=Wo[:, :, b1 * 128:b1 * 128 + 128], in_=st[:]), r=[st], w=[Wo])
                            GATE = SB(es2, [128, D], F32, "GATE")
                            xt = [SB(es2, [128, D], F32) for _ in range(2)]
                            xo = [SB(es2, [128, D], F32) for _ in range(2)]
                            ssq = [SB(es2, [128, 1], F32) for _ in range(2)]
                            last_rows = None
                            ncg = max(1, D // 512)
                            cw = min(512, D)
                            for ti, (t0, parts) in enumerate(range_tiles(htok0, hn)):
                                x_ = xt[ti % 2]; xo_ = xo[ti % 2]; ss_ = ssq[ti % 2]
                                rows = tuple((idx if kind == 'p' else 2 + idx, p0, n) for (kind, idx, r0, n, p0) in parts)
                                if rows != last_rows:
                                    for (row, p0, n) in rows:
                                        c.dma('sp', GATE[p0:p0 + n, :], mod_d[row:row + 1, 2 * D:3 * D].partition_broadcast(n), w=[GATE])
                                    last_rows = rows
                                for (kind, idx, r0, n, p0) in parts:
                                    c.dma('sp', x_[p0:p0 + n, :], x_src(l, kind, idx, r0, n), w=[x_])
                                for cg in range(ncg):
                                    p_ = pp[pctr[0] % 3]; pctr[0] += 1
                                    cs = slice(cg * cw, (cg + 1) * cw)

                                    def mm(e):
                                        for kc in range(KC):
                                            ins = e.matmul(p_[:, 0:cw], m1[:, kc, t0:t0 + 128], Wo[:, kc, cs], start=(kc == 0), stop=(kc == KC - 1))
                                        return ins
                                    c.op('pe', mm, r=[m1, Wo], w=[p_])
                                    c.op('dve', lambda e: e.tensor_tensor(out=xo_[:, cs], in0=p_[:, 0:cw], in1=GATE[:, cs], op=ALU.mult), r=[p_, GATE], w=[xo_])
                                    c.op('dve', lambda e: e.tensor_tensor(out=xo_[:, cs], in0=xo_[:, cs], in1=x_[:, cs], op=ALU.add), r=[xo_, x_], w=[xo_])
                                if l < DEPTH - 1:
                                    c.dma('pool', xs_d[htok0 + t0:htok0 + t0 + 128, :], xo_[:], r=[xo_])
                                else:
                                    c.op('act', lambda e: e.activation(out=x_[:], in_=xo_[:], func=AF.Square, accum_out=ss_[:]), r=[xo_], w=[x_, ss_])
                                    c.op('act', lambda e: e.activation(out=ss_[:], in_=ss_[:], func=AF.Sqrt, scale=1.0 / D, bias=epsc[:]), r=[ss_, epsc], w=[ss_])
                                    c.op('dve', lambda e: e.reciprocal(out=ss_[:], in_=ss_[:]), r=[ss_], w=[ss_])
                                    c.op('dve', lambda e: e.scalar_tensor_tensor(out=xo_[:], in0=xo_[:], scalar=ss_[:, 0:1], in1=FNW[:], op0=ALU.mult, op1=ALU.mult), r=[xo_, ss_, FNW], w=[xo_])
                                    for (kind, idx, r0, n, p0) in parts:
                                        yo = yp if kind == 'p' else ys
                                        c.dma('pool', yo[idx, r0:r0 + n, :], xo_[p0:p0 + n, :], r=[xo_])
                        c.barrier()
        except _Stop:
            c.barrier()
            ges.pop_all()
            return nc
        c.barrier()
        c.es.close()
    return nc


def rel_bucket_np(rel):
    nb = 16
    max_exact = 8
    n = np.abs(rel)
    nf = np.maximum(n, 1).astype(np.float32)
    large = max_exact + (np.log(nf / max_exact) / np.float32(np.log(128 / max_exact)) * (nb - max_exact)).astype(np.int32)
    large = np.minimum(large, nb - 1)
    return np.where(rel > 0, nb, 0) + np.where(n < max_exact, n, large)


def make_ohr():
    r = np.arange(384) - 255
    b = rel_bucket_np(r)
    oh = np.zeros((32, 384), np.float32)
    oh[b, np.arange(384)] = 1.0
    return oh


_NC_CACHE = {}


def run(cfg, inputs, debug_outs=()):
    D, T, P, DEPTH = cfg['D'], cfg['T'], cfg['P'], cfg['DEPTH']
    key = (D, T, P, DEPTH, tuple(debug_outs))
    if key not in _NC_CACHE:
        _NC_CACHE[key] = build(cfg, debug_outs)
    nc = _NC_CACHE[key]
    f = lambda a: np.ascontiguousarray(np.asarray(a, dtype=np.float32))
    I = {k: f(v) for k, v in inputs.items()}
    ohr = make_ohr()
    in_maps = []
    for ci in range(NCORES):
        b = slice(2 * ci, 2 * ci + 2)
        m = {
            "xp": f(I['x_prompt'][b]), "xs": f(I['x_sample'][b]),
            "c4": f(np.concatenate([I['c_prompt'][b], I['c_sample'][b]], axis=0)),
            "ck": f(I['cache_k'][:, b].reshape(DEPTH, 2, P, 256)), "cv": f(I['cache_v'][:, b].reshape(DEPTH, 2, P, 256)),
            "cik": f(I['cache_idx_k'][:, b]), "sg": f(I['state_gdn'][:, b]), "scv": f(I['state_conv'][:, b]),
            "ohr": ohr,
        }
        for k in ("norm_w", "w_ada", "b_ada", "w_in", "w_conv", "a_log", "dt_bias", "gdn_norm_w", "w_branch_a", "w_branch_b", "w_out", "rel_bias", "final_norm_w"):
            m[k] = I[k]
        in_maps.append(m)
    res = run_bass_kernel_spmd(nc, in_maps, core_ids=list(range(NCORES)))
    R = res.results
    cat0 = lambda k: np.concatenate([r[k] for r in R], axis=0)
    cat1 = lambda k: np.concatenate([r[k] for r in R], axis=1)
    B = 2 * NCORES
    outs = (
        cat0("yp"), cat0("ys"),
        cat1("nkp").reshape(DEPTH, B, T, 2, 128), cat1("nvp").reshape(DEPTH, B, T, 2, 128), cat1("nikp"),
        cat1("ngp"), cat1("ncp"),
        cat1("nks").reshape(DEPTH, B, 64, 2, 128), cat1("nvs").reshape(DEPTH, B, 64, 2, 128), cat1("niks"),
        cat1("ngs"), cat1("ncs"),
    )
    if debug_outs:
        return outs, R
    return outs


def kernel(**inputs):
    cfg = dict(D=2048, T=2048, P=4096, DEPTH=2)
    return run(cfg, inputs)
```
